# Optimizing a Trainium2 kernel written in Bass

```python
import math
import jax
import jax.numpy as jnp
from jax import lax
import numpy as np

D_MODEL = 1024
BATCH = 8
SEQ = 4096
DEPTH = 4

CTX_LEN = 256
GRID_W = 64
N_MOD = 6
CONV_W = 512
CONV_K = 3
SSM_W = 512
SSM_GROUP = 16
SSM_GROUPS = SSM_W // SSM_GROUP
SSM_STATE = 64
NA_HEADS = 8
NA_HEAD_DIM = 64
NA_W = NA_HEADS * NA_HEAD_DIM
WIN_H = 8
WIN_W = 16
MLP_HIDDEN = 4 * D_MODEL
N_BRANCH = 3
IN_SIZES = (CONV_W, CONV_W, CONV_W, SSM_W, NA_W, NA_W, NA_W, D_MODEL, D_MODEL, D_MODEL)
IN_OFF = tuple(sum(IN_SIZES[:i]) for i in range(len(IN_SIZES) + 1))
IN_PROJ_W = IN_OFF[-1]
RMS_EPS = 1e-6
NEG_INF = -1e30
S5_MIN_DECAY = 1e-4

kernel_name = 'hybrid_conv_s5_natten_prefix_dit_trunk'


def rmsnorm(x, g):
    xf = x.astype(jnp.float32)
    y = xf * lax.rsqrt(jnp.mean(xf * xf, axis=-1, keepdims=True) + RMS_EPS)
    return (y * g.astype(jnp.float32)).astype(x.dtype)


def modulate(h, shift, scale):
    return h * (1.0 + scale) + shift


def adaln(cond, w_mod, b_mod, n):
    m = jax.nn.silu(cond) @ w_mod[:, :n * D_MODEL] + b_mod[:n * D_MODEL]
    m = m.reshape(-1, n, D_MODEL)
    return [m[:, i:i + 1, :] for i in range(n)]


def split_in(p):
    return [p[..., IN_OFF[i]:IN_OFF[i + 1]] for i in range(len(IN_SIZES))]


def depthwise_conv3(u, w):
    up = jnp.pad(u, ((0, 0), (1, 1), (0, 0)))
    return up[:, :-2] * w[0] + up[:, 1:-1] * w[1] + up[:, 2:] * w[2]


def short_conv_branch(xa, b_gate, c_gate, conv_w, w_out):
    return (b_gate * depthwise_conv3(c_gate * xa, conv_w)) @ w_out


def s5_discretize(lam_re, lam_im, log_step, b_re, b_im, c_re, c_im):
    f32 = jnp.float32
    lam = lax.complex(jnp.minimum(lam_re.astype(f32), -S5_MIN_DECAY), lam_im.astype(f32))
    lam_dt = lam * jnp.exp(log_step.astype(f32))[..., None]
    lam_bar = jnp.exp(lam_dt)
    b_bar = ((lam_bar - 1.0) / lam)[..., None] * lax.complex(b_re.astype(f32), b_im.astype(f32))
    c_mat = lax.complex(c_re.astype(f32), c_im.astype(f32))
    return lam_dt, lam_bar, b_bar, c_mat


def s5_drive(u, b_bar_d):
    ug = u.astype(jnp.float32).reshape(u.shape[:2] + (SSM_GROUPS, SSM_GROUP))
    return jnp.einsum('blgn,gpn->blgp', ug.astype(jnp.complex64), b_bar_d)


def _scan_combine(left, right):
    a_l, b_l = left
    a_r, b_r = right
    return a_r * a_l, a_r * b_l + b_r


def diag_scan(a, bu):
    a_seq = jnp.broadcast_to(a, bu.shape)
    return lax.associative_scan(_scan_combine, (a_seq, bu), axis=1)[1]


def s5_context_states(uc, s5p):
    _, lam_bar, b_bar, _ = s5p
    h_f = diag_scan(lam_bar[0], s5_drive(uc, b_bar[0]))
    h_b = jnp.flip(diag_scan(lam_bar[1], jnp.flip(s5_drive(uc, b_bar[1]), 1)), 1)
    return h_f, h_b


def s5_latent_states(ux, s5p, h0_f, h0_b):
    lam_dt, lam_bar, b_bar, _ = s5p
    steps = jnp.arange(1, ux.shape[1] + 1, dtype=jnp.float32)[:, None, None]
    h_f = diag_scan(lam_bar[0], s5_drive(ux, b_bar[0])) + jnp.exp(lam_dt[0] * steps) * h0_f[:, None]
    h_b = diag_scan(lam_bar[1], jnp.flip(s5_drive(ux, b_bar[1]), 1)) + jnp.exp(lam_dt[1] * steps) * h0_b[:, None]
    return h_f, jnp.flip(h_b, 1)


def s5_output(u, h_f, h_b, c_mat, d_skip, w_glu_a, w_glu_b):
    y = jnp.real(jnp.einsum('blgp,gnp->blgn', h_f, c_mat[0]) + jnp.einsum('blgp,gnp->blgn', h_b, c_mat[1]))
    y = y.reshape(u.shape).astype(u.dtype) + d_skip * u
    g = jax.nn.gelu(y)
    return (g @ w_glu_a) * jax.nn.sigmoid(g @ w_glu_b)


def to_heads(t):
    return t.reshape(t.shape[:2] + (NA_HEADS, NA_HEAD_DIM))


def context_attention(qc, kc, vc):
    scale = NA_HEAD_DIM ** -0.5
    s = jnp.einsum('bqhd,bkhd->bhqk', to_heads(qc), to_heads(kc)).astype(jnp.float32) * scale
    p = jax.nn.softmax(s, axis=-1).astype(vc.dtype)
    o = jnp.einsum('bhqk,bkhd->bqhd', p, to_heads(vc))
    return o.reshape(qc.shape[:2] + (NA_W,))


def neighbourhood_attention(q, k, v, kc, vc, rpb):
    bsz, seq = q.shape[:2]
    rows = seq // GRID_W
    win_h = min(WIN_H, rows)
    n_loc = win_h * GRID_W
    scale = NA_HEAD_DIM ** -0.5
    grid = (bsz, rows, GRID_W, NA_HEADS, NA_HEAD_DIM)
    qg, kg, vg = q.reshape(grid), k.reshape(grid), v.reshape(grid)
    kch, vch = to_heads(kc), to_heads(vc)
    col = jnp.arange(GRID_W)
    col_start = jnp.clip(col - WIN_W // 2, 0, GRID_W - WIN_W)
    key_in = (col[None, :] >= col_start[:, None]) & (col[None, :] < col_start[:, None] + WIN_W)
    dc_idx = jnp.clip(col[None, :] - col[:, None] + WIN_W - 1, 0, 2 * WIN_W - 2)
    rpb_cols = rpb[:, :, dc_idx]

    def row_block(r):
        r0 = jnp.clip(r - win_h // 2, 0, rows - win_h)
        q_r = lax.dynamic_index_in_dim(qg, r, axis=1, keepdims=False)
        k_r = lax.dynamic_slice_in_dim(kg, r0, win_h, axis=1)
        v_r = lax.dynamic_slice_in_dim(vg, r0, win_h, axis=1)
        dr = r0 + jnp.arange(win_h) - r
        bias = jnp.transpose(rpb_cols[:, dr + WIN_H - 1], (0, 2, 1, 3))
        s_loc = jnp.einsum('bqhd,bwkhd->bhqwk', q_r, k_r).astype(jnp.float32) * scale + bias
        s_loc = jnp.where(key_in[:, None, :], s_loc, NEG_INF)
        s_ctx = jnp.einsum('bqhd,bchd->bhqc', q_r, kch).astype(jnp.float32) * scale
        s = jnp.concatenate([s_loc.reshape(bsz, NA_HEADS, GRID_W, n_loc), s_ctx], axis=-1)
        p = jax.nn.softmax(s, axis=-1).astype(v.dtype)
        p_loc = p[..., :n_loc].reshape(bsz, NA_HEADS, GRID_W, win_h, GRID_W)
        return (jnp.einsum('bhqwk,bwkhd->bqhd', p_loc, v_r)
                + jnp.einsum('bhqc,bchd->bqhd', p[..., n_loc:], vch))

    o = lax.map(row_block, jnp.arange(rows))
    return jnp.moveaxis(o, 0, 1).reshape(bsz, seq, NA_W)


def gated_merge(ya, yb, yc, ga, gb, gc, w_out):
    return (jax.nn.sigmoid(ga) * ya + jax.nn.sigmoid(gb) * yb + jax.nn.sigmoid(gc) * yc) @ w_out


def sqrelu_mlp(h, w1, w2):
    return jnp.square(jax.nn.relu(h @ w1)) @ w2


def setup_inputs(seed: int = 0) -> dict:
    key = jax.random.key(seed)
    ks = iter(jax.random.split(key, 32))
    f32 = jnp.float32

    def nrm(shape, scale):
        return jax.random.normal(next(ks), shape, f32) * scale

    L, G, P, N = DEPTH, SSM_GROUPS, SSM_STATE, SSM_GROUP
    x = nrm((BATCH, SEQ, D_MODEL), 1.0)
    c = nrm((BATCH, D_MODEL), 1.0)
    ctx = nrm((BATCH, CTX_LEN, D_MODEL), 1.0)
    c_ctx = nrm((D_MODEL,), 1.0)
    w_mod = nrm((L, D_MODEL, N_MOD * D_MODEL), D_MODEL ** -0.5)
    b_mod = nrm((L, N_MOD * D_MODEL), 0.01)
    norm1_g = 1.0 + nrm((L, D_MODEL), 0.01)
    w_in = nrm((L, D_MODEL, IN_PROJ_W), D_MODEL ** -0.5)
    conv_w = nrm((L, CONV_K, CONV_W), CONV_K ** -0.5)
    conv_out = nrm((L, CONV_W, D_MODEL), CONV_W ** -0.5)
    s5_lam_re = -0.5 + nrm((L, 2, G, P), 0.01)
    s5_lam_im = math.pi * jnp.arange(P, dtype=f32) + nrm((L, 2, G, P), 0.01)
    s5_log_step = jax.random.uniform(next(ks), (L, 2, G), f32, math.log(1e-3), math.log(1e-1))
    s5_b_re = nrm((L, 2, G, P, N), (2 * N) ** -0.5)
    s5_b_im = nrm((L, 2, G, P, N), (2 * N) ** -0.5)
    s5_c_re = nrm((L, 2, G, N, P), (2 * P) ** -0.5)
    s5_c_im = nrm((L, 2, G, N, P), (2 * P) ** -0.5)
    s5_d = nrm((L, SSM_W), 1.0)
    s5_glu_a = nrm((L, SSM_W, D_MODEL), SSM_W ** -0.5)
    s5_glu_b = nrm((L, SSM_W, D_MODEL), SSM_W ** -0.5)
    na_rpb = nrm((L, NA_HEADS, 2 * WIN_H - 1, 2 * WIN_W - 1), 0.1)
    na_out = nrm((L, NA_W, D_MODEL), NA_W ** -0.5)
    w_out = nrm((L, D_MODEL, D_MODEL), D_MODEL ** -0.5)
    norm2_g = 1.0 + nrm((L, D_MODEL), 0.01)
    mlp_w1 = nrm((L, D_MODEL, MLP_HIDDEN), D_MODEL ** -0.5)
    mlp_w2 = nrm((L, MLP_HIDDEN, D_MODEL), MLP_HIDDEN ** -0.5)
    final_norm_g = 1.0 + nrm((D_MODEL,), 0.01)
    return {'x': x, 'c': c, 'ctx': ctx, 'c_ctx': c_ctx, 'w_mod': w_mod, 'b_mod': b_mod,
            'norm1_g': norm1_g, 'w_in': w_in, 'conv_w': conv_w, 'conv_out': conv_out,
            's5_lam_re': s5_lam_re, 's5_lam_im': s5_lam_im, 's5_log_step': s5_log_step,
            's5_b_re': s5_b_re, 's5_b_im': s5_b_im, 's5_c_re': s5_c_re, 's5_c_im': s5_c_im,
            's5_d': s5_d, 's5_glu_a': s5_glu_a, 's5_glu_b': s5_glu_b, 'na_rpb': na_rpb,
            'na_out': na_out, 'w_out': w_out, 'norm2_g': norm2_g, 'mlp_w1': mlp_w1,
            'mlp_w2': mlp_w2, 'final_norm_g': final_norm_g}


def reference(x, c, ctx, c_ctx, w_mod, b_mod, norm1_g, w_in, conv_w, conv_out,
              s5_lam_re, s5_lam_im, s5_log_step, s5_b_re, s5_b_im, s5_c_re, s5_c_im,
              s5_d, s5_glu_a, s5_glu_b, na_rpb, na_out, w_out, norm2_g, mlp_w1, mlp_w2,
              final_norm_g):
    cx = ctx
    for l in range(DEPTH):
        ctx_out = l < DEPTH - 1
        mx = adaln(c, w_mod[l], b_mod[l], N_MOD)
        mc = adaln(c_ctx, w_mod[l], b_mod[l], N_MOD if ctx_out else 2)
        hx = modulate(rmsnorm(x, norm1_g[l]), mx[0], mx[1])
        hc = modulate(rmsnorm(cx, norm1_g[l]), mc[0], mc[1])
        s5p = s5_discretize(s5_lam_re[l], s5_lam_im[l], s5_log_step[l],
                            s5_b_re[l], s5_b_im[l], s5_c_re[l], s5_c_im[l])
        xa, xb, xcg, ux, qx, kx, vx, gax, gbx, gcx = split_in(hx @ w_in[l])
        if ctx_out:
            ca, cb, ccg, uc, qc, kc, vc, gac, gbc, gcc = split_in(hc @ w_in[l])
        else:
            uc = hc @ w_in[l][:, IN_OFF[3]:IN_OFF[4]]
            kc, vc = jnp.split(hc @ w_in[l][:, IN_OFF[5]:IN_OFF[7]], 2, axis=-1)
        hc_f, hc_b = s5_context_states(uc, s5p)
        hx_f, hx_b = s5_latent_states(ux, s5p, hc_f[:, -1], hc_b[:, 0])
        ya = short_conv_branch(xa, xb, xcg, conv_w[l], conv_out[l])
        yb = s5_output(ux, hx_f, hx_b, s5p[3], s5_d[l], s5_glu_a[l], s5_glu_b[l])
        yc = neighbourhood_attention(qx, kx, vx, kc, vc, na_rpb[l]) @ na_out[l]
        x_mix = gated_merge(ya, yb, yc, gax, gbx, gcx, w_out[l])
        if ctx_out:
            ya_c = short_conv_branch(ca, cb, ccg, conv_w[l], conv_out[l])
            yb_c = s5_output(uc, hc_f, hc_b, s5p[3], s5_d[l], s5_glu_a[l], s5_glu_b[l])
            yc_c = context_attention(qc, kc, vc) @ na_out[l]
            c_mix = gated_merge(ya_c, yb_c, yc_c, gac, gbc, gcc, w_out[l])
            cx = cx + mc[2] * c_mix
            cx = cx + mc[5] * sqrelu_mlp(modulate(rmsnorm(cx, norm2_g[l]), mc[3], mc[4]),
                                         mlp_w1[l], mlp_w2[l])
        x = x + mx[2] * x_mix
        x = x + mx[5] * sqrelu_mlp(modulate(rmsnorm(x, norm2_g[l]), mx[3], mx[4]),
                                   mlp_w1[l], mlp_w2[l])
    return rmsnorm(x, final_norm_g)
```

```python
import math
from contextlib import ExitStack
import numpy as np
import concourse.bass as bass
import concourse.mybir as mybir
from concourse.bass_utils import run_bass_kernel_spmd

F32 = mybir.dt.float32
BF16 = mybir.dt.bfloat16
I32 = mybir.dt.int32
ALU = mybir.AluOpType
AF = mybir.ActivationFunctionType

SEM_LIMIT = 30000
SAME_ENG_WAITS = True
N_DMA_SEMS = 40

DEPTH = 4
D = 1024
T = 4352
CTX = 256
SEQ = 4096
NIN = 6656
OFF_XA, OFF_XB, OFF_XC, OFF_U, OFF_Q, OFF_K, OFF_V, OFF_GA, OFF_GB, OFF_GC = (
    0, 512, 1024, 1536, 2048, 2560, 3072, 3584, 4608, 5632)
EPS = 1e-6
NTYPE = 21
LCH = 256
NCH = T // LCH


class Prog:
    def __init__(self, nc):
        self.nc = nc
        self.ops = []
        self.last_w = {}
        self.readers = {}
        self.engs = {'pe': nc.tensor, 'dve': nc.vector, 'act': nc.scalar,
                     'pool': nc.gpsimd, 'sp': nc.sync}
        self._bar_from = 0

    def op(self, eng, fn, r=(), w=(), dma=False):
        i = len(self.ops)
        deps = set()
        for k in r:
            if k in self.last_w:
                deps.add(self.last_w[k])
        for k in w:
            if k in self.last_w:
                deps.add(self.last_w[k])
            for j in self.readers.get(k, ()):
                deps.add(j)
        fd = set()
        for j in deps:
            oj = self.ops[j]
            if oj['eng'] == eng and not oj['dma'] and not dma:
                if eng == 'pe' or not SAME_ENG_WAITS:
                    continue
                israw = any(self.last_w.get(k) == j for k in list(r) + list(w))
                if not israw:
                    continue
            fd.add(j)
        for k in w:
            self.last_w[k] = i
            self.readers[k] = []
        for k in r:
            self.readers.setdefault(k, []).append(i)
        self.ops.append(dict(eng=eng, fn=fn, deps=fd, dma=dma, sig=False))
        return i

    def dma(self, q, out, in_, r=(), w=(), **kw):
        return self.op(q, lambda e: e.dma_start(out=out, in_=in_, **kw), r, w, dma=True)

    def barrier(self):
        deps = set()
        lastc = {}
        for idx, o in enumerate(self.ops):
            if o['fn'] is None:
                continue
            if o['dma']:
                if idx >= self._bar_from:
                    deps.add(idx)
            else:
                lastc[o['eng']] = idx
        deps |= set(lastc.values())
        self._bar_from = len(self.ops)
        for e in self.engs:
            self.ops.append(dict(eng=e, fn=None, deps=set(deps), dma=False, sig=False))
        self.last_w = {}
        self.readers = {}

    def emit(self):
        nc = self.nc
        ops = self.ops
        for o in ops:
            for j in o['deps']:
                ops[j]['sig'] = True
        dma_sems = [nc.alloc_semaphore(name=f"dq{i}") for i in range(N_DMA_SEMS)]
        dma_cnt = [0] * N_DMA_SEMS
        dma_last = [None] * N_DMA_SEMS
        eng_sem = {}
        eng_cnt = {}
        nsem = [0]

        def new_eng_sem(e):
            nsem[0] += 1
            eng_sem[e] = nc.alloc_semaphore(name=f"s_{e}_{nsem[0]}")
            eng_cnt[e] = 0

        for e in self.engs:
            new_eng_sem(e)
        rr = 0
        for idx, o in enumerate(ops):
            if o['dma']:
                s = rr % N_DMA_SEMS
                rr += 1
                if dma_cnt[s] + 16 > SEM_LIMIT:
                    if dma_last[s] is not None:
                        o['deps'].add(dma_last[s])
                    dma_sems[s] = nc.alloc_semaphore(name=f"dq{s}_{idx}")
                    dma_cnt[s] = 0
                    dma_last[s] = None
                if dma_last[s] is not None:
                    o['deps'].add(dma_last[s])
                dma_cnt[s] += 16
                o['done'] = (dma_sems[s], dma_cnt[s])
                dma_last[s] = idx
            elif o['sig'] and o['fn'] is not None:
                e = o['eng']
                if eng_cnt[e] + 1 > SEM_LIMIT:
                    new_eng_sem(e)
                eng_cnt[e] += 1
                o['done'] = (eng_sem[e], eng_cnt[e])
            else:
                o['done'] = None

        def resolve(j, acc, seen):
            if j in seen:
                return
            seen.add(j)
            oj = ops[j]
            if oj['done'] is not None:
                acc.add(j)
            elif oj['fn'] is None:
                for jj in oj['deps']:
                    resolve(jj, acc, seen)
            else:
                raise RuntimeError("dep on unsignaled op")

        per_eng = {e: [] for e in self.engs}
        for idx, o in enumerate(ops):
            per_eng[o['eng']].append(idx)
        self.n_inst = {e: len(v) for e, v in per_eng.items()}
        with nc.Block() as block:
            def make(e):
                def body(eng):
                    waited = {}
                    for idx in per_eng[e]:
                        o = ops[idx]
                        acc = set()
                        seen = set()
                        for j in o['deps']:
                            resolve(j, acc, seen)
                        need = {}
                        for j in acc:
                            sem, val = ops[j]['done']
                            key = id(sem)
                            if waited.get(key, 0) >= val:
                                continue
                            if key not in need or need[key][1] < val:
                                need[key] = (sem, val)
                        for key, (sem, val) in need.items():
                            eng.wait_ge(sem, val)
                            waited[key] = val
                        if o['fn'] is not None:
                            inst = o['fn'](eng)
                            if o['done'] is not None:
                                sem, val = o['done']
                                inst.then_inc(sem, 16 if o['dma'] else 1)
                return body
            block.tensor(make('pe'))
            block.vector(make('dve'))
            block.scalar(make('act'))
            block.gpsimd(make('pool'))
            block.sync(make('sp'))


def _na_tile_plan():
    types = {}
    plan = []
    for j in range(32):
        r0a = min(max(2 * j - 4, 0), 56)
        r0b = min(max(2 * j + 1 - 4, 0), 56)
        tlo = r0a // 2
        thi = (r0b + 7) // 2
        lst = []
        for t in range(tlo, thi + 1):
            key = (t - j, r0a - 2 * j, r0b - 2 * j)
            if key not in types:
                types[key] = len(types)
            lst.append((t, types[key]))
        plan.append(lst)
    return plan, types


def _na_bias_index():
    plan, types = _na_tile_plan()
    nt = len(types)
    idx_r = np.zeros((nt, 128, 128), np.int64)
    idx_c = np.zeros((nt, 128, 128), np.int64)
    mask = np.zeros((nt, 128, 128), np.float32)
    col = np.arange(64)
    cs = np.clip(col - 8, 0, 48)
    for (delta, ra, rb), ty in types.items():
        for qr2 in range(2):
            r0rel = (ra, rb)[qr2]
            for kr2 in range(2):
                krel = 2 * delta + kr2
                dr = krel - qr2
                row_ok = (krel >= r0rel) and (krel < r0rel + 8)
                for qc in range(64):
                    kc = col
                    ok = row_ok & (kc >= cs[qc]) & (kc < cs[qc] + 16)
                    q = qr2 * 64 + qc
                    k = kr2 * 64 + kc
                    idx_r[ty, k, q] = np.clip(dr + 7, 0, 14)
                    idx_c[ty, k, q] = np.clip(kc - qc + 15, 0, 30)
                    mask[ty, k, q] = np.where(ok, 0.0, -1e30)
    return plan, nt, idx_r, idx_c, mask


_NA = None


def _na():
    global _NA
    if _NA is None:
        _NA = _na_bias_index()
    return _NA


def _prep_shared(inp):
    L = DEPTH
    sh = {}
    f = lambda a: np.ascontiguousarray(a, dtype=np.float32)
    for k in ('w_mod', 'w_in', 'conv_out', 's5_glu_a', 's5_glu_b', 'na_out', 'w_out',
              'mlp_w1', 'mlp_w2'):
        sh[k] = f(inp[k])
    sh['b_mod'] = f(inp['b_mod'])
    sh['norm1_g'] = f(inp['norm1_g'])
    sh['norm2_g'] = f(inp['norm2_g'])
    sh['final_g'] = f(inp['final_norm_g']).reshape(1, D)
    sh['conv_w'] = f(inp['conv_w'].reshape(L, 3, 4, 128).transpose(0, 3, 2, 1))
    def gp(a):
        a = a.reshape(L, 2, 16, 2, 64)
        return a.transpose(0, 3, 4, 1, 2).reshape(L, 128, 32)
    ls = np.broadcast_to(inp['s5_log_step'][:, :, :, None], (L, 2, 32, 64))
    sh['s5p'] = f(np.stack([gp(inp['s5_lam_re']), gp(inp['s5_lam_im']), gp(ls)], axis=2))
    def bt(B):
        out = np.zeros((L, 2, 16, 128, 128), np.float32)
        for i in range(16):
            for g2 in range(2):
                g = 2 * i + g2
                gl = g % 8
                out[:, :, i, gl * 16:(gl + 1) * 16, g2 * 64:(g2 + 1) * 64] = \
                    B[:, :, g].transpose(0, 1, 3, 2)
        return out
    sh['s5BT'] = f(np.stack([bt(inp['s5_b_re']), bt(inp['s5_b_im'])], axis=3))
    def cm(C):
        out = np.zeros((L, 2, 16, 128, 128), np.float32)
        for i in range(16):
            for g2 in range(2):
                g = 2 * i + g2
                gl = g % 8
                out[:, :, i, g2 * 64:(g2 + 1) * 64, gl * 16:(gl + 1) * 16] = \
                    C[:, :, g].transpose(0, 1, 3, 2)
        return out
    sh['s5CM'] = f(np.stack([cm(inp['s5_c_re']), cm(inp['s5_c_im'])], axis=3))
    sh['s5d'] = f(inp['s5_d'].reshape(L, 4, 128).transpose(0, 2, 1))
    plan, nt, idx_r, idx_c, mask = _na()
    rpb = inp['na_rpb']
    sh['rpbg'] = f(rpb[:, :, idx_r, idx_c])
    sh['na_mask'] = f(mask)
    sh['ident'] = np.eye(128, dtype=np.float32)
    return sh


class Builder:
    def __init__(self, layers=DEPTH, dbg=(), stop=None, only=None):
        self.layers = layers
        self.stop = stop
        self.only = only
        self.dbg = set(dbg)
        nc = bass.Bass("TRN2", target_bir_lowering=False)
        self.nc = nc
        self.P = Prog(nc)
        self.inputs = {}
        self.cnt = {}

    def din(self, name, shape, dt=F32):
        t = self.nc.dram_tensor(name, list(shape), dt, kind="ExternalInput")
        self.inputs[name] = t
        return t

    def dscr(self, name, shape, dt):
        kind = "ExternalOutput" if name in self.dbg else "Internal"
        return self.nc.dram_tensor(name, list(shape), dt, kind=kind)

    def uid(self):
        self._uid = getattr(self, '_uid', 0) + 1
        return self._uid

    def rot(self, name, n):
        c = self.cnt.get(name, 0)
        self.cnt[name] = c + 1
        return c % n

    def mm(self, out, lhsT, rhs, start, stop, r, w):
        self.P.op('pe', lambda e: e.matmul(out, lhsT, rhs, start=start, stop=stop), r, w)

    def act(self, out, in_, func, r, w, **kw):
        self.P.op('act', lambda e: e.activation(out, in_, func, **kw), r, w)

    def tt(self, eng, out, a, b, op, r, w):
        self.P.op(eng, lambda e: e.tensor_tensor(out, a, b, op), r, w)

    def ts(self, eng, out, a, s1, s2, op0, op1, r, w):
        if op1 is None:
            self.P.op(eng, lambda e: e.tensor_scalar(out, a, s1, None, op0), r, w)
        else:
            self.P.op(eng, lambda e: e.tensor_scalar(out, a, s1, s2, op0, op1), r, w)

    def stt(self, out, a, s, b, op0, op1, r, w):
        self.P.op('dve', lambda e: e.scalar_tensor_tensor(out, a, s, b, op0, op1), r, w)

    def cp(self, eng, out, in_, r, w):
        if eng == 'act':
            self.P.op('act', lambda e: e.activation(out, in_, AF.Copy), r, w)
        else:
            self.P.op(eng, lambda e: e.tensor_copy(out, in_), r, w)

    def build(self):
        nc, P = self.nc, self.P
        L = DEPTH
        nt = _na()[1]
        self.nt = nt
        I = {}
        I['xin'] = self.din('xin', [T, D])
        I['cT'] = self.din('cT', [128, 16])
        I['w_mod'] = self.din('w_mod', [L, D, 6 * D])
        I['b_mod'] = self.din('b_mod', [L, 6 * D])
        I['norm1_g'] = self.din('norm1_g', [L, D])
        I['norm2_g'] = self.din('norm2_g', [L, D])
        I['final_g'] = self.din('final_g', [1, D])
        I['w_in'] = self.din('w_in', [L, D, NIN])
        I['conv_w'] = self.din('conv_w', [L, 128, 4, 3])
        I['conv_out'] = self.din('conv_out', [L, 512, D])
        I['s5p'] = self.din('s5p', [L, 128, 3, 32])
        I['s5BT'] = self.din('s5BT', [L, 2, 16, 2, 128, 128])
        I['s5CM'] = self.din('s5CM', [L, 2, 16, 2, 128, 128])
        I['s5d'] = self.din('s5d', [L, 128, 4])
        I['s5_glu_a'] = self.din('s5_glu_a', [L, 512, D])
        I['s5_glu_b'] = self.din('s5_glu_b', [L, 512, D])
        I['rpbg'] = self.din('rpbg', [L, 8, nt, 128, 128])
        I['na_mask'] = self.din('na_mask', [nt, 128, 128])
        I['na_out'] = self.din('na_out', [L, 512, D])
        I['w_out'] = self.din('w_out', [L, D, D])
        I['mlp_w1'] = self.din('mlp_w1', [L, D, 4 * D])
        I['mlp_w2'] = self.din('mlp_w2', [L, 4 * D, D])
        I['ident'] = self.din('ident', [128, 128])
        self.I = I
        self.out = nc.dram_tensor('out', [SEQ, D], F32, kind="ExternalOutput")
        S = {}
        S['xres'] = self.dscr('xres', [T, D], F32)
        S['hxT'] = self.dscr('hxT', [D, T], BF16)
        S['h2T'] = self.dscr('h2T', [D, T], BF16)
        S['convbT'] = self.dscr('convbT', [512, T], BF16)
        S['gT'] = self.dscr('gT', [512, T], BF16)
        S['attnT'] = self.dscr('attnT', [512, T], BF16)
        S['modv'] = self.dscr('modv', [L, 2, 6 * D], F32)
        S['mgT'] = self.dscr('mgT', [D, T], BF16)
        S['hidT'] = self.dscr('hidT', [4 * D, T], BF16)
        self.S = S

        with ExitStack() as gs:
            self.ps = [gs.enter_context(nc.psum_tensor(f"ps{i}", [128, 512], F32))
                       for i in range(7)]
            self.psT = gs.enter_context(nc.psum_tensor("psT", [128, 1024], BF16))
            self.ident = gs.enter_context(nc.sbuf_tensor("ident_sb", [128, 128], BF16))
            self.ones = gs.enter_context(nc.sbuf_tensor("ones_sb", [128, 128], BF16))
            P.dma('pool', self.ident[:], I['ident'].ap(), w=['ident'])
            P.op('dve', lambda e: e.memset(self.ones[:], 1.0), w=['ones'])
            P.barrier()
            seq = [('adaln', lambda: self.phase_adaln()),
                   ('norm', lambda: self.phase_norm(0, src=I['xin'], kind='n1'))]
            for l in range(self.layers):
                seq += [(f'conv{l}', lambda l=l: self.phase_conv(l)),
                        (f'attn{l}', lambda l=l: self.phase_attn(l)),
                        (f's5{l}', lambda l=l: self.phase_s5(l)),
                        (f'merge{l}', lambda l=l: self.phase_merge(l, src=(I['xin'] if l == 0 else S['xres']))),
                        (f'mlp{l}', lambda l=l: self.phase_mlp(l))]
            for name, fn in seq:
                if self.only is not None and name not in self.only:
                    continue
                fn()
                P.barrier()
                if name == self.stop:
                    break
            P.emit()
        return nc

    def phase_adaln(self):
        nc, P, I, S = self.nc, self.P, self.I, self.S
        with ExitStack() as st:
            sb = lambda n, s, d: st.enter_context(nc.sbuf_tensor(f"{n}_u{self.uid()}", s, d))
            cT = sb("ad_cT", [128, 16], F32)
            sil = sb("ad_sil", [128, 16], BF16)
            wt = [sb(f"ad_w{i}", [128, 8, 512], BF16) for i in range(3)]
            bm = sb("ad_bm", [2, 6 * D], F32)
            row = [sb(f"ad_row{i}", [2, 512], F32) for i in range(2)]
            P.dma('sp', cT[:], I['cT'].ap(), w=['cT'])
            self.act(sil[:], cT[:], AF.Silu, r=['cT'], w=['sil'])
            silv = sil[:].rearrange("p (k j) -> p k j", j=2)
            for l in range(DEPTH):
                bsrc = bass.AP(I['b_mod'], l * 6 * D, [[0, 2], [1, 6 * D]])
                P.dma('sp', bm[:], bsrc, w=['bm'])
                wv = I['w_mod'].ap()[l].rearrange("(k p) c -> p k c", p=128)
                for ct in range(12):
                    s = self.rot('adw', 3)
                    P.dma('pool', wt[s][:], wv[:, :, ct * 512:(ct + 1) * 512], w=[('adw', s)])
                    pst = self.ps[ct % 2]
                    for k in range(8):
                        self.mm(pst[0:2, :], silv[:, k, :], wt[s][:, k, :], k == 0, k == 7,
                                r=['sil', ('adw', s)], w=[('ps', ct % 2)])
                    rs = self.rot('adrow', 2)
                    self.tt('dve', row[rs][:], pst[0:2, :], bm[:, ct * 512:(ct + 1) * 512], ALU.add,
                            r=[('ps', ct % 2), 'bm'], w=[('adrow', rs)])
                    P.dma('sp', S['modv'].ap()[l][:, ct * 512:(ct + 1) * 512], row[rs][:],
                          r=[('adrow', rs)])

    def load_rep(self, dst, tensor, offset, key):
        self.P.dma('sp', dst, bass.AP(tensor, offset, [[0, 128], [1, D]]), w=[key])

    def mod_tiles(self, st, l, cond, names, gsrc=None):
        nc = self.nc
        res = {}
        base = (l * 2 + cond) * 6 * D
        idx = {'S1': 0, 'G1': 1, 'A1': 2, 'S2': 3, 'G2': 4, 'A2': 5}
        for n in names:
            t = st.enter_context(nc.sbuf_tensor(f"mod_{n}_{cond}_{self.rot('modt', 1 << 30)}", [128, D], F32))
            key = ('mod', n, cond)
            self.load_rep(t[:], self.S['modv'], base + idx[n] * D, key)
            if n in ('G1', 'G2'):
                g = st.enter_context(nc.sbuf_tensor(f"modg_{n}_{cond}_{self.rot('modt', 1 << 30)}", [128, D], F32))
                gt = self.I['norm1_g'] if n == 'G1' else self.I['norm2_g']
                self.load_rep(g[:], gt, l * D, ('modg', n, cond))
                self.stt(t[:], t[:], 1.0, g[:], ALU.add, ALU.mult, r=[key, ('modg', n, cond)], w=[key])
            res[n] = (t, key)
        return res

    def norm_sub(self, xt, xkey, G, S_, hb, hkey, scr):
        junk, ss, rstd, tmp = scr['junk'], scr['ss'], scr['rstd'], scr['tmp']
        k = scr['k']
        self.act(junk[:], xt, AF.Square, r=[xkey], w=[('nj', k), ('ss', k)], accum_out=ss[:])
        self.ts('dve', rstd[:], ss[:], 1.0 / D, EPS, ALU.mult, ALU.add, r=[('ss', k)], w=[('rstd', k)])
        self.act(rstd[:], rstd[:], AF.Sqrt, r=[('rstd', k)], w=[('rstd', k)])
        self.P.op('dve', lambda e: e.reciprocal(rstd[:], rstd[:]), r=[('rstd', k)], w=[('rstd', k)])
        if S_ is None:
            self.stt(hb, xt, rstd[:], G[0][:], ALU.mult, ALU.mult, r=[xkey, ('rstd', k), G[1]], w=[hkey])
        else:
            self.stt(tmp[:], xt, rstd[:], G[0][:], ALU.mult, ALU.mult, r=[xkey, ('rstd', k), G[1]],
                     w=[('ntmp', k)])
            self.tt('pool', hb, tmp[:], S_[0][:], ALU.add, r=[('ntmp', k), S_[1]], w=[hkey])

    def norm_scratch(self, st, tag):
        nc = self.nc
        out = []
        for k in range(2):
            out.append(dict(
                junk=st.enter_context(nc.sbuf_tensor(f"{tag}_junk{k}_u{self.uid()}", [128, D], BF16)),
                ss=st.enter_context(nc.sbuf_tensor(f"{tag}_ss{k}_u{self.uid()}", [128, 1], F32)),
                rstd=st.enter_context(nc.sbuf_tensor(f"{tag}_rstd{k}_u{self.uid()}", [128, 1], F32)),
                tmp=st.enter_context(nc.sbuf_tensor(f"{tag}_tmp{k}_u{self.uid()}", [128, D], F32)),
                k=(tag, k)))
        return out

    def transpose_out(self, hb, hkey, hT, hTkey, sub):
        for kc in range(8):
            self.P.op('pe', lambda e, kc=kc: e.transpose(self.psT[:, kc * 128:(kc + 1) * 128],
                                                          hb[:, kc * 128:(kc + 1) * 128], self.ident[:]),
                      r=[hkey, 'ident'], w=['psT'])
        self.cp('act', hT[:, :, sub * 128:(sub + 1) * 128],
                self.psT[:].rearrange("p (k t) -> p k t", t=128), r=['psT'], w=[hTkey])

    TILES = [(0, 256)] + [(256 + 512 * i, 512) for i in range(8)]

    def phase_norm(self, l, src, kind):
        nc, P, I, S = self.nc, self.P, self.I, self.S
        with ExitStack() as st:
            sb = lambda n, s, d: st.enter_context(nc.sbuf_tensor(f"{n}_u{self.uid()}", s, d))
            mods = [self.mod_tiles(st, l, c, ('G1', 'S1')) for c in (0, 1)]
            xt = [sb(f"pn_x{i}", [128, D], F32) for i in range(3)]
            hb = [sb(f"pn_hb{i}", [128, D], BF16) for i in range(2)]
            hT = [sb(f"pn_hT{i}", [128, 8, 512], BF16) for i in range(2)]
            scr = self.norm_scratch(st, "pn")
            dst = S['hxT'].ap().rearrange("(k p) t -> p k t", p=128)
            for (t0, n) in self.TILES:
                cond = 1 if t0 < CTX else 0
                hs = self.rot('pn_hT', 2)
                for sub in range(n // 128):
                    xs = self.rot('pn_x', 3)
                    P.dma('sp', xt[xs][:], src.ap()[t0 + sub * 128:t0 + (sub + 1) * 128, :], w=[('pn_x', xs)])
                    bs = self.rot('pn_hb', 2)
                    self.norm_sub(xt[xs][:], ('pn_x', xs), mods[cond]['G1'], mods[cond]['S1'],
                                  hb[bs][:], ('pn_hb', bs), scr[bs])
                    self.transpose_out(hb[bs], ('pn_hb', bs), hT[hs], ('pn_hT', hs), sub)
                P.dma('act', dst[:, :, t0:t0 + n], hT[hs][:, :, 0:n], r=[('pn_hT', hs)])

    TT512 = [(512 * i, 512) for i in range(8)] + [(4096, 256)]

    def load_w(self, dst, src_ap, key, r=()):
        self.P.dma('pool', dst, src_ap, r=r, w=[key])

    def phase_conv(self, l):
        nc, P, I, S = self.nc, self.P, self.I, self.S
        with ExitStack() as st:
            sb = lambda n, s, d: st.enter_context(nc.sbuf_tensor(f"{n}_u{self.uid()}", s, d))
            wc = [sb(f"cv_w{i}", [128, 3, 8, 128], BF16) for i in range(2)]
            hx = [sb(f"cv_hx{i}", [128, 8, 512], BF16) for i in range(2)]
            vb = sb("cv_v", [128, T + 4], F32)
            xbb = sb("cv_xb", [128, T], F32)
            tmp = [sb(f"cv_tmp{i}", [128, 512], F32) for i in range(2)]
            acc = [sb(f"cv_acc{i}", [128, 1024], F32) for i in range(2)]
            ob = [sb(f"cv_o{i}", [128, T], BF16) for i in range(2)]
            cw = sb("cv_cw", [128, 12], F32)
            P.dma('sp', cw[:], I['conv_w'].ap()[l].rearrange("p q j -> p (q j)"), w=['cw'])
            P.op('dve', lambda e: e.memset(vb[:], 0.0), w=['vb'])
            hsrc = S['hxT'].ap().rearrange("(k p) t -> p k t", p=128)
            win = I['w_in'].ap()[l].rearrange("(k p) c -> p k c", p=128)
            for q in range(4):
                ws = self.rot('cv_w', 2)
                for j, off in enumerate((OFF_XA, OFF_XB, OFF_XC)):
                    self.load_w(wc[ws][:, j, :, :], win[:, :, off + q * 128: off + (q + 1) * 128], ('cv_w', ws, j))
                for (t0, n) in self.TT512:
                    hs = self.rot('cv_hx', 2)
                    P.dma('sp', hx[hs][:, :, 0:n], hsrc[:, :, t0:t0 + n], w=[('cv_hx', hs)])
                    for j in range(3):
                        for k in range(8):
                            self.mm(self.ps[j][:, 0:n], wc[ws][:, j, k, :], hx[hs][:, k, 0:n], k == 0, k == 7,
                                    r=[('cv_w', ws, j), ('cv_hx', hs)], w=[('ps', j)])
                    ts_ = self.rot('cv_tmp', 2)
                    self.cp('act', tmp[ts_][:, 0:n], self.ps[2][:, 0:n], r=[('ps', 2)], w=[('cv_tmp', ts_)])
                    segs = []
                    if t0 < CTX:
                        segs.append((t0, CTX - t0, 1 + t0))
                        segs.append((CTX, t0 + n - CTX, 3 + CTX))
                    else:
                        segs.append((t0, n, 3 + t0))
                    for (a0, an, c0) in segs:
                        self.tt('dve', vb[:, c0:c0 + an], self.ps[0][:, a0 - t0:a0 - t0 + an],
                                tmp[ts_][:, a0 - t0:a0 - t0 + an], ALU.mult,
                                r=[('ps', 0), ('cv_tmp', ts_), 'vb'], w=['vb'])
                    self.cp('act', xbb[:, t0:t0 + n], self.ps[1][:, 0:n], r=[('ps', 1)], w=['xbb'])
                os_ = self.rot('cv_o', 2)
                pieces = [(0, 256, 1)] + [(256 + 1024 * i, 1024, 3 + 256 + 1024 * i) for i in range(4)]
                for (a0, an, c0) in pieces:
                    as_ = self.rot('cv_acc', 2)
                    A = acc[as_][:, 0:an]
                    ak = ('cv_acc', as_)
                    self.ts('dve', A, vb[:, c0 - 1:c0 - 1 + an], cw[:, q * 3:q * 3 + 1], None, ALU.mult, None,
                            r=['vb', 'cw'], w=[ak])
                    self.stt(A, vb[:, c0:c0 + an], cw[:, q * 3 + 1:q * 3 + 2], A, ALU.mult, ALU.add,
                             r=['vb', 'cw', ak], w=[ak])
                    self.stt(A, vb[:, c0 + 1:c0 + 1 + an], cw[:, q * 3 + 2:q * 3 + 3], A, ALU.mult, ALU.add,
                             r=['vb', 'cw', ak], w=[ak])
                    self.tt('pool', ob[os_][:, a0:a0 + an], A, xbb[:, a0:a0 + an], ALU.mult,
                            r=[ak, 'xbb'], w=[('cv_o', os_)])
                P.dma('act', S['convbT'].ap()[q * 128:(q + 1) * 128, :], ob[os_][:], r=[('cv_o', os_)])


    def phase_attn(self, l):
        nc, P, I, S = self.nc, self.P, self.I, self.S
        plan, nt = _na()[0], self.nt
        last = (l == DEPTH - 1)
        with ExitStack() as st:
            sb = lambda n, s, d: st.enter_context(nc.sbuf_tensor(f"{n}_u{self.uid()}", s, d))
            wq = [sb(f"at_w{i}", [128, 3, 8, 128], BF16) for i in range(2)]
            hx = [sb(f"at_hx{i}", [128, 8, 512], BF16) for i in range(2)]
            qT = [sb(f"at_q{i}", [128, T], BF16) for i in range(2)]
            kT = [sb(f"at_k{i}", [128, T], BF16) for i in range(2)]
            V = [sb(f"at_v{i}", [128, 34, 128], BF16) for i in range(2)]
            aT = [sb(f"at_a{i}", [128, T], BF16) for i in range(2)]
            msk = sb("at_mask", [128, nt, 128], F32)
            rg = [sb(f"at_rg{i}", [128, 128], F32) for i in range(3)]
            bias = [sb(f"at_bias{i}", [128, 2, nt, 128], BF16) for i in range(2)]
            PT = [sb(f"at_pt{i}", [128, 128], BF16) for i in range(4)]
            rec = [sb(f"at_rec{i}", [128, 128], F32) for i in range(2)]
            P.dma('sp', msk[:], I['na_mask'].ap().rearrange("t k q -> k t q"), w=['msk'])
            hsrc = S['hxT'].ap().rearrange("(k p) t -> p k t", p=128)
            win = I['w_in'].ap()[l].rearrange("(k p) c -> p k c", p=128)
            for hp in range(4):
                ws = self.rot('at_w', 2)
                bsl = self.rot('at_b', 2)
                for j, off in enumerate((OFF_Q, OFF_K, OFF_V)):
                    self.load_w(wq[ws][:, j, :, :], win[:, :, off + hp * 128: off + (hp + 1) * 128], ('at_w', ws, j))
                for hh in range(2):
                    for ty in range(nt):
                        rs = self.rot('at_rg', 3)
                        P.dma('sp', rg[rs][:], I['rpbg'].ap()[l, hp * 2 + hh, ty], w=[('at_rg', rs)])
                        self.stt(bias[bsl][:, hh, ty, :], rg[rs][:], 8.0, msk[:, ty, :], ALU.mult, ALU.add,
                                 r=[('at_rg', rs), 'msk'], w=[('at_bias', bsl)])
                for (t0, n) in self.TT512:
                    hs = self.rot('at_hx', 2)
                    P.dma('sp', hx[hs][:, :, 0:n], hsrc[:, :, t0:t0 + n], w=[('at_hx', hs)])
                    for j, dstT in ((0, qT[bsl]), (1, kT[bsl])):
                        for k in range(8):
                            self.mm(self.ps[j][:, 0:n], wq[ws][:, j, k, :], hx[hs][:, k, 0:n], k == 0, k == 7,
                                    r=[('at_w', ws, j), ('at_hx', hs)], w=[('ps', j)])
                        self.cp('act' if j == 0 else 'dve', dstT[:, t0:t0 + n], self.ps[j][:, 0:n], r=[('ps', j)],
                                w=[('at_qk', bsl, j)])
                    for sub in range(n // 128):
                        for k in range(8):
                            self.mm(self.ps[0][:, sub * 128:(sub + 1) * 128], hx[hs][:, k, sub * 128:(sub + 1) * 128],
                                    wq[ws][:, 2, k, :], k == 0, k == 7,
                                    r=[('at_w', ws, 2), ('at_hx', hs)], w=[('ps', 0)])
                    ti0 = t0 // 128
                    self.cp('act', V[bsl][:, ti0:ti0 + n // 128, :],
                            self.ps[0][:, 0:n].rearrange("p (s c) -> p s c", c=128), r=[('ps', 0)], w=[('at_v', bsl)])
                qlist = []
                if not last:
                    for qi in range(2):
                        qlist.append((qi * 128, [(0, None), (128, None)]))
                for j in range(32):
                    kt = [(CTX + t * 128, ty) for (t, ty) in plan[j]] + [(0, None), (128, None)]
                    qlist.append((CTX + j * 128, kt))
                items = []
                for hh in range(2):
                    for (q0, kts) in qlist:
                        osl = self.rot('at_o', 2)
                        nk = len(kts)
                        for ki, (k0, ty) in enumerate(kts):
                            items.append(dict(hh=hh, q0=q0, ki=ki, nk=nk, k0=k0, ty=ty, osl=osl))

                def stage_s(it):
                    pb = it['hh'] * 64
                    ssl = self.rot('at_s', 3)
                    it['ssl'] = ssl
                    it['psl'] = self.rot('at_ptslot', 4)
                    psS = self.ps[ssl][:, 0:128]
                    skey = ('ps', ssl)
                    q0, k0, ty = it['q0'], it['k0'], it['ty']
                    self.mm(psS, kT[bsl][pb:pb + 64, k0:k0 + 128], qT[bsl][pb:pb + 64, q0:q0 + 128],
                            True, ty is None, r=[('at_qk', bsl, 0), ('at_qk', bsl, 1)], w=[skey])
                    if ty is not None:
                        self.mm(psS, self.ident[:], bias[bsl][:, it['hh'], ty, :], False, True,
                                r=['ident', ('at_bias', bsl)], w=[skey])
                    self.act(PT[it['psl']][:], psS, AF.Exp, r=[skey], w=[('at_pt', it['psl'])], scale=0.125)

                def stage_pv(it):
                    pb = it['hh'] * 64
                    ssl, osl, ki, nk, k0, q0 = it['psl'], it['osl'], it['ki'], it['nk'], it['k0'], it['q0']
                    psO = self.ps[3 + osl]
                    psU = self.ps[5 + osl]
                    pkey = ('at_pt', ssl)
                    self.mm(psO[:, 0:128], V[bsl][:, k0 // 128, :], PT[ssl][:], ki == 0, ki == nk - 1,
                            r=[('at_v', bsl), pkey], w=[('psO', osl)])
                    self.mm(psU[:, 0:128], self.ones[:], PT[ssl][:], ki == 0, ki == nk - 1,
                            r=['ones', pkey], w=[('psU', osl)])
                    if ki == nk - 1:
                        self.P.op('dve', lambda e: e.reciprocal(rec[osl][pb:pb + 64, :], psU[pb:pb + 64, 0:128]),
                                  r=[('psU', osl)], w=[('at_rec', osl)])
                        self.tt('dve', aT[bsl][pb:pb + 64, q0:q0 + 128], psO[pb:pb + 64, 0:128],
                                rec[osl][pb:pb + 64, :], ALU.mult, r=[('psO', osl), ('at_rec', osl)],
                                w=[('at_a', bsl)])
                LA = 2
                for i in range(len(items) + LA):
                    if i < len(items):
                        stage_s(items[i])
                    if i - LA >= 0:
                        stage_pv(items[i - LA])
                a0 = CTX if last else 0
                P.dma('act', S['attnT'].ap()[hp * 128:(hp + 1) * 128, a0:T], aT[bsl][:, a0:T], r=[('at_a', bsl)])

    def phase_s5(self, l):
        nc, P, I, S = self.nc, self.P, self.I, self.S
        Lc = LCH
        TWO_PI = 2.0 * math.pi
        with ExitStack() as st:
            sb = lambda n, s, d: st.enter_context(nc.sbuf_tensor(f"{n}_u{self.uid()}", s, d))
            prm = sb("s5_prm", [128, 3, 32], F32)
            names = ['dt', 'a', 'adt', 'bdt', 'r1', 'kf', 'red', 'sn', 'shf', 'cs', 'nr', 'ni', 'den',
                     'cre', 'cim', 't1', 't2']
            A = {n: sb(f"s5_{n}", [128, 32], F32) for n in names}
            ki = sb("s5_ki", [128, 32], I32)
            phr = sb("s5_phr", [128, 9, 32], F32)
            phi = sb("s5_phi", [128, 9, 32], F32)
            dsk = sb("s5_dsk", [128, 4], F32)
            zero = sb("s5_zero", [128, Lc], F32)
            P.dma('sp', prm[:], I['s5p'].ap()[l], w=['prm'])
            P.dma('sp', dsk[:], I['s5d'].ap()[l], w=['dsk'])
            P.op('dve', lambda e: e.memset(zero[:], 0.0), w=['zero'])
            K_ = ['prm']
            lre, lim, lst = prm[:, 0, :], prm[:, 1, :], prm[:, 2, :]
            a = lambda n: A[n][:]
            self.act(a('dt'), lst, AF.Exp, r=K_, w=K_)
            self.ts('dve', a('a'), lre, -1e-4, None, ALU.min, None, r=K_, w=K_)
            self.tt('dve', a('adt'), a('a'), a('dt'), ALU.mult, r=K_, w=K_)
            self.tt('dve', a('bdt'), lim, a('dt'), ALU.mult, r=K_, w=K_)
            self.act(a('r1'), a('adt'), AF.Exp, r=K_, w=K_)
            self.ts('dve', a('kf'), a('bdt'), 1.0 / TWO_PI, None, ALU.mult, None, r=K_, w=K_)
            self.cp('dve', ki[:], a('kf'), r=K_, w=K_)
            self.cp('dve', a('kf'), ki[:], r=K_, w=K_)
            self.stt(a('red'), a('kf'), -TWO_PI, a('bdt'), ALU.mult, ALU.add, r=K_, w=K_)
            self.ts('dve', a('red'), a('red'), 3.141592, -3.141592, ALU.min, ALU.max, r=K_, w=K_)
            self.act(a('sn'), a('red'), AF.Sin, r=K_, w=K_)
            self.act(a('shf'), a('red'), AF.Sin, r=K_, w=K_, scale=0.5)
            self.tt('dve', a('cs'), a('shf'), a('shf'), ALU.mult, r=K_, w=K_)
            self.ts('dve', a('cs'), a('cs'), -2.0, 1.0, ALU.mult, ALU.add, r=K_, w=K_)
            self.tt('dve', a('nr'), a('r1'), a('cs'), ALU.mult, r=K_, w=K_)
            self.ts('dve', a('nr'), a('nr'), -1.0, None, ALU.add, None, r=K_, w=K_)
            self.tt('dve', a('ni'), a('r1'), a('sn'), ALU.mult, r=K_, w=K_)
            self.tt('dve', a('den'), a('a'), a('a'), ALU.mult, r=K_, w=K_)
            self.tt('dve', a('t1'), lim, lim, ALU.mult, r=K_, w=K_)
            self.tt('dve', a('den'), a('den'), a('t1'), ALU.add, r=K_, w=K_)
            P.op('dve', lambda e: e.reciprocal(a('den'), a('den')), r=K_, w=K_)
            self.tt('dve', a('t1'), a('nr'), a('a'), ALU.mult, r=K_, w=K_)
            self.tt('dve', a('t2'), a('ni'), lim, ALU.mult, r=K_, w=K_)
            self.tt('dve', a('t1'), a('t1'), a('t2'), ALU.add, r=K_, w=K_)
            self.tt('dve', a('cre'), a('t1'), a('den'), ALU.mult, r=K_, w=K_)
            self.tt('dve', a('t1'), a('ni'), a('a'), ALU.mult, r=K_, w=K_)
            self.tt('dve', a('t2'), a('nr'), lim, ALU.mult, r=K_, w=K_)
            self.tt('dve', a('t1'), a('t1'), a('t2'), ALU.subtract, r=K_, w=K_)
            self.tt('dve', a('cim'), a('t1'), a('den'), ALU.mult, r=K_, w=K_)
            self.cp('dve', phr[:, 0, :], a('cs'), r=K_, w=K_)
            self.cp('dve', phi[:, 0, :], a('sn'), r=K_, w=K_)
            for j in range(1, 9):
                self.tt('dve', a('t1'), phr[:, j - 1, :], phr[:, j - 1, :], ALU.mult, r=K_, w=K_)
                self.tt('dve', a('t2'), phi[:, j - 1, :], phi[:, j - 1, :], ALU.mult, r=K_, w=K_)
                self.tt('dve', phr[:, j, :], a('t1'), a('t2'), ALU.subtract, r=K_, w=K_)
                self.tt('dve', a('t1'), phr[:, j - 1, :], phi[:, j - 1, :], ALU.mult, r=K_, w=K_)
                self.ts('dve', phi[:, j, :], a('t1'), 2.0, None, ALU.mult, None, r=K_, w=K_)

            wu = [sb(f"s5_wu{i}", [128, 8, 128], BF16) for i in range(2)]
            hx = [sb(f"s5_hx{i}", [128, 8, 512], BF16) for i in range(2)]
            uT = [sb(f"s5_uT{i}", [128, T], BF16) for i in range(2)]
            BT = [sb(f"s5_BT{i}", [128, 16, 128], BF16) for i in range(2)]
            CM = [sb(f"s5_CM{i}", [128, 16, 128], BF16) for i in range(2)]
            ybuf = sb("s5_y", [128, T], F32)
            gq = [sb(f"s5_g{i}", [128, T], BF16) for i in range(2)]
            tab = [{n: sb(f"s5_tab{il}_{n}", [128, Lc], F32) for n in ('DTr', 'DTi', 'nDTi', 'nDTr', 'MTr', 'MTi', 'Rc')}
                   for il in range(4)]
            ttmp = sb("s5_ttmp", [128, Lc], F32)
            NZ = 8
            zb = [{n: sb(f"s5_z{i}_{n}", [128, Lc], F32) for n in ('zr', 'zi', 'tm', 'tn', 'gr', 'gi')} for i in range(NZ)]
            db = [{n: sb(f"s5_d{i}_{n}", [128, Lc], F32) for n in ('t1', 't2', 't3', 't4')} for i in range(2)]
            hb = [[sb(f"s5_h{il}_{i}", [128, 2, Lc], BF16) for i in range(2)] for il in range(4)]
            ini = [[sb(f"s5_ini{il}_{i}", [128, 2], F32) for i in range(2)] for il in range(4)]
            itmp = [sb(f"s5_itmp{i}", [128, 2], F32) for i in range(4)]
            nphi = sb("s5_nphi", [128, 32], F32)
            self.ts('dve', nphi[:], phi[:, 8, :], -1.0, None, ALU.mult, None, r=K_, w=K_)

            hsrc = S['hxT'].ap().rearrange("(k p) t -> p k t", p=128)
            win = I['w_in'].ap()[l].rearrange("(k p) c -> p k c", p=128)
            for q in range(4):
                us = self.rot('s5_u', 2)
                self.load_w(wu[us][:], win[:, :, OFF_U + q * 128: OFF_U + (q + 1) * 128], ('s5_wu', us))
                for d in range(2):
                    for c in range(2):
                        self.load_w(BT[us][:, d * 8 + c * 4: d * 8 + c * 4 + 4, :] if False else
                                    BT[us][:].rearrange("p (d i c) k -> p d i c k", d=2, i=4)[:, d, :, c, :],
                                    I['s5BT'].ap()[l, d, 4 * q:4 * q + 4, c].rearrange("i r k -> r i k"),
                                    ('s5_BT', us, d, c))
                        self.load_w(CM[us][:].rearrange("p (d i c) k -> p d i c k", d=2, i=4)[:, d, :, c, :],
                                    I['s5CM'].ap()[l, d, 4 * q:4 * q + 4, c].rearrange("i r k -> r i k"),
                                    ('s5_CM', us, d, c))
                BTv = BT[us][:].rearrange("p (d i c) k -> p d i c k", d=2, i=4)
                CMv = CM[us][:].rearrange("p (d i c) k -> p d i c k", d=2, i=4)
                ukey = ('s5_uT', us)
                for (t0, n) in self.TT512:
                    hs = self.rot('s5_hx', 2)
                    P.dma('sp', hx[hs][:, :, 0:n], hsrc[:, :, t0:t0 + n], w=[('s5_hx', hs)])
                    for k in range(8):
                        self.mm(self.ps[6][:, 0:n], wu[us][:, k, :], hx[hs][:, k, 0:n], k == 0, k == 7,
                                r=[('s5_wu', us), ('s5_hx', hs)], w=[('ps', 6)])
                    self.cp('act', uT[us][:, t0:t0 + n], self.ps[6][:, 0:n], r=[('ps', 6)], w=[ukey])
                for d in range(2):
                    for il in range(4):
                        c = d * 16 + 4 * q + il
                        tb = tab[il]
                        tk = ('s5_tab', il)
                        sc = lambda arr, j=None: (arr[:, c:c + 1] if j is None else arr[:, j, c:c + 1])
                        P.op('dve', lambda e, tb=tb: e.memset(tb['DTr'][:, 0:1], 1.0), w=[tk])
                        P.op('dve', lambda e, tb=tb: e.memset(tb['DTi'][:, 0:1], 0.0), w=[tk])
                        for j in range(8):
                            n = 1 << j
                            pr, pi = sc(phr, j), sc(phi, j)
                            self.ts('dve', ttmp[:, 0:n], tb['DTi'][:, 0:n], pi, None, ALU.mult, None,
                                    r=[tk, 'prm'], w=['ttmp'])
                            self.stt(tb['DTr'][:, n:2 * n], tb['DTr'][:, 0:n], pr, ttmp[:, 0:n], ALU.mult, ALU.subtract,
                                     r=[tk, 'prm', 'ttmp'], w=[tk])
                            self.ts('dve', ttmp[:, 0:n], tb['DTi'][:, 0:n], pr, None, ALU.mult, None,
                                    r=[tk, 'prm'], w=['ttmp'])
                            self.stt(tb['DTi'][:, n:2 * n], tb['DTr'][:, 0:n], pi, ttmp[:, 0:n], ALU.mult, ALU.add,
                                     r=[tk, 'prm', 'ttmp'], w=[tk])
                        cre, cim = sc(A['cre'][:]), sc(A['cim'][:])
                        self.ts('dve', ttmp[:], tb['DTi'][:], cim, None, ALU.mult, None, r=[tk, 'prm'], w=['ttmp'])
                        self.stt(tb['MTr'][:], tb['DTr'][:], cre, ttmp[:], ALU.mult, ALU.add, r=[tk, 'prm', 'ttmp'], w=[tk])
                        self.ts('dve', ttmp[:], tb['DTi'][:], cre, None, ALU.mult, None, r=[tk, 'prm'], w=['ttmp'])
                        self.stt(tb['MTi'][:], tb['DTr'][:], cim, ttmp[:], ALU.mult, ALU.subtract, r=[tk, 'prm', 'ttmp'], w=[tk])
                        self.ts('pool', tb['nDTi'][:], tb['DTi'][:], -1.0, None, ALU.mult, None, r=[tk], w=[tk])
                        self.ts('pool', tb['nDTr'][:], tb['DTr'][:], -1.0, None, ALU.mult, None, r=[tk], w=[tk])
                        self.ts('dve', tb['Rc'][:], zero[:], sc(A['r1'][:]), None, ALU.add, None, r=['zero', 'prm'], w=[tk])
                        P.op('dve', lambda e, il=il: e.memset(ini[il][0][:], 0.0), w=[('s5_ini', il, 0)])
                    order = list(range(NCH)) if d == 0 else [0] + list(range(NCH - 1, 0, -1))

                    def emit_y(kk, cs):
                        tok0 = cs * Lc
                        ys = self.rot('s5_psY', 2)
                        psY = self.ps[4 + ys]
                        ykey = ('ps', 4 + ys)
                        hsl = kk % 2
                        for il in range(4):
                            for cc in range(2):
                                self.mm(psY[:, 0:Lc], CMv[:, d, il, cc, :], hb[il][hsl][:, cc, :],
                                        il == 0 and cc == 0, il == 3 and cc == 1,
                                        r=[('s5_CM', us, d, cc), ('s5_h', il, hsl)], w=[ykey])
                        if d == 0:
                            self.stt(ybuf[:, tok0:tok0 + Lc], uT[us][:, tok0:tok0 + Lc], dsk[:, q:q + 1], psY[:, 0:Lc],
                                     ALU.mult, ALU.add, r=[ukey, 'dsk', ykey], w=[('s5_y', cs)])
                        else:
                            self.tt('dve', ybuf[:, tok0:tok0 + Lc], psY[:, 0:Lc], ybuf[:, tok0:tok0 + Lc], ALU.add,
                                    r=[ykey, ('s5_y', cs)], w=[('s5_y', cs)])

                    pend = None
                    for kk, cs in enumerate(order):
                        tok0 = cs * Lc
                        par = kk % 2
                        hsl = kk % 2
                        ctxs = []
                        for il in range(4):
                            bs = self.rot('s5_psB', 4)
                            psB = self.ps[bs]
                            bkey = ('ps', bs)
                            for cc in range(2):
                                self.mm(psB[:, cc * Lc:(cc + 1) * Lc], BTv[:, d, il, cc, :], uT[us][:, tok0:tok0 + Lc],
                                        True, True, r=[('s5_BT', us, d, cc), ukey], w=[bkey])
                            if d == 0:
                                bre, bim = psB[:, 0:Lc], psB[:, Lc:2 * Lc]
                            else:
                                bre, bim = psB[:, Lc - 1::-1][:, 0:Lc], psB[:, 2 * Lc - 1:Lc - 1:-1]
                            zs = self.rot('s5_z', NZ)
                            ctxs.append(dict(il=il, bkey=bkey, bre=bre, bim=bim, zs=zs, z=zb[zs], zk=('s5_z', zs),
                                             gk=('s5_g', zs), tb=tab[il], tk=('s5_tab', il),
                                             c=d * 16 + 4 * q + il))
                        if pend is not None:
                            emit_y(*pend)
                        for cx in ctxs:
                            z, tb, zk, tk, bkey, bre, bim = cx['z'], cx['tb'], cx['zk'], cx['tk'], cx['bkey'], cx['bre'], cx['bim']
                            self.tt('dve', z['zr'][:], bre, tb['MTr'][:], ALU.mult, r=[bkey, tk], w=[(zk, 'zr')])
                            self.tt('dve', z['tm'][:], bim, tb['MTi'][:], ALU.mult, r=[bkey, tk], w=[(zk, 'tm')])
                            self.tt('dve', z['zi'][:], bre, tb['MTi'][:], ALU.mult, r=[bkey, tk], w=[(zk, 'zi')])
                            self.tt('dve', z['tn'][:], bim, tb['MTr'][:], ALU.mult, r=[bkey, tk], w=[(zk, 'tn')])
                        for cx in ctxs:
                            z, zk = cx['z'], cx['zk']
                            self.tt('dve', z['zr'][:], z['zr'][:], z['tm'][:], ALU.subtract, r=[(zk, 'zr'), (zk, 'tm')], w=[(zk, 'zr')])
                            self.tt('dve', z['zi'][:], z['zi'][:], z['tn'][:], ALU.add, r=[(zk, 'zi'), (zk, 'tn')], w=[(zk, 'zi')])
                        for cx in ctxs:
                            z, tb, zk, tk, gk, il = cx['z'], cx['tb'], cx['zk'], cx['tk'], cx['gk'], cx['il']
                            ik = ('s5_ini', il, par)
                            iv = ini[il][par]
                            P.op('dve', lambda e, z=z, tb=tb, iv=iv: e.tensor_tensor_scan(
                                z['gr'][:], tb['Rc'][:], z['zr'][:], iv[:, 0:1], ALU.mult, ALU.add),
                                r=[(zk, 'zr'), tk, ik], w=[(gk, 'r')])
                            P.op('dve', lambda e, z=z, tb=tb, iv=iv: e.tensor_tensor_scan(
                                z['gi'][:], tb['Rc'][:], z['zi'][:], iv[:, 1:2], ALU.mult, ALU.add),
                                r=[(zk, 'zi'), tk, ik], w=[(gk, 'i')])
                        for cx in ctxs:
                            z, gk, il, c = cx['z'], cx['gk'], cx['il'], cx['c']
                            ink = ('s5_ini', il, 1 - par)
                            inx = ini[il][1 - par]
                            pr, pi, npi = phr[:, 8, c:c + 1], phi[:, 8, c:c + 1], nphi[:, c:c + 1]
                            gre, gie = z['gr'][:, Lc - 1:Lc], z['gi'][:, Lc - 1:Lc]
                            itk = ('itmp', il)
                            self.act(itmp[il][:, 0:1], gie, AF.Identity, r=[(gk, 'i'), 'prm'], w=[itk], scale=npi)
                            self.act(itmp[il][:, 1:2], gie, AF.Identity, r=[(gk, 'i'), 'prm'], w=[itk], scale=pr)
                            self.act(inx[:, 0:1], gre, AF.Identity, r=[(gk, 'r'), 'prm', itk], w=[ink], scale=pr,
                                     bias=itmp[il][:, 0:1])
                            self.act(inx[:, 1:2], gre, AF.Identity, r=[(gk, 'r'), 'prm', itk], w=[ink], scale=pi,
                                     bias=itmp[il][:, 1:2])
                        for cx in ctxs:
                            z, tb, tk, gk, il = cx['z'], cx['tb'], cx['tk'], cx['gk'], cx['il']
                            ds = self.rot('s5_d', 2)
                            dd = db[ds]
                            dk = ('s5_d', ds)
                            hk = ('s5_h', il, hsl)
                            hre = hb[il][hsl][:, 0, :]
                            him = hb[il][hsl][:, 1, :]
                            if d == 1:
                                hre = hb[il][hsl][:, 0, ::-1]
                                him = hb[il][hsl][:, 1, ::-1]
                            self.tt('pool', dd['t1'][:], z['gr'][:], tb['DTr'][:], ALU.mult, r=[(gk, 'r'), tk], w=[(dk, 1)])
                            self.tt('pool', dd['t2'][:], z['gi'][:], tb['nDTi'][:], ALU.mult, r=[(gk, 'i'), tk], w=[(dk, 2)])
                            self.tt('pool', dd['t3'][:], z['gr'][:], tb['nDTi'][:], ALU.mult, r=[(gk, 'r'), tk], w=[(dk, 3)])
                            self.tt('pool', dd['t4'][:], z['gi'][:], tb['nDTr'][:], ALU.mult, r=[(gk, 'i'), tk], w=[(dk, 4)])
                            self.tt('pool', hre, dd['t1'][:], dd['t2'][:], ALU.add, r=[(dk, 1), (dk, 2)], w=[hk])
                            self.tt('pool', him, dd['t3'][:], dd['t4'][:], ALU.add, r=[(dk, 3), (dk, 4)], w=[hk])
                        pend = (kk, cs)
                    emit_y(*pend)
                gs_ = self.rot('s5_gq', 2)
                for cs in range(NCH):
                    self.act(gq[gs_][:, cs * Lc:(cs + 1) * Lc], ybuf[:, cs * Lc:(cs + 1) * Lc], AF.Gelu_apprx_tanh,
                             r=[('s5_y', cs)], w=[('s5_gq', gs_)])
                P.dma('act', S['gT'].ap()[q * 128:(q + 1) * 128, :], gq[gs_][:], r=[('s5_gq', gs_)])

    def phase_merge(self, l, src):
        nc, P, I, S = self.nc, self.P, self.I, self.S
        last = (l == DEPTH - 1)
        tiles = self.TILES[1:] if last else self.TILES
        with ExitStack() as st:
            sb = lambda n, s, d: st.enter_context(nc.sbuf_tensor(f"{n}_u{self.uid()}", s, d))
            wco = sb("mg_wco", [128, 4, D], BF16)
            wga = sb("mg_wga", [128, 4, D], BF16)
            wgb = sb("mg_wgb", [128, 4, D], BF16)
            wno = sb("mg_wno", [128, 4, D], BF16)
            wg = sb("mg_wg", [128, 8, 3 * D], BF16)
            hx = [sb(f"mg_hx{i}", [128, 8, 512], BF16) for i in range(2)]
            br = [[sb(f"mg_br{j}_{i}", [128, 4, 512], BF16) for i in range(2)] for j in range(3)]
            mg = [sb(f"mg_mg{i}", [128, 8, 512], BF16) for i in range(2)]
            sg = [sb(f"mg_sg{i}", [128, 512], F32) for i in range(4)]
            tm = [sb(f"mg_tm{i}", [128, 512], F32) for i in range(4)]
            acc = [sb(f"mg_acc{i}", [128, 512], F32) for i in range(2)]
            for wt_, nm in ((wco, 'conv_out'), (wga, 's5_glu_a'), (wgb, 's5_glu_b'), (wno, 'na_out')):
                for kc in range(4):
                    self.load_w(wt_[:, kc, :], I[nm].ap()[l][kc * 128:(kc + 1) * 128, :], ('mg_w', nm))
            win = I['w_in'].ap()[l].rearrange("(k p) c -> p k c", p=128)
            for k in range(8):
                self.load_w(wg[:, k, :], win[:, k, OFF_GA:OFF_GA + 3 * D], 'mg_wg')
            hsrc = S['hxT'].ap().rearrange("(k p) t -> p k t", p=128)
            bsrc = [S[nm].ap().rearrange("(k p) t -> p k t", p=128) for nm in ('convbT', 'gT', 'attnT')]
            mdst = S['mgT'].ap().rearrange("(k p) t -> p k t", p=128)
            for (t0, n) in tiles:
                hs = self.rot('mg_hx', 2)
                P.dma('sp', hx[hs][:, :, 0:n], hsrc[:, :, t0:t0 + n], w=[('mg_hx', hs)])
                for j in range(3):
                    P.dma('sp', br[j][hs][:, :, 0:n], bsrc[j][:, :, t0:t0 + n], w=[('mg_br', j, hs)])
                ms = self.rot('mg_mg', 2)
                for fo in range(8):
                    fs = slice(fo * 128, (fo + 1) * 128)

                    def proj(bank, w_, x_, nk, wkeys, xkey, coff=0):
                        for k in range(nk):
                            self.mm(self.ps[bank][:, 0:n], w_[:, k, coff + fo * 128: coff + (fo + 1) * 128],
                                    x_[:, k, 0:n], k == 0, k == nk - 1, r=wkeys + [xkey], w=[('ps', bank)])
                    hk = ('mg_hx', hs)
                    proj(0, wco, br[0][hs], 4, [('mg_w', 'conv_out')], ('mg_br', 0, hs))
                    proj(1, wg, hx[hs], 8, ['mg_wg'], hk, 0)
                    proj(2, wga, br[1][hs], 4, [('mg_w', 's5_glu_a')], ('mg_br', 1, hs))
                    proj(3, wgb, br[1][hs], 4, [('mg_w', 's5_glu_b')], ('mg_br', 1, hs))
                    proj(4, wg, hx[hs], 8, ['mg_wg'], hk, D)
                    proj(5, wno, br[2][hs], 4, [('mg_w', 'na_out')], ('mg_br', 2, hs))
                    proj(6, wg, hx[hs], 8, ['mg_wg'], hk, 2 * D)
                    a_ = self.rot('mg_acc', 2)
                    A = acc[a_][:, 0:n]
                    ak = ('mg_acc', a_)
                    sgs = [self.rot('mg_sg', 4) for _ in range(4)]
                    tms = [self.rot('mg_tm', 4) for _ in range(2)]
                    self.act(sg[sgs[0]][:, 0:n], self.ps[1][:, 0:n], AF.Sigmoid, r=[('ps', 1)], w=[('mg_sg', sgs[0])])
                    self.tt('dve', A, self.ps[0][:, 0:n], sg[sgs[0]][:, 0:n], ALU.mult,
                            r=[('ps', 0), ('mg_sg', sgs[0])], w=[ak])
                    self.act(sg[sgs[1]][:, 0:n], self.ps[3][:, 0:n], AF.Sigmoid, r=[('ps', 3)], w=[('mg_sg', sgs[1])])
                    self.tt('dve', tm[tms[0]][:, 0:n], self.ps[2][:, 0:n], sg[sgs[1]][:, 0:n], ALU.mult,
                            r=[('ps', 2), ('mg_sg', sgs[1])], w=[('mg_tm', tms[0])])
                    self.act(sg[sgs[2]][:, 0:n], self.ps[4][:, 0:n], AF.Sigmoid, r=[('ps', 4)], w=[('mg_sg', sgs[2])])
                    self.tt('pool', tm[tms[0]][:, 0:n], tm[tms[0]][:, 0:n], sg[sgs[2]][:, 0:n], ALU.mult,
                            r=[('mg_tm', tms[0]), ('mg_sg', sgs[2])], w=[('mg_tm', tms[0])])
                    self.tt('pool', A, A, tm[tms[0]][:, 0:n], ALU.add, r=[ak, ('mg_tm', tms[0])], w=[ak])
                    self.act(sg[sgs[3]][:, 0:n], self.ps[6][:, 0:n], AF.Sigmoid, r=[('ps', 6)], w=[('mg_sg', sgs[3])])
                    self.tt('dve', tm[tms[1]][:, 0:n], self.ps[5][:, 0:n], sg[sgs[3]][:, 0:n], ALU.mult,
                            r=[('ps', 5), ('mg_sg', sgs[3])], w=[('mg_tm', tms[1])])
                    self.tt('pool', mg[ms][:, fo, 0:n], A, tm[tms[1]][:, 0:n], ALU.add,
                            r=[ak, ('mg_tm', tms[1])], w=[('mg_mg', ms)])
                P.dma('act', mdst[:, :, t0:t0 + n], mg[ms][:, :, 0:n], r=[('mg_mg', ms)])
        P.barrier()
        with ExitStack() as st:
            sb = lambda n, s, d: st.enter_context(nc.sbuf_tensor(f"{n}_u{self.uid()}", s, d))
            wo = sb("mo_wo", [128, 8, D], BF16)
            wov = I['w_out'].ap()[l].rearrange("(k p) c -> p k c", p=128)
            for k in range(8):
                self.load_w(wo[:, k, :], wov[:, k, :], 'mo_wo')
            mgt = [sb(f"mo_mg{i}", [128, 8, 512], BF16) for i in range(2)]
            xt = [sb(f"mo_x{i}", [128, D], F32) for i in range(2)]
            xn = [sb(f"mo_xn{i}", [128, D], F32) for i in range(2)]
            hb = [sb(f"mo_hb{i}", [128, D], BF16) for i in range(2)]
            hT = [sb(f"mo_hT{i}", [128, 8, 512], BF16) for i in range(2)]
            scr = self.norm_scratch(st, "mo")
            msrc = S['mgT'].ap().rearrange("(k p) t -> p k t", p=128)
            hdst = S['h2T'].ap().rearrange("(k p) t -> p k t", p=128)
            mods = None
            cur = None
            for (t0, n) in tiles:
                cond = 1 if t0 < CTX else 0
                if cur != cond:
                    if mods is None:
                        mods = self.mod_tiles(st, l, cond, ('A1', 'G2', 'S2'))
                        gt2 = st.enter_context(nc.sbuf_tensor(f"mo_g2_u{self.uid()}", [128, D], F32))
                    else:
                        self.reload_mod(mods, l, cond, gt2)
                    cur = cond
                hs = self.rot('mo_mg', 2)
                P.dma('sp', mgt[hs][:, :, 0:n], msrc[:, :, t0:t0 + n], w=[('mo_mg', hs)])
                ts_ = self.rot('mo_hT', 2)
                for sub in range(n // 128):
                    xs = self.rot('mo_x', 2)
                    r0 = t0 + sub * 128
                    P.dma('sp', xt[xs][:], src.ap()[r0:r0 + 128, :], w=[('mo_x', xs)])
                    for half in range(2):
                        for k in range(8):
                            self.mm(self.ps[half][:, :], mgt[hs][:, k, sub * 128:(sub + 1) * 128],
                                    wo[:, k, half * 512:(half + 1) * 512], k == 0, k == 7,
                                    r=[('mo_mg', hs), 'mo_wo'], w=[('ps', half)])
                        self.tt('dve', xn[xs][:, half * 512:(half + 1) * 512], self.ps[half][:, :],
                                mods['A1'][0][:, half * 512:(half + 1) * 512], ALU.mult,
                                r=[('ps', half), mods['A1'][1]], w=[('mo_xn', xs)])
                    self.tt('pool', xn[xs][:], xn[xs][:], xt[xs][:], ALU.add, r=[('mo_xn', xs), ('mo_x', xs)],
                            w=[('mo_xn', xs)])
                    P.dma('act', S['xres'].ap()[r0:r0 + 128, :], xn[xs][:], r=[('mo_xn', xs)])
                    bs = self.rot('mo_hb', 2)
                    self.norm_sub(xn[xs][:], ('mo_xn', xs), mods['G2'], mods['S2'], hb[bs][:], ('mo_hb', bs), scr[bs])
                    self.transpose_out(hb[bs], ('mo_hb', bs), hT[ts_], ('mo_hT', ts_), sub)
                P.dma('act', hdst[:, :, t0:t0 + n], hT[ts_][:, :, 0:n], r=[('mo_hT', ts_)])

    def reload_mod(self, mods, l, cond, gtmp):
        base = (l * 2 + cond) * 6 * D
        idx = {'S1': 0, 'G1': 1, 'A1': 2, 'S2': 3, 'G2': 4, 'A2': 5}
        for n, (t, key) in mods.items():
            self.load_rep(t[:], self.S['modv'], base + idx[n] * D, key)
            if n in ('G1', 'G2'):
                gt = self.I['norm1_g'] if n == 'G1' else self.I['norm2_g']
                self.load_rep(gtmp[:], gt, l * D, 'modgtmp')
                self.stt(t[:], t[:], 1.0, gtmp[:], ALU.add, ALU.mult, r=[key, 'modgtmp'], w=[key])

    def phase_mlp(self, l):
        nc, P, I, S = self.nc, self.P, self.I, self.S
        last = (l == DEPTH - 1)
        tiles = self.TILES[1:] if last else self.TILES
        with ExitStack() as st:
            sb = lambda n, s, d: st.enter_context(nc.sbuf_tensor(f"{n}_u{self.uid()}", s, d))
            w1 = sb("ml_w1", [128, 8, 4 * D], BF16)
            w1v = I['mlp_w1'].ap()[l].rearrange("(k p) c -> p k c", p=128)
            for k in range(8):
                for hf in range(2):
                    self.load_w(w1[:, k, hf * 2048:(hf + 1) * 2048], w1v[:, k, hf * 2048:(hf + 1) * 2048], 'ml_w1')
            h2 = [sb(f"ml_h2{i}", [128, 8, 512], BF16) for i in range(2)]
            rl = [sb(f"ml_rl{i}", [128, 512], F32) for i in range(3)]
            hd = [sb(f"ml_hd{i}", [128, 8, 512], BF16) for i in range(2)]
            hsrc = S['h2T'].ap().rearrange("(k p) t -> p k t", p=128)
            ddst = S['hidT'].ap().rearrange("(k p) t -> p k t", p=128)
            for (t0, n) in tiles:
                hs = self.rot('ml_h2', 2)
                P.dma('sp', h2[hs][:, :, 0:n], hsrc[:, :, t0:t0 + n], w=[('ml_h2', hs)])
                for fg in range(4):
                    ds = self.rot('ml_hd', 2)
                    for fi in range(8):
                        fc = fg * 8 + fi
                        bank = self.rot('ml_bank', 6)
                        for k in range(8):
                            self.mm(self.ps[bank][:, 0:n], w1[:, k, fc * 128:(fc + 1) * 128], h2[hs][:, k, 0:n],
                                    k == 0, k == 7, r=['ml_w1', ('ml_h2', hs)], w=[('ps', bank)])
                        rs = self.rot('ml_rl', 3)
                        self.act(rl[rs][:, 0:n], self.ps[bank][:, 0:n], AF.Relu, r=[('ps', bank)], w=[('ml_rl', rs)])
                        self.tt('dve' if fi % 2 == 0 else 'pool', hd[ds][:, fi, 0:n], rl[rs][:, 0:n], rl[rs][:, 0:n],
                                ALU.mult, r=[('ml_rl', rs)], w=[('ml_hd', ds)])
                    P.dma('act', ddst[:, fg * 8:(fg + 1) * 8, t0:t0 + n], hd[ds][:, :, 0:n], r=[('ml_hd', ds)])
        P.barrier()
        with ExitStack() as st:
            sb = lambda n, s, d: st.enter_context(nc.sbuf_tensor(f"{n}_u{self.uid()}", s, d))
            w2 = sb("ml_w2", [128, 32, D], BF16)
            w2v = I['mlp_w2'].ap()[l].rearrange("(k p) c -> p k c", p=128)
            for k in range(32):
                self.load_w(w2[:, k, :], w2v[:, k, :], 'ml_w2')
            hdt = [sb(f"ml_hdt{i}", [128, 32, 256], BF16) for i in range(2)]
            xt = [sb(f"ml_x{i}", [128, D], F32) for i in range(2)]
            xn = [sb(f"ml_xn{i}", [128, D], F32) for i in range(2)]
            hb = [sb(f"ml_hb{i}", [128, D], BF16) for i in range(2)]
            hT = [sb(f"ml_hT{i}", [128, 8, 256], BF16) for i in range(2)]
            ob = [sb(f"ml_ob{i}", [128, D], F32) for i in range(2)]
            scr = self.norm_scratch(st, "ml")
            dsrc = S['hidT'].ap().rearrange("(k p) t -> p k t", p=128)
            hdst = S['hxT'].ap().rearrange("(k p) t -> p k t", p=128)
            fin = None
            if last:
                fin = sb("ml_fin", [128, D], F32)
                self.load_rep(fin[:], I['final_g'], 0, 'ml_fin')
            amods = None
            nmods = None
            cur = None
            tiles256 = []
            for (t0, n) in tiles:
                for h in range(n // 256):
                    tiles256.append((t0 + h * 256, 256))
            for (t0, n) in tiles256:
                cond = 1 if t0 < CTX else 0
                if cur != cond:
                    if amods is None:
                        amods = self.mod_tiles(st, l, cond, ('A2',))
                        gt1 = st.enter_context(nc.sbuf_tensor(f"ml_g1_u{self.uid()}", [128, D], F32))
                        if not last:
                            nmods = self.mod_tiles(st, l + 1, cond, ('G1', 'S1'))
                    else:
                        self.reload_mod(amods, l, cond, gt1)
                        if not last:
                            self.reload_mod(nmods, l + 1, cond, gt1)
                    cur = cond
                hs = self.rot('ml_hdt', 2)
                for kq in range(4):
                    P.dma('sp', hdt[hs][:, kq * 8:(kq + 1) * 8, :], dsrc[:, kq * 8:(kq + 1) * 8, t0:t0 + n],
                          w=[('ml_hdt', hs)])
                ts_ = self.rot('ml_hT', 2)
                for sub in range(2):
                    xs = self.rot('ml_x', 2)
                    r0 = t0 + sub * 128
                    P.dma('sp', xt[xs][:], S['xres'].ap()[r0:r0 + 128, :], w=[('ml_x', xs)])
                    for half in range(2):
                        bank = self.rot('ml_bank2', 4)
                        for k in range(32):
                            self.mm(self.ps[bank][:, :], hdt[hs][:, k, sub * 128:(sub + 1) * 128],
                                    w2[:, k, half * 512:(half + 1) * 512], k == 0, k == 31,
                                    r=[('ml_hdt', hs), 'ml_w2'], w=[('ps', bank)])
                        self.tt('dve', xn[xs][:, half * 512:(half + 1) * 512], self.ps[bank][:, :],
                                amods['A2'][0][:, half * 512:(half + 1) * 512], ALU.mult,
                                r=[('ps', bank), amods['A2'][1]], w=[('ml_xn', xs)])
                    self.tt('pool', xn[xs][:], xn[xs][:], xt[xs][:], ALU.add, r=[('ml_xn', xs), ('ml_x', xs)],
                            w=[('ml_xn', xs)])
                    bs = self.rot('ml_hb', 2)
                    if not last:
                        P.dma('act', S['xres'].ap()[r0:r0 + 128, :], xn[xs][:], r=[('ml_xn', xs)])
                        self.norm_sub(xn[xs][:], ('ml_xn', xs), nmods['G1'], nmods['S1'], hb[bs][:], ('ml_hb', bs), scr[bs])
                        self.transpose_out(hb[bs], ('ml_hb', bs), hT[ts_], ('ml_hT', ts_), sub)
                    else:
                        self.norm_sub(xn[xs][:], ('ml_xn', xs), (fin, 'ml_fin'), None, ob[bs][:], ('ml_ob', bs), scr[bs])
                        P.dma('act', self.out.ap()[r0 - CTX:r0 - CTX + 128, :], ob[bs][:], r=[('ml_ob', bs)])
                if not last:
                    P.dma('act', hdst[:, :, t0:t0 + n], hT[ts_][:, :, 0:n], r=[('ml_hT', ts_)])

def _core_inputs(inp, sh, b):
    d = dict(sh)
    d['xin'] = np.ascontiguousarray(np.concatenate([inp['ctx'][b], inp['x'][b]], axis=0), dtype=np.float32)
    cT = np.stack([inp['c'][b].reshape(8, 128).T, inp['c_ctx'].reshape(8, 128).T], axis=2)
    d['cT'] = np.ascontiguousarray(cT.reshape(128, 16), dtype=np.float32)
    return d


def kernel(**inputs):
    inp = {k: np.asarray(v) for k, v in inputs.items()}
    sh = _prep_shared(inp)
    bld = Builder()
    nc = bld.build()
    in_maps = [_core_inputs(inp, sh, b) for b in range(8)]
    in_maps = [{k: m[k] for k in bld.inputs} for m in in_maps]
    res = run_bass_kernel_spmd(nc, in_maps, core_ids=list(range(8)))
    return np.stack([np.asarray(r['out']) for r in res.results], axis=0).astype(np.float32)
```

```python
import math
from contextlib import ExitStack
import numpy as np
import concourse.bass as bass
import concourse.mybir as mybir
from concourse.bass_utils import run_bass_kernel_spmd

F32 = mybir.dt.float32
BF16 = mybir.dt.bfloat16
I32 = mybir.dt.int32
ALU = mybir.AluOpType
AF = mybir.ActivationFunctionType

SEM_LIMIT = 30000
SAME_ENG_WAITS = True
N_DMA_SEMS = 40

DEPTH = 4
D = 1024
T = 4352
CTX = 256
SEQ = 4096
NIN = 6656
OFF_XA, OFF_XB, OFF_XC, OFF_U, OFF_Q, OFF_K, OFF_V, OFF_GA, OFF_GB, OFF_GC = (
    0, 512, 1024, 1536, 2048, 2560, 3072, 3584, 4608, 5632)
EPS = 1e-6
NTYPE = 21
LCH = 256
NCH = T // LCH


class Prog:
    def __init__(self, nc):
        self.nc = nc
        self.ops = []
        self.last_w = {}
        self.readers = {}
        self.engs = {'pe': nc.tensor, 'dve': nc.vector, 'act': nc.scalar,
                     'pool': nc.gpsimd, 'sp': nc.sync}
        self._bar_from = 0

    def op(self, eng, fn, r=(), w=(), dma=False):
        i = len(self.ops)
        deps = set()
        for k in r:
            if k in self.last_w:
                deps.add(self.last_w[k])
        for k in w:
            if k in self.last_w:
                deps.add(self.last_w[k])
            for j in self.readers.get(k, ()):
                deps.add(j)
        fd = set()
        for j in deps:
            oj = self.ops[j]
            if oj['eng'] == eng and not oj['dma'] and not dma:
                if eng == 'pe' or not SAME_ENG_WAITS:
                    continue
                israw = any(self.last_w.get(k) == j for k in list(r) + list(w))
                if not israw:
                    continue
            fd.add(j)
        for k in w:
            self.last_w[k] = i
            self.readers[k] = []
        for k in r:
            self.readers.setdefault(k, []).append(i)
        self.ops.append(dict(eng=eng, fn=fn, deps=fd, dma=dma, sig=False))
        return i

    def dma(self, q, out, in_, r=(), w=(), **kw):
        return self.op(q, lambda e: e.dma_start(out=out, in_=in_, **kw), r, w, dma=True)

    def barrier(self):
        deps = set()
        lastc = {}
        for idx, o in enumerate(self.ops):
            if o['fn'] is None:
                continue
            if o['dma']:
                if idx >= self._bar_from:
                    deps.add(idx)
            else:
                lastc[o['eng']] = idx
        deps |= set(lastc.values())
        self._bar_from = len(self.ops)
        for e in self.engs:
            self.ops.append(dict(eng=e, fn=None, deps=set(deps), dma=False, sig=False))
        self.last_w = {}
        self.readers = {}

    def emit(self):
        nc = self.nc
        ops = self.ops
        for o in ops:
            for j in o['deps']:
                ops[j]['sig'] = True
        dma_sems = [nc.alloc_semaphore(name=f"dq{i}") for i in range(N_DMA_SEMS)]
        dma_cnt = [0] * N_DMA_SEMS
        dma_last = [None] * N_DMA_SEMS
        eng_sem = {}
        eng_cnt = {}
        nsem = [0]

        def new_eng_sem(e):
            nsem[0] += 1
            eng_sem[e] = nc.alloc_semaphore(name=f"s_{e}_{nsem[0]}")
            eng_cnt[e] = 0

        for e in self.engs:
            new_eng_sem(e)
        rr = 0
        for idx, o in enumerate(ops):
            if o['dma']:
                s = rr % N_DMA_SEMS
                rr += 1
                if dma_cnt[s] + 16 > SEM_LIMIT:
                    if dma_last[s] is not None:
                        o['deps'].add(dma_last[s])
                    dma_sems[s] = nc.alloc_semaphore(name=f"dq{s}_{idx}")
                    dma_cnt[s] = 0
                    dma_last[s] = None
                if dma_last[s] is not None:
                    o['deps'].add(dma_last[s])
                dma_cnt[s] += 16
                o['done'] = (dma_sems[s], dma_cnt[s])
                dma_last[s] = idx
            elif o['sig'] and o['fn'] is not None:
                e = o['eng']
                if eng_cnt[e] + 1 > SEM_LIMIT:
                    new_eng_sem(e)
                eng_cnt[e] += 1
                o['done'] = (eng_sem[e], eng_cnt[e])
            else:
                o['done'] = None

        def resolve(j, acc, seen):
            if j in seen:
                return
            seen.add(j)
            oj = ops[j]
            if oj['done'] is not None:
                acc.add(j)
            elif oj['fn'] is None:
                for jj in oj['deps']:
                    resolve(jj, acc, seen)
            else:
                raise RuntimeError("dep on unsignaled op")

        per_eng = {e: [] for e in self.engs}
        for idx, o in enumerate(ops):
            per_eng[o['eng']].append(idx)
        self.n_inst = {e: len(v) for e, v in per_eng.items()}
        with nc.Block() as block:
            def make(e):
                def body(eng):
                    waited = {}
                    for idx in per_eng[e]:
                        o = ops[idx]
                        acc = set()
                        seen = set()
                        for j in o['deps']:
                            resolve(j, acc, seen)
                        need = {}
                        for j in acc:
                            sem, val = ops[j]['done']
                            key = id(sem)
                            if waited.get(key, 0) >= val:
                                continue
                            if key not in need or need[key][1] < val:
                                need[key] = (sem, val)
                        for key, (sem, val) in need.items():
                            eng.wait_ge(sem, val)
                            waited[key] = val
                        if o['fn'] is not None:
                            inst = o['fn'](eng)
                            if o['done'] is not None:
                                sem, val = o['done']
                                inst.then_inc(sem, 16 if o['dma'] else 1)
                return body
            block.tensor(make('pe'))
            block.vector(make('dve'))
            block.scalar(make('act'))
            block.gpsimd(make('pool'))
            block.sync(make('sp'))


def _na_tile_plan():
    types = {}
    plan = []
    for j in range(32):
        r0a = min(max(2 * j - 4, 0), 56)
        r0b = min(max(2 * j + 1 - 4, 0), 56)
        tlo = r0a // 2
        thi = (r0b + 7) // 2
        lst = []
        for t in range(tlo, thi + 1):
            key = (t - j, r0a - 2 * j, r0b - 2 * j)
            if key not in types:
                types[key] = len(types)
            lst.append((t, types[key]))
        plan.append(lst)
    return plan, types


def _na_bias_index():
    plan, types = _na_tile_plan()
    nt = len(types)
    idx_r = np.zeros((nt, 128, 128), np.int64)
    idx_c = np.zeros((nt, 128, 128), np.int64)
    mask = np.zeros((nt, 128, 128), np.float32)
    col = np.arange(64)
    cs = np.clip(col - 8, 0, 48)
    for (delta, ra, rb), ty in types.items():
        for qr2 in range(2):
            r0rel = (ra, rb)[qr2]
            for kr2 in range(2):
                krel = 2 * delta + kr2
                dr = krel - qr2
                row_ok = (krel >= r0rel) and (krel < r0rel + 8)
                for qc in range(64):
                    kc = col
                    ok = row_ok & (kc >= cs[qc]) & (kc < cs[qc] + 16)
                    q = qr2 * 64 + qc
                    k = kr2 * 64 + kc
                    idx_r[ty, k, q] = np.clip(dr + 7, 0, 14)
                    idx_c[ty, k, q] = np.clip(kc - qc + 15, 0, 30)
                    mask[ty, k, q] = np.where(ok, 0.0, -1e30)
    return plan, nt, idx_r, idx_c, mask


_NA = None


def _na():
    global _NA
    if _NA is None:
        _NA = _na_bias_index()
    return _NA


def _prep_shared(inp):
    L = DEPTH
    sh = {}
    f = lambda a: np.ascontiguousarray(a, dtype=np.float32)
    for k in ('w_mod', 'w_in', 'conv_out', 's5_glu_a', 's5_glu_b', 'na_out', 'w_out',
              'mlp_w1', 'mlp_w2'):
        sh[k] = f(inp[k])
    sh['b_mod'] = f(inp['b_mod'])
    sh['norm1_g'] = f(inp['norm1_g'])
    sh['norm2_g'] = f(inp['norm2_g'])
    sh['final_g'] = f(inp['final_norm_g']).reshape(1, D)
    sh['conv_w'] = f(inp['conv_w'].reshape(L, 3, 4, 128).transpose(0, 3, 2, 1))
    def gp(a):
        a = a.reshape(L, 2, 16, 2, 64)
        return a.transpose(0, 3, 4, 1, 2).reshape(L, 128, 32)
    ls = np.broadcast_to(inp['s5_log_step'][:, :, :, None], (L, 2, 32, 64))
    sh['s5p'] = f(np.stack([gp(inp['s5_lam_re']), gp(inp['s5_lam_im']), gp(ls)], axis=2))
    def bt(B):
        out = np.zeros((L, 2, 16, 128, 128), np.float32)
        for i in range(16):
            for g2 in range(2):
                g = 2 * i + g2
                gl = g % 8
                out[:, :, i, gl * 16:(gl + 1) * 16, g2 * 64:(g2 + 1) * 64] = \
                    B[:, :, g].transpose(0, 1, 3, 2)
        return out
    sh['s5BT'] = f(np.stack([bt(inp['s5_b_re']), bt(inp['s5_b_im'])], axis=3))
    def cm(C):
        out = np.zeros((L, 2, 16, 128, 128), np.float32)
        for i in range(16):
            for g2 in range(2):
                g = 2 * i + g2
                gl = g % 8
                out[:, :, i, g2 * 64:(g2 + 1) * 64, gl * 16:(gl + 1) * 16] = \
                    C[:, :, g].transpose(0, 1, 3, 2)
        return out
    sh['s5CM'] = f(np.stack([cm(inp['s5_c_re']), cm(inp['s5_c_im'])], axis=3))
    sh['s5d'] = f(inp['s5_d'].reshape(L, 4, 128).transpose(0, 2, 1))
    plan, nt, idx_r, idx_c, mask = _na()
    rpb = inp['na_rpb']
    sh['rpbg'] = f(rpb[:, :, idx_r, idx_c])
    sh['na_mask'] = f(mask)
    sh['ident'] = np.eye(128, dtype=np.float32)
    return sh


class Builder:
    def __init__(self, layers=DEPTH, dbg=(), stop=None, only=None):
        self.layers = layers
        self.stop = stop
        self.only = only
        self.dbg = set(dbg)
        nc = bass.Bass("TRN2", target_bir_lowering=False)
        self.nc = nc
        self.P = Prog(nc)
        self.inputs = {}
        self.cnt = {}

    def din(self, name, shape, dt=F32):
        t = self.nc.dram_tensor(name, list(shape), dt, kind="ExternalInput")
        self.inputs[name] = t
        return t

    def dscr(self, name, shape, dt):
        kind = "ExternalOutput" if name in self.dbg else "Internal"
        return self.nc.dram_tensor(name, list(shape), dt, kind=kind)

    def uid(self):
        self._uid = getattr(self, '_uid', 0) + 1
        return self._uid

    def rot(self, name, n):
        c = self.cnt.get(name, 0)
        self.cnt[name] = c + 1
        return c % n

    def mm(self, out, lhsT, rhs, start, stop, r, w):
        self.P.op('pe', lambda e: e.matmul(out, lhsT, rhs, start=start, stop=stop), r, w)

    def act(self, out, in_, func, r, w, **kw):
        self.P.op('act', lambda e: e.activation(out, in_, func, **kw), r, w)

    def tt(self, eng, out, a, b, op, r, w):
        self.P.op(eng, lambda e: e.tensor_tensor(out, a, b, op), r, w)

    def ts(self, eng, out, a, s1, s2, op0, op1, r, w):
        if op1 is None:
            self.P.op(eng, lambda e: e.tensor_scalar(out, a, s1, None, op0), r, w)
        else:
            self.P.op(eng, lambda e: e.tensor_scalar(out, a, s1, s2, op0, op1), r, w)

    def stt(self, out, a, s, b, op0, op1, r, w):
        self.P.op('dve', lambda e: e.scalar_tensor_tensor(out, a, s, b, op0, op1), r, w)

    def cp(self, eng, out, in_, r, w):
        if eng == 'act':
            self.P.op('act', lambda e: e.activation(out, in_, AF.Copy), r, w)
        else:
            self.P.op(eng, lambda e: e.tensor_copy(out, in_), r, w)

    def build(self):
        nc, P = self.nc, self.P
        L = DEPTH
        nt = _na()[1]
        self.nt = nt
        I = {}
        I['xin'] = self.din('xin', [T, D])
        I['cT'] = self.din('cT', [128, 16])
        I['w_mod'] = self.din('w_mod', [L, D, 6 * D])
        I['b_mod'] = self.din('b_mod', [L, 6 * D])
        I['norm1_g'] = self.din('norm1_g', [L, D])
        I['norm2_g'] = self.din('norm2_g', [L, D])
        I['final_g'] = self.din('final_g', [1, D])
        I['w_in'] = self.din('w_in', [L, D, NIN])
        I['conv_w'] = self.din('conv_w', [L, 128, 4, 3])
        I['conv_out'] = self.din('conv_out', [L, 512, D])
        I['s5p'] = self.din('s5p', [L, 128, 3, 32])
        I['s5BT'] = self.din('s5BT', [L, 2, 16, 2, 128, 128])
        I['s5CM'] = self.din('s5CM', [L, 2, 16, 2, 128, 128])
        I['s5d'] = self.din('s5d', [L, 128, 4])
        I['s5_glu_a'] = self.din('s5_glu_a', [L, 512, D])
        I['s5_glu_b'] = self.din('s5_glu_b', [L, 512, D])
        I['rpbg'] = self.din('rpbg', [L, 8, nt, 128, 128])
        I['na_mask'] = self.din('na_mask', [nt, 128, 128])
        I['na_out'] = self.din('na_out', [L, 512, D])
        I['w_out'] = self.din('w_out', [L, D, D])
        I['mlp_w1'] = self.din('mlp_w1', [L, D, 4 * D])
        I['mlp_w2'] = self.din('mlp_w2', [L, 4 * D, D])
        I['ident'] = self.din('ident', [128, 128])
        self.I = I
        self.out = nc.dram_tensor('out', [SEQ, D], F32, kind="ExternalOutput")
        S = {}
        S['xres'] = self.dscr('xres', [T, D], F32)
        S['hxT'] = self.dscr('hxT', [D, T], BF16)
        S['h2T'] = self.dscr('h2T', [D, T], BF16)
        S['convbT'] = self.dscr('convbT', [512, T], BF16)
        S['gT'] = self.dscr('gT', [512, T], BF16)
        S['attnT'] = self.dscr('attnT', [512, T], BF16)
        S['modv'] = self.dscr('modv', [L, 2, 6 * D], F32)
        S['mgT'] = self.dscr('mgT', [D, T], BF16)
        S['hidT'] = self.dscr('hidT', [4 * D, T], BF16)
        self.S = S

        with ExitStack() as gs:
            self.ps = [gs.enter_context(nc.psum_tensor(f"ps{i}", [128, 512], F32))
                       for i in range(7)]
            self.psT = gs.enter_context(nc.psum_tensor("psT", [128, 1024], BF16))
            self.ident = gs.enter_context(nc.sbuf_tensor("ident_sb", [128, 128], BF16))
            self.ones = gs.enter_context(nc.sbuf_tensor("ones_sb", [128, 128], BF16))
            P.dma('pool', self.ident[:], I['ident'].ap(), w=['ident'])
            P.op('dve', lambda e: e.memset(self.ones[:], 1.0), w=['ones'])
            P.barrier()
            seq = [('adaln', lambda: self.phase_adaln()),
                   ('norm', lambda: self.phase_norm(0, src=I['xin'], kind='n1'))]
            for l in range(self.layers):
                seq += [(f'conv{l}', lambda l=l: self.phase_conv(l)),
                        (f'attn{l}', lambda l=l: self.phase_attn(l)),
                        (f's5{l}', lambda l=l: self.phase_s5(l)),
                        (f'merge{l}', lambda l=l: self.phase_merge(l, src=(I['xin'] if l == 0 else S['xres']))),
                        (f'mlp{l}', lambda l=l: self.phase_mlp(l))]
            for name, fn in seq:
                if self.only is not None and name not in self.only:
                    continue
                fn()
                P.barrier()
                if name == self.stop:
                    break
            P.emit()
        return nc

    def phase_adaln(self):
        nc, P, I, S = self.nc, self.P, self.I, self.S
        with ExitStack() as st:
            sb = lambda n, s, d: st.enter_context(nc.sbuf_tensor(f"{n}_u{self.uid()}", s, d))
            cT = sb("ad_cT", [128, 16], F32)
            sil = sb("ad_sil", [128, 16], BF16)
            wt = [sb(f"ad_w{i}", [128, 8, 512], BF16) for i in range(3)]
            bm = sb("ad_bm", [2, 6 * D], F32)
            row = [sb(f"ad_row{i}", [2, 512], F32) for i in range(2)]
            P.dma('sp', cT[:], I['cT'].ap(), w=['cT'])
            self.act(sil[:], cT[:], AF.Silu, r=['cT'], w=['sil'])
            silv = sil[:].rearrange("p (k j) -> p k j", j=2)
            for l in range(DEPTH):
                bsrc = bass.AP(I['b_mod'], l * 6 * D, [[0, 2], [1, 6 * D]])
                P.dma('sp', bm[:], bsrc, w=['bm'])
                wv = I['w_mod'].ap()[l].rearrange("(k p) c -> p k c", p=128)
                for ct in range(12):
                    s = self.rot('adw', 3)
                    P.dma('pool', wt[s][:], wv[:, :, ct * 512:(ct + 1) * 512], w=[('adw', s)])
                    pst = self.ps[ct % 2]
                    for k in range(8):
                        self.mm(pst[0:2, :], silv[:, k, :], wt[s][:, k, :], k == 0, k == 7,
                                r=['sil', ('adw', s)], w=[('ps', ct % 2)])
                    rs = self.rot('adrow', 2)
                    self.tt('dve', row[rs][:], pst[0:2, :], bm[:, ct * 512:(ct + 1) * 512], ALU.add,
                            r=[('ps', ct % 2), 'bm'], w=[('adrow', rs)])
                    P.dma('sp', S['modv'].ap()[l][:, ct * 512:(ct + 1) * 512], row[rs][:],
                          r=[('adrow', rs)])

    def load_rep(self, dst, tensor, offset, key):
        self.P.dma('sp', dst, bass.AP(tensor, offset, [[0, 128], [1, D]]), w=[key])

    def mod_tiles(self, st, l, cond, names, gsrc=None):
        nc = self.nc
        res = {}
        base = (l * 2 + cond) * 6 * D
        idx = {'S1': 0, 'G1': 1, 'A1': 2, 'S2': 3, 'G2': 4, 'A2': 5}
        for n in names:
            t = st.enter_context(nc.sbuf_tensor(f"mod_{n}_{cond}_{self.rot('modt', 1 << 30)}", [128, D], F32))
            key = ('mod', n, cond)
            self.load_rep(t[:], self.S['modv'], base + idx[n] * D, key)
            if n in ('G1', 'G2'):
                g = st.enter_context(nc.sbuf_tensor(f"modg_{n}_{cond}_{self.rot('modt', 1 << 30)}", [128, D], F32))
                gt = self.I['norm1_g'] if n == 'G1' else self.I['norm2_g']
                self.load_rep(g[:], gt, l * D, ('modg', n, cond))
                self.stt(t[:], t[:], 1.0, g[:], ALU.add, ALU.mult, r=[key, ('modg', n, cond)], w=[key])
            res[n] = (t, key)
        return res

    def norm_sub(self, xt, xkey, G, S_, hb, hkey, scr):
        junk, ss, rstd, tmp = scr['junk'], scr['ss'], scr['rstd'], scr['tmp']
        k = scr['k']
        self.act(junk[:], xt, AF.Square, r=[xkey], w=[('nj', k), ('ss', k)], accum_out=ss[:])
        self.ts('dve', rstd[:], ss[:], 1.0 / D, EPS, ALU.mult, ALU.add, r=[('ss', k)], w=[('rstd', k)])
        self.act(rstd[:], rstd[:], AF.Sqrt, r=[('rstd', k)], w=[('rstd', k)])
        self.P.op('dve', lambda e: e.reciprocal(rstd[:], rstd[:]), r=[('rstd', k)], w=[('rstd', k)])
        if S_ is None:
            self.stt(hb, xt, rstd[:], G[0][:], ALU.mult, ALU.mult, r=[xkey, ('rstd', k), G[1]], w=[hkey])
        else:
            self.stt(tmp[:], xt, rstd[:], G[0][:], ALU.mult, ALU.mult, r=[xkey, ('rstd', k), G[1]],
                     w=[('ntmp', k)])
            self.tt('pool', hb, tmp[:], S_[0][:], ALU.add, r=[('ntmp', k), S_[1]], w=[hkey])

    def norm_scratch(self, st, tag):
        nc = self.nc
        out = []
        for k in range(2):
            out.append(dict(
                junk=st.enter_context(nc.sbuf_tensor(f"{tag}_junk{k}_u{self.uid()}", [128, D], BF16)),
                ss=st.enter_context(nc.sbuf_tensor(f"{tag}_ss{k}_u{self.uid()}", [128, 1], F32)),
                rstd=st.enter_context(nc.sbuf_tensor(f"{tag}_rstd{k}_u{self.uid()}", [128, 1], F32)),
                tmp=st.enter_context(nc.sbuf_tensor(f"{tag}_tmp{k}_u{self.uid()}", [128, D], F32)),
                k=(tag, k)))
        return out

    def transpose_out(self, hb, hkey, hT, hTkey, sub):
        for kc in range(8):
            self.P.op('pe', lambda e, kc=kc: e.transpose(self.psT[:, kc * 128:(kc + 1) * 128],
                                                          hb[:, kc * 128:(kc + 1) * 128], self.ident[:]),
                      r=[hkey, 'ident'], w=['psT'])
        self.cp('act', hT[:, :, sub * 128:(sub + 1) * 128],
                self.psT[:].rearrange("p (k t) -> p k t", t=128), r=['psT'], w=[hTkey])

    TILES = [(0, 256)] + [(256 + 512 * i, 512) for i in range(8)]

    def phase_norm(self, l, src, kind):
        nc, P, I, S = self.nc, self.P, self.I, self.S
        with ExitStack() as st:
            sb = lambda n, s, d: st.enter_context(nc.sbuf_tensor(f"{n}_u{self.uid()}", s, d))
            mods = [self.mod_tiles(st, l, c, ('G1', 'S1')) for c in (0, 1)]
            xt = [sb(f"pn_x{i}", [128, D], F32) for i in range(3)]
            hb = [sb(f"pn_hb{i}", [128, D], BF16) for i in range(2)]
            hT = [sb(f"pn_hT{i}", [128, 8, 512], BF16) for i in range(2)]
            scr = self.norm_scratch(st, "pn")
            dst = S['hxT'].ap().rearrange("(k p) t -> p k t", p=128)
            for (t0, n) in self.TILES:
                cond = 1 if t0 < CTX else 0
                hs = self.rot('pn_hT', 2)
                for sub in range(n // 128):
                    xs = self.rot('pn_x', 3)
                    P.dma('sp', xt[xs][:], src.ap()[t0 + sub * 128:t0 + (sub + 1) * 128, :], w=[('pn_x', xs)])
                    bs = self.rot('pn_hb', 2)
                    self.norm_sub(xt[xs][:], ('pn_x', xs), mods[cond]['G1'], mods[cond]['S1'],
                                  hb[bs][:], ('pn_hb', bs), scr[bs])
                    self.transpose_out(hb[bs], ('pn_hb', bs), hT[hs], ('pn_hT', hs), sub)
                P.dma('act', dst[:, :, t0:t0 + n], hT[hs][:, :, 0:n], r=[('pn_hT', hs)])

    TT512 = [(512 * i, 512) for i in range(8)] + [(4096, 256)]

    def load_w(self, dst, src_ap, key, r=()):
        self.P.dma('pool', dst, src_ap, r=r, w=[key])

    def phase_conv(self, l):
        nc, P, I, S = self.nc, self.P, self.I, self.S
        with ExitStack() as st:
            sb = lambda n, s, d: st.enter_context(nc.sbuf_tensor(f"{n}_u{self.uid()}", s, d))
            wc = [sb(f"cv_w{i}", [128, 3, 8, 128], BF16) for i in range(2)]
            hx = [sb(f"cv_hx{i}", [128, 8, 512], BF16) for i in range(2)]
            vb = sb("cv_v", [128, T + 4], F32)
            xbb = sb("cv_xb", [128, T], F32)
            tmp = [sb(f"cv_tmp{i}", [128, 512], F32) for i in range(2)]
            acc = [sb(f"cv_acc{i}", [128, 1024], F32) for i in range(2)]
            ob = [sb(f"cv_o{i}", [128, T], BF16) for i in range(2)]
            cw = sb("cv_cw", [128, 12], F32)
            P.dma('sp', cw[:], I['conv_w'].ap()[l].rearrange("p q j -> p (q j)"), w=['cw'])
            P.op('dve', lambda e: e.memset(vb[:], 0.0), w=['vb'])
            hsrc = S['hxT'].ap().rearrange("(k p) t -> p k t", p=128)
            win = I['w_in'].ap()[l].rearrange("(k p) c -> p k c", p=128)
            for q in range(4):
                ws = self.rot('cv_w', 2)
                for j, off in enumerate((OFF_XA, OFF_XB, OFF_XC)):
                    self.load_w(wc[ws][:, j, :, :], win[:, :, off + q * 128: off + (q + 1) * 128], ('cv_w', ws, j))
                for (t0, n) in self.TT512:
                    hs = self.rot('cv_hx', 2)
                    P.dma('sp', hx[hs][:, :, 0:n], hsrc[:, :, t0:t0 + n], w=[('cv_hx', hs)])
                    for j in range(3):
                        for k in range(8):
                            self.mm(self.ps[j][:, 0:n], wc[ws][:, j, k, :], hx[hs][:, k, 0:n], k == 0, k == 7,
                                    r=[('cv_w', ws, j), ('cv_hx', hs)], w=[('ps', j)])
                    ts_ = self.rot('cv_tmp', 2)
                    self.cp('act', tmp[ts_][:, 0:n], self.ps[2][:, 0:n], r=[('ps', 2)], w=[('cv_tmp', ts_)])
                    segs = []
                    if t0 < CTX:
                        segs.append((t0, CTX - t0, 1 + t0))
                        segs.append((CTX, t0 + n - CTX, 3 + CTX))
                    else:
                        segs.append((t0, n, 3 + t0))
                    for (a0, an, c0) in segs:
                        self.tt('dve', vb[:, c0:c0 + an], self.ps[0][:, a0 - t0:a0 - t0 + an],
                                tmp[ts_][:, a0 - t0:a0 - t0 + an], ALU.mult,
                                r=[('ps', 0), ('cv_tmp', ts_), 'vb'], w=['vb'])
                    self.cp('act', xbb[:, t0:t0 + n], self.ps[1][:, 0:n], r=[('ps', 1)], w=['xbb'])
                os_ = self.rot('cv_o', 2)
                pieces = [(0, 256, 1)] + [(256 + 1024 * i, 1024, 3 + 256 + 1024 * i) for i in range(4)]
                for (a0, an, c0) in pieces:
                    as_ = self.rot('cv_acc', 2)
                    A = acc[as_][:, 0:an]
                    ak = ('cv_acc', as_)
                    self.ts('dve', A, vb[:, c0 - 1:c0 - 1 + an], cw[:, q * 3:q * 3 + 1], None, ALU.mult, None,
                            r=['vb', 'cw'], w=[ak])
                    self.stt(A, vb[:, c0:c0 + an], cw[:, q * 3 + 1:q * 3 + 2], A, ALU.mult, ALU.add,
                             r=['vb', 'cw', ak], w=[ak])
                    self.stt(A, vb[:, c0 + 1:c0 + 1 + an], cw[:, q * 3 + 2:q * 3 + 3], A, ALU.mult, ALU.add,
                             r=['vb', 'cw', ak], w=[ak])
                    self.tt('pool', ob[os_][:, a0:a0 + an], A, xbb[:, a0:a0 + an], ALU.mult,
                            r=[ak, 'xbb'], w=[('cv_o', os_)])
                P.dma('act', S['convbT'].ap()[q * 128:(q + 1) * 128, :], ob[os_][:], r=[('cv_o', os_)])


    def phase_attn(self, l):
        nc, P, I, S = self.nc, self.P, self.I, self.S
        plan, nt = _na()[0], self.nt
        last = (l == DEPTH - 1)
        with ExitStack() as st:
            sb = lambda n, s, d: st.enter_context(nc.sbuf_tensor(f"{n}_u{self.uid()}", s, d))
            wq = [sb(f"at_w{i}", [128, 3, 8, 128], BF16) for i in range(2)]
            hx = [sb(f"at_hx{i}", [128, 8, 512], BF16) for i in range(2)]
            qT = [sb(f"at_q{i}", [128, T], BF16) for i in range(2)]
            kT = [sb(f"at_k{i}", [128, T], BF16) for i in range(2)]
            V = [sb(f"at_v{i}", [128, 34, 128], BF16) for i in range(2)]
            aT = [sb(f"at_a{i}", [128, T], BF16) for i in range(2)]
            msk = sb("at_mask", [128, nt, 128], F32)
            rg = [sb(f"at_rg{i}", [128, 128], F32) for i in range(3)]
            bias = [sb(f"at_bias{i}", [128, 2, nt, 128], BF16) for i in range(2)]
            PT = [sb(f"at_pt{i}", [128, 128], BF16) for i in range(4)]
            rec = [sb(f"at_rec{i}", [128, 128], F32) for i in range(2)]
            P.dma('sp', msk[:], I['na_mask'].ap().rearrange("t k q -> k t q"), w=['msk'])
            hsrc = S['hxT'].ap().rearrange("(k p) t -> p k t", p=128)
            win = I['w_in'].ap()[l].rearrange("(k p) c -> p k c", p=128)
            for hp in range(4):
                ws = self.rot('at_w', 2)
                bsl = self.rot('at_b', 2)
                for j, off in enumerate((OFF_Q, OFF_K, OFF_V)):
                    self.load_w(wq[ws][:, j, :, :], win[:, :, off + hp * 128: off + (hp + 1) * 128], ('at_w', ws, j))
                for hh in range(2):
                    for ty in range(nt):
                        rs = self.rot('at_rg', 3)
                        P.dma('sp', rg[rs][:], I['rpbg'].ap()[l, hp * 2 + hh, ty], w=[('at_rg', rs)])
                        self.stt(bias[bsl][:, hh, ty, :], rg[rs][:], 8.0, msk[:, ty, :], ALU.mult, ALU.add,
                                 r=[('at_rg', rs), 'msk'], w=[('at_bias', bsl)])
                for (t0, n) in self.TT512:
                    hs = self.rot('at_hx', 2)
                    P.dma('sp', hx[hs][:, :, 0:n], hsrc[:, :, t0:t0 + n], w=[('at_hx', hs)])
                    for j, dstT in ((0, qT[bsl]), (1, kT[bsl])):
                        for k in range(8):
                            self.mm(self.ps[j][:, 0:n], wq[ws][:, j, k, :], hx[hs][:, k, 0:n], k == 0, k == 7,
                                    r=[('at_w', ws, j), ('at_hx', hs)], w=[('ps', j)])
                        self.cp('act' if j == 0 else 'dve', dstT[:, t0:t0 + n], self.ps[j][:, 0:n], r=[('ps', j)],
                                w=[('at_qk', bsl, j)])
                    for sub in range(n // 128):
                        for k in range(8):
                            self.mm(self.ps[0][:, sub * 128:(sub + 1) * 128], hx[hs][:, k, sub * 128:(sub + 1) * 128],
                                    wq[ws][:, 2, k, :], k == 0, k == 7,
                                    r=[('at_w', ws, 2), ('at_hx', hs)], w=[('ps', 0)])
                    ti0 = t0 // 128
                    self.cp('act', V[bsl][:, ti0:ti0 + n // 128, :],
                            self.ps[0][:, 0:n].rearrange("p (s c) -> p s c", c=128), r=[('ps', 0)], w=[('at_v', bsl)])
                qlist = []
                if not last:
                    for qi in range(2):
                        qlist.append((qi * 128, [(0, None), (128, None)]))
                for j in range(32):
                    kt = [(CTX + t * 128, ty) for (t, ty) in plan[j]] + [(0, None), (128, None)]
                    qlist.append((CTX + j * 128, kt))
                items = []
                for hh in range(2):
                    for (q0, kts) in qlist:
                        osl = self.rot('at_o', 2)
                        nk = len(kts)
                        for ki, (k0, ty) in enumerate(kts):
                            items.append(dict(hh=hh, q0=q0, ki=ki, nk=nk, k0=k0, ty=ty, osl=osl))

                def stage_s(it):
                    pb = it['hh'] * 64
                    ssl = self.rot('at_s', 3)
                    it['ssl'] = ssl
                    it['psl'] = self.rot('at_ptslot', 4)
                    psS = self.ps[ssl][:, 0:128]
                    skey = ('ps', ssl)
                    q0, k0, ty = it['q0'], it['k0'], it['ty']
                    self.mm(psS, kT[bsl][pb:pb + 64, k0:k0 + 128], qT[bsl][pb:pb + 64, q0:q0 + 128],
                            True, ty is None, r=[('at_qk', bsl, 0), ('at_qk', bsl, 1)], w=[skey])
                    if ty is not None:
                        self.mm(psS, self.ident[:], bias[bsl][:, it['hh'], ty, :], False, True,
                                r=['ident', ('at_bias', bsl)], w=[skey])
                    self.act(PT[it['psl']][:], psS, AF.Exp, r=[skey], w=[('at_pt', it['psl'])], scale=0.125)

                def stage_pv(it):
                    pb = it['hh'] * 64
                    ssl, osl, ki, nk, k0, q0 = it['psl'], it['osl'], it['ki'], it['nk'], it['k0'], it['q0']
                    psO = self.ps[3 + osl]
                    psU = self.ps[5 + osl]
                    pkey = ('at_pt', ssl)
                    self.mm(psO[:, 0:128], V[bsl][:, k0 // 128, :], PT[ssl][:], ki == 0, ki == nk - 1,
                            r=[('at_v', bsl), pkey], w=[('psO', osl)])
                    self.mm(psU[:, 0:128], self.ones[:], PT[ssl][:], ki == 0, ki == nk - 1,
                            r=['ones', pkey], w=[('psU', osl)])
                    if ki == nk - 1:
                        self.P.op('dve', lambda e: e.reciprocal(rec[osl][pb:pb + 64, :], psU[pb:pb + 64, 0:128]),
                                  r=[('psU', osl)], w=[('at_rec', osl)])
                        self.tt('dve', aT[bsl][pb:pb + 64, q0:q0 + 128], psO[pb:pb + 64, 0:128],
                                rec[osl][pb:pb + 64, :], ALU.mult, r=[('psO', osl), ('at_rec', osl)],
                                w=[('at_a', bsl)])
                LA = 2
                for i in range(len(items) + LA):
                    if i < len(items):
                        stage_s(items[i])
                    if i - LA >= 0:
                        stage_pv(items[i - LA])
                a0 = CTX if last else 0
                P.dma('act', S['attnT'].ap()[hp * 128:(hp + 1) * 128, a0:T], aT[bsl][:, a0:T], r=[('at_a', bsl)])

    def phase_s5(self, l):
        nc, P, I, S = self.nc, self.P, self.I, self.S
        Lc = LCH
        TWO_PI = 2.0 * math.pi
        with ExitStack() as st:
            sb = lambda n, s, d: st.enter_context(nc.sbuf_tensor(f"{n}_u{self.uid()}", s, d))
            prm = sb("s5_prm", [128, 3, 32], F32)
            names = ['dt', 'a', 'adt', 'bdt', 'r1', 'kf', 'red', 'sn', 'shf', 'cs', 'nr', 'ni', 'den',
                     'cre', 'cim', 't1', 't2']
            A = {n: sb(f"s5_{n}", [128, 32], F32) for n in names}
            ki = sb("s5_ki", [128, 32], I32)
            phr = sb("s5_phr", [128, 9, 32], F32)
            phi = sb("s5_phi", [128, 9, 32], F32)
            dsk = sb("s5_dsk", [128, 4], F32)
            zero = sb("s5_zero", [128, Lc], F32)
            P.dma('sp', prm[:], I['s5p'].ap()[l], w=['prm'])
            P.dma('sp', dsk[:], I['s5d'].ap()[l], w=['dsk'])
            P.op('dve', lambda e: e.memset(zero[:], 0.0), w=['zero'])
            K_ = ['prm']
            lre, lim, lst = prm[:, 0, :], prm[:, 1, :], prm[:, 2, :]
            a = lambda n: A[n][:]
            self.act(a('dt'), lst, AF.Exp, r=K_, w=K_)
            self.ts('dve', a('a'), lre, -1e-4, None, ALU.min, None, r=K_, w=K_)
            self.tt('dve', a('adt'), a('a'), a('dt'), ALU.mult, r=K_, w=K_)
            self.tt('dve', a('bdt'), lim, a('dt'), ALU.mult, r=K_, w=K_)
            self.act(a('r1'), a('adt'), AF.Exp, r=K_, w=K_)
            self.ts('dve', a('kf'), a('bdt'), 1.0 / TWO_PI, None, ALU.mult, None, r=K_, w=K_)
            self.cp('dve', ki[:], a('kf'), r=K_, w=K_)
            self.cp('dve', a('kf'), ki[:], r=K_, w=K_)
            self.stt(a('red'), a('kf'), -TWO_PI, a('bdt'), ALU.mult, ALU.add, r=K_, w=K_)
            self.ts('dve', a('red'), a('red'), 3.141592, -3.141592, ALU.min, ALU.max, r=K_, w=K_)
            self.act(a('sn'), a('red'), AF.Sin, r=K_, w=K_)
            self.act(a('shf'), a('red'), AF.Sin, r=K_, w=K_, scale=0.5)
            self.tt('dve', a('cs'), a('shf'), a('shf'), ALU.mult, r=K_, w=K_)
            self.ts('dve', a('cs'), a('cs'), -2.0, 1.0, ALU.mult, ALU.add, r=K_, w=K_)
            self.tt('dve', a('nr'), a('r1'), a('cs'), ALU.mult, r=K_, w=K_)
            self.ts('dve', a('nr'), a('nr'), -1.0, None, ALU.add, None, r=K_, w=K_)
            self.tt('dve', a('ni'), a('r1'), a('sn'), ALU.mult, r=K_, w=K_)
            self.tt('dve', a('den'), a('a'), a('a'), ALU.mult, r=K_, w=K_)
            self.tt('dve', a('t1'), lim, lim, ALU.mult, r=K_, w=K_)
            self.tt('dve', a('den'), a('den'), a('t1'), ALU.add, r=K_, w=K_)
            P.op('dve', lambda e: e.reciprocal(a('den'), a('den')), r=K_, w=K_)
            self.tt('dve', a('t1'), a('nr'), a('a'), ALU.mult, r=K_, w=K_)
            self.tt('dve', a('t2'), a('ni'), lim, ALU.mult, r=K_, w=K_)
            self.tt('dve', a('t1'), a('t1'), a('t2'), ALU.add, r=K_, w=K_)
            self.tt('dve', a('cre'), a('t1'), a('den'), ALU.mult, r=K_, w=K_)
            self.tt('dve', a('t1'), a('ni'), a('a'), ALU.mult, r=K_, w=K_)
            self.tt('dve', a('t2'), a('nr'), lim, ALU.mult, r=K_, w=K_)
            self.tt('dve', a('t1'), a('t1'), a('t2'), ALU.subtract, r=K_, w=K_)
            self.tt('dve', a('cim'), a('t1'), a('den'), ALU.mult, r=K_, w=K_)
            self.cp('dve', phr[:, 0, :], a('cs'), r=K_, w=K_)
            self.cp('dve', phi[:, 0, :], a('sn'), r=K_, w=K_)
            for j in range(1, 9):
                self.tt('dve', a('t1'), phr[:, j - 1, :], phr[:, j - 1, :], ALU.mult, r=K_, w=K_)
                self.tt('dve', a('t2'), phi[:, j - 1, :], phi[:, j - 1, :], ALU.mult, r=K_, w=K_)
                self.tt('dve', phr[:, j, :], a('t1'), a('t2'), ALU.subtract, r=K_, w=K_)
                self.tt('dve', a('t1'), phr[:, j - 1, :], phi[:, j - 1, :], ALU.mult, r=K_, w=K_)
                self.ts('dve', phi[:, j, :], a('t1'), 2.0, None, ALU.mult, None, r=K_, w=K_)

            wu = [sb(f"s5_wu{i}", [128, 8, 128], BF16) for i in range(2)]
            hx = [sb(f"s5_hx{i}", [128, 8, 512], BF16) for i in range(2)]
            uT = [sb(f"s5_uT{i}", [128, T], BF16) for i in range(2)]
            BT = [sb(f"s5_BT{i}", [128, 16, 128], BF16) for i in range(2)]
            CM = [sb(f"s5_CM{i}", [128, 16, 128], BF16) for i in range(2)]
            ybuf = sb("s5_y", [128, T], F32)
            gq = [sb(f"s5_g{i}", [128, T], BF16) for i in range(2)]
            tab = [{n: sb(f"s5_tab{il}_{n}", [128, Lc], F32) for n in ('DTr', 'DTi', 'nDTi', 'nDTr', 'MTr', 'MTi', 'nMTi', 'Rc')}
                   for il in range(4)]
            ttmp = sb("s5_ttmp", [128, Lc], F32)
            NZ = 8
            zb = [{n: sb(f"s5_z{i}_{n}", [128, Lc], F32) for n in ('pa', 'pb', 'pc', 'pd', 'gr', 'gi')} for i in range(NZ)]
            db = [{n: sb(f"s5_d{i}_{n}", [128, Lc], F32) for n in ('t1', 't2', 't3', 't4')} for i in range(2)]
            hb = [[sb(f"s5_h{il}_{i}", [128, 4, Lc], BF16) for i in range(2)] for il in range(4)]
            identf = sb("s5_identf", [128, 128], F32)
            P.dma('sp', identf[:], I['ident'].ap(), w=['identf'])
            ini = [[sb(f"s5_ini{il}_{i}", [128, 2], F32) for i in range(2)] for il in range(4)]
            itmp = [sb(f"s5_itmp{i}", [128, 2], F32) for i in range(4)]
            nphi = sb("s5_nphi", [128, 32], F32)
            self.ts('dve', nphi[:], phi[:, 8, :], -1.0, None, ALU.mult, None, r=K_, w=K_)

            hsrc = S['hxT'].ap().rearrange("(k p) t -> p k t", p=128)
            win = I['w_in'].ap()[l].rearrange("(k p) c -> p k c", p=128)
            for q in range(4):
                us = self.rot('s5_u', 2)
                self.load_w(wu[us][:], win[:, :, OFF_U + q * 128: OFF_U + (q + 1) * 128], ('s5_wu', us))
                for d in range(2):
                    for c in range(2):
                        self.load_w(BT[us][:, d * 8 + c * 4: d * 8 + c * 4 + 4, :] if False else
                                    BT[us][:].rearrange("p (d i c) k -> p d i c k", d=2, i=4)[:, d, :, c, :],
                                    I['s5BT'].ap()[l, d, 4 * q:4 * q + 4, c].rearrange("i r k -> r i k"),
                                    ('s5_BT', us, d, c))
                        self.load_w(CM[us][:].rearrange("p (d i c) k -> p d i c k", d=2, i=4)[:, d, :, c, :],
                                    I['s5CM'].ap()[l, d, 4 * q:4 * q + 4, c].rearrange("i r k -> r i k"),
                                    ('s5_CM', us, d, c))
                BTv = BT[us][:].rearrange("p (d i c) k -> p d i c k", d=2, i=4)
                CMv = CM[us][:].rearrange("p (d i c) k -> p d i c k", d=2, i=4)
                ukey = ('s5_uT', us)
                for (t0, n) in self.TT512:
                    hs = self.rot('s5_hx', 2)
                    P.dma('sp', hx[hs][:, :, 0:n], hsrc[:, :, t0:t0 + n], w=[('s5_hx', hs)])
                    for k in range(8):
                        self.mm(self.ps[6][:, 0:n], wu[us][:, k, :], hx[hs][:, k, 0:n], k == 0, k == 7,
                                r=[('s5_wu', us), ('s5_hx', hs)], w=[('ps', 6)])
                    self.cp('act', uT[us][:, t0:t0 + n], self.ps[6][:, 0:n], r=[('ps', 6)], w=[ukey])
                for d in range(2):
                    for il in range(4):
                        c = d * 16 + 4 * q + il
                        tb = tab[il]
                        tk = ('s5_tab', il)
                        sc = lambda arr, j=None: (arr[:, c:c + 1] if j is None else arr[:, j, c:c + 1])
                        P.op('dve', lambda e, tb=tb: e.memset(tb['DTr'][:, 0:1], 1.0), w=[tk])
                        P.op('dve', lambda e, tb=tb: e.memset(tb['DTi'][:, 0:1], 0.0), w=[tk])
                        for j in range(8):
                            n = 1 << j
                            pr, pi = sc(phr, j), sc(phi, j)
                            self.ts('dve', ttmp[:, 0:n], tb['DTi'][:, 0:n], pi, None, ALU.mult, None,
                                    r=[tk, 'prm'], w=['ttmp'])
                            self.stt(tb['DTr'][:, n:2 * n], tb['DTr'][:, 0:n], pr, ttmp[:, 0:n], ALU.mult, ALU.subtract,
                                     r=[tk, 'prm', 'ttmp'], w=[tk])
                            self.ts('dve', ttmp[:, 0:n], tb['DTi'][:, 0:n], pr, None, ALU.mult, None,
                                    r=[tk, 'prm'], w=['ttmp'])
                            self.stt(tb['DTi'][:, n:2 * n], tb['DTr'][:, 0:n], pi, ttmp[:, 0:n], ALU.mult, ALU.add,
                                     r=[tk, 'prm', 'ttmp'], w=[tk])
                        cre, cim = sc(A['cre'][:]), sc(A['cim'][:])
                        self.ts('dve', ttmp[:], tb['DTi'][:], cim, None, ALU.mult, None, r=[tk, 'prm'], w=['ttmp'])
                        self.stt(tb['MTr'][:], tb['DTr'][:], cre, ttmp[:], ALU.mult, ALU.add, r=[tk, 'prm', 'ttmp'], w=[tk])
                        self.ts('dve', ttmp[:], tb['DTi'][:], cre, None, ALU.mult, None, r=[tk, 'prm'], w=['ttmp'])
                        self.stt(tb['MTi'][:], tb['DTr'][:], cim, ttmp[:], ALU.mult, ALU.subtract, r=[tk, 'prm', 'ttmp'], w=[tk])
                        self.ts('pool', tb['nDTi'][:], tb['DTi'][:], -1.0, None, ALU.mult, None, r=[tk], w=[tk])
                        self.ts('pool', tb['nDTr'][:], tb['DTr'][:], -1.0, None, ALU.mult, None, r=[tk], w=[tk])
                        self.ts('pool', tb['nMTi'][:], tb['MTi'][:], -1.0, None, ALU.mult, None, r=[tk], w=[tk])
                        self.ts('dve', tb['Rc'][:], zero[:], sc(A['r1'][:]), None, ALU.add, None, r=['zero', 'prm'], w=[tk])
                        P.op('dve', lambda e, il=il: e.memset(ini[il][0][:], 0.0), w=[('s5_ini', il, 0)])
                    order = list(range(NCH)) if d == 0 else [0] + list(range(NCH - 1, 0, -1))

                    def emit_y(kk, cs):
                        tok0 = cs * Lc
                        psY = self.ps[4]
                        ykey = ('ps', 4)
                        hsl = kk % 2
                        for il in range(4):
                            for j in range(4):
                                self.mm(psY[:, 0:Lc], CMv[:, d, il, j // 2, :], hb[il][hsl][:, j, :],
                                        il == 0 and j == 0, il == 3 and j == 3,
                                        r=[('s5_CM', us, d, j // 2), (('s5_h', il, hsl), j)], w=[ykey])
                        if d == 0:
                            self.stt(ybuf[:, tok0:tok0 + Lc], uT[us][:, tok0:tok0 + Lc], dsk[:, q:q + 1], psY[:, 0:Lc],
                                     ALU.mult, ALU.add, r=[ukey, 'dsk', ykey], w=[('s5_y', cs)])
                        else:
                            self.tt('dve', ybuf[:, tok0:tok0 + Lc], psY[:, 0:Lc], ybuf[:, tok0:tok0 + Lc], ALU.add,
                                    r=[ykey, ('s5_y', cs)], w=[('s5_y', cs)])

                    pend = None
                    for kk, cs in enumerate(order):
                        tok0 = cs * Lc
                        par = kk % 2
                        hsl = kk % 2
                        ctxs = []
                        for il in range(4):
                            bs = self.rot('s5_psB', 2)
                            psB = self.ps[bs]
                            bkey = ('ps', bs)
                            for cc in range(2):
                                self.mm(psB[:, cc * Lc:(cc + 1) * Lc], BTv[:, d, il, cc, :], uT[us][:, tok0:tok0 + Lc],
                                        True, True, r=[('s5_BT', us, d, cc), ukey], w=[bkey])
                            if d == 0:
                                bre, bim = psB[:, 0:Lc], psB[:, Lc:2 * Lc]
                            else:
                                bre, bim = psB[:, Lc - 1::-1][:, 0:Lc], psB[:, 2 * Lc - 1:Lc - 1:-1]
                            zs = self.rot('s5_z', NZ)
                            z, zk, tb, tk = zb[zs], ('s5_z', zs), tab[il], ('s5_tab', il)
                            self.tt('dve', z['pa'][:], bre, tb['MTr'][:], ALU.mult, r=[bkey, tk], w=[(zk, 'pa')])
                            self.tt('dve', z['pb'][:], bim, tb['nMTi'][:], ALU.mult, r=[bkey, tk], w=[(zk, 'pb')])
                            self.tt('dve', z['pc'][:], bre, tb['MTi'][:], ALU.mult, r=[bkey, tk], w=[(zk, 'pc')])
                            self.tt('dve', z['pd'][:], bim, tb['MTr'][:], ALU.mult, r=[bkey, tk], w=[(zk, 'pd')])
                            zbank = (2, 3, 5, 6)[il]
                            psZ = self.ps[zbank]
                            zkey = ('ps', zbank)
                            self.mm(psZ[:, 0:Lc], identf[:], z['pa'][:], True, False, r=['identf', (zk, 'pa')], w=[zkey])
                            self.mm(psZ[:, 0:Lc], identf[:], z['pb'][:], False, True, r=['identf', (zk, 'pb')], w=[zkey])
                            self.mm(psZ[:, Lc:2 * Lc], identf[:], z['pc'][:], True, False, r=['identf', (zk, 'pc')], w=[zkey])
                            self.mm(psZ[:, Lc:2 * Lc], identf[:], z['pd'][:], False, True, r=['identf', (zk, 'pd')], w=[zkey])
                            ctxs.append(dict(il=il, z=z, zk=zk, gk=('s5_g', zs), tb=tb, tk=tk, psZ=psZ, zkey=zkey,
                                             c=d * 16 + 4 * q + il))
                        if pend is not None:
                            emit_y(*pend)
                        for cx in ctxs:
                            z, tb, tk, gk, il, psZ, zkey = cx['z'], cx['tb'], cx['tk'], cx['gk'], cx['il'], cx['psZ'], cx['zkey']
                            ik = ('s5_ini', il, par)
                            iv = ini[il][par]
                            P.op('dve', lambda e, z=z, tb=tb, iv=iv, psZ=psZ: e.tensor_tensor_scan(
                                z['gr'][:], tb['Rc'][:], psZ[:, 0:Lc], iv[:, 0:1], ALU.mult, ALU.add),
                                r=[zkey, tk, ik], w=[(gk, 'r')])
                            P.op('dve', lambda e, z=z, tb=tb, iv=iv, psZ=psZ: e.tensor_tensor_scan(
                                z['gi'][:], tb['Rc'][:], psZ[:, Lc:2 * Lc], iv[:, 1:2], ALU.mult, ALU.add),
                                r=[zkey, tk, ik], w=[(gk, 'i')])
                        for cx in ctxs:
                            z, gk, il, c = cx['z'], cx['gk'], cx['il'], cx['c']
                            ink = ('s5_ini', il, 1 - par)
                            inx = ini[il][1 - par]
                            pr, pi, npi = phr[:, 8, c:c + 1], phi[:, 8, c:c + 1], nphi[:, c:c + 1]
                            gre, gie = z['gr'][:, Lc - 1:Lc], z['gi'][:, Lc - 1:Lc]
                            itk = ('itmp', il)
                            self.act(itmp[il][:, 0:1], gie, AF.Identity, r=[(gk, 'i'), 'prm'], w=[itk], scale=npi)
                            self.act(itmp[il][:, 1:2], gie, AF.Identity, r=[(gk, 'i'), 'prm'], w=[itk], scale=pr)
                            self.act(inx[:, 0:1], gre, AF.Identity, r=[(gk, 'r'), 'prm', itk], w=[ink], scale=pr,
                                     bias=itmp[il][:, 0:1])
                            self.act(inx[:, 1:2], gre, AF.Identity, r=[(gk, 'r'), 'prm', itk], w=[ink], scale=pi,
                                     bias=itmp[il][:, 1:2])
                        for cx in ctxs:
                            z, tb, tk, gk, il = cx['z'], cx['tb'], cx['tk'], cx['gk'], cx['il']
                            hk = ('s5_h', il, hsl)
                            hv = [hb[il][hsl][:, j, :] if d == 0 else hb[il][hsl][:, j, ::-1] for j in range(4)]
                            self.tt('pool', hv[0], z['gr'][:], tb['DTr'][:], ALU.mult, r=[(gk, 'r'), tk], w=[(hk, 0)])
                            self.tt('pool', hv[1], z['gi'][:], tb['nDTi'][:], ALU.mult, r=[(gk, 'i'), tk], w=[(hk, 1)])
                            self.tt('pool', hv[2], z['gr'][:], tb['nDTi'][:], ALU.mult, r=[(gk, 'r'), tk], w=[(hk, 2)])
                            self.tt('dve', hv[3], z['gi'][:], tb['nDTr'][:], ALU.mult, r=[(gk, 'i'), tk], w=[(hk, 3)])
                        pend = (kk, cs)
                    emit_y(*pend)
                gs_ = self.rot('s5_gq', 2)
                for cs in range(NCH):
                    self.act(gq[gs_][:, cs * Lc:(cs + 1) * Lc], ybuf[:, cs * Lc:(cs + 1) * Lc], AF.Gelu_apprx_tanh,
                             r=[('s5_y', cs)], w=[('s5_gq', gs_)])
                P.dma('act', S['gT'].ap()[q * 128:(q + 1) * 128, :], gq[gs_][:], r=[('s5_gq', gs_)])

    def phase_merge(self, l, src):
        nc, P, I, S = self.nc, self.P, self.I, self.S
        last = (l == DEPTH - 1)
        tiles = self.TILES[1:] if last else self.TILES
        with ExitStack() as st:
            sb = lambda n, s, d: st.enter_context(nc.sbuf_tensor(f"{n}_u{self.uid()}", s, d))
            wco = sb("mg_wco", [128, 4, D], BF16)
            wga = sb("mg_wga", [128, 4, D], BF16)
            wgb = sb("mg_wgb", [128, 4, D], BF16)
            wno = sb("mg_wno", [128, 4, D], BF16)
            wg = sb("mg_wg", [128, 8, 3 * D], BF16)
            hx = [sb(f"mg_hx{i}", [128, 8, 512], BF16) for i in range(2)]
            br = [[sb(f"mg_br{j}_{i}", [128, 4, 512], BF16) for i in range(2)] for j in range(3)]
            mg = [sb(f"mg_mg{i}", [128, 8, 512], BF16) for i in range(2)]
            sg = [sb(f"mg_sg{i}", [128, 512], F32) for i in range(4)]
            tm = [sb(f"mg_tm{i}", [128, 512], F32) for i in range(4)]
            acc = [sb(f"mg_acc{i}", [128, 512], F32) for i in range(2)]
            for wt_, nm in ((wco, 'conv_out'), (wga, 's5_glu_a'), (wgb, 's5_glu_b'), (wno, 'na_out')):
                for kc in range(4):
                    self.load_w(wt_[:, kc, :], I[nm].ap()[l][kc * 128:(kc + 1) * 128, :], ('mg_w', nm))
            win = I['w_in'].ap()[l].rearrange("(k p) c -> p k c", p=128)
            for k in range(8):
                self.load_w(wg[:, k, :], win[:, k, OFF_GA:OFF_GA + 3 * D], 'mg_wg')
            hsrc = S['hxT'].ap().rearrange("(k p) t -> p k t", p=128)
            bsrc = [S[nm].ap().rearrange("(k p) t -> p k t", p=128) for nm in ('convbT', 'gT', 'attnT')]
            mdst = S['mgT'].ap().rearrange("(k p) t -> p k t", p=128)
            for (t0, n) in tiles:
                hs = self.rot('mg_hx', 2)
                P.dma('sp', hx[hs][:, :, 0:n], hsrc[:, :, t0:t0 + n], w=[('mg_hx', hs)])
                for j in range(3):
                    P.dma('sp', br[j][hs][:, :, 0:n], bsrc[j][:, :, t0:t0 + n], w=[('mg_br', j, hs)])
                ms = self.rot('mg_mg', 2)
                for fo in range(8):
                    fs = slice(fo * 128, (fo + 1) * 128)

                    def proj(bank, w_, x_, nk, wkeys, xkey, coff=0):
                        for k in range(nk):
                            self.mm(self.ps[bank][:, 0:n], w_[:, k, coff + fo * 128: coff + (fo + 1) * 128],
                                    x_[:, k, 0:n], k == 0, k == nk - 1, r=wkeys + [xkey], w=[('ps', bank)])
                    hk = ('mg_hx', hs)
                    proj(0, wco, br[0][hs], 4, [('mg_w', 'conv_out')], ('mg_br', 0, hs))
                    proj(1, wg, hx[hs], 8, ['mg_wg'], hk, 0)
                    proj(2, wga, br[1][hs], 4, [('mg_w', 's5_glu_a')], ('mg_br', 1, hs))
                    proj(3, wgb, br[1][hs], 4, [('mg_w', 's5_glu_b')], ('mg_br', 1, hs))
                    proj(4, wg, hx[hs], 8, ['mg_wg'], hk, D)
                    proj(5, wno, br[2][hs], 4, [('mg_w', 'na_out')], ('mg_br', 2, hs))
                    proj(6, wg, hx[hs], 8, ['mg_wg'], hk, 2 * D)
                    a_ = self.rot('mg_acc', 2)
                    A = acc[a_][:, 0:n]
                    ak = ('mg_acc', a_)
                    sgs = [self.rot('mg_sg', 4) for _ in range(4)]
                    tms = [self.rot('mg_tm', 4) for _ in range(2)]
                    self.act(sg[sgs[0]][:, 0:n], self.ps[1][:, 0:n], AF.Sigmoid, r=[('ps', 1)], w=[('mg_sg', sgs[0])])
                    self.tt('dve', A, self.ps[0][:, 0:n], sg[sgs[0]][:, 0:n], ALU.mult,
                            r=[('ps', 0), ('mg_sg', sgs[0])], w=[ak])
                    self.act(sg[sgs[1]][:, 0:n], self.ps[3][:, 0:n], AF.Sigmoid, r=[('ps', 3)], w=[('mg_sg', sgs[1])])
                    self.tt('dve', tm[tms[0]][:, 0:n], self.ps[2][:, 0:n], sg[sgs[1]][:, 0:n], ALU.mult,
                            r=[('ps', 2), ('mg_sg', sgs[1])], w=[('mg_tm', tms[0])])
                    self.act(sg[sgs[2]][:, 0:n], self.ps[4][:, 0:n], AF.Sigmoid, r=[('ps', 4)], w=[('mg_sg', sgs[2])])
                    self.tt('pool', tm[tms[0]][:, 0:n], tm[tms[0]][:, 0:n], sg[sgs[2]][:, 0:n], ALU.mult,
                            r=[('mg_tm', tms[0]), ('mg_sg', sgs[2])], w=[('mg_tm', tms[0])])
                    self.tt('pool', A, A, tm[tms[0]][:, 0:n], ALU.add, r=[ak, ('mg_tm', tms[0])], w=[ak])
                    self.act(sg[sgs[3]][:, 0:n], self.ps[6][:, 0:n], AF.Sigmoid, r=[('ps', 6)], w=[('mg_sg', sgs[3])])
                    self.tt('dve', tm[tms[1]][:, 0:n], self.ps[5][:, 0:n], sg[sgs[3]][:, 0:n], ALU.mult,
                            r=[('ps', 5), ('mg_sg', sgs[3])], w=[('mg_tm', tms[1])])
                    self.tt('pool', mg[ms][:, fo, 0:n], A, tm[tms[1]][:, 0:n], ALU.add,
                            r=[ak, ('mg_tm', tms[1])], w=[('mg_mg', ms)])
                P.dma('act', mdst[:, :, t0:t0 + n], mg[ms][:, :, 0:n], r=[('mg_mg', ms)])
        P.barrier()
        with ExitStack() as st:
            sb = lambda n, s, d: st.enter_context(nc.sbuf_tensor(f"{n}_u{self.uid()}", s, d))
            wo = sb("mo_wo", [128, 8, D], BF16)
            wov = I['w_out'].ap()[l].rearrange("(k p) c -> p k c", p=128)
            for k in range(8):
                self.load_w(wo[:, k, :], wov[:, k, :], 'mo_wo')
            mgt = [sb(f"mo_mg{i}", [128, 8, 512], BF16) for i in range(2)]
            xt = [sb(f"mo_x{i}", [128, D], F32) for i in range(2)]
            xn = [sb(f"mo_xn{i}", [128, D], F32) for i in range(2)]
            hb = [sb(f"mo_hb{i}", [128, D], BF16) for i in range(2)]
            hT = [sb(f"mo_hT{i}", [128, 8, 512], BF16) for i in range(2)]
            scr = self.norm_scratch(st, "mo")
            msrc = S['mgT'].ap().rearrange("(k p) t -> p k t", p=128)
            hdst = S['h2T'].ap().rearrange("(k p) t -> p k t", p=128)
            mods = None
            cur = None
            for (t0, n) in tiles:
                cond = 1 if t0 < CTX else 0
                if cur != cond:
                    if mods is None:
                        mods = self.mod_tiles(st, l, cond, ('A1', 'G2', 'S2'))
                        gt2 = st.enter_context(nc.sbuf_tensor(f"mo_g2_u{self.uid()}", [128, D], F32))
                    else:
                        self.reload_mod(mods, l, cond, gt2)
                    cur = cond
                hs = self.rot('mo_mg', 2)
                P.dma('sp', mgt[hs][:, :, 0:n], msrc[:, :, t0:t0 + n], w=[('mo_mg', hs)])
                ts_ = self.rot('mo_hT', 2)
                for sub in range(n // 128):
                    xs = self.rot('mo_x', 2)
                    r0 = t0 + sub * 128
                    P.dma('sp', xt[xs][:], src.ap()[r0:r0 + 128, :], w=[('mo_x', xs)])
                    for half in range(2):
                        for k in range(8):
                            self.mm(self.ps[half][:, :], mgt[hs][:, k, sub * 128:(sub + 1) * 128],
                                    wo[:, k, half * 512:(half + 1) * 512], k == 0, k == 7,
                                    r=[('mo_mg', hs), 'mo_wo'], w=[('ps', half)])
                        self.tt('dve', xn[xs][:, half * 512:(half + 1) * 512], self.ps[half][:, :],
                                mods['A1'][0][:, half * 512:(half + 1) * 512], ALU.mult,
                                r=[('ps', half), mods['A1'][1]], w=[('mo_xn', xs)])
                    self.tt('pool', xn[xs][:], xn[xs][:], xt[xs][:], ALU.add, r=[('mo_xn', xs), ('mo_x', xs)],
                            w=[('mo_xn', xs)])
                    P.dma('act', S['xres'].ap()[r0:r0 + 128, :], xn[xs][:], r=[('mo_xn', xs)])
                    bs = self.rot('mo_hb', 2)
                    self.norm_sub(xn[xs][:], ('mo_xn', xs), mods['G2'], mods['S2'], hb[bs][:], ('mo_hb', bs), scr[bs])
                    self.transpose_out(hb[bs], ('mo_hb', bs), hT[ts_], ('mo_hT', ts_), sub)
                P.dma('act', hdst[:, :, t0:t0 + n], hT[ts_][:, :, 0:n], r=[('mo_hT', ts_)])

    def reload_mod(self, mods, l, cond, gtmp):
        base = (l * 2 + cond) * 6 * D
        idx = {'S1': 0, 'G1': 1, 'A1': 2, 'S2': 3, 'G2': 4, 'A2': 5}
        for n, (t, key) in mods.items():
            self.load_rep(t[:], self.S['modv'], base + idx[n] * D, key)
            if n in ('G1', 'G2'):
                gt = self.I['norm1_g'] if n == 'G1' else self.I['norm2_g']
                self.load_rep(gtmp[:], gt, l * D, 'modgtmp')
                self.stt(t[:], t[:], 1.0, gtmp[:], ALU.add, ALU.mult, r=[key, 'modgtmp'], w=[key])

    def phase_mlp(self, l):
        nc, P, I, S = self.nc, self.P, self.I, self.S
        last = (l == DEPTH - 1)
        tiles = self.TILES[1:] if last else self.TILES
        with ExitStack() as st:
            sb = lambda n, s, d: st.enter_context(nc.sbuf_tensor(f"{n}_u{self.uid()}", s, d))
            w1 = sb("ml_w1", [128, 8, 4 * D], BF16)
            w1v = I['mlp_w1'].ap()[l].rearrange("(k p) c -> p k c", p=128)
            for k in range(8):
                for hf in range(2):
                    self.load_w(w1[:, k, hf * 2048:(hf + 1) * 2048], w1v[:, k, hf * 2048:(hf + 1) * 2048], 'ml_w1')
            h2 = [sb(f"ml_h2{i}", [128, 8, 512], BF16) for i in range(2)]
            rl = [sb(f"ml_rl{i}", [128, 512], F32) for i in range(3)]
            hd = [sb(f"ml_hd{i}", [128, 8, 512], BF16) for i in range(2)]
            hsrc = S['h2T'].ap().rearrange("(k p) t -> p k t", p=128)
            ddst = S['hidT'].ap().rearrange("(k p) t -> p k t", p=128)
            for (t0, n) in tiles:
                hs = self.rot('ml_h2', 2)
                P.dma('sp', h2[hs][:, :, 0:n], hsrc[:, :, t0:t0 + n], w=[('ml_h2', hs)])
                for fg in range(4):
                    ds = self.rot('ml_hd', 2)
                    for fi in range(8):
                        fc = fg * 8 + fi
                        bank = self.rot('ml_bank', 6)
                        for k in range(8):
                            self.mm(self.ps[bank][:, 0:n], w1[:, k, fc * 128:(fc + 1) * 128], h2[hs][:, k, 0:n],
                                    k == 0, k == 7, r=['ml_w1', ('ml_h2', hs)], w=[('ps', bank)])
                        rs = self.rot('ml_rl', 3)
                        self.act(rl[rs][:, 0:n], self.ps[bank][:, 0:n], AF.Relu, r=[('ps', bank)], w=[('ml_rl', rs)])
                        self.tt('dve' if fi % 2 == 0 else 'pool', hd[ds][:, fi, 0:n], rl[rs][:, 0:n], rl[rs][:, 0:n],
                                ALU.mult, r=[('ml_rl', rs)], w=[('ml_hd', ds)])
                    P.dma('act', ddst[:, fg * 8:(fg + 1) * 8, t0:t0 + n], hd[ds][:, :, 0:n], r=[('ml_hd', ds)])
        P.barrier()
        with ExitStack() as st:
            sb = lambda n, s, d: st.enter_context(nc.sbuf_tensor(f"{n}_u{self.uid()}", s, d))
            w2 = sb("ml_w2", [128, 32, D], BF16)
            w2v = I['mlp_w2'].ap()[l].rearrange("(k p) c -> p k c", p=128)
            for k in range(32):
                self.load_w(w2[:, k, :], w2v[:, k, :], 'ml_w2')
            hdt = [sb(f"ml_hdt{i}", [128, 32, 256], BF16) for i in range(2)]
            xt = [sb(f"ml_x{i}", [128, D], F32) for i in range(2)]
            xn = [sb(f"ml_xn{i}", [128, D], F32) for i in range(2)]
            hb = [sb(f"ml_hb{i}", [128, D], BF16) for i in range(2)]
            hT = [sb(f"ml_hT{i}", [128, 8, 256], BF16) for i in range(2)]
            ob = [sb(f"ml_ob{i}", [128, D], F32) for i in range(2)]
            scr = self.norm_scratch(st, "ml")
            dsrc = S['hidT'].ap().rearrange("(k p) t -> p k t", p=128)
            hdst = S['hxT'].ap().rearrange("(k p) t -> p k t", p=128)
            fin = None
            if last:
                fin = sb("ml_fin", [128, D], F32)
                self.load_rep(fin[:], I['final_g'], 0, 'ml_fin')
            amods = None
            nmods = None
            cur = None
            tiles256 = []
            for (t0, n) in tiles:
                for h in range(n // 256):
                    tiles256.append((t0 + h * 256, 256))
            for (t0, n) in tiles256:
                cond = 1 if t0 < CTX else 0
                if cur != cond:
                    if amods is None:
                        amods = self.mod_tiles(st, l, cond, ('A2',))
                        gt1 = st.enter_context(nc.sbuf_tensor(f"ml_g1_u{self.uid()}", [128, D], F32))
                        if not last:
                            nmods = self.mod_tiles(st, l + 1, cond, ('G1', 'S1'))
                    else:
                        self.reload_mod(amods, l, cond, gt1)
                        if not last:
                            self.reload_mod(nmods, l + 1, cond, gt1)
                    cur = cond
                hs = self.rot('ml_hdt', 2)
                for kq in range(4):
                    P.dma('sp', hdt[hs][:, kq * 8:(kq + 1) * 8, :], dsrc[:, kq * 8:(kq + 1) * 8, t0:t0 + n],
                          w=[('ml_hdt', hs)])
                ts_ = self.rot('ml_hT', 2)
                for sub in range(2):
                    xs = self.rot('ml_x', 2)
                    r0 = t0 + sub * 128
                    P.dma('sp', xt[xs][:], S['xres'].ap()[r0:r0 + 128, :], w=[('ml_x', xs)])
                    for half in range(2):
                        bank = self.rot('ml_bank2', 4)
                        for k in range(32):
                            self.mm(self.ps[bank][:, :], hdt[hs][:, k, sub * 128:(sub + 1) * 128],
                                    w2[:, k, half * 512:(half + 1) * 512], k == 0, k == 31,
                                    r=[('ml_hdt', hs), 'ml_w2'], w=[('ps', bank)])
                        self.tt('dve', xn[xs][:, half * 512:(half + 1) * 512], self.ps[bank][:, :],
                                amods['A2'][0][:, half * 512:(half + 1) * 512], ALU.mult,
                                r=[('ps', bank), amods['A2'][1]], w=[('ml_xn', xs)])
                    self.tt('pool', xn[xs][:], xn[xs][:], xt[xs][:], ALU.add, r=[('ml_xn', xs), ('ml_x', xs)],
                            w=[('ml_xn', xs)])
                    bs = self.rot('ml_hb', 2)
                    if not last:
                        P.dma('act', S['xres'].ap()[r0:r0 + 128, :], xn[xs][:], r=[('ml_xn', xs)])
                        self.norm_sub(xn[xs][:], ('ml_xn', xs), nmods['G1'], nmods['S1'], hb[bs][:], ('ml_hb', bs), scr[bs])
                        self.transpose_out(hb[bs], ('ml_hb', bs), hT[ts_], ('ml_hT', ts_), sub)
                    else:
                        self.norm_sub(xn[xs][:], ('ml_xn', xs), (fin, 'ml_fin'), None, ob[bs][:], ('ml_ob', bs), scr[bs])
                        P.dma('act', self.out.ap()[r0 - CTX:r0 - CTX + 128, :], ob[bs][:], r=[('ml_ob', bs)])
                if not last:
                    P.dma('act', hdst[:, :, t0:t0 + n], hT[ts_][:, :, 0:n], r=[('ml_hT', ts_)])

def _core_inputs(inp, sh, b):
    d = dict(sh)
    d['xin'] = np.ascontiguousarray(np.concatenate([inp['ctx'][b], inp['x'][b]], axis=0), dtype=np.float32)
    cT = np.stack([inp['c'][b].reshape(8, 128).T, inp['c_ctx'].reshape(8, 128).T], axis=2)
    d['cT'] = np.ascontiguousarray(cT.reshape(128, 16), dtype=np.float32)
    return d


def kernel(**inputs):
    inp = {k: np.asarray(v) for k, v in inputs.items()}
    sh = _prep_shared(inp)
    bld = Builder()
    nc = bld.build()
    in_maps = [_core_inputs(inp, sh, b) for b in range(8)]
    in_maps = [{k: m[k] for k in bld.inputs} for m in in_maps]
    res = run_bass_kernel_spmd(nc, in_maps, core_ids=list(range(8)))
    return np.stack([np.asarray(r['out']) for r in res.results], axis=0).astype(np.float32)
```

```python
import math
from contextlib import ExitStack
import numpy as np
import concourse.bass as bass
import concourse.mybir as mybir
from concourse.bass_utils import run_bass_kernel_spmd

F32 = mybir.dt.float32
BF16 = mybir.dt.bfloat16
I32 = mybir.dt.int32
ALU = mybir.AluOpType
AF = mybir.ActivationFunctionType

SEM_LIMIT = 30000
SAME_ENG_WAITS = True
N_DMA_SEMS = 40

DEPTH = 4
D = 1024
T = 4352
CTX = 256
SEQ = 4096
NIN = 6656
OFF_XA, OFF_XB, OFF_XC, OFF_U, OFF_Q, OFF_K, OFF_V, OFF_GA, OFF_GB, OFF_GC = (
    0, 512, 1024, 1536, 2048, 2560, 3072, 3584, 4608, 5632)
EPS = 1e-6
NTYPE = 21
LCH = 256
NCH = T // LCH


class Prog:
    def __init__(self, nc):
        self.nc = nc
        self.ops = []
        self.last_w = {}
        self.readers = {}
        self.engs = {'pe': nc.tensor, 'dve': nc.vector, 'act': nc.scalar,
                     'pool': nc.gpsimd, 'sp': nc.sync}
        self._bar_from = 0

    def op(self, eng, fn, r=(), w=(), dma=False):
        i = len(self.ops)
        deps = set()
        for k in r:
            if k in self.last_w:
                deps.add(self.last_w[k])
        for k in w:
            if k in self.last_w:
                deps.add(self.last_w[k])
            for j in self.readers.get(k, ()):
                deps.add(j)
        fd = set()
        for j in deps:
            oj = self.ops[j]
            if oj['eng'] == eng and not oj['dma'] and not dma:
                if eng == 'pe' or not SAME_ENG_WAITS:
                    continue
                israw = any(self.last_w.get(k) == j for k in list(r) + list(w))
                if not israw:
                    continue
            fd.add(j)
        for k in w:
            self.last_w[k] = i
            self.readers[k] = []
        for k in r:
            self.readers.setdefault(k, []).append(i)
        self.ops.append(dict(eng=eng, fn=fn, deps=fd, dma=dma, sig=False))
        return i

    def dma(self, q, out, in_, r=(), w=(), **kw):
        return self.op(q, lambda e: e.dma_start(out=out, in_=in_, **kw), r, w, dma=True)

    def barrier(self):
        deps = set()
        lastc = {}
        for idx, o in enumerate(self.ops):
            if o['fn'] is None:
                continue
            if o['dma']:
                if idx >= self._bar_from:
                    deps.add(idx)
            else:
                lastc[o['eng']] = idx
        deps |= set(lastc.values())
        self._bar_from = len(self.ops)
        for e in self.engs:
            self.ops.append(dict(eng=e, fn=None, deps=set(deps), dma=False, sig=False))
        self.last_w = {}
        self.readers = {}

    def emit(self):
        nc = self.nc
        ops = self.ops
        for o in ops:
            for j in o['deps']:
                ops[j]['sig'] = True
        dma_sems = [nc.alloc_semaphore(name=f"dq{i}") for i in range(N_DMA_SEMS)]
        dma_cnt = [0] * N_DMA_SEMS
        dma_last = [None] * N_DMA_SEMS
        eng_sem = {}
        eng_cnt = {}
        nsem = [0]

        def new_eng_sem(e):
            nsem[0] += 1
            eng_sem[e] = nc.alloc_semaphore(name=f"s_{e}_{nsem[0]}")
            eng_cnt[e] = 0

        for e in self.engs:
            new_eng_sem(e)
        rr = 0
        for idx, o in enumerate(ops):
            if o['dma']:
                s = rr % N_DMA_SEMS
                rr += 1
                if dma_cnt[s] + 16 > SEM_LIMIT:
                    if dma_last[s] is not None:
                        o['deps'].add(dma_last[s])
                    dma_sems[s] = nc.alloc_semaphore(name=f"dq{s}_{idx}")
                    dma_cnt[s] = 0
                    dma_last[s] = None
                if dma_last[s] is not None:
                    o['deps'].add(dma_last[s])
                dma_cnt[s] += 16
                o['done'] = (dma_sems[s], dma_cnt[s])
                dma_last[s] = idx
            elif o['sig'] and o['fn'] is not None:
                e = o['eng']
                if eng_cnt[e] + 1 > SEM_LIMIT:
                    new_eng_sem(e)
                eng_cnt[e] += 1
                o['done'] = (eng_sem[e], eng_cnt[e])
            else:
                o['done'] = None

        def resolve(j, acc, seen):
            if j in seen:
                return
            seen.add(j)
            oj = ops[j]
            if oj['done'] is not None:
                acc.add(j)
            elif oj['fn'] is None:
                for jj in oj['deps']:
                    resolve(jj, acc, seen)
            else:
                raise RuntimeError("dep on unsignaled op")

        per_eng = {e: [] for e in self.engs}
        for idx, o in enumerate(ops):
            per_eng[o['eng']].append(idx)
        self.n_inst = {e: len(v) for e, v in per_eng.items()}
        with nc.Block() as block:
            def make(e):
                def body(eng):
                    waited = {}
                    for idx in per_eng[e]:
                        o = ops[idx]
                        acc = set()
                        seen = set()
                        for j in o['deps']:
                            resolve(j, acc, seen)
                        need = {}
                        for j in acc:
                            sem, val = ops[j]['done']
                            key = id(sem)
                            if waited.get(key, 0) >= val:
                                continue
                            if key not in need or need[key][1] < val:
                                need[key] = (sem, val)
                        for key, (sem, val) in need.items():
                            eng.wait_ge(sem, val)
                            waited[key] = val
                        if o['fn'] is not None:
                            inst = o['fn'](eng)
                            if o['done'] is not None:
                                sem, val = o['done']
                                inst.then_inc(sem, 16 if o['dma'] else 1)
                return body
            block.tensor(make('pe'))
            block.vector(make('dve'))
            block.scalar(make('act'))
            block.gpsimd(make('pool'))
            block.sync(make('sp'))


def _na_tile_plan():
    types = {}
    plan = []
    for j in range(32):
        r0a = min(max(2 * j - 4, 0), 56)
        r0b = min(max(2 * j + 1 - 4, 0), 56)
        tlo = r0a // 2
        thi = (r0b + 7) // 2
        lst = []
        for t in range(tlo, thi + 1):
            key = (t - j, r0a - 2 * j, r0b - 2 * j)
            if key not in types:
                types[key] = len(types)
            lst.append((t, types[key]))
        plan.append(lst)
    return plan, types


def _na_bias_index():
    plan, types = _na_tile_plan()
    nt = len(types)
    idx_r = np.zeros((nt, 128, 128), np.int64)
    idx_c = np.zeros((nt, 128, 128), np.int64)
    mask = np.zeros((nt, 128, 128), np.float32)
    col = np.arange(64)
    cs = np.clip(col - 8, 0, 48)
    for (delta, ra, rb), ty in types.items():
        for qr2 in range(2):
            r0rel = (ra, rb)[qr2]
            for kr2 in range(2):
                krel = 2 * delta + kr2
                dr = krel - qr2
                row_ok = (krel >= r0rel) and (krel < r0rel + 8)
                for qc in range(64):
                    kc = col
                    ok = row_ok & (kc >= cs[qc]) & (kc < cs[qc] + 16)
                    q = qr2 * 64 + qc
                    k = kr2 * 64 + kc
                    idx_r[ty, k, q] = np.clip(dr + 7, 0, 14)
                    idx_c[ty, k, q] = np.clip(kc - qc + 15, 0, 30)
                    mask[ty, k, q] = np.where(ok, 0.0, -1e30)
    return plan, nt, idx_r, idx_c, mask


_NA = None


def _na():
    global _NA
    if _NA is None:
        _NA = _na_bias_index()
    return _NA


def _prep_shared(inp):
    L = DEPTH
    sh = {}
    f = lambda a: np.ascontiguousarray(a, dtype=np.float32)
    for k in ('w_mod', 'w_in', 'conv_out', 's5_glu_a', 's5_glu_b', 'na_out', 'w_out',
              'mlp_w1', 'mlp_w2'):
        sh[k] = f(inp[k])
    sh['b_mod'] = f(inp['b_mod'])
    sh['norm1_g'] = f(inp['norm1_g'])
    sh['norm2_g'] = f(inp['norm2_g'])
    sh['final_g'] = f(inp['final_norm_g']).reshape(1, D)
    sh['conv_w'] = f(inp['conv_w'].reshape(L, 3, 4, 128).transpose(0, 3, 2, 1))
    def gp(a):
        a = a.reshape(L, 2, 16, 2, 64)
        return a.transpose(0, 3, 4, 1, 2).reshape(L, 128, 32)
    ls = np.broadcast_to(inp['s5_log_step'][:, :, :, None], (L, 2, 32, 64))
    sh['s5p'] = f(np.stack([gp(inp['s5_lam_re']), gp(inp['s5_lam_im']), gp(ls)], axis=2))
    def bt(B):
        out = np.zeros((L, 2, 16, 128, 128), np.float32)
        for i in range(16):
            for g2 in range(2):
                g = 2 * i + g2
                gl = g % 8
                out[:, :, i, gl * 16:(gl + 1) * 16, g2 * 64:(g2 + 1) * 64] = \
                    B[:, :, g].transpose(0, 1, 3, 2)
        return out
    sh['s5BT'] = f(np.stack([bt(inp['s5_b_re']), bt(inp['s5_b_im'])], axis=3))
    def cm(C):
        out = np.zeros((L, 2, 16, 128, 128), np.float32)
        for i in range(16):
            for g2 in range(2):
                g = 2 * i + g2
                gl = g % 8
                out[:, :, i, g2 * 64:(g2 + 1) * 64, gl * 16:(gl + 1) * 16] = \
                    C[:, :, g].transpose(0, 1, 3, 2)
        return out
    sh['s5CM'] = f(np.stack([cm(inp['s5_c_re']), cm(inp['s5_c_im'])], axis=3))
    sh['s5d'] = f(inp['s5_d'].reshape(L, 4, 128).transpose(0, 2, 1))
    plan, nt, idx_r, idx_c, mask = _na()
    rpb = inp['na_rpb']
    sh['rpbg'] = f(rpb[:, :, idx_r, idx_c])
    sh['na_mask'] = f(mask)
    sh['ident'] = np.eye(128, dtype=np.float32)
    return sh


class Builder:
    def __init__(self, layers=DEPTH, dbg=(), stop=None, only=None):
        self.layers = layers
        self.stop = stop
        self.only = only
        self.dbg = set(dbg)
        nc = bass.Bass("TRN2", target_bir_lowering=False)
        self.nc = nc
        self.P = Prog(nc)
        self.inputs = {}
        self.cnt = {}

    def din(self, name, shape, dt=F32):
        t = self.nc.dram_tensor(name, list(shape), dt, kind="ExternalInput")
        self.inputs[name] = t
        return t

    def dscr(self, name, shape, dt):
        kind = "ExternalOutput" if name in self.dbg else "Internal"
        return self.nc.dram_tensor(name, list(shape), dt, kind=kind)

    def uid(self):
        self._uid = getattr(self, '_uid', 0) + 1
        return self._uid

    def rot(self, name, n):
        c = self.cnt.get(name, 0)
        self.cnt[name] = c + 1
        return c % n

    def mm(self, out, lhsT, rhs, start, stop, r, w):
        self.P.op('pe', lambda e: e.matmul(out, lhsT, rhs, start=start, stop=stop), r, w)

    def act(self, out, in_, func, r, w, **kw):
        self.P.op('act', lambda e: e.activation(out, in_, func, **kw), r, w)

    def tt(self, eng, out, a, b, op, r, w):
        self.P.op(eng, lambda e: e.tensor_tensor(out, a, b, op), r, w)

    def ts(self, eng, out, a, s1, s2, op0, op1, r, w):
        if op1 is None:
            self.P.op(eng, lambda e: e.tensor_scalar(out, a, s1, None, op0), r, w)
        else:
            self.P.op(eng, lambda e: e.tensor_scalar(out, a, s1, s2, op0, op1), r, w)

    def stt(self, out, a, s, b, op0, op1, r, w):
        self.P.op('dve', lambda e: e.scalar_tensor_tensor(out, a, s, b, op0, op1), r, w)

    def cp(self, eng, out, in_, r, w):
        if eng == 'act':
            self.P.op('act', lambda e: e.activation(out, in_, AF.Copy), r, w)
        else:
            self.P.op(eng, lambda e: e.tensor_copy(out, in_), r, w)

    def build(self):
        nc, P = self.nc, self.P
        L = DEPTH
        nt = _na()[1]
        self.nt = nt
        I = {}
        I['xin'] = self.din('xin', [T, D])
        I['cT'] = self.din('cT', [128, 16])
        I['w_mod'] = self.din('w_mod', [L, D, 6 * D])
        I['b_mod'] = self.din('b_mod', [L, 6 * D])
        I['norm1_g'] = self.din('norm1_g', [L, D])
        I['norm2_g'] = self.din('norm2_g', [L, D])
        I['final_g'] = self.din('final_g', [1, D])
        I['w_in'] = self.din('w_in', [L, D, NIN])
        I['conv_w'] = self.din('conv_w', [L, 128, 4, 3])
        I['conv_out'] = self.din('conv_out', [L, 512, D])
        I['s5p'] = self.din('s5p', [L, 128, 3, 32])
        I['s5BT'] = self.din('s5BT', [L, 2, 16, 2, 128, 128])
        I['s5CM'] = self.din('s5CM', [L, 2, 16, 2, 128, 128])
        I['s5d'] = self.din('s5d', [L, 128, 4])
        I['s5_glu_a'] = self.din('s5_glu_a', [L, 512, D])
        I['s5_glu_b'] = self.din('s5_glu_b', [L, 512, D])
        I['rpbg'] = self.din('rpbg', [L, 8, nt, 128, 128])
        I['na_mask'] = self.din('na_mask', [nt, 128, 128])
        I['na_out'] = self.din('na_out', [L, 512, D])
        I['w_out'] = self.din('w_out', [L, D, D])
        I['mlp_w1'] = self.din('mlp_w1', [L, D, 4 * D])
        I['mlp_w2'] = self.din('mlp_w2', [L, 4 * D, D])
        I['ident'] = self.din('ident', [128, 128])
        self.I = I
        self.out = nc.dram_tensor('out', [SEQ, D], F32, kind="ExternalOutput")
        S = {}
        S['xres'] = self.dscr('xres', [T, D], F32)
        S['hxT'] = self.dscr('hxT', [D, T], BF16)
        S['h2T'] = self.dscr('h2T', [D, T], BF16)
        S['convbT'] = self.dscr('convbT', [512, T], BF16)
        S['gT'] = self.dscr('gT', [512, T], BF16)
        S['attnT'] = self.dscr('attnT', [512, T], BF16)
        S['modv'] = self.dscr('modv', [L, 2, 6 * D], F32)
        S['mgT'] = self.dscr('mgT', [D, T], BF16)
        S['hidT'] = self.dscr('hidT', [4 * D, T], BF16)
        self.S = S

        with ExitStack() as gs:
            self.ps = [gs.enter_context(nc.psum_tensor(f"ps{i}", [128, 512], F32))
                       for i in range(7)]
            self.psT = gs.enter_context(nc.psum_tensor("psT", [128, 1024], BF16))
            self.ident = gs.enter_context(nc.sbuf_tensor("ident_sb", [128, 128], BF16))
            self.ones = gs.enter_context(nc.sbuf_tensor("ones_sb", [128, 128], BF16))
            P.dma('pool', self.ident[:], I['ident'].ap(), w=['ident'])
            P.op('dve', lambda e: e.memset(self.ones[:], 1.0), w=['ones'])
            P.barrier()
            seq = [('adaln', lambda: self.phase_adaln()),
                   ('norm', lambda: self.phase_norm(0, src=I['xin'], kind='n1'))]
            for l in range(self.layers):
                seq += [(f'conv{l}', lambda l=l: self.phase_conv(l)),
                        (f'attn{l}', lambda l=l: self.phase_attn(l)),
                        (f's5{l}', lambda l=l: self.phase_s5(l)),
                        (f'merge{l}', lambda l=l: self.phase_merge(l, src=(I['xin'] if l == 0 else S['xres']))),
                        (f'mlp{l}', lambda l=l: self.phase_mlp(l))]
            for name, fn in seq:
                if self.only is not None and name not in self.only:
                    continue
                fn()
                P.barrier()
                if name == self.stop:
                    break
            P.emit()
        return nc

    def phase_adaln(self):
        nc, P, I, S = self.nc, self.P, self.I, self.S
        with ExitStack() as st:
            sb = lambda n, s, d: st.enter_context(nc.sbuf_tensor(f"{n}_u{self.uid()}", s, d))
            cT = sb("ad_cT", [128, 16], F32)
            sil = sb("ad_sil", [128, 16], BF16)
            wt = [sb(f"ad_w{i}", [128, 8, 512], BF16) for i in range(3)]
            bm = sb("ad_bm", [2, 6 * D], F32)
            row = [sb(f"ad_row{i}", [2, 512], F32) for i in range(2)]
            P.dma('sp', cT[:], I['cT'].ap(), w=['cT'])
            self.act(sil[:], cT[:], AF.Silu, r=['cT'], w=['sil'])
            silv = sil[:].rearrange("p (k j) -> p k j", j=2)
            for l in range(DEPTH):
                bsrc = bass.AP(I['b_mod'], l * 6 * D, [[0, 2], [1, 6 * D]])
                P.dma('sp', bm[:], bsrc, w=['bm'])
                wv = I['w_mod'].ap()[l].rearrange("(k p) c -> p k c", p=128)
                for ct in range(12):
                    s = self.rot('adw', 3)
                    P.dma('pool', wt[s][:], wv[:, :, ct * 512:(ct + 1) * 512], w=[('adw', s)])
                    pst = self.ps[ct % 2]
                    for k in range(8):
                        self.mm(pst[0:2, :], silv[:, k, :], wt[s][:, k, :], k == 0, k == 7,
                                r=['sil', ('adw', s)], w=[('ps', ct % 2)])
                    rs = self.rot('adrow', 2)
                    self.tt('dve', row[rs][:], pst[0:2, :], bm[:, ct * 512:(ct + 1) * 512], ALU.add,
                            r=[('ps', ct % 2), 'bm'], w=[('adrow', rs)])
                    P.dma('sp', S['modv'].ap()[l][:, ct * 512:(ct + 1) * 512], row[rs][:],
                          r=[('adrow', rs)])

    def load_rep(self, dst, tensor, offset, key):
        self.P.dma('sp', dst, bass.AP(tensor, offset, [[0, 128], [1, D]]), w=[key])

    def mod_tiles(self, st, l, cond, names, gsrc=None):
        nc = self.nc
        res = {}
        base = (l * 2 + cond) * 6 * D
        idx = {'S1': 0, 'G1': 1, 'A1': 2, 'S2': 3, 'G2': 4, 'A2': 5}
        for n in names:
            t = st.enter_context(nc.sbuf_tensor(f"mod_{n}_{cond}_{self.rot('modt', 1 << 30)}", [128, D], F32))
            key = ('mod', n, cond)
            self.load_rep(t[:], self.S['modv'], base + idx[n] * D, key)
            if n in ('G1', 'G2'):
                g = st.enter_context(nc.sbuf_tensor(f"modg_{n}_{cond}_{self.rot('modt', 1 << 30)}", [128, D], F32))
                gt = self.I['norm1_g'] if n == 'G1' else self.I['norm2_g']
                self.load_rep(g[:], gt, l * D, ('modg', n, cond))
                self.stt(t[:], t[:], 1.0, g[:], ALU.add, ALU.mult, r=[key, ('modg', n, cond)], w=[key])
            res[n] = (t, key)
        return res

    def norm_sub(self, xt, xkey, G, S_, hb, hkey, scr):
        junk, ss, rstd, tmp = scr['junk'], scr['ss'], scr['rstd'], scr['tmp']
        k = scr['k']
        self.act(junk[:], xt, AF.Square, r=[xkey], w=[('nj', k), ('ss', k)], accum_out=ss[:])
        self.ts('dve', rstd[:], ss[:], 1.0 / D, EPS, ALU.mult, ALU.add, r=[('ss', k)], w=[('rstd', k)])
        self.act(rstd[:], rstd[:], AF.Sqrt, r=[('rstd', k)], w=[('rstd', k)])
        self.P.op('dve', lambda e: e.reciprocal(rstd[:], rstd[:]), r=[('rstd', k)], w=[('rstd', k)])
        if S_ is None:
            self.stt(hb, xt, rstd[:], G[0][:], ALU.mult, ALU.mult, r=[xkey, ('rstd', k), G[1]], w=[hkey])
        else:
            self.stt(tmp[:], xt, rstd[:], G[0][:], ALU.mult, ALU.mult, r=[xkey, ('rstd', k), G[1]],
                     w=[('ntmp', k)])
            self.tt('pool', hb, tmp[:], S_[0][:], ALU.add, r=[('ntmp', k), S_[1]], w=[hkey])

    def norm_scratch(self, st, tag):
        nc = self.nc
        out = []
        for k in range(2):
            out.append(dict(
                junk=st.enter_context(nc.sbuf_tensor(f"{tag}_junk{k}_u{self.uid()}", [128, D], BF16)),
                ss=st.enter_context(nc.sbuf_tensor(f"{tag}_ss{k}_u{self.uid()}", [128, 1], F32)),
                rstd=st.enter_context(nc.sbuf_tensor(f"{tag}_rstd{k}_u{self.uid()}", [128, 1], F32)),
                tmp=st.enter_context(nc.sbuf_tensor(f"{tag}_tmp{k}_u{self.uid()}", [128, D], F32)),
                k=(tag, k)))
        return out

    def transpose_out(self, hb, hkey, hT, hTkey, sub):
        for kc in range(8):
            self.P.op('pe', lambda e, kc=kc: e.transpose(self.psT[:, kc * 128:(kc + 1) * 128],
                                                          hb[:, kc * 128:(kc + 1) * 128], self.ident[:]),
                      r=[hkey, 'ident'], w=['psT'])
        self.cp('act', hT[:, :, sub * 128:(sub + 1) * 128],
                self.psT[:].rearrange("p (k t) -> p k t", t=128), r=['psT'], w=[hTkey])

    TILES = [(0, 256)] + [(256 + 512 * i, 512) for i in range(8)]

    def phase_norm(self, l, src, kind):
        nc, P, I, S = self.nc, self.P, self.I, self.S
        with ExitStack() as st:
            sb = lambda n, s, d: st.enter_context(nc.sbuf_tensor(f"{n}_u{self.uid()}", s, d))
            mods = [self.mod_tiles(st, l, c, ('G1', 'S1')) for c in (0, 1)]
            xt = [sb(f"pn_x{i}", [128, D], F32) for i in range(3)]
            hb = [sb(f"pn_hb{i}", [128, D], BF16) for i in range(2)]
            hT = [sb(f"pn_hT{i}", [128, 8, 512], BF16) for i in range(2)]
            scr = self.norm_scratch(st, "pn")
            dst = S['hxT'].ap().rearrange("(k p) t -> p k t", p=128)
            for (t0, n) in self.TILES:
                cond = 1 if t0 < CTX else 0
                hs = self.rot('pn_hT', 2)
                for sub in range(n // 128):
                    xs = self.rot('pn_x', 3)
                    P.dma('sp', xt[xs][:], src.ap()[t0 + sub * 128:t0 + (sub + 1) * 128, :], w=[('pn_x', xs)])
                    bs = self.rot('pn_hb', 2)
                    self.norm_sub(xt[xs][:], ('pn_x', xs), mods[cond]['G1'], mods[cond]['S1'],
                                  hb[bs][:], ('pn_hb', bs), scr[bs])
                    self.transpose_out(hb[bs], ('pn_hb', bs), hT[hs], ('pn_hT', hs), sub)
                P.dma('act', dst[:, :, t0:t0 + n], hT[hs][:, :, 0:n], r=[('pn_hT', hs)])

    TT512 = [(512 * i, 512) for i in range(8)] + [(4096, 256)]

    def load_w(self, dst, src_ap, key, r=()):
        self.P.dma('pool', dst, src_ap, r=r, w=[key])

    def phase_conv(self, l):
        nc, P, I, S = self.nc, self.P, self.I, self.S
        with ExitStack() as st:
            sb = lambda n, s, d: st.enter_context(nc.sbuf_tensor(f"{n}_u{self.uid()}", s, d))
            wc = [sb(f"cv_w{i}", [128, 3, 8, 128], BF16) for i in range(2)]
            hx = [sb(f"cv_hx{i}", [128, 8, 512], BF16) for i in range(2)]
            vbs = [sb(f"cv_v{i}", [128, T + 4], F32) for i in range(2)]
            xbbs = [sb(f"cv_xb{i}", [128, T], F32) for i in range(2)]
            tmp = [sb(f"cv_tmp{i}", [128, 512], F32) for i in range(2)]
            acc = [sb(f"cv_acc{i}", [128, 1024], F32) for i in range(2)]
            ob = [sb(f"cv_o{i}", [128, T], BF16) for i in range(2)]
            cw = sb("cv_cw", [128, 12], F32)
            P.dma('sp', cw[:], I['conv_w'].ap()[l].rearrange("p q j -> p (q j)"), w=['cw'])
            for i in range(2):
                P.op('pool', lambda e, i=i: e.memset(vbs[i][:], 0.0), w=[('vb', i)])
            hsrc = S['hxT'].ap().rearrange("(k p) t -> p k t", p=128)
            win = I['w_in'].ap()[l].rearrange("(k p) c -> p k c", p=128)
            for q in range(4):
                ws = self.rot('cv_w', 2)
                vb, xbb = vbs[q % 2], xbbs[q % 2]
                vbk, xbk = ('vb', q % 2), ('xbb', q % 2)
                for j, off in enumerate((OFF_XA, OFF_XB, OFF_XC)):
                    self.load_w(wc[ws][:, j, :, :], win[:, :, off + q * 128: off + (q + 1) * 128], ('cv_w', ws, j))
                for (t0, n) in self.TT512:
                    hs = self.rot('cv_hx', 2)
                    P.dma('sp', hx[hs][:, :, 0:n], hsrc[:, :, t0:t0 + n], w=[('cv_hx', hs)])
                    for j in range(3):
                        for k in range(8):
                            self.mm(self.ps[j][:, 0:n], wc[ws][:, j, k, :], hx[hs][:, k, 0:n], k == 0, k == 7,
                                    r=[('cv_w', ws, j), ('cv_hx', hs)], w=[('ps', j)])
                    ts_ = self.rot('cv_tmp', 2)
                    self.cp('act', tmp[ts_][:, 0:n], self.ps[2][:, 0:n], r=[('ps', 2)], w=[('cv_tmp', ts_)])
                    segs = []
                    if t0 < CTX:
                        segs.append((t0, CTX - t0, 1 + t0))
                        segs.append((CTX, t0 + n - CTX, 3 + CTX))
                    else:
                        segs.append((t0, n, 3 + t0))
                    for (a0, an, c0) in segs:
                        self.tt('dve', vb[:, c0:c0 + an], self.ps[0][:, a0 - t0:a0 - t0 + an],
                                tmp[ts_][:, a0 - t0:a0 - t0 + an], ALU.mult,
                                r=[('ps', 0), ('cv_tmp', ts_), vbk], w=[vbk])
                    self.cp('act', xbb[:, t0:t0 + n], self.ps[1][:, 0:n], r=[('ps', 1)], w=[xbk])
                os_ = self.rot('cv_o', 2)
                pieces = [(0, 256, 1)] + [(256 + 1024 * i, 1024, 3 + 256 + 1024 * i) for i in range(4)]
                for (a0, an, c0) in pieces:
                    as_ = self.rot('cv_acc', 2)
                    A = acc[as_][:, 0:an]
                    ak = ('cv_acc', as_)
                    self.ts('dve', A, vb[:, c0 - 1:c0 - 1 + an], cw[:, q * 3:q * 3 + 1], None, ALU.mult, None,
                            r=[vbk, 'cw'], w=[ak])
                    self.stt(A, vb[:, c0:c0 + an], cw[:, q * 3 + 1:q * 3 + 2], A, ALU.mult, ALU.add,
                             r=[vbk, 'cw', ak], w=[ak])
                    self.stt(A, vb[:, c0 + 1:c0 + 1 + an], cw[:, q * 3 + 2:q * 3 + 3], A, ALU.mult, ALU.add,
                             r=[vbk, 'cw', ak], w=[ak])
                    self.tt('pool', ob[os_][:, a0:a0 + an], A, xbb[:, a0:a0 + an], ALU.mult,
                            r=[ak, xbk], w=[('cv_o', os_)])
                P.dma('act', S['convbT'].ap()[q * 128:(q + 1) * 128, :], ob[os_][:], r=[('cv_o', os_)])


    def phase_attn(self, l):
        nc, P, I, S = self.nc, self.P, self.I, self.S
        plan, nt = _na()[0], self.nt
        last = (l == DEPTH - 1)
        with ExitStack() as st:
            sb = lambda n, s, d: st.enter_context(nc.sbuf_tensor(f"{n}_u{self.uid()}", s, d))
            wq = [sb(f"at_w{i}", [128, 3, 8, 128], BF16) for i in range(2)]
            hx = [sb(f"at_hx{i}", [128, 8, 512], BF16) for i in range(2)]
            qT = [sb(f"at_q{i}", [128, T], BF16) for i in range(2)]
            kT = [sb(f"at_k{i}", [128, T], BF16) for i in range(2)]
            V = [sb(f"at_v{i}", [128, 34, 128], BF16) for i in range(2)]
            aT = [sb(f"at_a{i}", [128, T], BF16) for i in range(2)]
            msk = sb("at_mask", [128, nt, 128], F32)
            rg = [sb(f"at_rg{i}", [128, 128], F32) for i in range(3)]
            bias = [sb(f"at_bias{i}", [128, 2, nt, 128], BF16) for i in range(2)]
            PT = [sb(f"at_pt{i}", [128, 128], BF16) for i in range(4)]
            rec = [sb(f"at_rec{i}", [128, 128], F32) for i in range(2)]
            P.dma('sp', msk[:], I['na_mask'].ap().rearrange("t k q -> k t q"), w=['msk'])
            hsrc = S['hxT'].ap().rearrange("(k p) t -> p k t", p=128)
            win = I['w_in'].ap()[l].rearrange("(k p) c -> p k c", p=128)
            for hp in range(4):
                ws = self.rot('at_w', 2)
                bsl = self.rot('at_b', 2)
                for j, off in enumerate((OFF_Q, OFF_K, OFF_V)):
                    self.load_w(wq[ws][:, j, :, :], win[:, :, off + hp * 128: off + (hp + 1) * 128], ('at_w', ws, j))
                for hh in range(2):
                    for ty in range(nt):
                        rs = self.rot('at_rg', 3)
                        P.dma('sp', rg[rs][:], I['rpbg'].ap()[l, hp * 2 + hh, ty], w=[('at_rg', rs)])
                        self.stt(bias[bsl][:, hh, ty, :], rg[rs][:], 8.0, msk[:, ty, :], ALU.mult, ALU.add,
                                 r=[('at_rg', rs), 'msk'], w=[('at_bias', bsl)])
                for (t0, n) in self.TT512:
                    hs = self.rot('at_hx', 2)
                    P.dma('sp', hx[hs][:, :, 0:n], hsrc[:, :, t0:t0 + n], w=[('at_hx', hs)])
                    for j, dstT in ((0, qT[bsl]), (1, kT[bsl])):
                        for k in range(8):
                            self.mm(self.ps[j][:, 0:n], wq[ws][:, j, k, :], hx[hs][:, k, 0:n], k == 0, k == 7,
                                    r=[('at_w', ws, j), ('at_hx', hs)], w=[('ps', j)])
                        self.cp('act' if j == 0 else 'dve', dstT[:, t0:t0 + n], self.ps[j][:, 0:n], r=[('ps', j)],
                                w=[('at_qk', bsl, j)])
                    for sub in range(n // 128):
                        for k in range(8):
                            self.mm(self.ps[0][:, sub * 128:(sub + 1) * 128], hx[hs][:, k, sub * 128:(sub + 1) * 128],
                                    wq[ws][:, 2, k, :], k == 0, k == 7,
                                    r=[('at_w', ws, 2), ('at_hx', hs)], w=[('ps', 0)])
                    ti0 = t0 // 128
                    self.cp('act', V[bsl][:, ti0:ti0 + n // 128, :],
                            self.ps[0][:, 0:n].rearrange("p (s c) -> p s c", c=128), r=[('ps', 0)], w=[('at_v', bsl)])
                qlist = []
                if not last:
                    for qi in range(2):
                        qlist.append((qi * 128, [(0, None), (128, None)]))
                for j in range(32):
                    kt = [(CTX + t * 128, ty) for (t, ty) in plan[j]] + [(0, None), (128, None)]
                    qlist.append((CTX + j * 128, kt))
                items = []
                for hh in range(2):
                    for (q0, kts) in qlist:
                        osl = self.rot('at_o', 2)
                        nk = len(kts)
                        for ki, (k0, ty) in enumerate(kts):
                            items.append(dict(hh=hh, q0=q0, ki=ki, nk=nk, k0=k0, ty=ty, osl=osl))

                def stage_s(it):
                    pb = it['hh'] * 64
                    ssl = self.rot('at_s', 3)
                    it['ssl'] = ssl
                    it['psl'] = self.rot('at_ptslot', 4)
                    psS = self.ps[ssl][:, 0:128]
                    skey = ('ps', ssl)
                    q0, k0, ty = it['q0'], it['k0'], it['ty']
                    self.mm(psS, kT[bsl][pb:pb + 64, k0:k0 + 128], qT[bsl][pb:pb + 64, q0:q0 + 128],
                            True, ty is None, r=[('at_qk', bsl, 0), ('at_qk', bsl, 1)], w=[skey])
                    if ty is not None:
                        self.mm(psS, self.ident[:], bias[bsl][:, it['hh'], ty, :], False, True,
                                r=['ident', ('at_bias', bsl)], w=[skey])
                    self.act(PT[it['psl']][:], psS, AF.Exp, r=[skey], w=[('at_pt', it['psl'])], scale=0.125)

                def stage_pv(it):
                    pb = it['hh'] * 64
                    ssl, osl, ki, nk, k0, q0 = it['psl'], it['osl'], it['ki'], it['nk'], it['k0'], it['q0']
                    psO = self.ps[3 + osl]
                    psU = self.ps[5 + osl]
                    pkey = ('at_pt', ssl)
                    self.mm(psO[:, 0:128], V[bsl][:, k0 // 128, :], PT[ssl][:], ki == 0, ki == nk - 1,
                            r=[('at_v', bsl), pkey], w=[('psO', osl)])
                    self.mm(psU[:, 0:128], self.ones[:], PT[ssl][:], ki == 0, ki == nk - 1,
                            r=['ones', pkey], w=[('psU', osl)])
                    if ki == nk - 1:
                        self.P.op('dve', lambda e: e.reciprocal(rec[osl][pb:pb + 64, :], psU[pb:pb + 64, 0:128]),
                                  r=[('psU', osl)], w=[('at_rec', osl)])
                        self.tt('dve', aT[bsl][pb:pb + 64, q0:q0 + 128], psO[pb:pb + 64, 0:128],
                                rec[osl][pb:pb + 64, :], ALU.mult, r=[('psO', osl), ('at_rec', osl)],
                                w=[('at_a', bsl)])
                LA = 2
                for i in range(len(items) + LA):
                    if i < len(items):
                        stage_s(items[i])
                    if i - LA >= 0:
                        stage_pv(items[i - LA])
                a0 = CTX if last else 0
                P.dma('act', S['attnT'].ap()[hp * 128:(hp + 1) * 128, a0:T], aT[bsl][:, a0:T], r=[('at_a', bsl)])

    def phase_s5(self, l):
        nc, P, I, S = self.nc, self.P, self.I, self.S
        Lc = LCH
        TWO_PI = 2.0 * math.pi
        with ExitStack() as st:
            sb = lambda n, s, d: st.enter_context(nc.sbuf_tensor(f"{n}_u{self.uid()}", s, d))
            prm = sb("s5_prm", [128, 3, 32], F32)
            names = ['dt', 'a', 'adt', 'bdt', 'r1', 'kf', 'red', 'sn', 'shf', 'cs', 'nr', 'ni', 'den',
                     'cre', 'cim', 't1', 't2']
            A = {n: sb(f"s5_{n}", [128, 32], F32) for n in names}
            ki = sb("s5_ki", [128, 32], I32)
            phr = sb("s5_phr", [128, 9, 32], F32)
            phi = sb("s5_phi", [128, 9, 32], F32)
            dsk = sb("s5_dsk", [128, 4], F32)
            zero = sb("s5_zero", [128, Lc], F32)
            P.dma('sp', prm[:], I['s5p'].ap()[l], w=['prm'])
            P.dma('sp', dsk[:], I['s5d'].ap()[l], w=['dsk'])
            P.op('dve', lambda e: e.memset(zero[:], 0.0), w=['zero'])
            K_ = ['prm']
            lre, lim, lst = prm[:, 0, :], prm[:, 1, :], prm[:, 2, :]
            a = lambda n: A[n][:]
            self.act(a('dt'), lst, AF.Exp, r=K_, w=K_)
            self.ts('dve', a('a'), lre, -1e-4, None, ALU.min, None, r=K_, w=K_)
            self.tt('dve', a('adt'), a('a'), a('dt'), ALU.mult, r=K_, w=K_)
            self.tt('dve', a('bdt'), lim, a('dt'), ALU.mult, r=K_, w=K_)
            self.act(a('r1'), a('adt'), AF.Exp, r=K_, w=K_)
            self.ts('dve', a('kf'), a('bdt'), 1.0 / TWO_PI, None, ALU.mult, None, r=K_, w=K_)
            self.cp('dve', ki[:], a('kf'), r=K_, w=K_)
            self.cp('dve', a('kf'), ki[:], r=K_, w=K_)
            self.stt(a('red'), a('kf'), -TWO_PI, a('bdt'), ALU.mult, ALU.add, r=K_, w=K_)
            self.ts('dve', a('red'), a('red'), 3.141592, -3.141592, ALU.min, ALU.max, r=K_, w=K_)
            self.act(a('sn'), a('red'), AF.Sin, r=K_, w=K_)
            self.act(a('shf'), a('red'), AF.Sin, r=K_, w=K_, scale=0.5)
            self.tt('dve', a('cs'), a('shf'), a('shf'), ALU.mult, r=K_, w=K_)
            self.ts('dve', a('cs'), a('cs'), -2.0, 1.0, ALU.mult, ALU.add, r=K_, w=K_)
            self.tt('dve', a('nr'), a('r1'), a('cs'), ALU.mult, r=K_, w=K_)
            self.ts('dve', a('nr'), a('nr'), -1.0, None, ALU.add, None, r=K_, w=K_)
            self.tt('dve', a('ni'), a('r1'), a('sn'), ALU.mult, r=K_, w=K_)
            self.tt('dve', a('den'), a('a'), a('a'), ALU.mult, r=K_, w=K_)
            self.tt('dve', a('t1'), lim, lim, ALU.mult, r=K_, w=K_)
            self.tt('dve', a('den'), a('den'), a('t1'), ALU.add, r=K_, w=K_)
            P.op('dve', lambda e: e.reciprocal(a('den'), a('den')), r=K_, w=K_)
            self.tt('dve', a('t1'), a('nr'), a('a'), ALU.mult, r=K_, w=K_)
            self.tt('dve', a('t2'), a('ni'), lim, ALU.mult, r=K_, w=K_)
            self.tt('dve', a('t1'), a('t1'), a('t2'), ALU.add, r=K_, w=K_)
            self.tt('dve', a('cre'), a('t1'), a('den'), ALU.mult, r=K_, w=K_)
            self.tt('dve', a('t1'), a('ni'), a('a'), ALU.mult, r=K_, w=K_)
            self.tt('dve', a('t2'), a('nr'), lim, ALU.mult, r=K_, w=K_)
            self.tt('dve', a('t1'), a('t1'), a('t2'), ALU.subtract, r=K_, w=K_)
            self.tt('dve', a('cim'), a('t1'), a('den'), ALU.mult, r=K_, w=K_)
            self.cp('dve', phr[:, 0, :], a('cs'), r=K_, w=K_)
            self.cp('dve', phi[:, 0, :], a('sn'), r=K_, w=K_)
            for j in range(1, 9):
                self.tt('dve', a('t1'), phr[:, j - 1, :], phr[:, j - 1, :], ALU.mult, r=K_, w=K_)
                self.tt('dve', a('t2'), phi[:, j - 1, :], phi[:, j - 1, :], ALU.mult, r=K_, w=K_)
                self.tt('dve', phr[:, j, :], a('t1'), a('t2'), ALU.subtract, r=K_, w=K_)
                self.tt('dve', a('t1'), phr[:, j - 1, :], phi[:, j - 1, :], ALU.mult, r=K_, w=K_)
                self.ts('dve', phi[:, j, :], a('t1'), 2.0, None, ALU.mult, None, r=K_, w=K_)

            wu = [sb(f"s5_wu{i}", [128, 8, 128], BF16) for i in range(2)]
            hx = [sb(f"s5_hx{i}", [128, 8, 512], BF16) for i in range(2)]
            uT = [sb(f"s5_uT{i}", [128, T], BF16) for i in range(2)]
            BT = [sb(f"s5_BT{i}", [128, 16, 128], BF16) for i in range(2)]
            CM = [sb(f"s5_CM{i}", [128, 16, 128], BF16) for i in range(2)]
            ybuf = sb("s5_y", [128, T], F32)
            gq = [sb(f"s5_g{i}", [128, T], BF16) for i in range(2)]
            tab = [{n: sb(f"s5_tab{il}_{n}", [128, Lc], F32) for n in ('DTr', 'DTi', 'nDTi', 'nDTr', 'MTr', 'MTi', 'nMTi', 'Rc')}
                   for il in range(4)]
            ttmp = sb("s5_ttmp", [128, Lc], F32)
            NZ = 8
            zb = [{n: sb(f"s5_z{i}_{n}", [128, Lc], F32) for n in ('gr', 'gi')} for i in range(NZ)]
            zp = [{n: sb(f"s5_zp{i}_{n}", [128, Lc], BF16) for n in ('pa', 'pb', 'pc', 'pd')} for i in range(NZ)]
            db = [{n: sb(f"s5_d{i}_{n}", [128, Lc], F32) for n in ('t1', 't2', 't3', 't4')} for i in range(2)]
            hb = [[sb(f"s5_h{il}_{i}", [128, 4, Lc], BF16) for i in range(2)] for il in range(4)]
            identf = sb("s5_identf", [128, 128], F32)
            P.dma('sp', identf[:], I['ident'].ap(), w=['identf'])
            ini = [[sb(f"s5_ini{il}_{i}", [128, 2], F32) for i in range(2)] for il in range(4)]
            itmp = [sb(f"s5_itmp{i}", [128, 2], F32) for i in range(4)]
            nphi = sb("s5_nphi", [128, 32], F32)
            self.ts('dve', nphi[:], phi[:, 8, :], -1.0, None, ALU.mult, None, r=K_, w=K_)

            hsrc = S['hxT'].ap().rearrange("(k p) t -> p k t", p=128)
            win = I['w_in'].ap()[l].rearrange("(k p) c -> p k c", p=128)
            for q in range(4):
                us = self.rot('s5_u', 2)
                self.load_w(wu[us][:], win[:, :, OFF_U + q * 128: OFF_U + (q + 1) * 128], ('s5_wu', us))
                for d in range(2):
                    for c in range(2):
                        self.load_w(BT[us][:, d * 8 + c * 4: d * 8 + c * 4 + 4, :] if False else
                                    BT[us][:].rearrange("p (d i c) k -> p d i c k", d=2, i=4)[:, d, :, c, :],
                                    I['s5BT'].ap()[l, d, 4 * q:4 * q + 4, c].rearrange("i r k -> r i k"),
                                    ('s5_BT', us, d, c))
                        self.load_w(CM[us][:].rearrange("p (d i c) k -> p d i c k", d=2, i=4)[:, d, :, c, :],
                                    I['s5CM'].ap()[l, d, 4 * q:4 * q + 4, c].rearrange("i r k -> r i k"),
                                    ('s5_CM', us, d, c))
                BTv = BT[us][:].rearrange("p (d i c) k -> p d i c k", d=2, i=4)
                CMv = CM[us][:].rearrange("p (d i c) k -> p d i c k", d=2, i=4)
                ukey = ('s5_uT', us)
                for (t0, n) in self.TT512:
                    hs = self.rot('s5_hx', 2)
                    P.dma('sp', hx[hs][:, :, 0:n], hsrc[:, :, t0:t0 + n], w=[('s5_hx', hs)])
                    for k in range(8):
                        self.mm(self.ps[6][:, 0:n], wu[us][:, k, :], hx[hs][:, k, 0:n], k == 0, k == 7,
                                r=[('s5_wu', us), ('s5_hx', hs)], w=[('ps', 6)])
                    self.cp('act', uT[us][:, t0:t0 + n], self.ps[6][:, 0:n], r=[('ps', 6)], w=[ukey])
                for d in range(2):
                    for il in range(4):
                        c = d * 16 + 4 * q + il
                        tb = tab[il]
                        tk = ('s5_tab', il)
                        sc = lambda arr, j=None: (arr[:, c:c + 1] if j is None else arr[:, j, c:c + 1])
                        P.op('dve', lambda e, tb=tb: e.memset(tb['DTr'][:, 0:1], 1.0), w=[tk])
                        P.op('dve', lambda e, tb=tb: e.memset(tb['DTi'][:, 0:1], 0.0), w=[tk])
                        for j in range(8):
                            n = 1 << j
                            pr, pi = sc(phr, j), sc(phi, j)
                            self.ts('dve', ttmp[:, 0:n], tb['DTi'][:, 0:n], pi, None, ALU.mult, None,
                                    r=[tk, 'prm'], w=['ttmp'])
                            self.stt(tb['DTr'][:, n:2 * n], tb['DTr'][:, 0:n], pr, ttmp[:, 0:n], ALU.mult, ALU.subtract,
                                     r=[tk, 'prm', 'ttmp'], w=[tk])
                            self.ts('dve', ttmp[:, 0:n], tb['DTi'][:, 0:n], pr, None, ALU.mult, None,
                                    r=[tk, 'prm'], w=['ttmp'])
                            self.stt(tb['DTi'][:, n:2 * n], tb['DTr'][:, 0:n], pi, ttmp[:, 0:n], ALU.mult, ALU.add,
                                     r=[tk, 'prm', 'ttmp'], w=[tk])
                        cre, cim = sc(A['cre'][:]), sc(A['cim'][:])
                        self.ts('dve', ttmp[:], tb['DTi'][:], cim, None, ALU.mult, None, r=[tk, 'prm'], w=['ttmp'])
                        self.stt(tb['MTr'][:], tb['DTr'][:], cre, ttmp[:], ALU.mult, ALU.add, r=[tk, 'prm', 'ttmp'], w=[tk])
                        self.ts('dve', ttmp[:], tb['DTi'][:], cre, None, ALU.mult, None, r=[tk, 'prm'], w=['ttmp'])
                        self.stt(tb['MTi'][:], tb['DTr'][:], cim, ttmp[:], ALU.mult, ALU.subtract, r=[tk, 'prm', 'ttmp'], w=[tk])
                        self.ts('pool', tb['nDTi'][:], tb['DTi'][:], -1.0, None, ALU.mult, None, r=[tk], w=[tk])
                        self.ts('pool', tb['nDTr'][:], tb['DTr'][:], -1.0, None, ALU.mult, None, r=[tk], w=[tk])
                        self.ts('pool', tb['nMTi'][:], tb['MTi'][:], -1.0, None, ALU.mult, None, r=[tk], w=[tk])
                        self.ts('dve', tb['Rc'][:], zero[:], sc(A['r1'][:]), None, ALU.add, None, r=['zero', 'prm'], w=[tk])
                        P.op('dve', lambda e, il=il: e.memset(ini[il][0][:], 0.0), w=[('s5_ini', il, 0)])
                    order = list(range(NCH)) if d == 0 else [0] + list(range(NCH - 1, 0, -1))

                    def emit_y(kk, cs):
                        tok0 = cs * Lc
                        psY = self.ps[4]
                        ykey = ('ps', 4)
                        hsl = kk % 2
                        for il in range(4):
                            for j in range(4):
                                self.mm(psY[:, 0:Lc], CMv[:, d, il, j // 2, :], hb[il][hsl][:, j, :],
                                        il == 0 and j == 0, il == 3 and j == 3,
                                        r=[('s5_CM', us, d, j // 2), (('s5_h', il, hsl), j)], w=[ykey])
                        if d == 0:
                            self.stt(ybuf[:, tok0:tok0 + Lc], uT[us][:, tok0:tok0 + Lc], dsk[:, q:q + 1], psY[:, 0:Lc],
                                     ALU.mult, ALU.add, r=[ukey, 'dsk', ykey], w=[('s5_y', cs)])
                        else:
                            self.tt('dve', ybuf[:, tok0:tok0 + Lc], psY[:, 0:Lc], ybuf[:, tok0:tok0 + Lc], ALU.add,
                                    r=[ykey, ('s5_y', cs)], w=[('s5_y', cs)])

                    pend = None
                    for kk, cs in enumerate(order):
                        tok0 = cs * Lc
                        par = kk % 2
                        hsl = kk % 2
                        ctxs = []
                        for il in range(4):
                            bs = self.rot('s5_psB', 2)
                            psB = self.ps[bs]
                            bkey = ('ps', bs)
                            for cc in range(2):
                                self.mm(psB[:, cc * Lc:(cc + 1) * Lc], BTv[:, d, il, cc, :], uT[us][:, tok0:tok0 + Lc],
                                        True, True, r=[('s5_BT', us, d, cc), ukey], w=[bkey])
                            if d == 0:
                                bre, bim = psB[:, 0:Lc], psB[:, Lc:2 * Lc]
                            else:
                                bre, bim = psB[:, Lc - 1::-1][:, 0:Lc], psB[:, 2 * Lc - 1:Lc - 1:-1]
                            zs = self.rot('s5_z', NZ)
                            z, zk, tb, tk = zb[zs], ('s5_z', zs), tab[il], ('s5_tab', il)
                            self.tt('dve', zp[zs]['pa'][:], bre, tb['MTr'][:], ALU.mult, r=[bkey, tk], w=[(zk, 'pa')])
                            self.tt('dve', zp[zs]['pb'][:], bim, tb['nMTi'][:], ALU.mult, r=[bkey, tk], w=[(zk, 'pb')])
                            self.tt('dve', zp[zs]['pc'][:], bre, tb['MTi'][:], ALU.mult, r=[bkey, tk], w=[(zk, 'pc')])
                            self.tt('dve', zp[zs]['pd'][:], bim, tb['MTr'][:], ALU.mult, r=[bkey, tk], w=[(zk, 'pd')])
                            zbank = (2, 3, 5, 6)[il]
                            psZ = self.ps[zbank]
                            zkey = ('ps', zbank)
                            self.mm(psZ[:, 0:Lc], self.ident[:], zp[zs]['pa'][:], True, False, r=['ident', (zk, 'pa')], w=[zkey])
                            self.mm(psZ[:, 0:Lc], self.ident[:], zp[zs]['pb'][:], False, True, r=['ident', (zk, 'pb')], w=[zkey])
                            self.mm(psZ[:, Lc:2 * Lc], self.ident[:], zp[zs]['pc'][:], True, False, r=['ident', (zk, 'pc')], w=[zkey])
                            self.mm(psZ[:, Lc:2 * Lc], self.ident[:], zp[zs]['pd'][:], False, True, r=['ident', (zk, 'pd')], w=[zkey])
                            ctxs.append(dict(il=il, z=z, zk=zk, gk=('s5_g', zs), tb=tb, tk=tk, psZ=psZ, zkey=zkey,
                                             c=d * 16 + 4 * q + il))
                        for cx in ctxs:
                            z, tb, tk, gk, il, psZ, zkey = cx['z'], cx['tb'], cx['tk'], cx['gk'], cx['il'], cx['psZ'], cx['zkey']
                            ik = ('s5_ini', il, par)
                            iv = ini[il][par]
                            P.op('dve', lambda e, z=z, tb=tb, iv=iv, psZ=psZ: e.tensor_tensor_scan(
                                z['gr'][:], tb['Rc'][:], psZ[:, 0:Lc], iv[:, 0:1], ALU.mult, ALU.add),
                                r=[zkey, tk, ik], w=[(gk, 'r')])
                            P.op('dve', lambda e, z=z, tb=tb, iv=iv, psZ=psZ: e.tensor_tensor_scan(
                                z['gi'][:], tb['Rc'][:], psZ[:, Lc:2 * Lc], iv[:, 1:2], ALU.mult, ALU.add),
                                r=[zkey, tk, ik], w=[(gk, 'i')])
                        for cx in ctxs:
                            z, gk, il, c = cx['z'], cx['gk'], cx['il'], cx['c']
                            ink = ('s5_ini', il, 1 - par)
                            inx = ini[il][1 - par]
                            pr, pi, npi = phr[:, 8, c:c + 1], phi[:, 8, c:c + 1], nphi[:, c:c + 1]
                            gre, gie = z['gr'][:, Lc - 1:Lc], z['gi'][:, Lc - 1:Lc]
                            itk = ('itmp', il)
                            self.act(itmp[il][:, 0:1], gie, AF.Identity, r=[(gk, 'i'), 'prm'], w=[itk], scale=npi)
                            self.act(itmp[il][:, 1:2], gie, AF.Identity, r=[(gk, 'i'), 'prm'], w=[itk], scale=pr)
                            self.act(inx[:, 0:1], gre, AF.Identity, r=[(gk, 'r'), 'prm', itk], w=[ink], scale=pr,
                                     bias=itmp[il][:, 0:1])
                            self.act(inx[:, 1:2], gre, AF.Identity, r=[(gk, 'r'), 'prm', itk], w=[ink], scale=pi,
                                     bias=itmp[il][:, 1:2])
                        for cx in ctxs:
                            z, tb, tk, gk, il = cx['z'], cx['tb'], cx['tk'], cx['gk'], cx['il']
                            hk = ('s5_h', il, hsl)
                            hv = [hb[il][hsl][:, j, :] if d == 0 else hb[il][hsl][:, j, ::-1] for j in range(4)]
                            self.tt('pool', hv[0], z['gr'][:], tb['DTr'][:], ALU.mult, r=[(gk, 'r'), tk], w=[(hk, 0)])
                            self.tt('pool', hv[1], z['gi'][:], tb['nDTi'][:], ALU.mult, r=[(gk, 'i'), tk], w=[(hk, 1)])
                            self.tt('pool', hv[2], z['gr'][:], tb['nDTi'][:], ALU.mult, r=[(gk, 'r'), tk], w=[(hk, 2)])
                            self.tt('dve', hv[3], z['gi'][:], tb['nDTr'][:], ALU.mult, r=[(gk, 'i'), tk], w=[(hk, 3)])
                        if pend is not None:
                            emit_y(*pend)
                        pend = (kk, cs)
                    emit_y(*pend)
                gs_ = self.rot('s5_gq', 2)
                for cs in range(NCH):
                    self.act(gq[gs_][:, cs * Lc:(cs + 1) * Lc], ybuf[:, cs * Lc:(cs + 1) * Lc], AF.Gelu_apprx_tanh,
                             r=[('s5_y', cs)], w=[('s5_gq', gs_)])
                P.dma('act', S['gT'].ap()[q * 128:(q + 1) * 128, :], gq[gs_][:], r=[('s5_gq', gs_)])

    def phase_merge(self, l, src):
        nc, P, I, S = self.nc, self.P, self.I, self.S
        last = (l == DEPTH - 1)
        tiles = self.TILES[1:] if last else self.TILES
        with ExitStack() as st:
            sb = lambda n, s, d: st.enter_context(nc.sbuf_tensor(f"{n}_u{self.uid()}", s, d))
            wco = sb("mg_wco", [128, 4, D], BF16)
            wga = sb("mg_wga", [128, 4, D], BF16)
            wgb = sb("mg_wgb", [128, 4, D], BF16)
            wno = sb("mg_wno", [128, 4, D], BF16)
            wg = sb("mg_wg", [128, 8, 3 * D], BF16)
            hx = [sb(f"mg_hx{i}", [128, 8, 512], BF16) for i in range(2)]
            br = [[sb(f"mg_br{j}_{i}", [128, 4, 512], BF16) for i in range(2)] for j in range(3)]
            mg = [sb(f"mg_mg{i}", [128, 8, 512], BF16) for i in range(2)]
            sg = [sb(f"mg_sg{i}", [128, 512], F32) for i in range(4)]
            tm = [sb(f"mg_tm{i}", [128, 512], F32) for i in range(4)]
            acc = [sb(f"mg_acc{i}", [128, 512], F32) for i in range(2)]
            for wt_, nm in ((wco, 'conv_out'), (wga, 's5_glu_a'), (wgb, 's5_glu_b'), (wno, 'na_out')):
                for kc in range(4):
                    self.load_w(wt_[:, kc, :], I[nm].ap()[l][kc * 128:(kc + 1) * 128, :], ('mg_w', nm))
            win = I['w_in'].ap()[l].rearrange("(k p) c -> p k c", p=128)
            for k in range(8):
                self.load_w(wg[:, k, :], win[:, k, OFF_GA:OFF_GA + 3 * D], 'mg_wg')
            hsrc = S['hxT'].ap().rearrange("(k p) t -> p k t", p=128)
            bsrc = [S[nm].ap().rearrange("(k p) t -> p k t", p=128) for nm in ('convbT', 'gT', 'attnT')]
            mdst = S['mgT'].ap().rearrange("(k p) t -> p k t", p=128)
            for (t0, n) in tiles:
                hs = self.rot('mg_hx', 2)
                P.dma('sp', hx[hs][:, :, 0:n], hsrc[:, :, t0:t0 + n], w=[('mg_hx', hs)])
                for j in range(3):
                    P.dma('sp', br[j][hs][:, :, 0:n], bsrc[j][:, :, t0:t0 + n], w=[('mg_br', j, hs)])
                ms = self.rot('mg_mg', 2)
                for fo in range(8):
                    fs = slice(fo * 128, (fo + 1) * 128)

                    def proj(bank, w_, x_, nk, wkeys, xkey, coff=0):
                        for k in range(nk):
                            self.mm(self.ps[bank][:, 0:n], w_[:, k, coff + fo * 128: coff + (fo + 1) * 128],
                                    x_[:, k, 0:n], k == 0, k == nk - 1, r=wkeys + [xkey], w=[('ps', bank)])
                    hk = ('mg_hx', hs)
                    proj(0, wco, br[0][hs], 4, [('mg_w', 'conv_out')], ('mg_br', 0, hs))
                    proj(1, wg, hx[hs], 8, ['mg_wg'], hk, 0)
                    proj(2, wga, br[1][hs], 4, [('mg_w', 's5_glu_a')], ('mg_br', 1, hs))
                    proj(3, wgb, br[1][hs], 4, [('mg_w', 's5_glu_b')], ('mg_br', 1, hs))
                    proj(4, wg, hx[hs], 8, ['mg_wg'], hk, D)
                    proj(5, wno, br[2][hs], 4, [('mg_w', 'na_out')], ('mg_br', 2, hs))
                    proj(6, wg, hx[hs], 8, ['mg_wg'], hk, 2 * D)
                    a_ = self.rot('mg_acc', 2)
                    A = acc[a_][:, 0:n]
                    ak = ('mg_acc', a_)
                    sgs = [self.rot('mg_sg', 4) for _ in range(4)]
                    tms = [self.rot('mg_tm', 4) for _ in range(2)]
                    self.act(sg[sgs[0]][:, 0:n], self.ps[1][:, 0:n], AF.Sigmoid, r=[('ps', 1)], w=[('mg_sg', sgs[0])])
                    self.tt('dve', A, self.ps[0][:, 0:n], sg[sgs[0]][:, 0:n], ALU.mult,
                            r=[('ps', 0), ('mg_sg', sgs[0])], w=[ak])
                    self.act(sg[sgs[1]][:, 0:n], self.ps[3][:, 0:n], AF.Sigmoid, r=[('ps', 3)], w=[('mg_sg', sgs[1])])
                    self.tt('dve', tm[tms[0]][:, 0:n], self.ps[2][:, 0:n], sg[sgs[1]][:, 0:n], ALU.mult,
                            r=[('ps', 2), ('mg_sg', sgs[1])], w=[('mg_tm', tms[0])])
                    self.act(sg[sgs[2]][:, 0:n], self.ps[4][:, 0:n], AF.Sigmoid, r=[('ps', 4)], w=[('mg_sg', sgs[2])])
                    self.tt('pool', tm[tms[0]][:, 0:n], tm[tms[0]][:, 0:n], sg[sgs[2]][:, 0:n], ALU.mult,
                            r=[('mg_tm', tms[0]), ('mg_sg', sgs[2])], w=[('mg_tm', tms[0])])
                    self.tt('pool', A, A, tm[tms[0]][:, 0:n], ALU.add, r=[ak, ('mg_tm', tms[0])], w=[ak])
                    self.act(sg[sgs[3]][:, 0:n], self.ps[6][:, 0:n], AF.Sigmoid, r=[('ps', 6)], w=[('mg_sg', sgs[3])])
                    self.tt('dve', tm[tms[1]][:, 0:n], self.ps[5][:, 0:n], sg[sgs[3]][:, 0:n], ALU.mult,
                            r=[('ps', 5), ('mg_sg', sgs[3])], w=[('mg_tm', tms[1])])
                    self.tt('pool', mg[ms][:, fo, 0:n], A, tm[tms[1]][:, 0:n], ALU.add,
                            r=[ak, ('mg_tm', tms[1])], w=[('mg_mg', ms)])
                P.dma('act', mdst[:, :, t0:t0 + n], mg[ms][:, :, 0:n], r=[('mg_mg', ms)])
        P.barrier()
        with ExitStack() as st:
            sb = lambda n, s, d: st.enter_context(nc.sbuf_tensor(f"{n}_u{self.uid()}", s, d))
            wo = sb("mo_wo", [128, 8, D], BF16)
            wov = I['w_out'].ap()[l].rearrange("(k p) c -> p k c", p=128)
            for k in range(8):
                self.load_w(wo[:, k, :], wov[:, k, :], 'mo_wo')
            mgt = [sb(f"mo_mg{i}", [128, 8, 512], BF16) for i in range(2)]
            xt = [sb(f"mo_x{i}", [128, D], F32) for i in range(2)]
            xn = [sb(f"mo_xn{i}", [128, D], F32) for i in range(2)]
            hb = [sb(f"mo_hb{i}", [128, D], BF16) for i in range(2)]
            hT = [sb(f"mo_hT{i}", [128, 8, 512], BF16) for i in range(2)]
            scr = self.norm_scratch(st, "mo")
            msrc = S['mgT'].ap().rearrange("(k p) t -> p k t", p=128)
            hdst = S['h2T'].ap().rearrange("(k p) t -> p k t", p=128)
            mods = None
            cur = None
            for (t0, n) in tiles:
                cond = 1 if t0 < CTX else 0
                if cur != cond:
                    if mods is None:
                        mods = self.mod_tiles(st, l, cond, ('A1', 'G2', 'S2'))
                        gt2 = st.enter_context(nc.sbuf_tensor(f"mo_g2_u{self.uid()}", [128, D], F32))
                    else:
                        self.reload_mod(mods, l, cond, gt2)
                    cur = cond
                hs = self.rot('mo_mg', 2)
                P.dma('sp', mgt[hs][:, :, 0:n], msrc[:, :, t0:t0 + n], w=[('mo_mg', hs)])
                ts_ = self.rot('mo_hT', 2)
                for sub in range(n // 128):
                    xs = self.rot('mo_x', 2)
                    r0 = t0 + sub * 128
                    P.dma('sp', xt[xs][:], src.ap()[r0:r0 + 128, :], w=[('mo_x', xs)])
                    for half in range(2):
                        for k in range(8):
                            self.mm(self.ps[half][:, :], mgt[hs][:, k, sub * 128:(sub + 1) * 128],
                                    wo[:, k, half * 512:(half + 1) * 512], k == 0, k == 7,
                                    r=[('mo_mg', hs), 'mo_wo'], w=[('ps', half)])
                        self.tt('dve', xn[xs][:, half * 512:(half + 1) * 512], self.ps[half][:, :],
                                mods['A1'][0][:, half * 512:(half + 1) * 512], ALU.mult,
                                r=[('ps', half), mods['A1'][1]], w=[('mo_xn', xs)])
                    self.tt('pool', xn[xs][:], xn[xs][:], xt[xs][:], ALU.add, r=[('mo_xn', xs), ('mo_x', xs)],
                            w=[('mo_xn', xs)])
                    P.dma('act', S['xres'].ap()[r0:r0 + 128, :], xn[xs][:], r=[('mo_xn', xs)])
                    bs = self.rot('mo_hb', 2)
                    self.norm_sub(xn[xs][:], ('mo_xn', xs), mods['G2'], mods['S2'], hb[bs][:], ('mo_hb', bs), scr[bs])
                    self.transpose_out(hb[bs], ('mo_hb', bs), hT[ts_], ('mo_hT', ts_), sub)
                P.dma('act', hdst[:, :, t0:t0 + n], hT[ts_][:, :, 0:n], r=[('mo_hT', ts_)])

    def reload_mod(self, mods, l, cond, gtmp):
        base = (l * 2 + cond) * 6 * D
        idx = {'S1': 0, 'G1': 1, 'A1': 2, 'S2': 3, 'G2': 4, 'A2': 5}
        for n, (t, key) in mods.items():
            self.load_rep(t[:], self.S['modv'], base + idx[n] * D, key)
            if n in ('G1', 'G2'):
                gt = self.I['norm1_g'] if n == 'G1' else self.I['norm2_g']
                self.load_rep(gtmp[:], gt, l * D, 'modgtmp')
                self.stt(t[:], t[:], 1.0, gtmp[:], ALU.add, ALU.mult, r=[key, 'modgtmp'], w=[key])

    def phase_mlp(self, l):
        nc, P, I, S = self.nc, self.P, self.I, self.S
        last = (l == DEPTH - 1)
        tiles = self.TILES[1:] if last else self.TILES
        with ExitStack() as st:
            sb = lambda n, s, d: st.enter_context(nc.sbuf_tensor(f"{n}_u{self.uid()}", s, d))
            w1 = sb("ml_w1", [128, 8, 4 * D], BF16)
            w1v = I['mlp_w1'].ap()[l].rearrange("(k p) c -> p k c", p=128)
            for k in range(8):
                for hf in range(2):
                    self.load_w(w1[:, k, hf * 2048:(hf + 1) * 2048], w1v[:, k, hf * 2048:(hf + 1) * 2048], 'ml_w1')
            h2 = [sb(f"ml_h2{i}", [128, 8, 512], BF16) for i in range(2)]
            rl = [sb(f"ml_rl{i}", [128, 512], F32) for i in range(3)]
            hd = [sb(f"ml_hd{i}", [128, 8, 512], BF16) for i in range(2)]
            hsrc = S['h2T'].ap().rearrange("(k p) t -> p k t", p=128)
            ddst = S['hidT'].ap().rearrange("(k p) t -> p k t", p=128)
            for (t0, n) in tiles:
                hs = self.rot('ml_h2', 2)
                P.dma('sp', h2[hs][:, :, 0:n], hsrc[:, :, t0:t0 + n], w=[('ml_h2', hs)])
                for fg in range(4):
                    ds = self.rot('ml_hd', 2)
                    for fi in range(8):
                        fc = fg * 8 + fi
                        bank = self.rot('ml_bank', 6)
                        for k in range(8):
                            self.mm(self.ps[bank][:, 0:n], w1[:, k, fc * 128:(fc + 1) * 128], h2[hs][:, k, 0:n],
                                    k == 0, k == 7, r=['ml_w1', ('ml_h2', hs)], w=[('ps', bank)])
                        rs = self.rot('ml_rl', 3)
                        self.act(rl[rs][:, 0:n], self.ps[bank][:, 0:n], AF.Relu, r=[('ps', bank)], w=[('ml_rl', rs)])
                        self.tt('dve' if fi % 2 == 0 else 'pool', hd[ds][:, fi, 0:n], rl[rs][:, 0:n], rl[rs][:, 0:n],
                                ALU.mult, r=[('ml_rl', rs)], w=[('ml_hd', ds)])
                    P.dma('act', ddst[:, fg * 8:(fg + 1) * 8, t0:t0 + n], hd[ds][:, :, 0:n], r=[('ml_hd', ds)])
        P.barrier()
        with ExitStack() as st:
            sb = lambda n, s, d: st.enter_context(nc.sbuf_tensor(f"{n}_u{self.uid()}", s, d))
            w2 = sb("ml_w2", [128, 32, D], BF16)
            w2v = I['mlp_w2'].ap()[l].rearrange("(k p) c -> p k c", p=128)
            for k in range(32):
                self.load_w(w2[:, k, :], w2v[:, k, :], 'ml_w2')
            hdt = [sb(f"ml_hdt{i}", [128, 32, 256], BF16) for i in range(2)]
            xt = [sb(f"ml_x{i}", [128, D], F32) for i in range(2)]
            xn = [sb(f"ml_xn{i}", [128, D], F32) for i in range(2)]
            hb = [sb(f"ml_hb{i}", [128, D], BF16) for i in range(2)]
            hT = [sb(f"ml_hT{i}", [128, 8, 256], BF16) for i in range(2)]
            ob = [sb(f"ml_ob{i}", [128, D], F32) for i in range(2)]
            scr = self.norm_scratch(st, "ml")
            dsrc = S['hidT'].ap().rearrange("(k p) t -> p k t", p=128)
            hdst = S['hxT'].ap().rearrange("(k p) t -> p k t", p=128)
            fin = None
            if last:
                fin = sb("ml_fin", [128, D], F32)
                self.load_rep(fin[:], I['final_g'], 0, 'ml_fin')
            amods = None
            nmods = None
            cur = None
            tiles256 = []
            for (t0, n) in tiles:
                for h in range(n // 256):
                    tiles256.append((t0 + h * 256, 256))
            for (t0, n) in tiles256:
                cond = 1 if t0 < CTX else 0
                if cur != cond:
                    if amods is None:
                        amods = self.mod_tiles(st, l, cond, ('A2',))
                        gt1 = st.enter_context(nc.sbuf_tensor(f"ml_g1_u{self.uid()}", [128, D], F32))
                        if not last:
                            nmods = self.mod_tiles(st, l + 1, cond, ('G1', 'S1'))
                    else:
                        self.reload_mod(amods, l, cond, gt1)
                        if not last:
                            self.reload_mod(nmods, l + 1, cond, gt1)
                    cur = cond
                hs = self.rot('ml_hdt', 2)
                for kq in range(4):
                    P.dma('sp', hdt[hs][:, kq * 8:(kq + 1) * 8, :], dsrc[:, kq * 8:(kq + 1) * 8, t0:t0 + n],
                          w=[('ml_hdt', hs)])
                ts_ = self.rot('ml_hT', 2)
                for sub in range(2):
                    xs = self.rot('ml_x', 2)
                    r0 = t0 + sub * 128
                    P.dma('sp', xt[xs][:], S['xres'].ap()[r0:r0 + 128, :], w=[('ml_x', xs)])
                    for half in range(2):
                        bank = self.rot('ml_bank2', 4)
                        for k in range(32):
                            self.mm(self.ps[bank][:, :], hdt[hs][:, k, sub * 128:(sub + 1) * 128],
                                    w2[:, k, half * 512:(half + 1) * 512], k == 0, k == 31,
                                    r=[('ml_hdt', hs), 'ml_w2'], w=[('ps', bank)])
                        self.tt('dve', xn[xs][:, half * 512:(half + 1) * 512], self.ps[bank][:, :],
                                amods['A2'][0][:, half * 512:(half + 1) * 512], ALU.mult,
                                r=[('ps', bank), amods['A2'][1]], w=[('ml_xn', xs)])
                    self.tt('pool', xn[xs][:], xn[xs][:], xt[xs][:], ALU.add, r=[('ml_xn', xs), ('ml_x', xs)],
                            w=[('ml_xn', xs)])
                    bs = self.rot('ml_hb', 2)
                    if not last:
                        P.dma('act', S['xres'].ap()[r0:r0 + 128, :], xn[xs][:], r=[('ml_xn', xs)])
                        self.norm_sub(xn[xs][:], ('ml_xn', xs), nmods['G1'], nmods['S1'], hb[bs][:], ('ml_hb', bs), scr[bs])
                        self.transpose_out(hb[bs], ('ml_hb', bs), hT[ts_], ('ml_hT', ts_), sub)
                    else:
                        self.norm_sub(xn[xs][:], ('ml_xn', xs), (fin, 'ml_fin'), None, ob[bs][:], ('ml_ob', bs), scr[bs])
                        P.dma('act', self.out.ap()[r0 - CTX:r0 - CTX + 128, :], ob[bs][:], r=[('ml_ob', bs)])
                if not last:
                    P.dma('act', hdst[:, :, t0:t0 + n], hT[ts_][:, :, 0:n], r=[('ml_hT', ts_)])

def _core_inputs(inp, sh, b):
    d = dict(sh)
    d['xin'] = np.ascontiguousarray(np.concatenate([inp['ctx'][b], inp['x'][b]], axis=0), dtype=np.float32)
    cT = np.stack([inp['c'][b].reshape(8, 128).T, inp['c_ctx'].reshape(8, 128).T], axis=2)
    d['cT'] = np.ascontiguousarray(cT.reshape(128, 16), dtype=np.float32)
    return d


def kernel(**inputs):
    inp = {k: np.asarray(v) for k, v in inputs.items()}
    sh = _prep_shared(inp)
    bld = Builder()
    nc = bld.build()
    in_maps = [_core_inputs(inp, sh, b) for b in range(8)]
    in_maps = [{k: m[k] for k in bld.inputs} for m in in_maps]
    res = run_bass_kernel_spmd(nc, in_maps, core_ids=list(range(8)))
    return np.stack([np.asarray(r['out']) for r in res.results], axis=0).astype(np.float32)
```

```python
import math
from contextlib import ExitStack
import numpy as np
import concourse.bass as bass
import concourse.mybir as mybir
from concourse.bass_utils import run_bass_kernel_spmd

F32 = mybir.dt.float32
BF16 = mybir.dt.bfloat16
I32 = mybir.dt.int32
ALU = mybir.AluOpType
AF = mybir.ActivationFunctionType

SEM_LIMIT = 30000
SAME_ENG_WAITS = True
N_DMA_SEMS = 40

DEPTH = 4
D = 1024
T = 4352
CTX = 256
SEQ = 4096
NIN = 6656
OFF_XA, OFF_XB, OFF_XC, OFF_U, OFF_Q, OFF_K, OFF_V, OFF_GA, OFF_GB, OFF_GC = (
    0, 512, 1024, 1536, 2048, 2560, 3072, 3584, 4608, 5632)
EPS = 1e-6
NTYPE = 21
LCH = 256
NCH = T // LCH


class Prog:
    def __init__(self, nc):
        self.nc = nc
        self.ops = []
        self.last_w = {}
        self.readers = {}
        self.engs = {'pe': nc.tensor, 'dve': nc.vector, 'act': nc.scalar,
                     'pool': nc.gpsimd, 'sp': nc.sync}
        self._bar_from = 0

    def op(self, eng, fn, r=(), w=(), dma=False):
        i = len(self.ops)
        deps = set()
        for k in r:
            if k in self.last_w:
                deps.add(self.last_w[k])
        for k in w:
            if k in self.last_w:
                deps.add(self.last_w[k])
            for j in self.readers.get(k, ()):
                deps.add(j)
        fd = set()
        for j in deps:
            oj = self.ops[j]
            if oj['eng'] == eng and not oj['dma'] and not dma:
                if eng == 'pe' or not SAME_ENG_WAITS:
                    continue
                israw = any(self.last_w.get(k) == j for k in list(r) + list(w))
                if not israw:
                    continue
            fd.add(j)
        for k in w:
            self.last_w[k] = i
            self.readers[k] = []
        for k in r:
            self.readers.setdefault(k, []).append(i)
        self.ops.append(dict(eng=eng, fn=fn, deps=fd, dma=dma, sig=False))
        return i

    def dma(self, q, out, in_, r=(), w=(), **kw):
        return self.op(q, lambda e: e.dma_start(out=out, in_=in_, **kw), r, w, dma=True)

    def barrier(self):
        deps = set()
        lastc = {}
        for idx, o in enumerate(self.ops):
            if o['fn'] is None:
                continue
            if o['dma']:
                if idx >= self._bar_from:
                    deps.add(idx)
            else:
                lastc[o['eng']] = idx
        deps |= set(lastc.values())
        self._bar_from = len(self.ops)
        for e in self.engs:
            self.ops.append(dict(eng=e, fn=None, deps=set(deps), dma=False, sig=False))
        self.last_w = {}
        self.readers = {}

    def emit(self):
        nc = self.nc
        ops = self.ops
        for o in ops:
            for j in o['deps']:
                ops[j]['sig'] = True
        dma_sems = [nc.alloc_semaphore(name=f"dq{i}") for i in range(N_DMA_SEMS)]
        dma_cnt = [0] * N_DMA_SEMS
        dma_last = [None] * N_DMA_SEMS
        eng_sem = {}
        eng_cnt = {}
        nsem = [0]

        def new_eng_sem(e):
            nsem[0] += 1
            eng_sem[e] = nc.alloc_semaphore(name=f"s_{e}_{nsem[0]}")
            eng_cnt[e] = 0

        for e in self.engs:
            new_eng_sem(e)
        rr = 0
        for idx, o in enumerate(ops):
            if o['dma']:
                s = rr % N_DMA_SEMS
                rr += 1
                if dma_cnt[s] + 16 > SEM_LIMIT:
                    if dma_last[s] is not None:
                        o['deps'].add(dma_last[s])
                    dma_sems[s] = nc.alloc_semaphore(name=f"dq{s}_{idx}")
                    dma_cnt[s] = 0
                    dma_last[s] = None
                if dma_last[s] is not None:
                    o['deps'].add(dma_last[s])
                dma_cnt[s] += 16
                o['done'] = (dma_sems[s], dma_cnt[s])
                dma_last[s] = idx
            elif o['sig'] and o['fn'] is not None:
                e = o['eng']
                if eng_cnt[e] + 1 > SEM_LIMIT:
                    new_eng_sem(e)
                eng_cnt[e] += 1
                o['done'] = (eng_sem[e], eng_cnt[e])
            else:
                o['done'] = None

        def resolve(j, acc, seen):
            if j in seen:
                return
            seen.add(j)
            oj = ops[j]
            if oj['done'] is not None:
                acc.add(j)
            elif oj['fn'] is None:
                for jj in oj['deps']:
                    resolve(jj, acc, seen)
            else:
                raise RuntimeError("dep on unsignaled op")

        per_eng = {e: [] for e in self.engs}
        for idx, o in enumerate(ops):
            per_eng[o['eng']].append(idx)
        self.n_inst = {e: len(v) for e, v in per_eng.items()}
        with nc.Block() as block:
            def make(e):
                def body(eng):
                    waited = {}
                    for idx in per_eng[e]:
                        o = ops[idx]
                        acc = set()
                        seen = set()
                        for j in o['deps']:
                            resolve(j, acc, seen)
                        need = {}
                        for j in acc:
                            sem, val = ops[j]['done']
                            key = id(sem)
                            if waited.get(key, 0) >= val:
                                continue
                            if key not in need or need[key][1] < val:
                                need[key] = (sem, val)
                        for key, (sem, val) in need.items():
                            eng.wait_ge(sem, val)
                            waited[key] = val
                        if o['fn'] is not None:
                            inst = o['fn'](eng)
                            if o['done'] is not None:
                                sem, val = o['done']
                                inst.then_inc(sem, 16 if o['dma'] else 1)
                return body
            block.tensor(make('pe'))
            block.vector(make('dve'))
            block.scalar(make('act'))
            block.gpsimd(make('pool'))
            block.sync(make('sp'))


def _na_tile_plan():
    types = {}
    plan = []
    for j in range(32):
        r0a = min(max(2 * j - 4, 0), 56)
        r0b = min(max(2 * j + 1 - 4, 0), 56)
        tlo = r0a // 2
        thi = (r0b + 7) // 2
        lst = []
        for t in range(tlo, thi + 1):
            key = (t - j, r0a - 2 * j, r0b - 2 * j)
            if key not in types:
                types[key] = len(types)
            lst.append((t, types[key]))
        plan.append(lst)
    return plan, types


def _na_bias_index():
    plan, types = _na_tile_plan()
    nt = len(types)
    idx_r = np.zeros((nt, 128, 128), np.int64)
    idx_c = np.zeros((nt, 128, 128), np.int64)
    mask = np.zeros((nt, 128, 128), np.float32)
    col = np.arange(64)
    cs = np.clip(col - 8, 0, 48)
    for (delta, ra, rb), ty in types.items():
        for qr2 in range(2):
            r0rel = (ra, rb)[qr2]
            for kr2 in range(2):
                krel = 2 * delta + kr2
                dr = krel - qr2
                row_ok = (krel >= r0rel) and (krel < r0rel + 8)
                for qc in range(64):
                    kc = col
                    ok = row_ok & (kc >= cs[qc]) & (kc < cs[qc] + 16)
                    q = qr2 * 64 + qc
                    k = kr2 * 64 + kc
                    idx_r[ty, k, q] = np.clip(dr + 7, 0, 14)
                    idx_c[ty, k, q] = np.clip(kc - qc + 15, 0, 30)
                    mask[ty, k, q] = np.where(ok, 0.0, -1e30)
    return plan, nt, idx_r, idx_c, mask


_NA = None


def _na():
    global _NA
    if _NA is None:
        _NA = _na_bias_index()
    return _NA


def _prep_shared(inp):
    L = DEPTH
    sh = {}
    f = lambda a: np.ascontiguousarray(a, dtype=np.float32)
    for k in ('w_mod', 'w_in', 'conv_out', 's5_glu_a', 's5_glu_b', 'na_out', 'w_out',
              'mlp_w1', 'mlp_w2'):
        sh[k] = f(inp[k])
    sh['b_mod'] = f(inp['b_mod'])
    sh['norm1_g'] = f(inp['norm1_g'])
    sh['norm2_g'] = f(inp['norm2_g'])
    sh['final_g'] = f(inp['final_norm_g']).reshape(1, D)
    sh['conv_w'] = f(inp['conv_w'].reshape(L, 3, 4, 128).transpose(0, 3, 2, 1))
    def gp(a):
        a = a.reshape(L, 2, 16, 2, 64)
        return a.transpose(0, 3, 4, 1, 2).reshape(L, 128, 32)
    ls = np.broadcast_to(inp['s5_log_step'][:, :, :, None], (L, 2, 32, 64))
    sh['s5p'] = f(np.stack([gp(inp['s5_lam_re']), gp(inp['s5_lam_im']), gp(ls)], axis=2))
    def bt(B):
        out = np.zeros((L, 2, 16, 128, 128), np.float32)
        for i in range(16):
            for g2 in range(2):
                g = 2 * i + g2
                gl = g % 8
                out[:, :, i, gl * 16:(gl + 1) * 16, g2 * 64:(g2 + 1) * 64] = \
                    B[:, :, g].transpose(0, 1, 3, 2)
        return out
    sh['s5BT'] = f(np.stack([bt(inp['s5_b_re']), bt(inp['s5_b_im'])], axis=3))
    def cm(C):
        out = np.zeros((L, 2, 16, 128, 128), np.float32)
        for i in range(16):
            for g2 in range(2):
                g = 2 * i + g2
                gl = g % 8
                out[:, :, i, g2 * 64:(g2 + 1) * 64, gl * 16:(gl + 1) * 16] = \
                    C[:, :, g].transpose(0, 1, 3, 2)
        return out
    sh['s5CM'] = f(np.stack([cm(inp['s5_c_re']), cm(inp['s5_c_im'])], axis=3))
    sh['s5d'] = f(inp['s5_d'].reshape(L, 4, 128).transpose(0, 2, 1))
    plan, nt, idx_r, idx_c, mask = _na()
    rpb = inp['na_rpb']
    sh['rpbg'] = f(rpb[:, :, idx_r, idx_c])
    sh['na_mask'] = f(mask)
    sh['ident'] = np.eye(128, dtype=np.float32)
    return sh


class Builder:
    def __init__(self, layers=DEPTH, dbg=(), stop=None, only=None):
        self.layers = layers
        self.stop = stop
        self.only = only
        self.dbg = set(dbg)
        nc = bass.Bass("TRN2", target_bir_lowering=False)
        self.nc = nc
        self.P = Prog(nc)
        self.inputs = {}
        self.cnt = {}

    def din(self, name, shape, dt=F32):
        t = self.nc.dram_tensor(name, list(shape), dt, kind="ExternalInput")
        self.inputs[name] = t
        return t

    def dscr(self, name, shape, dt):
        kind = "ExternalOutput" if name in self.dbg else "Internal"
        return self.nc.dram_tensor(name, list(shape), dt, kind=kind)

    def uid(self):
        self._uid = getattr(self, '_uid', 0) + 1
        return self._uid

    def rot(self, name, n):
        c = self.cnt.get(name, 0)
        self.cnt[name] = c + 1
        return c % n

    def mm(self, out, lhsT, rhs, start, stop, r, w):
        self.P.op('pe', lambda e: e.matmul(out, lhsT, rhs, start=start, stop=stop), r, w)

    def act(self, out, in_, func, r, w, **kw):
        self.P.op('act', lambda e: e.activation(out, in_, func, **kw), r, w)

    def tt(self, eng, out, a, b, op, r, w):
        self.P.op(eng, lambda e: e.tensor_tensor(out, a, b, op), r, w)

    def ts(self, eng, out, a, s1, s2, op0, op1, r, w):
        if op1 is None:
            self.P.op(eng, lambda e: e.tensor_scalar(out, a, s1, None, op0), r, w)
        else:
            self.P.op(eng, lambda e: e.tensor_scalar(out, a, s1, s2, op0, op1), r, w)

    def stt(self, out, a, s, b, op0, op1, r, w):
        self.P.op('dve', lambda e: e.scalar_tensor_tensor(out, a, s, b, op0, op1), r, w)

    def cp(self, eng, out, in_, r, w):
        if eng == 'act':
            self.P.op('act', lambda e: e.activation(out, in_, AF.Copy), r, w)
        else:
            self.P.op(eng, lambda e: e.tensor_copy(out, in_), r, w)

    def build(self):
        nc, P = self.nc, self.P
        L = DEPTH
        nt = _na()[1]
        self.nt = nt
        I = {}
        I['xin'] = self.din('xin', [T, D])
        I['cT'] = self.din('cT', [128, 16])
        I['w_mod'] = self.din('w_mod', [L, D, 6 * D])
        I['b_mod'] = self.din('b_mod', [L, 6 * D])
        I['norm1_g'] = self.din('norm1_g', [L, D])
        I['norm2_g'] = self.din('norm2_g', [L, D])
        I['final_g'] = self.din('final_g', [1, D])
        I['w_in'] = self.din('w_in', [L, D, NIN])
        I['conv_w'] = self.din('conv_w', [L, 128, 4, 3])
        I['conv_out'] = self.din('conv_out', [L, 512, D])
        I['s5p'] = self.din('s5p', [L, 128, 3, 32])
        I['s5BT'] = self.din('s5BT', [L, 2, 16, 2, 128, 128])
        I['s5CM'] = self.din('s5CM', [L, 2, 16, 2, 128, 128])
        I['s5d'] = self.din('s5d', [L, 128, 4])
        I['s5_glu_a'] = self.din('s5_glu_a', [L, 512, D])
        I['s5_glu_b'] = self.din('s5_glu_b', [L, 512, D])
        I['rpbg'] = self.din('rpbg', [L, 8, nt, 128, 128])
        I['na_mask'] = self.din('na_mask', [nt, 128, 128])
        I['na_out'] = self.din('na_out', [L, 512, D])
        I['w_out'] = self.din('w_out', [L, D, D])
        I['mlp_w1'] = self.din('mlp_w1', [L, D, 4 * D])
        I['mlp_w2'] = self.din('mlp_w2', [L, 4 * D, D])
        I['ident'] = self.din('ident', [128, 128])
        self.I = I
        self.out = nc.dram_tensor('out', [SEQ, D], F32, kind="ExternalOutput")
        S = {}
        S['xres'] = self.dscr('xres', [T, D], F32)
        S['hxT'] = self.dscr('hxT', [D, T], BF16)
        S['h2T'] = self.dscr('h2T', [D, T], BF16)
        S['convbT'] = self.dscr('convbT', [512, T], BF16)
        S['gT'] = self.dscr('gT', [512, T], BF16)
        S['attnT'] = self.dscr('attnT', [512, T], BF16)
        S['modv'] = self.dscr('modv', [L, 2, 6 * D], F32)
        S['mgT'] = self.dscr('mgT', [D, T], BF16)
        S['hidT'] = self.dscr('hidT', [4 * D, T], BF16)
        self.S = S

        with ExitStack() as gs:
            self.ps = [gs.enter_context(nc.psum_tensor(f"ps{i}", [128, 512], F32))
                       for i in range(7)]
            self.psT = gs.enter_context(nc.psum_tensor("psT", [128, 1024], BF16))
            self.ident = gs.enter_context(nc.sbuf_tensor("ident_sb", [128, 128], BF16))
            self.ones = gs.enter_context(nc.sbuf_tensor("ones_sb", [128, 128], BF16))
            P.dma('pool', self.ident[:], I['ident'].ap(), w=['ident'])
            P.op('dve', lambda e: e.memset(self.ones[:], 1.0), w=['ones'])
            P.barrier()
            seq = [('adaln', lambda: self.phase_adaln()),
                   ('norm', lambda: self.phase_norm(0, src=I['xin'], kind='n1'))]
            for l in range(self.layers):
                seq += [(f'conv{l}', lambda l=l: self.phase_conv(l)),
                        (f'attn{l}', lambda l=l: self.phase_attn(l)),
                        (f's5{l}', lambda l=l: self.phase_s5(l)),
                        (f'merge{l}', lambda l=l: self.phase_merge(l, src=(I['xin'] if l == 0 else S['xres']))),
                        (f'mlp{l}', lambda l=l: self.phase_mlp(l))]
            for name, fn in seq:
                if self.only is not None and name not in self.only:
                    continue
                fn()
                P.barrier()
                if name == self.stop:
                    break
            P.emit()
        return nc

    def phase_adaln(self):
        nc, P, I, S = self.nc, self.P, self.I, self.S
        with ExitStack() as st:
            sb = lambda n, s, d: st.enter_context(nc.sbuf_tensor(f"{n}_u{self.uid()}", s, d))
            cT = sb("ad_cT", [128, 16], F32)
            sil = sb("ad_sil", [128, 16], BF16)
            wt = [sb(f"ad_w{i}", [128, 8, 512], BF16) for i in range(3)]
            bm = sb("ad_bm", [2, 6 * D], F32)
            row = [sb(f"ad_row{i}", [2, 512], F32) for i in range(2)]
            P.dma('sp', cT[:], I['cT'].ap(), w=['cT'])
            self.act(sil[:], cT[:], AF.Silu, r=['cT'], w=['sil'])
            silv = sil[:].rearrange("p (k j) -> p k j", j=2)
            for l in range(DEPTH):
                bsrc = bass.AP(I['b_mod'], l * 6 * D, [[0, 2], [1, 6 * D]])
                P.dma('sp', bm[:], bsrc, w=['bm'])
                wv = I['w_mod'].ap()[l].rearrange("(k p) c -> p k c", p=128)
                for ct in range(12):
                    s = self.rot('adw', 3)
                    P.dma('pool', wt[s][:], wv[:, :, ct * 512:(ct + 1) * 512], w=[('adw', s)])
                    pst = self.ps[ct % 2]
                    for k in range(8):
                        self.mm(pst[0:2, :], silv[:, k, :], wt[s][:, k, :], k == 0, k == 7,
                                r=['sil', ('adw', s)], w=[('ps', ct % 2)])
                    rs = self.rot('adrow', 2)
                    self.tt('dve', row[rs][:], pst[0:2, :], bm[:, ct * 512:(ct + 1) * 512], ALU.add,
                            r=[('ps', ct % 2), 'bm'], w=[('adrow', rs)])
                    P.dma('sp', S['modv'].ap()[l][:, ct * 512:(ct + 1) * 512], row[rs][:],
                          r=[('adrow', rs)])

    def load_rep(self, dst, tensor, offset, key):
        self.P.dma('sp', dst, bass.AP(tensor, offset, [[0, 128], [1, D]]), w=[key])

    def mod_tiles(self, st, l, cond, names, gsrc=None):
        nc = self.nc
        res = {}
        base = (l * 2 + cond) * 6 * D
        idx = {'S1': 0, 'G1': 1, 'A1': 2, 'S2': 3, 'G2': 4, 'A2': 5}
        for n in names:
            t = st.enter_context(nc.sbuf_tensor(f"mod_{n}_{cond}_{self.rot('modt', 1 << 30)}", [128, D], F32))
            key = ('mod', n, cond)
            self.load_rep(t[:], self.S['modv'], base + idx[n] * D, key)
            if n in ('G1', 'G2'):
                g = st.enter_context(nc.sbuf_tensor(f"modg_{n}_{cond}_{self.rot('modt', 1 << 30)}", [128, D], F32))
                gt = self.I['norm1_g'] if n == 'G1' else self.I['norm2_g']
                self.load_rep(g[:], gt, l * D, ('modg', n, cond))
                self.stt(t[:], t[:], 1.0, g[:], ALU.add, ALU.mult, r=[key, ('modg', n, cond)], w=[key])
            res[n] = (t, key)
        return res

    def norm_sub(self, xt, xkey, G, S_, hb, hkey, scr):
        junk, ss, rstd, tmp = scr['junk'], scr['ss'], scr['rstd'], scr['tmp']
        k = scr['k']
        self.act(junk[:], xt, AF.Square, r=[xkey], w=[('nj', k), ('ss', k)], accum_out=ss[:])
        self.ts('dve', rstd[:], ss[:], 1.0 / D, EPS, ALU.mult, ALU.add, r=[('ss', k)], w=[('rstd', k)])
        self.act(rstd[:], rstd[:], AF.Sqrt, r=[('rstd', k)], w=[('rstd', k)])
        self.P.op('dve', lambda e: e.reciprocal(rstd[:], rstd[:]), r=[('rstd', k)], w=[('rstd', k)])
        if S_ is None:
            self.stt(hb, xt, rstd[:], G[0][:], ALU.mult, ALU.mult, r=[xkey, ('rstd', k), G[1]], w=[hkey])
        else:
            self.stt(tmp[:], xt, rstd[:], G[0][:], ALU.mult, ALU.mult, r=[xkey, ('rstd', k), G[1]],
                     w=[('ntmp', k)])
            self.tt('pool', hb, tmp[:], S_[0][:], ALU.add, r=[('ntmp', k), S_[1]], w=[hkey])

    def norm_scratch(self, st, tag):
        nc = self.nc
        out = []
        for k in range(2):
            out.append(dict(
                junk=st.enter_context(nc.sbuf_tensor(f"{tag}_junk{k}_u{self.uid()}", [128, D], BF16)),
                ss=st.enter_context(nc.sbuf_tensor(f"{tag}_ss{k}_u{self.uid()}", [128, 1], F32)),
                rstd=st.enter_context(nc.sbuf_tensor(f"{tag}_rstd{k}_u{self.uid()}", [128, 1], F32)),
                tmp=st.enter_context(nc.sbuf_tensor(f"{tag}_tmp{k}_u{self.uid()}", [128, D], F32)),
                k=(tag, k)))
        return out

    def transpose_out(self, hb, hkey, hT, hTkey, sub):
        for kc in range(8):
            self.P.op('pe', lambda e, kc=kc: e.transpose(self.psT[:, kc * 128:(kc + 1) * 128],
                                                          hb[:, kc * 128:(kc + 1) * 128], self.ident[:]),
                      r=[hkey, 'ident'], w=['psT'])
        self.cp('act', hT[:, :, sub * 128:(sub + 1) * 128],
                self.psT[:].rearrange("p (k t) -> p k t", t=128), r=['psT'], w=[hTkey])

    TILES = [(0, 256)] + [(256 + 512 * i, 512) for i in range(8)]

    def phase_norm(self, l, src, kind):
        nc, P, I, S = self.nc, self.P, self.I, self.S
        with ExitStack() as st:
            sb = lambda n, s, d: st.enter_context(nc.sbuf_tensor(f"{n}_u{self.uid()}", s, d))
            mods = [self.mod_tiles(st, l, c, ('G1', 'S1')) for c in (0, 1)]
            xt = [sb(f"pn_x{i}", [128, D], F32) for i in range(3)]
            hb = [sb(f"pn_hb{i}", [128, D], BF16) for i in range(2)]
            hT = [sb(f"pn_hT{i}", [128, 8, 512], BF16) for i in range(2)]
            scr = self.norm_scratch(st, "pn")
            dst = S['hxT'].ap().rearrange("(k p) t -> p k t", p=128)
            for (t0, n) in self.TILES:
                cond = 1 if t0 < CTX else 0
                hs = self.rot('pn_hT', 2)
                for sub in range(n // 128):
                    xs = self.rot('pn_x', 3)
                    P.dma('sp', xt[xs][:], src.ap()[t0 + sub * 128:t0 + (sub + 1) * 128, :], w=[('pn_x', xs)])
                    bs = self.rot('pn_hb', 2)
                    self.norm_sub(xt[xs][:], ('pn_x', xs), mods[cond]['G1'], mods[cond]['S1'],
                                  hb[bs][:], ('pn_hb', bs), scr[bs])
                    self.transpose_out(hb[bs], ('pn_hb', bs), hT[hs], ('pn_hT', hs), sub)
                P.dma('act', dst[:, :, t0:t0 + n], hT[hs][:, :, 0:n], r=[('pn_hT', hs)])

    TT512 = [(512 * i, 512) for i in range(8)] + [(4096, 256)]

    def load_w(self, dst, src_ap, key, r=()):
        self.P.dma('pool', dst, src_ap, r=r, w=[key])

    def phase_conv(self, l):
        nc, P, I, S = self.nc, self.P, self.I, self.S
        with ExitStack() as st:
            sb = lambda n, s, d: st.enter_context(nc.sbuf_tensor(f"{n}_u{self.uid()}", s, d))
            wc = [sb(f"cv_w{i}", [128, 3, 8, 128], BF16) for i in range(2)]
            hx = [sb(f"cv_hx{i}", [128, 8, 512], BF16) for i in range(2)]
            vbs = [sb(f"cv_v{i}", [128, T + 4], F32) for i in range(2)]
            xbbs = [sb(f"cv_xb{i}", [128, T], F32) for i in range(2)]
            tmp = [sb(f"cv_tmp{i}", [128, 512], F32) for i in range(2)]
            acc = [sb(f"cv_acc{i}", [128, 1024], F32) for i in range(2)]
            ob = [sb(f"cv_o{i}", [128, T], BF16) for i in range(2)]
            cw = sb("cv_cw", [128, 12], F32)
            P.dma('sp', cw[:], I['conv_w'].ap()[l].rearrange("p q j -> p (q j)"), w=['cw'])
            for i in range(2):
                P.op('pool', lambda e, i=i: e.memset(vbs[i][:], 0.0), w=[('vb', i)])
            hsrc = S['hxT'].ap().rearrange("(k p) t -> p k t", p=128)
            win = I['w_in'].ap()[l].rearrange("(k p) c -> p k c", p=128)
            for q in range(4):
                ws = self.rot('cv_w', 2)
                vb, xbb = vbs[q % 2], xbbs[q % 2]
                vbk, xbk = ('vb', q % 2), ('xbb', q % 2)
                for j, off in enumerate((OFF_XA, OFF_XB, OFF_XC)):
                    self.load_w(wc[ws][:, j, :, :], win[:, :, off + q * 128: off + (q + 1) * 128], ('cv_w', ws, j))
                for (t0, n) in self.TT512:
                    hs = self.rot('cv_hx', 2)
                    P.dma('sp', hx[hs][:, :, 0:n], hsrc[:, :, t0:t0 + n], w=[('cv_hx', hs)])
                    for j in range(3):
                        for k in range(8):
                            self.mm(self.ps[j][:, 0:n], wc[ws][:, j, k, :], hx[hs][:, k, 0:n], k == 0, k == 7,
                                    r=[('cv_w', ws, j), ('cv_hx', hs)], w=[('ps', j)])
                    ts_ = self.rot('cv_tmp', 2)
                    self.cp('act', tmp[ts_][:, 0:n], self.ps[2][:, 0:n], r=[('ps', 2)], w=[('cv_tmp', ts_)])
                    segs = []
                    if t0 < CTX:
                        segs.append((t0, CTX - t0, 1 + t0))
                        segs.append((CTX, t0 + n - CTX, 3 + CTX))
                    else:
                        segs.append((t0, n, 3 + t0))
                    for (a0, an, c0) in segs:
                        self.tt('dve', vb[:, c0:c0 + an], self.ps[0][:, a0 - t0:a0 - t0 + an],
                                tmp[ts_][:, a0 - t0:a0 - t0 + an], ALU.mult,
                                r=[('ps', 0), ('cv_tmp', ts_), vbk], w=[vbk])
                    self.cp('act', xbb[:, t0:t0 + n], self.ps[1][:, 0:n], r=[('ps', 1)], w=[xbk])
                os_ = self.rot('cv_o', 2)
                pieces = [(0, 256, 1)] + [(256 + 1024 * i, 1024, 3 + 256 + 1024 * i) for i in range(4)]
                for (a0, an, c0) in pieces:
                    as_ = self.rot('cv_acc', 2)
                    A = acc[as_][:, 0:an]
                    ak = ('cv_acc', as_)
                    self.ts('dve', A, vb[:, c0 - 1:c0 - 1 + an], cw[:, q * 3:q * 3 + 1], None, ALU.mult, None,
                            r=[vbk, 'cw'], w=[ak])
                    self.stt(A, vb[:, c0:c0 + an], cw[:, q * 3 + 1:q * 3 + 2], A, ALU.mult, ALU.add,
                             r=[vbk, 'cw', ak], w=[ak])
                    self.stt(A, vb[:, c0 + 1:c0 + 1 + an], cw[:, q * 3 + 2:q * 3 + 3], A, ALU.mult, ALU.add,
                             r=[vbk, 'cw', ak], w=[ak])
                    self.tt('pool', ob[os_][:, a0:a0 + an], A, xbb[:, a0:a0 + an], ALU.mult,
                            r=[ak, xbk], w=[('cv_o', os_)])
                P.dma('act', S['convbT'].ap()[q * 128:(q + 1) * 128, :], ob[os_][:], r=[('cv_o', os_)])


    def phase_attn(self, l):
        nc, P, I, S = self.nc, self.P, self.I, self.S
        plan, nt = _na()[0], self.nt
        last = (l == DEPTH - 1)
        with ExitStack() as st:
            sb = lambda n, s, d: st.enter_context(nc.sbuf_tensor(f"{n}_u{self.uid()}", s, d))
            wq = [sb(f"at_w{i}", [128, 3, 8, 128], BF16) for i in range(2)]
            hx = [sb(f"at_hx{i}", [128, 8, 512], BF16) for i in range(2)]
            qT = [sb(f"at_q{i}", [128, 2, T], BF16) for i in range(2)]
            kT = [sb(f"at_k{i}", [128, T], BF16) for i in range(2)]
            V = [sb(f"at_v{i}", [128, 34, 128], BF16) for i in range(2)]
            aT = [sb(f"at_a{i}", [128, T], BF16) for i in range(2)]
            msk = sb("at_mask", [128, nt, 128], F32)
            rg = [sb(f"at_rg{i}", [128, 128], F32) for i in range(3)]
            bias = [sb(f"at_bias{i}", [128, 2, nt, 128], BF16) for i in range(2)]
            sc = [sb(f"at_sc{i}", [128, 256], F32) for i in range(3)]
            PT = [sb(f"at_pt{i}", [128, 256], BF16) for i in range(4)]
            rec = [sb(f"at_rec{i}", [128, 128], F32) for i in range(2)]
            P.dma('sp', msk[:], I['na_mask'].ap().rearrange("t k q -> k t q"), w=['msk'])
            for i in range(2):
                P.op('pool', lambda e, i=i: e.memset(qT[i][:], 0.0), w=[('at_qk', i, 0)])
            hsrc = S['hxT'].ap().rearrange("(k p) t -> p k t", p=128)
            win = I['w_in'].ap()[l].rearrange("(k p) c -> p k c", p=128)
            for hp in range(4):
                ws = self.rot('at_w', 2)
                bsl = self.rot('at_b', 2)
                for j, off in enumerate((OFF_Q, OFF_K, OFF_V)):
                    self.load_w(wq[ws][:, j, :, :], win[:, :, off + hp * 128: off + (hp + 1) * 128], ('at_w', ws, j))
                for hh in range(2):
                    for ty in range(nt):
                        rs = self.rot('at_rg', 3)
                        P.dma('sp', rg[rs][:], I['rpbg'].ap()[l, hp * 2 + hh, ty], w=[('at_rg', rs)])
                        self.tt('pool', bias[bsl][:, hh, ty, :], rg[rs][:], msk[:, ty, :], ALU.add,
                                r=[('at_rg', rs), 'msk'], w=[('at_bias', bsl)])
                for (t0, n) in self.TT512:
                    hs = self.rot('at_hx', 2)
                    P.dma('sp', hx[hs][:, :, 0:n], hsrc[:, :, t0:t0 + n], w=[('at_hx', hs)])
                    for j in range(2):
                        for k in range(8):
                            self.mm(self.ps[j][:, 0:n], wq[ws][:, j, k, :], hx[hs][:, k, 0:n], k == 0, k == 7,
                                    r=[('at_w', ws, j), ('at_hx', hs)], w=[('ps', j)])
                    self.cp('act', qT[bsl][0:64, 0, t0:t0 + n], self.ps[0][0:64, 0:n], r=[('ps', 0)],
                            w=[('at_qk', bsl, 0)])
                    self.cp('act', qT[bsl][64:128, 1, t0:t0 + n], self.ps[0][64:128, 0:n], r=[('ps', 0)],
                            w=[('at_qk', bsl, 0)])
                    self.cp('dve', kT[bsl][:, t0:t0 + n], self.ps[1][:, 0:n], r=[('ps', 1)], w=[('at_qk', bsl, 1)])
                    for sub in range(n // 128):
                        for k in range(8):
                            self.mm(self.ps[0][:, sub * 128:(sub + 1) * 128], hx[hs][:, k, sub * 128:(sub + 1) * 128],
                                    wq[ws][:, 2, k, :], k == 0, k == 7,
                                    r=[('at_w', ws, 2), ('at_hx', hs)], w=[('ps', 0)])
                    ti0 = t0 // 128
                    self.cp('act', V[bsl][:, ti0:ti0 + n // 128, :],
                            self.ps[0][:, 0:n].rearrange("p (s c) -> p s c", c=128), r=[('ps', 0)], w=[('at_v', bsl)])
                qlist = []
                if not last:
                    for qi in range(2):
                        qlist.append((qi * 128, [(0, None), (128, None)]))
                for j in range(32):
                    kt = [(CTX + t * 128, ty) for (t, ty) in plan[j]] + [(0, None), (128, None)]
                    qlist.append((CTX + j * 128, kt))
                items = []
                for (q0, kts) in qlist:
                    osl = self.rot('at_o', 2)
                    for ki, (k0, ty) in enumerate(kts):
                        items.append(dict(q0=q0, ki=ki, nk=len(kts), k0=k0, ty=ty, osl=osl))

                def stage_s(it):
                    ssl = self.rot('at_s', 3)
                    it['psl'] = self.rot('at_ptslot', 4)
                    psS = self.ps[ssl][:, 0:256]
                    skey = ('ps', ssl)
                    q0, k0, ty = it['q0'], it['k0'], it['ty']
                    self.mm(psS, kT[bsl][:, k0:k0 + 128], qT[bsl][:, :, q0:q0 + 128], True, True,
                            r=[('at_qk', bsl, 0), ('at_qk', bsl, 1)], w=[skey])
                    pk = ('at_pt', it['psl'])
                    if ty is None:
                        self.act(PT[it['psl']][:], psS, AF.Exp, r=[skey], w=[pk], scale=0.125)
                    else:
                        cs_ = self.rot('at_sc', 3)
                        self.stt(sc[cs_][:].rearrange("p (h q) -> p h q", h=2),
                                 psS.rearrange("p (h q) -> p h q", h=2), 0.125, bias[bsl][:, :, ty, :],
                                 ALU.mult, ALU.add, r=[skey, ('at_bias', bsl)], w=[('at_sc', cs_)])
                        self.act(PT[it['psl']][:], sc[cs_][:], AF.Exp, r=[('at_sc', cs_)], w=[pk])

                def stage_pv(it):
                    psl, osl, ki, nk, k0, q0 = it['psl'], it['osl'], it['ki'], it['nk'], it['k0'], it['q0']
                    psO = self.ps[3 + osl]
                    psU = self.ps[5 + osl]
                    pkey = ('at_pt', psl)
                    self.mm(psO[:, 0:256], V[bsl][:, k0 // 128, :], PT[psl][:], ki == 0, ki == nk - 1,
                            r=[('at_v', bsl), pkey], w=[('psO', osl)])
                    self.mm(psU[:, 0:256], self.ones[:], PT[psl][:], ki == 0, ki == nk - 1,
                            r=['ones', pkey], w=[('psU', osl)])
                    if ki == nk - 1:
                        for hh in range(2):
                            pb = hh * 64
                            cs0 = hh * 128
                            self.P.op('dve', lambda e, pb=pb, cs0=cs0: e.reciprocal(
                                rec[osl][pb:pb + 64, :], psU[pb:pb + 64, cs0:cs0 + 128]),
                                r=[('psU', osl)], w=[('at_rec', osl, hh)])
                            self.tt('dve', aT[bsl][pb:pb + 64, q0:q0 + 128], psO[pb:pb + 64, cs0:cs0 + 128],
                                    rec[osl][pb:pb + 64, :], ALU.mult, r=[('psO', osl), ('at_rec', osl, hh)],
                                    w=[('at_a', bsl)])
                LA = 2
                for i in range(len(items) + LA):
                    if i < len(items):
                        stage_s(items[i])
                    if i - LA >= 0:
                        stage_pv(items[i - LA])
                a0 = CTX if last else 0
                P.dma('act', S['attnT'].ap()[hp * 128:(hp + 1) * 128, a0:T], aT[bsl][:, a0:T], r=[('at_a', bsl)])

    def phase_s5(self, l):
        nc, P, I, S = self.nc, self.P, self.I, self.S
        Lc = LCH
        TWO_PI = 2.0 * math.pi
        with ExitStack() as st:
            sb = lambda n, s, d: st.enter_context(nc.sbuf_tensor(f"{n}_u{self.uid()}", s, d))
            prm = sb("s5_prm", [128, 3, 32], F32)
            names = ['dt', 'a', 'adt', 'bdt', 'r1', 'kf', 'red', 'sn', 'shf', 'cs', 'nr', 'ni', 'den',
                     'cre', 'cim', 't1', 't2']
            A = {n: sb(f"s5_{n}", [128, 32], F32) for n in names}
            ki = sb("s5_ki", [128, 32], I32)
            phr = sb("s5_phr", [128, 9, 32], F32)
            phi = sb("s5_phi", [128, 9, 32], F32)
            dsk = sb("s5_dsk", [128, 4], F32)
            zero = sb("s5_zero", [128, Lc], F32)
            P.dma('sp', prm[:], I['s5p'].ap()[l], w=['prm'])
            P.dma('sp', dsk[:], I['s5d'].ap()[l], w=['dsk'])
            P.op('dve', lambda e: e.memset(zero[:], 0.0), w=['zero'])
            K_ = ['prm']
            lre, lim, lst = prm[:, 0, :], prm[:, 1, :], prm[:, 2, :]
            a = lambda n: A[n][:]
            self.act(a('dt'), lst, AF.Exp, r=K_, w=K_)
            self.ts('dve', a('a'), lre, -1e-4, None, ALU.min, None, r=K_, w=K_)
            self.tt('dve', a('adt'), a('a'), a('dt'), ALU.mult, r=K_, w=K_)
            self.tt('dve', a('bdt'), lim, a('dt'), ALU.mult, r=K_, w=K_)
            self.act(a('r1'), a('adt'), AF.Exp, r=K_, w=K_)
            self.ts('dve', a('kf'), a('bdt'), 1.0 / TWO_PI, None, ALU.mult, None, r=K_, w=K_)
            self.cp('dve', ki[:], a('kf'), r=K_, w=K_)
            self.cp('dve', a('kf'), ki[:], r=K_, w=K_)
            self.stt(a('red'), a('kf'), -TWO_PI, a('bdt'), ALU.mult, ALU.add, r=K_, w=K_)
            self.ts('dve', a('red'), a('red'), 3.141592, -3.141592, ALU.min, ALU.max, r=K_, w=K_)
            self.act(a('sn'), a('red'), AF.Sin, r=K_, w=K_)
            self.act(a('shf'), a('red'), AF.Sin, r=K_, w=K_, scale=0.5)
            self.tt('dve', a('cs'), a('shf'), a('shf'), ALU.mult, r=K_, w=K_)
            self.ts('dve', a('cs'), a('cs'), -2.0, 1.0, ALU.mult, ALU.add, r=K_, w=K_)
            self.tt('dve', a('nr'), a('r1'), a('cs'), ALU.mult, r=K_, w=K_)
            self.ts('dve', a('nr'), a('nr'), -1.0, None, ALU.add, None, r=K_, w=K_)
            self.tt('dve', a('ni'), a('r1'), a('sn'), ALU.mult, r=K_, w=K_)
            self.tt('dve', a('den'), a('a'), a('a'), ALU.mult, r=K_, w=K_)
            self.tt('dve', a('t1'), lim, lim, ALU.mult, r=K_, w=K_)
            self.tt('dve', a('den'), a('den'), a('t1'), ALU.add, r=K_, w=K_)
            P.op('dve', lambda e: e.reciprocal(a('den'), a('den')), r=K_, w=K_)
            self.tt('dve', a('t1'), a('nr'), a('a'), ALU.mult, r=K_, w=K_)
            self.tt('dve', a('t2'), a('ni'), lim, ALU.mult, r=K_, w=K_)
            self.tt('dve', a('t1'), a('t1'), a('t2'), ALU.add, r=K_, w=K_)
            self.tt('dve', a('cre'), a('t1'), a('den'), ALU.mult, r=K_, w=K_)
            self.tt('dve', a('t1'), a('ni'), a('a'), ALU.mult, r=K_, w=K_)
            self.tt('dve', a('t2'), a('nr'), lim, ALU.mult, r=K_, w=K_)
            self.tt('dve', a('t1'), a('t1'), a('t2'), ALU.subtract, r=K_, w=K_)
            self.tt('dve', a('cim'), a('t1'), a('den'), ALU.mult, r=K_, w=K_)
            self.cp('dve', phr[:, 0, :], a('cs'), r=K_, w=K_)
            self.cp('dve', phi[:, 0, :], a('sn'), r=K_, w=K_)
            for j in range(1, 9):
                self.tt('dve', a('t1'), phr[:, j - 1, :], phr[:, j - 1, :], ALU.mult, r=K_, w=K_)
                self.tt('dve', a('t2'), phi[:, j - 1, :], phi[:, j - 1, :], ALU.mult, r=K_, w=K_)
                self.tt('dve', phr[:, j, :], a('t1'), a('t2'), ALU.subtract, r=K_, w=K_)
                self.tt('dve', a('t1'), phr[:, j - 1, :], phi[:, j - 1, :], ALU.mult, r=K_, w=K_)
                self.ts('dve', phi[:, j, :], a('t1'), 2.0, None, ALU.mult, None, r=K_, w=K_)

            wu = [sb(f"s5_wu{i}", [128, 8, 128], BF16) for i in range(2)]
            hx = [sb(f"s5_hx{i}", [128, 8, 512], BF16) for i in range(2)]
            uT = [sb(f"s5_uT{i}", [128, T], BF16) for i in range(2)]
            BT = [sb(f"s5_BT{i}", [128, 16, 128], BF16) for i in range(2)]
            CM = [sb(f"s5_CM{i}", [128, 16, 128], BF16) for i in range(2)]
            ybuf = sb("s5_y", [128, T], F32)
            gq = [sb(f"s5_g{i}", [128, T], BF16) for i in range(2)]
            tab = [{n: sb(f"s5_tab{il}_{n}", [128, Lc], F32) for n in ('DTr', 'DTi', 'nDTi', 'nDTr', 'MTr', 'MTi', 'nMTi', 'Rc')}
                   for il in range(4)]
            ttmp = sb("s5_ttmp", [128, Lc], F32)
            NZ = 8
            zb = [{n: sb(f"s5_z{i}_{n}", [128, Lc], F32) for n in ('gr', 'gi')} for i in range(NZ)]
            zp = [{n: sb(f"s5_zp{i}_{n}", [128, Lc], BF16) for n in ('pa', 'pb', 'pc', 'pd')} for i in range(NZ)]
            db = [{n: sb(f"s5_d{i}_{n}", [128, Lc], F32) for n in ('t1', 't2', 't3', 't4')} for i in range(2)]
            hb = [[sb(f"s5_h{il}_{i}", [128, 4, Lc], BF16) for i in range(2)] for il in range(4)]
            identf = sb("s5_identf", [128, 128], F32)
            P.dma('sp', identf[:], I['ident'].ap(), w=['identf'])
            ini = [[sb(f"s5_ini{il}_{i}", [128, 2], F32) for i in range(2)] for il in range(4)]
            itmp = [sb(f"s5_itmp{i}", [128, 2], F32) for i in range(4)]
            nphi = sb("s5_nphi", [128, 32], F32)
            self.ts('dve', nphi[:], phi[:, 8, :], -1.0, None, ALU.mult, None, r=K_, w=K_)

            hsrc = S['hxT'].ap().rearrange("(k p) t -> p k t", p=128)
            win = I['w_in'].ap()[l].rearrange("(k p) c -> p k c", p=128)
            for q in range(4):
                us = self.rot('s5_u', 2)
                self.load_w(wu[us][:], win[:, :, OFF_U + q * 128: OFF_U + (q + 1) * 128], ('s5_wu', us))
                for d in range(2):
                    for c in range(2):
                        self.load_w(BT[us][:, d * 8 + c * 4: d * 8 + c * 4 + 4, :] if False else
                                    BT[us][:].rearrange("p (d i c) k -> p d i c k", d=2, i=4)[:, d, :, c, :],
                                    I['s5BT'].ap()[l, d, 4 * q:4 * q + 4, c].rearrange("i r k -> r i k"),
                                    ('s5_BT', us, d, c))
                        self.load_w(CM[us][:].rearrange("p (d i c) k -> p d i c k", d=2, i=4)[:, d, :, c, :],
                                    I['s5CM'].ap()[l, d, 4 * q:4 * q + 4, c].rearrange("i r k -> r i k"),
                                    ('s5_CM', us, d, c))
                BTv = BT[us][:].rearrange("p (d i c) k -> p d i c k", d=2, i=4)
                CMv = CM[us][:].rearrange("p (d i c) k -> p d i c k", d=2, i=4)
                ukey = ('s5_uT', us)
                for (t0, n) in self.TT512:
                    hs = self.rot('s5_hx', 2)
                    P.dma('sp', hx[hs][:, :, 0:n], hsrc[:, :, t0:t0 + n], w=[('s5_hx', hs)])
                    for k in range(8):
                        self.mm(self.ps[6][:, 0:n], wu[us][:, k, :], hx[hs][:, k, 0:n], k == 0, k == 7,
                                r=[('s5_wu', us), ('s5_hx', hs)], w=[('ps', 6)])
                    self.cp('act', uT[us][:, t0:t0 + n], self.ps[6][:, 0:n], r=[('ps', 6)], w=[ukey])
                for d in range(2):
                    for il in range(4):
                        c = d * 16 + 4 * q + il
                        tb = tab[il]
                        tk = ('s5_tab', il)
                        sc = lambda arr, j=None: (arr[:, c:c + 1] if j is None else arr[:, j, c:c + 1])
                        P.op('dve', lambda e, tb=tb: e.memset(tb['DTr'][:, 0:1], 1.0), w=[tk])
                        P.op('dve', lambda e, tb=tb: e.memset(tb['DTi'][:, 0:1], 0.0), w=[tk])
                        for j in range(8):
                            n = 1 << j
                            pr, pi = sc(phr, j), sc(phi, j)
                            self.ts('dve', ttmp[:, 0:n], tb['DTi'][:, 0:n], pi, None, ALU.mult, None,
                                    r=[tk, 'prm'], w=['ttmp'])
                            self.stt(tb['DTr'][:, n:2 * n], tb['DTr'][:, 0:n], pr, ttmp[:, 0:n], ALU.mult, ALU.subtract,
                                     r=[tk, 'prm', 'ttmp'], w=[tk])
                            self.ts('dve', ttmp[:, 0:n], tb['DTi'][:, 0:n], pr, None, ALU.mult, None,
                                    r=[tk, 'prm'], w=['ttmp'])
                            self.stt(tb['DTi'][:, n:2 * n], tb['DTr'][:, 0:n], pi, ttmp[:, 0:n], ALU.mult, ALU.add,
                                     r=[tk, 'prm', 'ttmp'], w=[tk])
                        cre, cim = sc(A['cre'][:]), sc(A['cim'][:])
                        self.ts('dve', ttmp[:], tb['DTi'][:], cim, None, ALU.mult, None, r=[tk, 'prm'], w=['ttmp'])
                        self.stt(tb['MTr'][:], tb['DTr'][:], cre, ttmp[:], ALU.mult, ALU.add, r=[tk, 'prm', 'ttmp'], w=[tk])
                        self.ts('dve', ttmp[:], tb['DTi'][:], cre, None, ALU.mult, None, r=[tk, 'prm'], w=['ttmp'])
                        self.stt(tb['MTi'][:], tb['DTr'][:], cim, ttmp[:], ALU.mult, ALU.subtract, r=[tk, 'prm', 'ttmp'], w=[tk])
                        self.ts('pool', tb['nDTi'][:], tb['DTi'][:], -1.0, None, ALU.mult, None, r=[tk], w=[tk])
                        self.ts('pool', tb['nDTr'][:], tb['DTr'][:], -1.0, None, ALU.mult, None, r=[tk], w=[tk])
                        self.ts('pool', tb['nMTi'][:], tb['MTi'][:], -1.0, None, ALU.mult, None, r=[tk], w=[tk])
                        self.ts('dve', tb['Rc'][:], zero[:], sc(A['r1'][:]), None, ALU.add, None, r=['zero', 'prm'], w=[tk])
                        P.op('dve', lambda e, il=il: e.memset(ini[il][0][:], 0.0), w=[('s5_ini', il, 0)])
                    order = list(range(NCH)) if d == 0 else [0] + list(range(NCH - 1, 0, -1))

                    def emit_y(kk, cs):
                        tok0 = cs * Lc
                        psY = self.ps[4]
                        ykey = ('ps', 4)
                        hsl = kk % 2
                        for il in range(4):
                            for j in range(4):
                                self.mm(psY[:, 0:Lc], CMv[:, d, il, j // 2, :], hb[il][hsl][:, j, :],
                                        il == 0 and j == 0, il == 3 and j == 3,
                                        r=[('s5_CM', us, d, j // 2), (('s5_h', il, hsl), j)], w=[ykey])
                        if d == 0:
                            self.stt(ybuf[:, tok0:tok0 + Lc], uT[us][:, tok0:tok0 + Lc], dsk[:, q:q + 1], psY[:, 0:Lc],
                                     ALU.mult, ALU.add, r=[ukey, 'dsk', ykey], w=[('s5_y', cs)])
                        else:
                            self.tt('dve', ybuf[:, tok0:tok0 + Lc], psY[:, 0:Lc], ybuf[:, tok0:tok0 + Lc], ALU.add,
                                    r=[ykey, ('s5_y', cs)], w=[('s5_y', cs)])

                    pend = None
                    for kk, cs in enumerate(order):
                        tok0 = cs * Lc
                        par = kk % 2
                        hsl = kk % 2
                        ctxs = []
                        def emit_bu(il_):
                            bs_ = self.rot('s5_psB', 2)
                            psB_ = self.ps[bs_]
                            for cc in range(2):
                                self.mm(psB_[:, cc * Lc:(cc + 1) * Lc], BTv[:, d, il_, cc, :], uT[us][:, tok0:tok0 + Lc],
                                        True, True, r=[('s5_BT', us, d, cc), ukey], w=[('ps', bs_)])
                            return bs_
                        nxt_bs = emit_bu(0)
                        for il in range(4):
                            bs = nxt_bs
                            if il < 3:
                                nxt_bs = emit_bu(il + 1)
                            psB = self.ps[bs]
                            bkey = ('ps', bs)
                            if d == 0:
                                bre, bim = psB[:, 0:Lc], psB[:, Lc:2 * Lc]
                            else:
                                bre, bim = psB[:, Lc - 1::-1][:, 0:Lc], psB[:, 2 * Lc - 1:Lc - 1:-1]
                            zs = self.rot('s5_z', NZ)
                            z, zk, tb, tk = zb[zs], ('s5_z', zs), tab[il], ('s5_tab', il)
                            self.tt('dve', zp[zs]['pa'][:], bre, tb['MTr'][:], ALU.mult, r=[bkey, tk], w=[(zk, 'pa')])
                            self.tt('dve', zp[zs]['pb'][:], bim, tb['nMTi'][:], ALU.mult, r=[bkey, tk], w=[(zk, 'pb')])
                            self.tt('dve', zp[zs]['pc'][:], bre, tb['MTi'][:], ALU.mult, r=[bkey, tk], w=[(zk, 'pc')])
                            self.tt('dve', zp[zs]['pd'][:], bim, tb['MTr'][:], ALU.mult, r=[bkey, tk], w=[(zk, 'pd')])
                            zbank = (2, 3, 5, 6)[il]
                            psZ = self.ps[zbank]
                            zkey = ('ps', zbank)
                            self.mm(psZ[:, 0:Lc], self.ident[:], zp[zs]['pa'][:], True, False, r=['ident', (zk, 'pa')], w=[zkey])
                            self.mm(psZ[:, 0:Lc], self.ident[:], zp[zs]['pb'][:], False, True, r=['ident', (zk, 'pb')], w=[zkey])
                            self.mm(psZ[:, Lc:2 * Lc], self.ident[:], zp[zs]['pc'][:], True, False, r=['ident', (zk, 'pc')], w=[zkey])
                            self.mm(psZ[:, Lc:2 * Lc], self.ident[:], zp[zs]['pd'][:], False, True, r=['ident', (zk, 'pd')], w=[zkey])
                            ctxs.append(dict(il=il, z=z, zk=zk, gk=('s5_g', zs), tb=tb, tk=tk, psZ=psZ, zkey=zkey,
                                             c=d * 16 + 4 * q + il))
                        for cx in ctxs:
                            z, tb, tk, gk, il, psZ, zkey = cx['z'], cx['tb'], cx['tk'], cx['gk'], cx['il'], cx['psZ'], cx['zkey']
                            ik = ('s5_ini', il, par)
                            iv = ini[il][par]
                            P.op('dve', lambda e, z=z, tb=tb, iv=iv, psZ=psZ: e.tensor_tensor_scan(
                                z['gr'][:], tb['Rc'][:], psZ[:, 0:Lc], iv[:, 0:1], ALU.mult, ALU.add),
                                r=[zkey, tk, ik], w=[(gk, 'r')])
                            P.op('dve', lambda e, z=z, tb=tb, iv=iv, psZ=psZ: e.tensor_tensor_scan(
                                z['gi'][:], tb['Rc'][:], psZ[:, Lc:2 * Lc], iv[:, 1:2], ALU.mult, ALU.add),
                                r=[zkey, tk, ik], w=[(gk, 'i')])
                        for cx in ctxs:
                            z, gk, il, c = cx['z'], cx['gk'], cx['il'], cx['c']
                            ink = ('s5_ini', il, 1 - par)
                            inx = ini[il][1 - par]
                            pr, pi, npi = phr[:, 8, c:c + 1], phi[:, 8, c:c + 1], nphi[:, c:c + 1]
                            gre, gie = z['gr'][:, Lc - 1:Lc], z['gi'][:, Lc - 1:Lc]
                            itk = ('itmp', il)
                            self.act(itmp[il][:, 0:1], gie, AF.Identity, r=[(gk, 'i'), 'prm'], w=[itk], scale=npi)
                            self.act(itmp[il][:, 1:2], gie, AF.Identity, r=[(gk, 'i'), 'prm'], w=[itk], scale=pr)
                            self.act(inx[:, 0:1], gre, AF.Identity, r=[(gk, 'r'), 'prm', itk], w=[ink], scale=pr,
                                     bias=itmp[il][:, 0:1])
                            self.act(inx[:, 1:2], gre, AF.Identity, r=[(gk, 'r'), 'prm', itk], w=[ink], scale=pi,
                                     bias=itmp[il][:, 1:2])
                        for cx in ctxs:
                            z, tb, tk, gk, il = cx['z'], cx['tb'], cx['tk'], cx['gk'], cx['il']
                            hk = ('s5_h', il, hsl)
                            hv = [hb[il][hsl][:, j, :] if d == 0 else hb[il][hsl][:, j, ::-1] for j in range(4)]
                            self.tt('pool', hv[0], z['gr'][:], tb['DTr'][:], ALU.mult, r=[(gk, 'r'), tk], w=[(hk, 0)])
                            self.tt('pool', hv[1], z['gi'][:], tb['nDTi'][:], ALU.mult, r=[(gk, 'i'), tk], w=[(hk, 1)])
                            self.tt('pool', hv[2], z['gr'][:], tb['nDTi'][:], ALU.mult, r=[(gk, 'r'), tk], w=[(hk, 2)])
                            self.tt('dve' if il < 2 else 'pool', hv[3], z['gi'][:], tb['nDTr'][:], ALU.mult, r=[(gk, 'i'), tk], w=[(hk, 3)])
                        if pend is not None:
                            emit_y(*pend)
                        pend = (kk, cs)
                    emit_y(*pend)
                gs_ = self.rot('s5_gq', 2)
                for cs in range(NCH):
                    self.act(gq[gs_][:, cs * Lc:(cs + 1) * Lc], ybuf[:, cs * Lc:(cs + 1) * Lc], AF.Gelu_apprx_tanh,
                             r=[('s5_y', cs)], w=[('s5_gq', gs_)])
                P.dma('act', S['gT'].ap()[q * 128:(q + 1) * 128, :], gq[gs_][:], r=[('s5_gq', gs_)])

    def phase_merge(self, l, src):
        nc, P, I, S = self.nc, self.P, self.I, self.S
        last = (l == DEPTH - 1)
        tiles = self.TILES[1:] if last else self.TILES
        with ExitStack() as st:
            sb = lambda n, s, d: st.enter_context(nc.sbuf_tensor(f"{n}_u{self.uid()}", s, d))
            wco = sb("mg_wco", [128, 4, D], BF16)
            wga = sb("mg_wga", [128, 4, D], BF16)
            wgb = sb("mg_wgb", [128, 4, D], BF16)
            wno = sb("mg_wno", [128, 4, D], BF16)
            wg = sb("mg_wg", [128, 8, 3 * D], BF16)
            hx = [sb(f"mg_hx{i}", [128, 8, 512], BF16) for i in range(2)]
            br = [[sb(f"mg_br{j}_{i}", [128, 4, 512], BF16) for i in range(2)] for j in range(3)]
            mg = [sb(f"mg_mg{i}", [128, 8, 512], BF16) for i in range(2)]
            sg = [sb(f"mg_sg{i}", [128, 512], F32) for i in range(4)]
            tm = [sb(f"mg_tm{i}", [128, 512], F32) for i in range(4)]
            acc = [sb(f"mg_acc{i}", [128, 512], F32) for i in range(2)]
            for wt_, nm in ((wco, 'conv_out'), (wga, 's5_glu_a'), (wgb, 's5_glu_b'), (wno, 'na_out')):
                for kc in range(4):
                    self.load_w(wt_[:, kc, :], I[nm].ap()[l][kc * 128:(kc + 1) * 128, :], ('mg_w', nm))
            win = I['w_in'].ap()[l].rearrange("(k p) c -> p k c", p=128)
            for k in range(8):
                self.load_w(wg[:, k, :], win[:, k, OFF_GA:OFF_GA + 3 * D], 'mg_wg')
            hsrc = S['hxT'].ap().rearrange("(k p) t -> p k t", p=128)
            bsrc = [S[nm].ap().rearrange("(k p) t -> p k t", p=128) for nm in ('convbT', 'gT', 'attnT')]
            mdst = S['mgT'].ap().rearrange("(k p) t -> p k t", p=128)
            for (t0, n) in tiles:
                hs = self.rot('mg_hx', 2)
                P.dma('sp', hx[hs][:, :, 0:n], hsrc[:, :, t0:t0 + n], w=[('mg_hx', hs)])
                for j in range(3):
                    P.dma('sp', br[j][hs][:, :, 0:n], bsrc[j][:, :, t0:t0 + n], w=[('mg_br', j, hs)])
                ms = self.rot('mg_mg', 2)
                for fo in range(8):
                    fs = slice(fo * 128, (fo + 1) * 128)

                    def proj(bank, w_, x_, nk, wkeys, xkey, coff=0):
                        for k in range(nk):
                            self.mm(self.ps[bank][:, 0:n], w_[:, k, coff + fo * 128: coff + (fo + 1) * 128],
                                    x_[:, k, 0:n], k == 0, k == nk - 1, r=wkeys + [xkey], w=[('ps', bank)])
                    hk = ('mg_hx', hs)
                    proj(0, wco, br[0][hs], 4, [('mg_w', 'conv_out')], ('mg_br', 0, hs))
                    proj(1, wg, hx[hs], 8, ['mg_wg'], hk, 0)
                    proj(2, wga, br[1][hs], 4, [('mg_w', 's5_glu_a')], ('mg_br', 1, hs))
                    proj(3, wgb, br[1][hs], 4, [('mg_w', 's5_glu_b')], ('mg_br', 1, hs))
                    proj(4, wg, hx[hs], 8, ['mg_wg'], hk, D)
                    proj(5, wno, br[2][hs], 4, [('mg_w', 'na_out')], ('mg_br', 2, hs))
                    proj(6, wg, hx[hs], 8, ['mg_wg'], hk, 2 * D)
                    a_ = self.rot('mg_acc', 2)
                    A = acc[a_][:, 0:n]
                    ak = ('mg_acc', a_)
                    sgs = [self.rot('mg_sg', 4) for _ in range(4)]
                    tms = [self.rot('mg_tm', 4) for _ in range(2)]
                    self.act(sg[sgs[0]][:, 0:n], self.ps[1][:, 0:n], AF.Sigmoid, r=[('ps', 1)], w=[('mg_sg', sgs[0])])
                    self.tt('dve', A, self.ps[0][:, 0:n], sg[sgs[0]][:, 0:n], ALU.mult,
                            r=[('ps', 0), ('mg_sg', sgs[0])], w=[ak])
                    self.act(sg[sgs[1]][:, 0:n], self.ps[3][:, 0:n], AF.Sigmoid, r=[('ps', 3)], w=[('mg_sg', sgs[1])])
                    self.tt('dve', tm[tms[0]][:, 0:n], self.ps[2][:, 0:n], sg[sgs[1]][:, 0:n], ALU.mult,
                            r=[('ps', 2), ('mg_sg', sgs[1])], w=[('mg_tm', tms[0])])
                    self.act(sg[sgs[2]][:, 0:n], self.ps[4][:, 0:n], AF.Sigmoid, r=[('ps', 4)], w=[('mg_sg', sgs[2])])
                    self.tt('pool', tm[tms[0]][:, 0:n], tm[tms[0]][:, 0:n], sg[sgs[2]][:, 0:n], ALU.mult,
                            r=[('mg_tm', tms[0]), ('mg_sg', sgs[2])], w=[('mg_tm', tms[0])])
                    self.tt('pool', A, A, tm[tms[0]][:, 0:n], ALU.add, r=[ak, ('mg_tm', tms[0])], w=[ak])
                    self.act(sg[sgs[3]][:, 0:n], self.ps[6][:, 0:n], AF.Sigmoid, r=[('ps', 6)], w=[('mg_sg', sgs[3])])
                    self.tt('dve', tm[tms[1]][:, 0:n], self.ps[5][:, 0:n], sg[sgs[3]][:, 0:n], ALU.mult,
                            r=[('ps', 5), ('mg_sg', sgs[3])], w=[('mg_tm', tms[1])])
                    self.tt('pool', mg[ms][:, fo, 0:n], A, tm[tms[1]][:, 0:n], ALU.add,
                            r=[ak, ('mg_tm', tms[1])], w=[('mg_mg', ms)])
                P.dma('act', mdst[:, :, t0:t0 + n], mg[ms][:, :, 0:n], r=[('mg_mg', ms)])
        P.barrier()
        with ExitStack() as st:
            sb = lambda n, s, d: st.enter_context(nc.sbuf_tensor(f"{n}_u{self.uid()}", s, d))
            wo = sb("mo_wo", [128, 8, D], BF16)
            wov = I['w_out'].ap()[l].rearrange("(k p) c -> p k c", p=128)
            for k in range(8):
                self.load_w(wo[:, k, :], wov[:, k, :], 'mo_wo')
            mgt = [sb(f"mo_mg{i}", [128, 8, 512], BF16) for i in range(2)]
            xt = [sb(f"mo_x{i}", [128, D], F32) for i in range(2)]
            xn = [sb(f"mo_xn{i}", [128, D], F32) for i in range(2)]
            hb = [sb(f"mo_hb{i}", [128, D], BF16) for i in range(2)]
            hT = [sb(f"mo_hT{i}", [128, 8, 512], BF16) for i in range(2)]
            scr = self.norm_scratch(st, "mo")
            msrc = S['mgT'].ap().rearrange("(k p) t -> p k t", p=128)
            hdst = S['h2T'].ap().rearrange("(k p) t -> p k t", p=128)
            mods = None
            cur = None
            for (t0, n) in tiles:
                cond = 1 if t0 < CTX else 0
                if cur != cond:
                    if mods is None:
                        mods = self.mod_tiles(st, l, cond, ('A1', 'G2', 'S2'))
                        gt2 = st.enter_context(nc.sbuf_tensor(f"mo_g2_u{self.uid()}", [128, D], F32))
                    else:
                        self.reload_mod(mods, l, cond, gt2)
                    cur = cond
                hs = self.rot('mo_mg', 2)
                P.dma('sp', mgt[hs][:, :, 0:n], msrc[:, :, t0:t0 + n], w=[('mo_mg', hs)])
                ts_ = self.rot('mo_hT', 2)
                for sub in range(n // 128):
                    xs = self.rot('mo_x', 2)
                    r0 = t0 + sub * 128
                    P.dma('sp', xt[xs][:], src.ap()[r0:r0 + 128, :], w=[('mo_x', xs)])
                    for half in range(2):
                        for k in range(8):
                            self.mm(self.ps[half][:, :], mgt[hs][:, k, sub * 128:(sub + 1) * 128],
                                    wo[:, k, half * 512:(half + 1) * 512], k == 0, k == 7,
                                    r=[('mo_mg', hs), 'mo_wo'], w=[('ps', half)])
                        self.tt('dve', xn[xs][:, half * 512:(half + 1) * 512], self.ps[half][:, :],
                                mods['A1'][0][:, half * 512:(half + 1) * 512], ALU.mult,
                                r=[('ps', half), mods['A1'][1]], w=[('mo_xn', xs)])
                    self.tt('pool', xn[xs][:], xn[xs][:], xt[xs][:], ALU.add, r=[('mo_xn', xs), ('mo_x', xs)],
                            w=[('mo_xn', xs)])
                    P.dma('act', S['xres'].ap()[r0:r0 + 128, :], xn[xs][:], r=[('mo_xn', xs)])
                    bs = self.rot('mo_hb', 2)
                    self.norm_sub(xn[xs][:], ('mo_xn', xs), mods['G2'], mods['S2'], hb[bs][:], ('mo_hb', bs), scr[bs])
                    self.transpose_out(hb[bs], ('mo_hb', bs), hT[ts_], ('mo_hT', ts_), sub)
                P.dma('act', hdst[:, :, t0:t0 + n], hT[ts_][:, :, 0:n], r=[('mo_hT', ts_)])

    def reload_mod(self, mods, l, cond, gtmp):
        base = (l * 2 + cond) * 6 * D
        idx = {'S1': 0, 'G1': 1, 'A1': 2, 'S2': 3, 'G2': 4, 'A2': 5}
        for n, (t, key) in mods.items():
            self.load_rep(t[:], self.S['modv'], base + idx[n] * D, key)
            if n in ('G1', 'G2'):
                gt = self.I['norm1_g'] if n == 'G1' else self.I['norm2_g']
                self.load_rep(gtmp[:], gt, l * D, 'modgtmp')
                self.stt(t[:], t[:], 1.0, gtmp[:], ALU.add, ALU.mult, r=[key, 'modgtmp'], w=[key])

    def phase_mlp(self, l):
        nc, P, I, S = self.nc, self.P, self.I, self.S
        last = (l == DEPTH - 1)
        tiles = self.TILES[1:] if last else self.TILES
        with ExitStack() as st:
            sb = lambda n, s, d: st.enter_context(nc.sbuf_tensor(f"{n}_u{self.uid()}", s, d))
            w1 = sb("ml_w1", [128, 8, 4 * D], BF16)
            w1v = I['mlp_w1'].ap()[l].rearrange("(k p) c -> p k c", p=128)
            for k in range(8):
                for hf in range(2):
                    self.load_w(w1[:, k, hf * 2048:(hf + 1) * 2048], w1v[:, k, hf * 2048:(hf + 1) * 2048], 'ml_w1')
            h2 = [sb(f"ml_h2{i}", [128, 8, 512], BF16) for i in range(2)]
            rl = [sb(f"ml_rl{i}", [128, 512], F32) for i in range(3)]
            hd = [sb(f"ml_hd{i}", [128, 8, 512], BF16) for i in range(2)]
            hsrc = S['h2T'].ap().rearrange("(k p) t -> p k t", p=128)
            ddst = S['hidT'].ap().rearrange("(k p) t -> p k t", p=128)
            for (t0, n) in tiles:
                hs = self.rot('ml_h2', 2)
                P.dma('sp', h2[hs][:, :, 0:n], hsrc[:, :, t0:t0 + n], w=[('ml_h2', hs)])
                for fg in range(4):
                    ds = self.rot('ml_hd', 2)
                    for fi in range(8):
                        fc = fg * 8 + fi
                        bank = self.rot('ml_bank', 6)
                        for k in range(8):
                            self.mm(self.ps[bank][:, 0:n], w1[:, k, fc * 128:(fc + 1) * 128], h2[hs][:, k, 0:n],
                                    k == 0, k == 7, r=['ml_w1', ('ml_h2', hs)], w=[('ps', bank)])
                        rs = self.rot('ml_rl', 3)
                        self.act(rl[rs][:, 0:n], self.ps[bank][:, 0:n], AF.Relu, r=[('ps', bank)], w=[('ml_rl', rs)])
                        self.tt('dve' if fi % 2 == 0 else 'pool', hd[ds][:, fi, 0:n], rl[rs][:, 0:n], rl[rs][:, 0:n],
                                ALU.mult, r=[('ml_rl', rs)], w=[('ml_hd', ds)])
                    P.dma('act', ddst[:, fg * 8:(fg + 1) * 8, t0:t0 + n], hd[ds][:, :, 0:n], r=[('ml_hd', ds)])
        P.barrier()
        with ExitStack() as st:
            sb = lambda n, s, d: st.enter_context(nc.sbuf_tensor(f"{n}_u{self.uid()}", s, d))
            w2 = sb("ml_w2", [128, 32, D], BF16)
            w2v = I['mlp_w2'].ap()[l].rearrange("(k p) c -> p k c", p=128)
            for k in range(32):
                self.load_w(w2[:, k, :], w2v[:, k, :], 'ml_w2')
            hdt = [sb(f"ml_hdt{i}", [128, 32, 256], BF16) for i in range(2)]
            xt = [sb(f"ml_x{i}", [128, D], F32) for i in range(2)]
            xn = [sb(f"ml_xn{i}", [128, D], F32) for i in range(2)]
            hb = [sb(f"ml_hb{i}", [128, D], BF16) for i in range(2)]
            hT = [sb(f"ml_hT{i}", [128, 8, 256], BF16) for i in range(2)]
            ob = [sb(f"ml_ob{i}", [128, D], F32) for i in range(2)]
            scr = self.norm_scratch(st, "ml")
            dsrc = S['hidT'].ap().rearrange("(k p) t -> p k t", p=128)
            hdst = S['hxT'].ap().rearrange("(k p) t -> p k t", p=128)
            fin = None
            if last:
                fin = sb("ml_fin", [128, D], F32)
                self.load_rep(fin[:], I['final_g'], 0, 'ml_fin')
            amods = None
            nmods = None
            cur = None
            tiles256 = []
            for (t0, n) in tiles:
                for h in range(n // 256):
                    tiles256.append((t0 + h * 256, 256))
            for (t0, n) in tiles256:
                cond = 1 if t0 < CTX else 0
                if cur != cond:
                    if amods is None:
                        amods = self.mod_tiles(st, l, cond, ('A2',))
                        gt1 = st.enter_context(nc.sbuf_tensor(f"ml_g1_u{self.uid()}", [128, D], F32))
                        if not last:
                            nmods = self.mod_tiles(st, l + 1, cond, ('G1', 'S1'))
                    else:
                        self.reload_mod(amods, l, cond, gt1)
                        if not last:
                            self.reload_mod(nmods, l + 1, cond, gt1)
                    cur = cond
                hs = self.rot('ml_hdt', 2)
                for kq in range(4):
                    P.dma('sp', hdt[hs][:, kq * 8:(kq + 1) * 8, :], dsrc[:, kq * 8:(kq + 1) * 8, t0:t0 + n],
                          w=[('ml_hdt', hs)])
                ts_ = self.rot('ml_hT', 2)
                for sub in range(2):
                    xs = self.rot('ml_x', 2)
                    r0 = t0 + sub * 128
                    P.dma('sp', xt[xs][:], S['xres'].ap()[r0:r0 + 128, :], w=[('ml_x', xs)])
                    for half in range(2):
                        bank = self.rot('ml_bank2', 4)
                        for k in range(32):
                            self.mm(self.ps[bank][:, :], hdt[hs][:, k, sub * 128:(sub + 1) * 128],
                                    w2[:, k, half * 512:(half + 1) * 512], k == 0, k == 31,
                                    r=[('ml_hdt', hs), 'ml_w2'], w=[('ps', bank)])
                        self.tt('dve', xn[xs][:, half * 512:(half + 1) * 512], self.ps[bank][:, :],
                                amods['A2'][0][:, half * 512:(half + 1) * 512], ALU.mult,
                                r=[('ps', bank), amods['A2'][1]], w=[('ml_xn', xs)])
                    self.tt('pool', xn[xs][:], xn[xs][:], xt[xs][:], ALU.add, r=[('ml_xn', xs), ('ml_x', xs)],
                            w=[('ml_xn', xs)])
                    bs = self.rot('ml_hb', 2)
                    if not last:
                        P.dma('act', S['xres'].ap()[r0:r0 + 128, :], xn[xs][:], r=[('ml_xn', xs)])
                        self.norm_sub(xn[xs][:], ('ml_xn', xs), nmods['G1'], nmods['S1'], hb[bs][:], ('ml_hb', bs), scr[bs])
                        self.transpose_out(hb[bs], ('ml_hb', bs), hT[ts_], ('ml_hT', ts_), sub)
                    else:
                        self.norm_sub(xn[xs][:], ('ml_xn', xs), (fin, 'ml_fin'), None, ob[bs][:], ('ml_ob', bs), scr[bs])
                        P.dma('act', self.out.ap()[r0 - CTX:r0 - CTX + 128, :], ob[bs][:], r=[('ml_ob', bs)])
                if not last:
                    P.dma('act', hdst[:, :, t0:t0 + n], hT[ts_][:, :, 0:n], r=[('ml_hT', ts_)])

def _core_inputs(inp, sh, b):
    d = dict(sh)
    d['xin'] = np.ascontiguousarray(np.concatenate([inp['ctx'][b], inp['x'][b]], axis=0), dtype=np.float32)
    cT = np.stack([inp['c'][b].reshape(8, 128).T, inp['c_ctx'].reshape(8, 128).T], axis=2)
    d['cT'] = np.ascontiguousarray(cT.reshape(128, 16), dtype=np.float32)
    return d


def kernel(**inputs):
    inp = {k: np.asarray(v) for k, v in inputs.items()}
    sh = _prep_shared(inp)
    bld = Builder()
    nc = bld.build()
    in_maps = [_core_inputs(inp, sh, b) for b in range(8)]
    in_maps = [{k: m[k] for k in bld.inputs} for m in in_maps]
    res = run_bass_kernel_spmd(nc, in_maps, core_ids=list(range(8)))
    return np.stack([np.asarray(r['out']) for r in res.results], axis=0).astype(np.float32)
```

```python
import math
from contextlib import ExitStack
import numpy as np
import concourse.bass as bass
import concourse.mybir as mybir
from concourse.bass_utils import run_bass_kernel_spmd

F32 = mybir.dt.float32
BF16 = mybir.dt.bfloat16
I32 = mybir.dt.int32
ALU = mybir.AluOpType
AF = mybir.ActivationFunctionType

SEM_LIMIT = 30000
SAME_ENG_WAITS = True
N_DMA_SEMS = 40

DEPTH = 4
D = 1024
T = 4352
CTX = 256
SEQ = 4096
NIN = 6656
OFF_XA, OFF_XB, OFF_XC, OFF_U, OFF_Q, OFF_K, OFF_V, OFF_GA, OFF_GB, OFF_GC = (
    0, 512, 1024, 1536, 2048, 2560, 3072, 3584, 4608, 5632)
EPS = 1e-6
NTYPE = 21
LCH = 256
NCH = T // LCH


class Prog:
    def __init__(self, nc):
        self.nc = nc
        self.ops = []
        self.last_w = {}
        self.readers = {}
        self.engs = {'pe': nc.tensor, 'dve': nc.vector, 'act': nc.scalar,
                     'pool': nc.gpsimd, 'sp': nc.sync}
        self._bar_from = 0

    def op(self, eng, fn, r=(), w=(), dma=False):
        i = len(self.ops)
        deps = set()
        for k in r:
            if k in self.last_w:
                deps.add(self.last_w[k])
        for k in w:
            if k in self.last_w:
                deps.add(self.last_w[k])
            for j in self.readers.get(k, ()):
                deps.add(j)
        fd = set()
        for j in deps:
            oj = self.ops[j]
            if oj['eng'] == eng and not oj['dma'] and not dma:
                if eng == 'pe' or not SAME_ENG_WAITS:
                    continue
                israw = any(self.last_w.get(k) == j for k in list(r) + list(w))
                if not israw:
                    continue
            fd.add(j)
        for k in w:
            self.last_w[k] = i
            self.readers[k] = []
        for k in r:
            self.readers.setdefault(k, []).append(i)
        self.ops.append(dict(eng=eng, fn=fn, deps=fd, dma=dma, sig=False))
        return i

    def dma(self, q, out, in_, r=(), w=(), **kw):
        return self.op(q, lambda e: e.dma_start(out=out, in_=in_, **kw), r, w, dma=True)

    def barrier(self):
        deps = set()
        lastc = {}
        for idx, o in enumerate(self.ops):
            if o['fn'] is None:
                continue
            if o['dma']:
                if idx >= self._bar_from:
                    deps.add(idx)
            else:
                lastc[o['eng']] = idx
        deps |= set(lastc.values())
        self._bar_from = len(self.ops)
        for e in self.engs:
            self.ops.append(dict(eng=e, fn=None, deps=set(deps), dma=False, sig=False))
        self.last_w = {}
        self.readers = {}

    def emit(self):
        nc = self.nc
        ops = self.ops
        for o in ops:
            for j in o['deps']:
                ops[j]['sig'] = True
        dma_sems = [nc.alloc_semaphore(name=f"dq{i}") for i in range(N_DMA_SEMS)]
        dma_cnt = [0] * N_DMA_SEMS
        dma_last = [None] * N_DMA_SEMS
        eng_sem = {}
        eng_cnt = {}
        nsem = [0]

        def new_eng_sem(e):
            nsem[0] += 1
            eng_sem[e] = nc.alloc_semaphore(name=f"s_{e}_{nsem[0]}")
            eng_cnt[e] = 0

        for e in self.engs:
            new_eng_sem(e)
        rr = 0
        for idx, o in enumerate(ops):
            if o['dma']:
                s = rr % N_DMA_SEMS
                rr += 1
                if dma_cnt[s] + 16 > SEM_LIMIT:
                    if dma_last[s] is not None:
                        o['deps'].add(dma_last[s])
                    dma_sems[s] = nc.alloc_semaphore(name=f"dq{s}_{idx}")
                    dma_cnt[s] = 0
                    dma_last[s] = None
                if dma_last[s] is not None:
                    o['deps'].add(dma_last[s])
                dma_cnt[s] += 16
                o['done'] = (dma_sems[s], dma_cnt[s])
                dma_last[s] = idx
            elif o['sig'] and o['fn'] is not None:
                e = o['eng']
                if eng_cnt[e] + 1 > SEM_LIMIT:
                    new_eng_sem(e)
                eng_cnt[e] += 1
                o['done'] = (eng_sem[e], eng_cnt[e])
            else:
                o['done'] = None

        def resolve(j, acc, seen):
            if j in seen:
                return
            seen.add(j)
            oj = ops[j]
            if oj['done'] is not None:
                acc.add(j)
            elif oj['fn'] is None:
                for jj in oj['deps']:
                    resolve(jj, acc, seen)
            else:
                raise RuntimeError("dep on unsignaled op")

        per_eng = {e: [] for e in self.engs}
        for idx, o in enumerate(ops):
            per_eng[o['eng']].append(idx)
        self.n_inst = {e: len(v) for e, v in per_eng.items()}
        with nc.Block() as block:
            def make(e):
                def body(eng):
                    waited = {}
                    for idx in per_eng[e]:
                        o = ops[idx]
                        acc = set()
                        seen = set()
                        for j in o['deps']:
                            resolve(j, acc, seen)
                        need = {}
                        for j in acc:
                            sem, val = ops[j]['done']
                            key = id(sem)
                            if waited.get(key, 0) >= val:
                                continue
                            if key not in need or need[key][1] < val:
                                need[key] = (sem, val)
                        for key, (sem, val) in need.items():
                            eng.wait_ge(sem, val)
                            waited[key] = val
                        if o['fn'] is not None:
                            inst = o['fn'](eng)
                            if o['done'] is not None:
                                sem, val = o['done']
                                inst.then_inc(sem, 16 if o['dma'] else 1)
                return body
            block.tensor(make('pe'))
            block.vector(make('dve'))
            block.scalar(make('act'))
            block.gpsimd(make('pool'))
            block.sync(make('sp'))


def _na_tile_plan():
    types = {}
    plan = []
    for j in range(32):
        r0a = min(max(2 * j - 4, 0), 56)
        r0b = min(max(2 * j + 1 - 4, 0), 56)
        tlo = r0a // 2
        thi = (r0b + 7) // 2
        lst = []
        for t in range(tlo, thi + 1):
            key = (t - j, r0a - 2 * j, r0b - 2 * j)
            if key not in types:
                types[key] = len(types)
            lst.append((t, types[key]))
        plan.append(lst)
    return plan, types


def _na_bias_index():
    plan, types = _na_tile_plan()
    nt = len(types)
    idx_r = np.zeros((nt, 128, 128), np.int64)
    idx_c = np.zeros((nt, 128, 128), np.int64)
    mask = np.zeros((nt, 128, 128), np.float32)
    col = np.arange(64)
    cs = np.clip(col - 8, 0, 48)
    for (delta, ra, rb), ty in types.items():
        for qr2 in range(2):
            r0rel = (ra, rb)[qr2]
            for kr2 in range(2):
                krel = 2 * delta + kr2
                dr = krel - qr2
                row_ok = (krel >= r0rel) and (krel < r0rel + 8)
                for qc in range(64):
                    kc = col
                    ok = row_ok & (kc >= cs[qc]) & (kc < cs[qc] + 16)
                    q = qr2 * 64 + qc
                    k = kr2 * 64 + kc
                    idx_r[ty, k, q] = np.clip(dr + 7, 0, 14)
                    idx_c[ty, k, q] = np.clip(kc - qc + 15, 0, 30)
                    mask[ty, k, q] = np.where(ok, 0.0, -1e30)
    return plan, nt, idx_r, idx_c, mask


_NA = None


def _na():
    global _NA
    if _NA is None:
        _NA = _na_bias_index()
    return _NA


def _prep_shared(inp):
    L = DEPTH
    sh = {}
    f = lambda a: np.ascontiguousarray(a, dtype=np.float32)
    for k in ('w_mod', 'w_in', 'conv_out', 's5_glu_a', 's5_glu_b', 'na_out', 'w_out',
              'mlp_w1', 'mlp_w2'):
        sh[k] = f(inp[k])
    sh['b_mod'] = f(inp['b_mod'])
    sh['norm1_g'] = f(inp['norm1_g'])
    sh['norm2_g'] = f(inp['norm2_g'])
    sh['final_g'] = f(inp['final_norm_g']).reshape(1, D)
    sh['conv_w'] = f(inp['conv_w'].reshape(L, 3, 4, 128).transpose(0, 3, 2, 1))
    def gp(a):
        a = a.reshape(L, 2, 16, 2, 64)
        return a.transpose(0, 3, 4, 1, 2).reshape(L, 128, 32)
    ls = np.broadcast_to(inp['s5_log_step'][:, :, :, None], (L, 2, 32, 64))
    sh['s5p'] = f(np.stack([gp(inp['s5_lam_re']), gp(inp['s5_lam_im']), gp(ls)], axis=2))
    def bt(B):
        out = np.zeros((L, 2, 16, 128, 128), np.float32)
        for i in range(16):
            for g2 in range(2):
                g = 2 * i + g2
                gl = g % 8
                out[:, :, i, gl * 16:(gl + 1) * 16, g2 * 64:(g2 + 1) * 64] = \
                    B[:, :, g].transpose(0, 1, 3, 2)
        return out
    sh['s5BT'] = f(np.stack([bt(inp['s5_b_re']), bt(inp['s5_b_im'])], axis=3))
    def cm(C):
        out = np.zeros((L, 2, 16, 128, 128), np.float32)
        for i in range(16):
            for g2 in range(2):
                g = 2 * i + g2
                gl = g % 8
                out[:, :, i, g2 * 64:(g2 + 1) * 64, gl * 16:(gl + 1) * 16] = \
                    C[:, :, g].transpose(0, 1, 3, 2)
        return out
    sh['s5CM'] = f(np.stack([cm(inp['s5_c_re']), cm(inp['s5_c_im'])], axis=3))
    sh['s5d'] = f(inp['s5_d'].reshape(L, 4, 128).transpose(0, 2, 1))
    plan, nt, idx_r, idx_c, mask = _na()
    rpb = inp['na_rpb']
    sh['rpbg'] = f(rpb[:, :, idx_r, idx_c])
    sh['na_mask'] = f(mask)
    sh['ident'] = np.eye(128, dtype=np.float32)
    return sh


class Builder:
    def __init__(self, layers=DEPTH, dbg=(), stop=None, only=None):
        self.layers = layers
        self.stop = stop
        self.only = only
        self.dbg = set(dbg)
        nc = bass.Bass("TRN2", target_bir_lowering=False)
        self.nc = nc
        self.P = Prog(nc)
        self.inputs = {}
        self.cnt = {}

    def din(self, name, shape, dt=F32):
        t = self.nc.dram_tensor(name, list(shape), dt, kind="ExternalInput")
        self.inputs[name] = t
        return t

    def dscr(self, name, shape, dt):
        kind = "ExternalOutput" if name in self.dbg else "Internal"
        return self.nc.dram_tensor(name, list(shape), dt, kind=kind)

    def uid(self):
        self._uid = getattr(self, '_uid', 0) + 1
        return self._uid

    def rot(self, name, n):
        c = self.cnt.get(name, 0)
        self.cnt[name] = c + 1
        return c % n

    def mm(self, out, lhsT, rhs, start, stop, r, w):
        self.P.op('pe', lambda e: e.matmul(out, lhsT, rhs, start=start, stop=stop), r, w)

    def act(self, out, in_, func, r, w, **kw):
        self.P.op('act', lambda e: e.activation(out, in_, func, **kw), r, w)

    def tt(self, eng, out, a, b, op, r, w):
        self.P.op(eng, lambda e: e.tensor_tensor(out, a, b, op), r, w)

    def ts(self, eng, out, a, s1, s2, op0, op1, r, w):
        if op1 is None:
            self.P.op(eng, lambda e: e.tensor_scalar(out, a, s1, None, op0), r, w)
        else:
            self.P.op(eng, lambda e: e.tensor_scalar(out, a, s1, s2, op0, op1), r, w)

    def stt(self, out, a, s, b, op0, op1, r, w):
        self.P.op('dve', lambda e: e.scalar_tensor_tensor(out, a, s, b, op0, op1), r, w)

    def cp(self, eng, out, in_, r, w):
        if eng == 'act':
            self.P.op('act', lambda e: e.activation(out, in_, AF.Copy), r, w)
        else:
            self.P.op(eng, lambda e: e.tensor_copy(out, in_), r, w)

    def build(self):
        nc, P = self.nc, self.P
        L = DEPTH
        nt = _na()[1]
        self.nt = nt
        I = {}
        I['xin'] = self.din('xin', [T, D])
        I['cT'] = self.din('cT', [128, 16])
        I['w_mod'] = self.din('w_mod', [L, D, 6 * D])
        I['b_mod'] = self.din('b_mod', [L, 6 * D])
        I['norm1_g'] = self.din('norm1_g', [L, D])
        I['norm2_g'] = self.din('norm2_g', [L, D])
        I['final_g'] = self.din('final_g', [1, D])
        I['w_in'] = self.din('w_in', [L, D, NIN])
        I['conv_w'] = self.din('conv_w', [L, 128, 4, 3])
        I['conv_out'] = self.din('conv_out', [L, 512, D])
        I['s5p'] = self.din('s5p', [L, 128, 3, 32])
        I['s5BT'] = self.din('s5BT', [L, 2, 16, 2, 128, 128])
        I['s5CM'] = self.din('s5CM', [L, 2, 16, 2, 128, 128])
        I['s5d'] = self.din('s5d', [L, 128, 4])
        I['s5_glu_a'] = self.din('s5_glu_a', [L, 512, D])
        I['s5_glu_b'] = self.din('s5_glu_b', [L, 512, D])
        I['rpbg'] = self.din('rpbg', [L, 8, nt, 128, 128])
        I['na_mask'] = self.din('na_mask', [nt, 128, 128])
        I['na_out'] = self.din('na_out', [L, 512, D])
        I['w_out'] = self.din('w_out', [L, D, D])
        I['mlp_w1'] = self.din('mlp_w1', [L, D, 4 * D])
        I['mlp_w2'] = self.din('mlp_w2', [L, 4 * D, D])
        I['ident'] = self.din('ident', [128, 128])
        self.I = I
        self.out = nc.dram_tensor('out', [SEQ, D], F32, kind="ExternalOutput")
        S = {}
        S['xres'] = self.dscr('xres', [T, D], F32)
        S['hxT'] = self.dscr('hxT', [D, T], BF16)
        S['h2T'] = self.dscr('h2T', [D, T], BF16)
        S['convbT'] = self.dscr('convbT', [512, T], BF16)
        S['gT'] = self.dscr('gT', [512, T], BF16)
        S['attnT'] = self.dscr('attnT', [512, T], BF16)
        S['modv'] = self.dscr('modv', [L, 2, 6 * D], F32)
        S['mgT'] = self.dscr('mgT', [D, T], BF16)
        S['hidT'] = self.dscr('hidT', [4 * D, T], BF16)
        self.S = S

        with ExitStack() as gs:
            self.ps = [gs.enter_context(nc.psum_tensor(f"ps{i}", [128, 512], F32))
                       for i in range(7)]
            self.psT = gs.enter_context(nc.psum_tensor("psT", [128, 1024], BF16))
            self.ident = gs.enter_context(nc.sbuf_tensor("ident_sb", [128, 128], BF16))
            self.ones = gs.enter_context(nc.sbuf_tensor("ones_sb", [128, 128], BF16))
            P.dma('pool', self.ident[:], I['ident'].ap(), w=['ident'])
            P.op('dve', lambda e: e.memset(self.ones[:], 1.0), w=['ones'])
            P.barrier()
            seq = [('adaln', lambda: self.phase_adaln()),
                   ('norm', lambda: self.phase_norm(0, src=I['xin'], kind='n1'))]
            for l in range(self.layers):
                seq += [(f'conv{l}', lambda l=l: self.phase_conv(l)),
                        (f'attn{l}', lambda l=l: self.phase_attn(l)),
                        (f's5{l}', lambda l=l: self.phase_s5(l)),
                        (f'merge{l}', lambda l=l: self.phase_merge(l, src=(I['xin'] if l == 0 else S['xres']))),
                        (f'mlp{l}', lambda l=l: self.phase_mlp(l))]
            for name, fn in seq:
                if self.only is not None and name not in self.only:
                    continue
                fn()
                P.barrier()
                if name == self.stop:
                    break
            P.emit()
        return nc

    def phase_adaln(self):
        nc, P, I, S = self.nc, self.P, self.I, self.S
        with ExitStack() as st:
            sb = lambda n, s, d: st.enter_context(nc.sbuf_tensor(f"{n}_u{self.uid()}", s, d))
            cT = sb("ad_cT", [128, 16], F32)
            sil = sb("ad_sil", [128, 16], BF16)
            wt = [sb(f"ad_w{i}", [128, 8, 512], BF16) for i in range(3)]
            bm = sb("ad_bm", [2, 6 * D], F32)
            row = [sb(f"ad_row{i}", [2, 512], F32) for i in range(2)]
            P.dma('sp', cT[:], I['cT'].ap(), w=['cT'])
            self.act(sil[:], cT[:], AF.Silu, r=['cT'], w=['sil'])
            silv = sil[:].rearrange("p (k j) -> p k j", j=2)
            for l in range(DEPTH):
                bsrc = bass.AP(I['b_mod'], l * 6 * D, [[0, 2], [1, 6 * D]])
                P.dma('sp', bm[:], bsrc, w=['bm'])
                wv = I['w_mod'].ap()[l].rearrange("(k p) c -> p k c", p=128)
                for ct in range(12):
                    s = self.rot('adw', 3)
                    P.dma('pool', wt[s][:], wv[:, :, ct * 512:(ct + 1) * 512], w=[('adw', s)])
                    pst = self.ps[ct % 2]
                    for k in range(8):
                        self.mm(pst[0:2, :], silv[:, k, :], wt[s][:, k, :], k == 0, k == 7,
                                r=['sil', ('adw', s)], w=[('ps', ct % 2)])
                    rs = self.rot('adrow', 2)
                    self.tt('dve', row[rs][:], pst[0:2, :], bm[:, ct * 512:(ct + 1) * 512], ALU.add,
                            r=[('ps', ct % 2), 'bm'], w=[('adrow', rs)])
                    P.dma('sp', S['modv'].ap()[l][:, ct * 512:(ct + 1) * 512], row[rs][:],
                          r=[('adrow', rs)])

    def load_rep(self, dst, tensor, offset, key):
        self.P.dma('sp', dst, bass.AP(tensor, offset, [[0, 128], [1, D]]), w=[key])

    def mod_tiles(self, st, l, cond, names, gsrc=None):
        nc = self.nc
        res = {}
        base = (l * 2 + cond) * 6 * D
        idx = {'S1': 0, 'G1': 1, 'A1': 2, 'S2': 3, 'G2': 4, 'A2': 5}
        for n in names:
            t = st.enter_context(nc.sbuf_tensor(f"mod_{n}_{cond}_{self.rot('modt', 1 << 30)}", [128, D], F32))
            key = ('mod', n, cond)
            self.load_rep(t[:], self.S['modv'], base + idx[n] * D, key)
            if n in ('G1', 'G2'):
                g = st.enter_context(nc.sbuf_tensor(f"modg_{n}_{cond}_{self.rot('modt', 1 << 30)}", [128, D], F32))
                gt = self.I['norm1_g'] if n == 'G1' else self.I['norm2_g']
                self.load_rep(g[:], gt, l * D, ('modg', n, cond))
                self.stt(t[:], t[:], 1.0, g[:], ALU.add, ALU.mult, r=[key, ('modg', n, cond)], w=[key])
            res[n] = (t, key)
        return res

    def norm_sub(self, xt, xkey, G, S_, hb, hkey, scr):
        junk, ss, rstd, tmp = scr['junk'], scr['ss'], scr['rstd'], scr['tmp']
        k = scr['k']
        self.act(junk[:], xt, AF.Square, r=[xkey], w=[('nj', k), ('ss', k)], accum_out=ss[:])
        self.ts('dve', rstd[:], ss[:], 1.0 / D, EPS, ALU.mult, ALU.add, r=[('ss', k)], w=[('rstd', k)])
        self.act(rstd[:], rstd[:], AF.Sqrt, r=[('rstd', k)], w=[('rstd', k)])
        self.P.op('dve', lambda e: e.reciprocal(rstd[:], rstd[:]), r=[('rstd', k)], w=[('rstd', k)])
        if S_ is None:
            self.stt(hb, xt, rstd[:], G[0][:], ALU.mult, ALU.mult, r=[xkey, ('rstd', k), G[1]], w=[hkey])
        else:
            self.stt(tmp[:], xt, rstd[:], G[0][:], ALU.mult, ALU.mult, r=[xkey, ('rstd', k), G[1]],
                     w=[('ntmp', k)])
            self.tt('pool', hb, tmp[:], S_[0][:], ALU.add, r=[('ntmp', k), S_[1]], w=[hkey])

    def norm_scratch(self, st, tag):
        nc = self.nc
        out = []
        for k in range(2):
            out.append(dict(
                junk=st.enter_context(nc.sbuf_tensor(f"{tag}_junk{k}_u{self.uid()}", [128, D], BF16)),
                ss=st.enter_context(nc.sbuf_tensor(f"{tag}_ss{k}_u{self.uid()}", [128, 1], F32)),
                rstd=st.enter_context(nc.sbuf_tensor(f"{tag}_rstd{k}_u{self.uid()}", [128, 1], F32)),
                tmp=st.enter_context(nc.sbuf_tensor(f"{tag}_tmp{k}_u{self.uid()}", [128, D], F32)),
                k=(tag, k)))
        return out

    def transpose_out(self, hb, hkey, hT, hTkey, sub):
        for kc in range(8):
            self.P.op('pe', lambda e, kc=kc: e.transpose(self.psT[:, kc * 128:(kc + 1) * 128],
                                                          hb[:, kc * 128:(kc + 1) * 128], self.ident[:]),
                      r=[hkey, 'ident'], w=['psT'])
        self.cp('act', hT[:, :, sub * 128:(sub + 1) * 128],
                self.psT[:].rearrange("p (k t) -> p k t", t=128), r=['psT'], w=[hTkey])

    TILES = [(0, 256)] + [(256 + 512 * i, 512) for i in range(8)]

    def phase_norm(self, l, src, kind):
        nc, P, I, S = self.nc, self.P, self.I, self.S
        with ExitStack() as st:
            sb = lambda n, s, d: st.enter_context(nc.sbuf_tensor(f"{n}_u{self.uid()}", s, d))
            mods = [self.mod_tiles(st, l, c, ('G1', 'S1')) for c in (0, 1)]
            xt = [sb(f"pn_x{i}", [128, D], F32) for i in range(3)]
            hb = [sb(f"pn_hb{i}", [128, D], BF16) for i in range(2)]
            hT = [sb(f"pn_hT{i}", [128, 8, 512], BF16) for i in range(2)]
            scr = self.norm_scratch(st, "pn")
            dst = S['hxT'].ap().rearrange("(k p) t -> p k t", p=128)
            for (t0, n) in self.TILES:
                cond = 1 if t0 < CTX else 0
                hs = self.rot('pn_hT', 2)
                for sub in range(n // 128):
                    xs = self.rot('pn_x', 3)
                    P.dma('sp', xt[xs][:], src.ap()[t0 + sub * 128:t0 + (sub + 1) * 128, :], w=[('pn_x', xs)])
                    bs = self.rot('pn_hb', 2)
                    self.norm_sub(xt[xs][:], ('pn_x', xs), mods[cond]['G1'], mods[cond]['S1'],
                                  hb[bs][:], ('pn_hb', bs), scr[bs])
                    self.transpose_out(hb[bs], ('pn_hb', bs), hT[hs], ('pn_hT', hs), sub)
                P.dma('act', dst[:, :, t0:t0 + n], hT[hs][:, :, 0:n], r=[('pn_hT', hs)])

    TT512 = [(512 * i, 512) for i in range(8)] + [(4096, 256)]

    def load_w(self, dst, src_ap, key, r=()):
        self.P.dma('pool', dst, src_ap, r=r, w=[key])

    def phase_conv(self, l):
        nc, P, I, S = self.nc, self.P, self.I, self.S
        with ExitStack() as st:
            sb = lambda n, s, d: st.enter_context(nc.sbuf_tensor(f"{n}_u{self.uid()}", s, d))
            wc = [sb(f"cv_w{i}", [128, 3, 8, 128], BF16) for i in range(2)]
            hx = [sb(f"cv_hx{i}", [128, 8, 512], BF16) for i in range(2)]
            vbs = [sb(f"cv_v{i}", [128, T + 4], F32) for i in range(2)]
            xbbs = [sb(f"cv_xb{i}", [128, T], F32) for i in range(2)]
            tmp = [sb(f"cv_tmp{i}", [128, 512], F32) for i in range(2)]
            acc = [sb(f"cv_acc{i}", [128, 1024], F32) for i in range(2)]
            ob = [sb(f"cv_o{i}", [128, T], BF16) for i in range(2)]
            cw = sb("cv_cw", [128, 12], F32)
            P.dma('sp', cw[:], I['conv_w'].ap()[l].rearrange("p q j -> p (q j)"), w=['cw'])
            for i in range(2):
                P.op('pool', lambda e, i=i: e.memset(vbs[i][:], 0.0), w=[('vb', i)])
            hsrc = S['hxT'].ap().rearrange("(k p) t -> p k t", p=128)
            win = I['w_in'].ap()[l].rearrange("(k p) c -> p k c", p=128)
            for q in range(4):
                ws = self.rot('cv_w', 2)
                vb, xbb = vbs[q % 2], xbbs[q % 2]
                vbk, xbk = ('vb', q % 2), ('xbb', q % 2)
                for j, off in enumerate((OFF_XA, OFF_XB, OFF_XC)):
                    self.load_w(wc[ws][:, j, :, :], win[:, :, off + q * 128: off + (q + 1) * 128], ('cv_w', ws, j))
                for (t0, n) in self.TT512:
                    hs = self.rot('cv_hx', 2)
                    P.dma('sp', hx[hs][:, :, 0:n], hsrc[:, :, t0:t0 + n], w=[('cv_hx', hs)])
                    for j in range(3):
                        for k in range(8):
                            self.mm(self.ps[j][:, 0:n], wc[ws][:, j, k, :], hx[hs][:, k, 0:n], k == 0, k == 7,
                                    r=[('cv_w', ws, j), ('cv_hx', hs)], w=[('ps', j)])
                    ts_ = self.rot('cv_tmp', 2)
                    self.cp('act', tmp[ts_][:, 0:n], self.ps[2][:, 0:n], r=[('ps', 2)], w=[('cv_tmp', ts_)])
                    segs = []
                    if t0 < CTX:
                        segs.append((t0, CTX - t0, 1 + t0))
                        segs.append((CTX, t0 + n - CTX, 3 + CTX))
                    else:
                        segs.append((t0, n, 3 + t0))
                    for (a0, an, c0) in segs:
                        self.tt('dve', vb[:, c0:c0 + an], self.ps[0][:, a0 - t0:a0 - t0 + an],
                                tmp[ts_][:, a0 - t0:a0 - t0 + an], ALU.mult,
                                r=[('ps', 0), ('cv_tmp', ts_), vbk], w=[vbk])
                    self.cp('act', xbb[:, t0:t0 + n], self.ps[1][:, 0:n], r=[('ps', 1)], w=[xbk])
                os_ = self.rot('cv_o', 2)
                pieces = [(0, 256, 1)] + [(256 + 1024 * i, 1024, 3 + 256 + 1024 * i) for i in range(4)]
                for (a0, an, c0) in pieces:
                    as_ = self.rot('cv_acc', 2)
                    A = acc[as_][:, 0:an]
                    ak = ('cv_acc', as_)
                    self.ts('dve', A, vb[:, c0 - 1:c0 - 1 + an], cw[:, q * 3:q * 3 + 1], None, ALU.mult, None,
                            r=[vbk, 'cw'], w=[ak])
                    self.stt(A, vb[:, c0:c0 + an], cw[:, q * 3 + 1:q * 3 + 2], A, ALU.mult, ALU.add,
                             r=[vbk, 'cw', ak], w=[ak])
                    self.stt(A, vb[:, c0 + 1:c0 + 1 + an], cw[:, q * 3 + 2:q * 3 + 3], A, ALU.mult, ALU.add,
                             r=[vbk, 'cw', ak], w=[ak])
                    self.tt('pool', ob[os_][:, a0:a0 + an], A, xbb[:, a0:a0 + an], ALU.mult,
                            r=[ak, xbk], w=[('cv_o', os_)])
                P.dma('act', S['convbT'].ap()[q * 128:(q + 1) * 128, :], ob[os_][:], r=[('cv_o', os_)])


    def phase_attn(self, l):
        nc, P, I, S = self.nc, self.P, self.I, self.S
        plan, nt = _na()[0], self.nt
        last = (l == DEPTH - 1)
        with ExitStack() as st:
            sb = lambda n, s, d: st.enter_context(nc.sbuf_tensor(f"{n}_u{self.uid()}", s, d))
            wq = [sb(f"at_w{i}", [128, 3, 8, 128], BF16) for i in range(2)]
            hx = [sb(f"at_hx{i}", [128, 8, 512], BF16) for i in range(2)]
            qT = [sb(f"at_q{i}", [128, 2, T], BF16) for i in range(2)]
            kT = [sb(f"at_k{i}", [128, T], BF16) for i in range(2)]
            V = [sb(f"at_v{i}", [128, 34, 128], BF16) for i in range(2)]
            aT = [sb(f"at_a{i}", [128, T], BF16) for i in range(2)]
            msk = sb("at_mask", [128, nt, 128], F32)
            rg = [sb(f"at_rg{i}", [128, 128], F32) for i in range(3)]
            bias = [sb(f"at_bias{i}", [128, 2, nt, 128], BF16) for i in range(2)]
            sc = [sb(f"at_sc{i}", [128, 256], F32) for i in range(3)]
            PT = [sb(f"at_pt{i}", [128, 256], BF16) for i in range(4)]
            rec = [sb(f"at_rec{i}", [128, 128], F32) for i in range(2)]
            P.dma('sp', msk[:], I['na_mask'].ap().rearrange("t k q -> k t q"), w=['msk'])
            for i in range(2):
                P.op('pool', lambda e, i=i: e.memset(qT[i][:], 0.0), w=[('at_qk', i, 0)])
            hsrc = S['hxT'].ap().rearrange("(k p) t -> p k t", p=128)
            win = I['w_in'].ap()[l].rearrange("(k p) c -> p k c", p=128)
            for hp in range(4):
                ws = self.rot('at_w', 2)
                bsl = self.rot('at_b', 2)
                for j, off in enumerate((OFF_Q, OFF_K, OFF_V)):
                    self.load_w(wq[ws][:, j, :, :], win[:, :, off + hp * 128: off + (hp + 1) * 128], ('at_w', ws, j))
                for hh in range(2):
                    for ty in range(nt):
                        rs = self.rot('at_rg', 3)
                        P.dma('sp', rg[rs][:], I['rpbg'].ap()[l, hp * 2 + hh, ty], w=[('at_rg', rs)])
                        self.tt('pool', bias[bsl][:, hh, ty, :], rg[rs][:], msk[:, ty, :], ALU.add,
                                r=[('at_rg', rs), 'msk'], w=[('at_bias', bsl)])
                for (t0, n) in self.TT512:
                    hs = self.rot('at_hx', 2)
                    P.dma('sp', hx[hs][:, :, 0:n], hsrc[:, :, t0:t0 + n], w=[('at_hx', hs)])
                    for j in range(2):
                        for k in range(8):
                            self.mm(self.ps[j][:, 0:n], wq[ws][:, j, k, :], hx[hs][:, k, 0:n], k == 0, k == 7,
                                    r=[('at_w', ws, j), ('at_hx', hs)], w=[('ps', j)])
                    self.cp('act', qT[bsl][0:64, 0, t0:t0 + n], self.ps[0][0:64, 0:n], r=[('ps', 0)],
                            w=[('at_qk', bsl, 0)])
                    self.cp('act', qT[bsl][64:128, 1, t0:t0 + n], self.ps[0][64:128, 0:n], r=[('ps', 0)],
                            w=[('at_qk', bsl, 0)])
                    self.cp('dve', kT[bsl][:, t0:t0 + n], self.ps[1][:, 0:n], r=[('ps', 1)], w=[('at_qk', bsl, 1)])
                    for sub in range(n // 128):
                        for k in range(8):
                            self.mm(self.ps[0][:, sub * 128:(sub + 1) * 128], hx[hs][:, k, sub * 128:(sub + 1) * 128],
                                    wq[ws][:, 2, k, :], k == 0, k == 7,
                                    r=[('at_w', ws, 2), ('at_hx', hs)], w=[('ps', 0)])
                    ti0 = t0 // 128
                    self.cp('act', V[bsl][:, ti0:ti0 + n // 128, :],
                            self.ps[0][:, 0:n].rearrange("p (s c) -> p s c", c=128), r=[('ps', 0)], w=[('at_v', bsl)])
                qlist = []
                if not last:
                    for qi in range(2):
                        qlist.append((qi * 128, [(0, None), (128, None)]))
                for j in range(32):
                    kt = [(CTX + t * 128, ty) for (t, ty) in plan[j]] + [(0, None), (128, None)]
                    qlist.append((CTX + j * 128, kt))
                items = []
                for (q0, kts) in qlist:
                    osl = self.rot('at_o', 2)
                    for ki, (k0, ty) in enumerate(kts):
                        items.append(dict(q0=q0, ki=ki, nk=len(kts), k0=k0, ty=ty, osl=osl))

                def stage_s(it):
                    ssl = self.rot('at_s', 3)
                    it['psl'] = self.rot('at_ptslot', 4)
                    psS = self.ps[ssl][:, 0:256]
                    skey = ('ps', ssl)
                    q0, k0, ty = it['q0'], it['k0'], it['ty']
                    self.mm(psS, kT[bsl][:, k0:k0 + 128], qT[bsl][:, :, q0:q0 + 128], True, True,
                            r=[('at_qk', bsl, 0), ('at_qk', bsl, 1)], w=[skey])
                    pk = ('at_pt', it['psl'])
                    if ty is None:
                        self.act(PT[it['psl']][:], psS, AF.Exp, r=[skey], w=[pk], scale=0.125)
                    else:
                        cs_ = self.rot('at_sc', 3)
                        self.stt(sc[cs_][:].rearrange("p (h q) -> p h q", h=2),
                                 psS.rearrange("p (h q) -> p h q", h=2), 0.125, bias[bsl][:, :, ty, :],
                                 ALU.mult, ALU.add, r=[skey, ('at_bias', bsl)], w=[('at_sc', cs_)])
                        self.act(PT[it['psl']][:], sc[cs_][:], AF.Exp, r=[('at_sc', cs_)], w=[pk])

                def stage_pv(it):
                    psl, osl, ki, nk, k0, q0 = it['psl'], it['osl'], it['ki'], it['nk'], it['k0'], it['q0']
                    psO = self.ps[3 + osl]
                    psU = self.ps[5 + osl]
                    pkey = ('at_pt', psl)
                    self.mm(psO[:, 0:256], V[bsl][:, k0 // 128, :], PT[psl][:], ki == 0, ki == nk - 1,
                            r=[('at_v', bsl), pkey], w=[('psO', osl)])
                    self.mm(psU[:, 0:256], self.ones[:], PT[psl][:], ki == 0, ki == nk - 1,
                            r=['ones', pkey], w=[('psU', osl)])
                    if ki == nk - 1:
                        fin_q.append((osl, q0))

                def finalize(osl, q0):
                    psO = self.ps[3 + osl]
                    psU = self.ps[5 + osl]
                    for hh in range(2):
                        pb = hh * 64
                        cs0 = hh * 128
                        if hh == 0:
                            self.P.op('dve', lambda e, pb=pb, cs0=cs0: e.reciprocal(
                                rec[osl][pb:pb + 64, :], psU[pb:pb + 64, cs0:cs0 + 128]),
                                r=[('psU', osl)], w=[('at_rec', osl, hh)])
                        else:
                            self.act(rec[osl][pb:pb + 64, :], psU[pb:pb + 64, cs0:cs0 + 128], AF.Ln,
                                     r=[('psU', osl)], w=[('at_rec', osl, hh)])
                            self.act(rec[osl][pb:pb + 64, :], rec[osl][pb:pb + 64, :], AF.Exp,
                                     r=[('at_rec', osl, hh)], w=[('at_rec', osl, hh)], scale=-1.0)
                        self.tt('dve', aT[bsl][pb:pb + 64, q0:q0 + 128], psO[pb:pb + 64, cs0:cs0 + 128],
                                rec[osl][pb:pb + 64, :], ALU.mult, r=[('psO', osl), ('at_rec', osl, hh)],
                                w=[('at_a', bsl)])
                fin_q = []
                fin_due = []
                LA = 2
                FD = 3
                for i in range(len(items) + LA + FD + 1):
                    if i < len(items):
                        stage_s(items[i])
                    if 0 <= i - LA < len(items):
                        itp = items[i - LA]
                        if itp['ki'] == 0:
                            for ent in [e_ for e_ in fin_due if e_[1][0] == itp['osl']]:
                                fin_due.remove(ent)
                                finalize(*ent[1])
                        stage_pv(itp)
                        while fin_q:
                            fin_due.append((i + FD, fin_q.pop(0)))
                    while fin_due and fin_due[0][0] <= i:
                        finalize(*fin_due.pop(0)[1])
                assert not fin_due and not fin_q
                a0 = CTX if last else 0
                P.dma('act', S['attnT'].ap()[hp * 128:(hp + 1) * 128, a0:T], aT[bsl][:, a0:T], r=[('at_a', bsl)])

    def phase_s5(self, l):
        nc, P, I, S = self.nc, self.P, self.I, self.S
        Lc = LCH
        TWO_PI = 2.0 * math.pi
        with ExitStack() as st:
            sb = lambda n, s, d: st.enter_context(nc.sbuf_tensor(f"{n}_u{self.uid()}", s, d))
            prm = sb("s5_prm", [128, 3, 32], F32)
            names = ['dt', 'a', 'adt', 'bdt', 'r1', 'kf', 'red', 'sn', 'shf', 'cs', 'nr', 'ni', 'den',
                     'cre', 'cim', 't1', 't2']
            A = {n: sb(f"s5_{n}", [128, 32], F32) for n in names}
            ki = sb("s5_ki", [128, 32], I32)
            phr = sb("s5_phr", [128, 9, 32], F32)
            phi = sb("s5_phi", [128, 9, 32], F32)
            dsk = sb("s5_dsk", [128, 4], F32)
            zero = sb("s5_zero", [128, Lc], F32)
            P.dma('sp', prm[:], I['s5p'].ap()[l], w=['prm'])
            P.dma('sp', dsk[:], I['s5d'].ap()[l], w=['dsk'])
            P.op('dve', lambda e: e.memset(zero[:], 0.0), w=['zero'])
            K_ = ['prm']
            lre, lim, lst = prm[:, 0, :], prm[:, 1, :], prm[:, 2, :]
            a = lambda n: A[n][:]
            self.act(a('dt'), lst, AF.Exp, r=K_, w=K_)
            self.ts('dve', a('a'), lre, -1e-4, None, ALU.min, None, r=K_, w=K_)
            self.tt('dve', a('adt'), a('a'), a('dt'), ALU.mult, r=K_, w=K_)
            self.tt('dve', a('bdt'), lim, a('dt'), ALU.mult, r=K_, w=K_)
            self.act(a('r1'), a('adt'), AF.Exp, r=K_, w=K_)
            self.ts('dve', a('kf'), a('bdt'), 1.0 / TWO_PI, None, ALU.mult, None, r=K_, w=K_)
            self.cp('dve', ki[:], a('kf'), r=K_, w=K_)
            self.cp('dve', a('kf'), ki[:], r=K_, w=K_)
            self.stt(a('red'), a('kf'), -TWO_PI, a('bdt'), ALU.mult, ALU.add, r=K_, w=K_)
            self.ts('dve', a('red'), a('red'), 3.141592, -3.141592, ALU.min, ALU.max, r=K_, w=K_)
            self.act(a('sn'), a('red'), AF.Sin, r=K_, w=K_)
            self.act(a('shf'), a('red'), AF.Sin, r=K_, w=K_, scale=0.5)
            self.tt('dve', a('cs'), a('shf'), a('shf'), ALU.mult, r=K_, w=K_)
            self.ts('dve', a('cs'), a('cs'), -2.0, 1.0, ALU.mult, ALU.add, r=K_, w=K_)
            self.tt('dve', a('nr'), a('r1'), a('cs'), ALU.mult, r=K_, w=K_)
            self.ts('dve', a('nr'), a('nr'), -1.0, None, ALU.add, None, r=K_, w=K_)
            self.tt('dve', a('ni'), a('r1'), a('sn'), ALU.mult, r=K_, w=K_)
            self.tt('dve', a('den'), a('a'), a('a'), ALU.mult, r=K_, w=K_)
            self.tt('dve', a('t1'), lim, lim, ALU.mult, r=K_, w=K_)
            self.tt('dve', a('den'), a('den'), a('t1'), ALU.add, r=K_, w=K_)
            P.op('dve', lambda e: e.reciprocal(a('den'), a('den')), r=K_, w=K_)
            self.tt('dve', a('t1'), a('nr'), a('a'), ALU.mult, r=K_, w=K_)
            self.tt('dve', a('t2'), a('ni'), lim, ALU.mult, r=K_, w=K_)
            self.tt('dve', a('t1'), a('t1'), a('t2'), ALU.add, r=K_, w=K_)
            self.tt('dve', a('cre'), a('t1'), a('den'), ALU.mult, r=K_, w=K_)
            self.tt('dve', a('t1'), a('ni'), a('a'), ALU.mult, r=K_, w=K_)
            self.tt('dve', a('t2'), a('nr'), lim, ALU.mult, r=K_, w=K_)
            self.tt('dve', a('t1'), a('t1'), a('t2'), ALU.subtract, r=K_, w=K_)
            self.tt('dve', a('cim'), a('t1'), a('den'), ALU.mult, r=K_, w=K_)
            self.cp('dve', phr[:, 0, :], a('cs'), r=K_, w=K_)
            self.cp('dve', phi[:, 0, :], a('sn'), r=K_, w=K_)
            for j in range(1, 9):
                self.tt('dve', a('t1'), phr[:, j - 1, :], phr[:, j - 1, :], ALU.mult, r=K_, w=K_)
                self.tt('dve', a('t2'), phi[:, j - 1, :], phi[:, j - 1, :], ALU.mult, r=K_, w=K_)
                self.tt('dve', phr[:, j, :], a('t1'), a('t2'), ALU.subtract, r=K_, w=K_)
                self.tt('dve', a('t1'), phr[:, j - 1, :], phi[:, j - 1, :], ALU.mult, r=K_, w=K_)
                self.ts('dve', phi[:, j, :], a('t1'), 2.0, None, ALU.mult, None, r=K_, w=K_)

            wu = [sb(f"s5_wu{i}", [128, 8, 128], BF16) for i in range(2)]
            hx = [sb(f"s5_hx{i}", [128, 8, 512], BF16) for i in range(2)]
            uT = [sb(f"s5_uT{i}", [128, T], BF16) for i in range(2)]
            BT = [sb(f"s5_BT{i}", [128, 16, 128], BF16) for i in range(2)]
            CM = [sb(f"s5_CM{i}", [128, 16, 128], BF16) for i in range(2)]
            ybuf = sb("s5_y", [128, T], F32)
            gq = [sb(f"s5_g{i}", [128, T], BF16) for i in range(2)]
            tab = [{n: sb(f"s5_tab{il}_{n}", [128, Lc], F32) for n in ('DTr', 'DTi', 'nDTi', 'nDTr', 'MTr', 'MTi', 'nMTi', 'Rc')}
                   for il in range(4)]
            ttmp = sb("s5_ttmp", [128, Lc], F32)
            NZ = 8
            zb = [{n: sb(f"s5_z{i}_{n}", [128, Lc], F32) for n in ('gr', 'gi')} for i in range(NZ)]
            zp = [{n: sb(f"s5_zp{i}_{n}", [128, Lc], BF16) for n in ('pa', 'pb', 'pc', 'pd')} for i in range(NZ)]
            db = [{n: sb(f"s5_d{i}_{n}", [128, Lc], F32) for n in ('t1', 't2', 't3', 't4')} for i in range(2)]
            hb = [[sb(f"s5_h{il}_{i}", [128, 4, Lc], BF16) for i in range(2)] for il in range(4)]
            identf = sb("s5_identf", [128, 128], F32)
            P.dma('sp', identf[:], I['ident'].ap(), w=['identf'])
            ini = [[sb(f"s5_ini{il}_{i}", [128, 2], F32) for i in range(2)] for il in range(4)]
            itmp = [sb(f"s5_itmp{i}", [128, 2], F32) for i in range(4)]
            nphi = sb("s5_nphi", [128, 32], F32)
            self.ts('dve', nphi[:], phi[:, 8, :], -1.0, None, ALU.mult, None, r=K_, w=K_)

            hsrc = S['hxT'].ap().rearrange("(k p) t -> p k t", p=128)
            win = I['w_in'].ap()[l].rearrange("(k p) c -> p k c", p=128)
            for q in range(4):
                us = self.rot('s5_u', 2)
                self.load_w(wu[us][:], win[:, :, OFF_U + q * 128: OFF_U + (q + 1) * 128], ('s5_wu', us))
                for d in range(2):
                    for c in range(2):
                        self.load_w(BT[us][:, d * 8 + c * 4: d * 8 + c * 4 + 4, :] if False else
                                    BT[us][:].rearrange("p (d i c) k -> p d i c k", d=2, i=4)[:, d, :, c, :],
                                    I['s5BT'].ap()[l, d, 4 * q:4 * q + 4, c].rearrange("i r k -> r i k"),
                                    ('s5_BT', us, d, c))
                        self.load_w(CM[us][:].rearrange("p (d i c) k -> p d i c k", d=2, i=4)[:, d, :, c, :],
                                    I['s5CM'].ap()[l, d, 4 * q:4 * q + 4, c].rearrange("i r k -> r i k"),
                                    ('s5_CM', us, d, c))
                BTv = BT[us][:].rearrange("p (d i c) k -> p d i c k", d=2, i=4)
                CMv = CM[us][:].rearrange("p (d i c) k -> p d i c k", d=2, i=4)
                ukey = ('s5_uT', us)
                for (t0, n) in self.TT512:
                    hs = self.rot('s5_hx', 2)
                    P.dma('sp', hx[hs][:, :, 0:n], hsrc[:, :, t0:t0 + n], w=[('s5_hx', hs)])
                    for k in range(8):
                        self.mm(self.ps[6][:, 0:n], wu[us][:, k, :], hx[hs][:, k, 0:n], k == 0, k == 7,
                                r=[('s5_wu', us), ('s5_hx', hs)], w=[('ps', 6)])
                    self.cp('act', uT[us][:, t0:t0 + n], self.ps[6][:, 0:n], r=[('ps', 6)], w=[ukey])
                for d in range(2):
                    for il in range(4):
                        c = d * 16 + 4 * q + il
                        tb = tab[il]
                        tk = ('s5_tab', il)
                        sc = lambda arr, j=None: (arr[:, c:c + 1] if j is None else arr[:, j, c:c + 1])
                        P.op('dve', lambda e, tb=tb: e.memset(tb['DTr'][:, 0:1], 1.0), w=[tk])
                        P.op('dve', lambda e, tb=tb: e.memset(tb['DTi'][:, 0:1], 0.0), w=[tk])
                        for j in range(8):
                            n = 1 << j
                            pr, pi = sc(phr, j), sc(phi, j)
                            self.ts('dve', ttmp[:, 0:n], tb['DTi'][:, 0:n], pi, None, ALU.mult, None,
                                    r=[tk, 'prm'], w=['ttmp'])
                            self.stt(tb['DTr'][:, n:2 * n], tb['DTr'][:, 0:n], pr, ttmp[:, 0:n], ALU.mult, ALU.subtract,
                                     r=[tk, 'prm', 'ttmp'], w=[tk])
                            self.ts('dve', ttmp[:, 0:n], tb['DTi'][:, 0:n], pr, None, ALU.mult, None,
                                    r=[tk, 'prm'], w=['ttmp'])
                            self.stt(tb['DTi'][:, n:2 * n], tb['DTr'][:, 0:n], pi, ttmp[:, 0:n], ALU.mult, ALU.add,
                                     r=[tk, 'prm', 'ttmp'], w=[tk])
                        cre, cim = sc(A['cre'][:]), sc(A['cim'][:])
                        self.ts('dve', ttmp[:], tb['DTi'][:], cim, None, ALU.mult, None, r=[tk, 'prm'], w=['ttmp'])
                        self.stt(tb['MTr'][:], tb['DTr'][:], cre, ttmp[:], ALU.mult, ALU.add, r=[tk, 'prm', 'ttmp'], w=[tk])
                        self.ts('dve', ttmp[:], tb['DTi'][:], cre, None, ALU.mult, None, r=[tk, 'prm'], w=['ttmp'])
                        self.stt(tb['MTi'][:], tb['DTr'][:], cim, ttmp[:], ALU.mult, ALU.subtract, r=[tk, 'prm', 'ttmp'], w=[tk])
                        self.ts('pool', tb['nDTi'][:], tb['DTi'][:], -1.0, None, ALU.mult, None, r=[tk], w=[tk])
                        self.ts('pool', tb['nDTr'][:], tb['DTr'][:], -1.0, None, ALU.mult, None, r=[tk], w=[tk])
                        self.ts('pool', tb['nMTi'][:], tb['MTi'][:], -1.0, None, ALU.mult, None, r=[tk], w=[tk])
                        self.ts('dve', tb['Rc'][:], zero[:], sc(A['r1'][:]), None, ALU.add, None, r=['zero', 'prm'], w=[tk])
                        P.op('dve', lambda e, il=il: e.memset(ini[il][0][:], 0.0), w=[('s5_ini', il, 0)])
                    order = list(range(NCH)) if d == 0 else [0] + list(range(NCH - 1, 0, -1))

                    def emit_y(kk, cs):
                        tok0 = cs * Lc
                        psY = self.ps[4]
                        ykey = ('ps', 4)
                        hsl = kk % 2
                        for il in range(4):
                            for j in range(4):
                                self.mm(psY[:, 0:Lc], CMv[:, d, il, j // 2, :], hb[il][hsl][:, j, :],
                                        il == 0 and j == 0, il == 3 and j == 3,
                                        r=[('s5_CM', us, d, j // 2), (('s5_h', il, hsl), j)], w=[ykey])
                        if d == 0:
                            self.stt(ybuf[:, tok0:tok0 + Lc], uT[us][:, tok0:tok0 + Lc], dsk[:, q:q + 1], psY[:, 0:Lc],
                                     ALU.mult, ALU.add, r=[ukey, 'dsk', ykey], w=[('s5_y', cs)])
                        else:
                            self.tt('dve', ybuf[:, tok0:tok0 + Lc], psY[:, 0:Lc], ybuf[:, tok0:tok0 + Lc], ALU.add,
                                    r=[ykey, ('s5_y', cs)], w=[('s5_y', cs)])

                    pend = None
                    for kk, cs in enumerate(order):
                        tok0 = cs * Lc
                        par = kk % 2
                        hsl = kk % 2
                        ctxs = []
                        def emit_bu(il_):
                            bs_ = self.rot('s5_psB', 2)
                            psB_ = self.ps[bs_]
                            for cc in range(2):
                                self.mm(psB_[:, cc * Lc:(cc + 1) * Lc], BTv[:, d, il_, cc, :], uT[us][:, tok0:tok0 + Lc],
                                        True, True, r=[('s5_BT', us, d, cc), ukey], w=[('ps', bs_)])
                            return bs_
                        nxt_bs = emit_bu(0)
                        for il in range(4):
                            bs = nxt_bs
                            if il < 3:
                                nxt_bs = emit_bu(il + 1)
                            psB = self.ps[bs]
                            bkey = ('ps', bs)
                            if d == 0:
                                bre, bim = psB[:, 0:Lc], psB[:, Lc:2 * Lc]
                            else:
                                bre, bim = psB[:, Lc - 1::-1][:, 0:Lc], psB[:, 2 * Lc - 1:Lc - 1:-1]
                            zs = self.rot('s5_z', NZ)
                            z, zk, tb, tk = zb[zs], ('s5_z', zs), tab[il], ('s5_tab', il)
                            self.tt('dve', zp[zs]['pa'][:], bre, tb['MTr'][:], ALU.mult, r=[bkey, tk], w=[(zk, 'pa')])
                            self.tt('dve', zp[zs]['pb'][:], bim, tb['nMTi'][:], ALU.mult, r=[bkey, tk], w=[(zk, 'pb')])
                            self.tt('dve', zp[zs]['pc'][:], bre, tb['MTi'][:], ALU.mult, r=[bkey, tk], w=[(zk, 'pc')])
                            self.tt('dve', zp[zs]['pd'][:], bim, tb['MTr'][:], ALU.mult, r=[bkey, tk], w=[(zk, 'pd')])
                            zbank = (2, 3, 5, 6)[il]
                            psZ = self.ps[zbank]
                            zkey = ('ps', zbank)
                            self.mm(psZ[:, 0:Lc], self.ident[:], zp[zs]['pa'][:], True, False, r=['ident', (zk, 'pa')], w=[zkey])
                            self.mm(psZ[:, 0:Lc], self.ident[:], zp[zs]['pb'][:], False, True, r=['ident', (zk, 'pb')], w=[zkey])
                            self.mm(psZ[:, Lc:2 * Lc], self.ident[:], zp[zs]['pc'][:], True, False, r=['ident', (zk, 'pc')], w=[zkey])
                            self.mm(psZ[:, Lc:2 * Lc], self.ident[:], zp[zs]['pd'][:], False, True, r=['ident', (zk, 'pd')], w=[zkey])
                            ctxs.append(dict(il=il, z=z, zk=zk, gk=('s5_g', zs), tb=tb, tk=tk, psZ=psZ, zkey=zkey,
                                             c=d * 16 + 4 * q + il))
                        for cx in ctxs:
                            z, tb, tk, gk, il, psZ, zkey = cx['z'], cx['tb'], cx['tk'], cx['gk'], cx['il'], cx['psZ'], cx['zkey']
                            ik = ('s5_ini', il, par)
                            iv = ini[il][par]
                            P.op('dve', lambda e, z=z, tb=tb, iv=iv, psZ=psZ: e.tensor_tensor_scan(
                                z['gr'][:], tb['Rc'][:], psZ[:, 0:Lc], iv[:, 0:1], ALU.mult, ALU.add),
                                r=[zkey, tk, ik], w=[(gk, 'r')])
                            P.op('dve', lambda e, z=z, tb=tb, iv=iv, psZ=psZ: e.tensor_tensor_scan(
                                z['gi'][:], tb['Rc'][:], psZ[:, Lc:2 * Lc], iv[:, 1:2], ALU.mult, ALU.add),
                                r=[zkey, tk, ik], w=[(gk, 'i')])
                        for cx in ctxs:
                            z, gk, il, c = cx['z'], cx['gk'], cx['il'], cx['c']
                            ink = ('s5_ini', il, 1 - par)
                            inx = ini[il][1 - par]
                            pr, pi, npi = phr[:, 8, c:c + 1], phi[:, 8, c:c + 1], nphi[:, c:c + 1]
                            gre, gie = z['gr'][:, Lc - 1:Lc], z['gi'][:, Lc - 1:Lc]
                            itk = ('itmp', il)
                            self.act(itmp[il][:, 0:1], gie, AF.Identity, r=[(gk, 'i'), 'prm'], w=[itk], scale=npi)
                            self.act(itmp[il][:, 1:2], gie, AF.Identity, r=[(gk, 'i'), 'prm'], w=[itk], scale=pr)
                            self.act(inx[:, 0:1], gre, AF.Identity, r=[(gk, 'r'), 'prm', itk], w=[ink], scale=pr,
                                     bias=itmp[il][:, 0:1])
                            self.act(inx[:, 1:2], gre, AF.Identity, r=[(gk, 'r'), 'prm', itk], w=[ink], scale=pi,
                                     bias=itmp[il][:, 1:2])
                        for cx in ctxs:
                            z, tb, tk, gk, il = cx['z'], cx['tb'], cx['tk'], cx['gk'], cx['il']
                            hk = ('s5_h', il, hsl)
                            hv = [hb[il][hsl][:, j, :] if d == 0 else hb[il][hsl][:, j, ::-1] for j in range(4)]
                            self.tt('pool', hv[0], z['gr'][:], tb['DTr'][:], ALU.mult, r=[(gk, 'r'), tk], w=[(hk, 0)])
                            self.tt('pool', hv[1], z['gi'][:], tb['nDTi'][:], ALU.mult, r=[(gk, 'i'), tk], w=[(hk, 1)])
                            self.tt('pool', hv[2], z['gr'][:], tb['nDTi'][:], ALU.mult, r=[(gk, 'r'), tk], w=[(hk, 2)])
                            self.tt('dve' if il < 2 else 'pool', hv[3], z['gi'][:], tb['nDTr'][:], ALU.mult, r=[(gk, 'i'), tk], w=[(hk, 3)])
                        if pend is not None:
                            emit_y(*pend)
                        pend = (kk, cs)
                    emit_y(*pend)
                gs_ = self.rot('s5_gq', 2)
                for cs in range(NCH):
                    self.act(gq[gs_][:, cs * Lc:(cs + 1) * Lc], ybuf[:, cs * Lc:(cs + 1) * Lc], AF.Gelu_apprx_tanh,
                             r=[('s5_y', cs)], w=[('s5_gq', gs_)])
                P.dma('act', S['gT'].ap()[q * 128:(q + 1) * 128, :], gq[gs_][:], r=[('s5_gq', gs_)])

    def phase_merge(self, l, src):
        nc, P, I, S = self.nc, self.P, self.I, self.S
        last = (l == DEPTH - 1)
        tiles = self.TILES[1:] if last else self.TILES
        with ExitStack() as st:
            sb = lambda n, s, d: st.enter_context(nc.sbuf_tensor(f"{n}_u{self.uid()}", s, d))
            wco = sb("mg_wco", [128, 4, D], BF16)
            wga = sb("mg_wga", [128, 4, D], BF16)
            wgb = sb("mg_wgb", [128, 4, D], BF16)
            wno = sb("mg_wno", [128, 4, D], BF16)
            wg = sb("mg_wg", [128, 8, 3 * D], BF16)
            hx = [sb(f"mg_hx{i}", [128, 8, 512], BF16) for i in range(2)]
            br = [[sb(f"mg_br{j}_{i}", [128, 4, 512], BF16) for i in range(2)] for j in range(3)]
            mg = [sb(f"mg_mg{i}", [128, 8, 512], BF16) for i in range(2)]
            sg = [sb(f"mg_sg{i}", [128, 512], F32) for i in range(4)]
            tm = [sb(f"mg_tm{i}", [128, 512], F32) for i in range(4)]
            acc = [sb(f"mg_acc{i}", [128, 512], F32) for i in range(2)]
            for wt_, nm in ((wco, 'conv_out'), (wga, 's5_glu_a'), (wgb, 's5_glu_b'), (wno, 'na_out')):
                for kc in range(4):
                    self.load_w(wt_[:, kc, :], I[nm].ap()[l][kc * 128:(kc + 1) * 128, :], ('mg_w', nm))
            win = I['w_in'].ap()[l].rearrange("(k p) c -> p k c", p=128)
            for k in range(8):
                self.load_w(wg[:, k, :], win[:, k, OFF_GA:OFF_GA + 3 * D], 'mg_wg')
            hsrc = S['hxT'].ap().rearrange("(k p) t -> p k t", p=128)
            bsrc = [S[nm].ap().rearrange("(k p) t -> p k t", p=128) for nm in ('convbT', 'gT', 'attnT')]
            mdst = S['mgT'].ap().rearrange("(k p) t -> p k t", p=128)
            for (t0, n) in tiles:
                hs = self.rot('mg_hx', 2)
                P.dma('sp', hx[hs][:, :, 0:n], hsrc[:, :, t0:t0 + n], w=[('mg_hx', hs)])
                for j in range(3):
                    P.dma('sp', br[j][hs][:, :, 0:n], bsrc[j][:, :, t0:t0 + n], w=[('mg_br', j, hs)])
                ms = self.rot('mg_mg', 2)
                for fo in range(8):
                    fs = slice(fo * 128, (fo + 1) * 128)

                    def proj(bank, w_, x_, nk, wkeys, xkey, coff=0):
                        for k in range(nk):
                            self.mm(self.ps[bank][:, 0:n], w_[:, k, coff + fo * 128: coff + (fo + 1) * 128],
                                    x_[:, k, 0:n], k == 0, k == nk - 1, r=wkeys + [xkey], w=[('ps', bank)])
                    hk = ('mg_hx', hs)
                    proj(0, wco, br[0][hs], 4, [('mg_w', 'conv_out')], ('mg_br', 0, hs))
                    proj(1, wg, hx[hs], 8, ['mg_wg'], hk, 0)
                    proj(2, wga, br[1][hs], 4, [('mg_w', 's5_glu_a')], ('mg_br', 1, hs))
                    proj(3, wgb, br[1][hs], 4, [('mg_w', 's5_glu_b')], ('mg_br', 1, hs))
                    proj(4, wg, hx[hs], 8, ['mg_wg'], hk, D)
                    proj(5, wno, br[2][hs], 4, [('mg_w', 'na_out')], ('mg_br', 2, hs))
                    proj(6, wg, hx[hs], 8, ['mg_wg'], hk, 2 * D)
                    a_ = self.rot('mg_acc', 2)
                    A = acc[a_][:, 0:n]
                    ak = ('mg_acc', a_)
                    sgs = [self.rot('mg_sg', 4) for _ in range(4)]
                    tms = [self.rot('mg_tm', 4) for _ in range(2)]
                    self.act(sg[sgs[0]][:, 0:n], self.ps[1][:, 0:n], AF.Sigmoid, r=[('ps', 1)], w=[('mg_sg', sgs[0])])
                    self.tt('dve', A, self.ps[0][:, 0:n], sg[sgs[0]][:, 0:n], ALU.mult,
                            r=[('ps', 0), ('mg_sg', sgs[0])], w=[ak])
                    self.act(sg[sgs[1]][:, 0:n], self.ps[3][:, 0:n], AF.Sigmoid, r=[('ps', 3)], w=[('mg_sg', sgs[1])])
                    self.tt('dve', tm[tms[0]][:, 0:n], self.ps[2][:, 0:n], sg[sgs[1]][:, 0:n], ALU.mult,
                            r=[('ps', 2), ('mg_sg', sgs[1])], w=[('mg_tm', tms[0])])
                    self.act(sg[sgs[2]][:, 0:n], self.ps[4][:, 0:n], AF.Sigmoid, r=[('ps', 4)], w=[('mg_sg', sgs[2])])
                    self.tt('pool', tm[tms[0]][:, 0:n], tm[tms[0]][:, 0:n], sg[sgs[2]][:, 0:n], ALU.mult,
                            r=[('mg_tm', tms[0]), ('mg_sg', sgs[2])], w=[('mg_tm', tms[0])])
                    self.tt('pool', A, A, tm[tms[0]][:, 0:n], ALU.add, r=[ak, ('mg_tm', tms[0])], w=[ak])
                    self.act(sg[sgs[3]][:, 0:n], self.ps[6][:, 0:n], AF.Sigmoid, r=[('ps', 6)], w=[('mg_sg', sgs[3])])
                    self.tt('dve', tm[tms[1]][:, 0:n], self.ps[5][:, 0:n], sg[sgs[3]][:, 0:n], ALU.mult,
                            r=[('ps', 5), ('mg_sg', sgs[3])], w=[('mg_tm', tms[1])])
                    self.tt('pool', mg[ms][:, fo, 0:n], A, tm[tms[1]][:, 0:n], ALU.add,
                            r=[ak, ('mg_tm', tms[1])], w=[('mg_mg', ms)])
                P.dma('act', mdst[:, :, t0:t0 + n], mg[ms][:, :, 0:n], r=[('mg_mg', ms)])
        P.barrier()
        with ExitStack() as st:
            sb = lambda n, s, d: st.enter_context(nc.sbuf_tensor(f"{n}_u{self.uid()}", s, d))
            wo = sb("mo_wo", [128, 8, D], BF16)
            wov = I['w_out'].ap()[l].rearrange("(k p) c -> p k c", p=128)
            for k in range(8):
                self.load_w(wo[:, k, :], wov[:, k, :], 'mo_wo')
            mgt = [sb(f"mo_mg{i}", [128, 8, 512], BF16) for i in range(2)]
            xt = [sb(f"mo_x{i}", [128, D], F32) for i in range(2)]
            xn = [sb(f"mo_xn{i}", [128, D], F32) for i in range(2)]
            hb = [sb(f"mo_hb{i}", [128, D], BF16) for i in range(2)]
            hT = [sb(f"mo_hT{i}", [128, 8, 512], BF16) for i in range(2)]
            scr = self.norm_scratch(st, "mo")
            msrc = S['mgT'].ap().rearrange("(k p) t -> p k t", p=128)
            hdst = S['h2T'].ap().rearrange("(k p) t -> p k t", p=128)
            mods = None
            cur = None
            for (t0, n) in tiles:
                cond = 1 if t0 < CTX else 0
                if cur != cond:
                    if mods is None:
                        mods = self.mod_tiles(st, l, cond, ('A1', 'G2', 'S2'))
                        gt2 = st.enter_context(nc.sbuf_tensor(f"mo_g2_u{self.uid()}", [128, D], F32))
                    else:
                        self.reload_mod(mods, l, cond, gt2)
                    cur = cond
                hs = self.rot('mo_mg', 2)
                P.dma('sp', mgt[hs][:, :, 0:n], msrc[:, :, t0:t0 + n], w=[('mo_mg', hs)])
                ts_ = self.rot('mo_hT', 2)
                for sub in range(n // 128):
                    xs = self.rot('mo_x', 2)
                    r0 = t0 + sub * 128
                    P.dma('sp', xt[xs][:], src.ap()[r0:r0 + 128, :], w=[('mo_x', xs)])
                    for half in range(2):
                        for k in range(8):
                            self.mm(self.ps[half][:, :], mgt[hs][:, k, sub * 128:(sub + 1) * 128],
                                    wo[:, k, half * 512:(half + 1) * 512], k == 0, k == 7,
                                    r=[('mo_mg', hs), 'mo_wo'], w=[('ps', half)])
                        self.tt('dve', xn[xs][:, half * 512:(half + 1) * 512], self.ps[half][:, :],
                                mods['A1'][0][:, half * 512:(half + 1) * 512], ALU.mult,
                                r=[('ps', half), mods['A1'][1]], w=[('mo_xn', xs)])
                    self.tt('pool', xn[xs][:], xn[xs][:], xt[xs][:], ALU.add, r=[('mo_xn', xs), ('mo_x', xs)],
                            w=[('mo_xn', xs)])
                    P.dma('act', S['xres'].ap()[r0:r0 + 128, :], xn[xs][:], r=[('mo_xn', xs)])
                    bs = self.rot('mo_hb', 2)
                    self.norm_sub(xn[xs][:], ('mo_xn', xs), mods['G2'], mods['S2'], hb[bs][:], ('mo_hb', bs), scr[bs])
                    self.transpose_out(hb[bs], ('mo_hb', bs), hT[ts_], ('mo_hT', ts_), sub)
                P.dma('act', hdst[:, :, t0:t0 + n], hT[ts_][:, :, 0:n], r=[('mo_hT', ts_)])

    def reload_mod(self, mods, l, cond, gtmp):
        base = (l * 2 + cond) * 6 * D
        idx = {'S1': 0, 'G1': 1, 'A1': 2, 'S2': 3, 'G2': 4, 'A2': 5}
        for n, (t, key) in mods.items():
            self.load_rep(t[:], self.S['modv'], base + idx[n] * D, key)
            if n in ('G1', 'G2'):
                gt = self.I['norm1_g'] if n == 'G1' else self.I['norm2_g']
                self.load_rep(gtmp[:], gt, l * D, 'modgtmp')
                self.stt(t[:], t[:], 1.0, gtmp[:], ALU.add, ALU.mult, r=[key, 'modgtmp'], w=[key])

    def phase_mlp(self, l):
        nc, P, I, S = self.nc, self.P, self.I, self.S
        last = (l == DEPTH - 1)
        tiles = self.TILES[1:] if last else self.TILES
        with ExitStack() as st:
            sb = lambda n, s, d: st.enter_context(nc.sbuf_tensor(f"{n}_u{self.uid()}", s, d))
            w1 = sb("ml_w1", [128, 8, 4 * D], BF16)
            w1v = I['mlp_w1'].ap()[l].rearrange("(k p) c -> p k c", p=128)
            for k in range(8):
                for hf in range(2):
                    self.load_w(w1[:, k, hf * 2048:(hf + 1) * 2048], w1v[:, k, hf * 2048:(hf + 1) * 2048], 'ml_w1')
            h2 = [sb(f"ml_h2{i}", [128, 8, 512], BF16) for i in range(2)]
            rl = [sb(f"ml_rl{i}", [128, 512], F32) for i in range(3)]
            hd = [sb(f"ml_hd{i}", [128, 8, 512], BF16) for i in range(2)]
            hsrc = S['h2T'].ap().rearrange("(k p) t -> p k t", p=128)
            ddst = S['hidT'].ap().rearrange("(k p) t -> p k t", p=128)
            for (t0, n) in tiles:
                hs = self.rot('ml_h2', 2)
                P.dma('sp', h2[hs][:, :, 0:n], hsrc[:, :, t0:t0 + n], w=[('ml_h2', hs)])
                for fg in range(4):
                    ds = self.rot('ml_hd', 2)
                    for fi in range(8):
                        fc = fg * 8 + fi
                        bank = self.rot('ml_bank', 6)
                        for k in range(8):
                            self.mm(self.ps[bank][:, 0:n], w1[:, k, fc * 128:(fc + 1) * 128], h2[hs][:, k, 0:n],
                                    k == 0, k == 7, r=['ml_w1', ('ml_h2', hs)], w=[('ps', bank)])
                        rs = self.rot('ml_rl', 3)
                        self.act(rl[rs][:, 0:n], self.ps[bank][:, 0:n], AF.Relu, r=[('ps', bank)], w=[('ml_rl', rs)])
                        self.tt('dve' if fi % 2 == 0 else 'pool', hd[ds][:, fi, 0:n], rl[rs][:, 0:n], rl[rs][:, 0:n],
                                ALU.mult, r=[('ml_rl', rs)], w=[('ml_hd', ds)])
                    P.dma('act', ddst[:, fg * 8:(fg + 1) * 8, t0:t0 + n], hd[ds][:, :, 0:n], r=[('ml_hd', ds)])
        P.barrier()
        with ExitStack() as st:
            sb = lambda n, s, d: st.enter_context(nc.sbuf_tensor(f"{n}_u{self.uid()}", s, d))
            w2 = sb("ml_w2", [128, 32, D], BF16)
            w2v = I['mlp_w2'].ap()[l].rearrange("(k p) c -> p k c", p=128)
            for k in range(32):
                self.load_w(w2[:, k, :], w2v[:, k, :], 'ml_w2')
            hdt = [sb(f"ml_hdt{i}", [128, 32, 256], BF16) for i in range(2)]
            xt = [sb(f"ml_x{i}", [128, D], F32) for i in range(2)]
            xn = [sb(f"ml_xn{i}", [128, D], F32) for i in range(2)]
            hb = [sb(f"ml_hb{i}", [128, D], BF16) for i in range(2)]
            hT = [sb(f"ml_hT{i}", [128, 8, 256], BF16) for i in range(2)]
            ob = [sb(f"ml_ob{i}", [128, D], F32) for i in range(2)]
            scr = self.norm_scratch(st, "ml")
            dsrc = S['hidT'].ap().rearrange("(k p) t -> p k t", p=128)
            hdst = S['hxT'].ap().rearrange("(k p) t -> p k t", p=128)
            fin = None
            if last:
                fin = sb("ml_fin", [128, D], F32)
                self.load_rep(fin[:], I['final_g'], 0, 'ml_fin')
            amods = None
            nmods = None
            cur = None
            tiles256 = []
            for (t0, n) in tiles:
                for h in range(n // 256):
                    tiles256.append((t0 + h * 256, 256))
            for (t0, n) in tiles256:
                cond = 1 if t0 < CTX else 0
                if cur != cond:
                    if amods is None:
                        amods = self.mod_tiles(st, l, cond, ('A2',))
                        gt1 = st.enter_context(nc.sbuf_tensor(f"ml_g1_u{self.uid()}", [128, D], F32))
                        if not last:
                            nmods = self.mod_tiles(st, l + 1, cond, ('G1', 'S1'))
                    else:
                        self.reload_mod(amods, l, cond, gt1)
                        if not last:
                            self.reload_mod(nmods, l + 1, cond, gt1)
                    cur = cond
                hs = self.rot('ml_hdt', 2)
                for kq in range(4):
                    P.dma('sp', hdt[hs][:, kq * 8:(kq + 1) * 8, :], dsrc[:, kq * 8:(kq + 1) * 8, t0:t0 + n],
                          w=[('ml_hdt', hs)])
                ts_ = self.rot('ml_hT', 2)
                for sub in range(2):
                    xs = self.rot('ml_x', 2)
                    r0 = t0 + sub * 128
                    P.dma('sp', xt[xs][:], S['xres'].ap()[r0:r0 + 128, :], w=[('ml_x', xs)])
                    for half in range(2):
                        bank = self.rot('ml_bank2', 4)
                        for k in range(32):
                            self.mm(self.ps[bank][:, :], hdt[hs][:, k, sub * 128:(sub + 1) * 128],
                                    w2[:, k, half * 512:(half + 1) * 512], k == 0, k == 31,
                                    r=[('ml_hdt', hs), 'ml_w2'], w=[('ps', bank)])
                        self.tt('dve', xn[xs][:, half * 512:(half + 1) * 512], self.ps[bank][:, :],
                                amods['A2'][0][:, half * 512:(half + 1) * 512], ALU.mult,
                                r=[('ps', bank), amods['A2'][1]], w=[('ml_xn', xs)])
                    self.tt('pool', xn[xs][:], xn[xs][:], xt[xs][:], ALU.add, r=[('ml_xn', xs), ('ml_x', xs)],
                            w=[('ml_xn', xs)])
                    bs = self.rot('ml_hb', 2)
                    if not last:
                        P.dma('act', S['xres'].ap()[r0:r0 + 128, :], xn[xs][:], r=[('ml_xn', xs)])
                        self.norm_sub(xn[xs][:], ('ml_xn', xs), nmods['G1'], nmods['S1'], hb[bs][:], ('ml_hb', bs), scr[bs])
                        self.transpose_out(hb[bs], ('ml_hb', bs), hT[ts_], ('ml_hT', ts_), sub)
                    else:
                        self.norm_sub(xn[xs][:], ('ml_xn', xs), (fin, 'ml_fin'), None, ob[bs][:], ('ml_ob', bs), scr[bs])
                        P.dma('act', self.out.ap()[r0 - CTX:r0 - CTX + 128, :], ob[bs][:], r=[('ml_ob', bs)])
                if not last:
                    P.dma('act', hdst[:, :, t0:t0 + n], hT[ts_][:, :, 0:n], r=[('ml_hT', ts_)])

def _core_inputs(inp, sh, b):
    d = dict(sh)
    d['xin'] = np.ascontiguousarray(np.concatenate([inp['ctx'][b], inp['x'][b]], axis=0), dtype=np.float32)
    cT = np.stack([inp['c'][b].reshape(8, 128).T, inp['c_ctx'].reshape(8, 128).T], axis=2)
    d['cT'] = np.ascontiguousarray(cT.reshape(128, 16), dtype=np.float32)
    return d


def kernel(**inputs):
    inp = {k: np.asarray(v) for k, v in inputs.items()}
    sh = _prep_shared(inp)
    bld = Builder()
    nc = bld.build()
    in_maps = [_core_inputs(inp, sh, b) for b in range(8)]
    in_maps = [{k: m[k] for k in bld.inputs} for m in in_maps]
    res = run_bass_kernel_spmd(nc, in_maps, core_ids=list(range(8)))
    return np.stack([np.asarray(r['out']) for r in res.results], axis=0).astype(np.float32)
```

```python
import math
from contextlib import ExitStack
import numpy as np
import concourse.bass as bass
import concourse.mybir as mybir
from concourse.bass_utils import run_bass_kernel_spmd

F32 = mybir.dt.float32
BF16 = mybir.dt.bfloat16
I32 = mybir.dt.int32
ALU = mybir.AluOpType
AF = mybir.ActivationFunctionType

SEM_LIMIT = 30000
SAME_ENG_WAITS = True
N_DMA_SEMS = 40

DEPTH = 4
D = 1024
T = 4352
CTX = 256
SEQ = 4096
NIN = 6656
OFF_XA, OFF_XB, OFF_XC, OFF_U, OFF_Q, OFF_K, OFF_V, OFF_GA, OFF_GB, OFF_GC = (
    0, 512, 1024, 1536, 2048, 2560, 3072, 3584, 4608, 5632)
EPS = 1e-6
NTYPE = 21
LCH = 256
NCH = T // LCH


class Prog:
    def __init__(self, nc):
        self.nc = nc
        self.ops = []
        self.last_w = {}
        self.readers = {}
        self.engs = {'pe': nc.tensor, 'dve': nc.vector, 'act': nc.scalar,
                     'pool': nc.gpsimd, 'sp': nc.sync}
        self._bar_from = 0

    def op(self, eng, fn, r=(), w=(), dma=False):
        i = len(self.ops)
        deps = set()
        for k in r:
            if k in self.last_w:
                deps.add(self.last_w[k])
        for k in w:
            if k in self.last_w:
                deps.add(self.last_w[k])
            for j in self.readers.get(k, ()):
                deps.add(j)
        fd = set()
        for j in deps:
            oj = self.ops[j]
            if oj['eng'] == eng and not oj['dma'] and not dma:
                if eng == 'pe' or not SAME_ENG_WAITS:
                    continue
                israw = any(self.last_w.get(k) == j for k in list(r) + list(w))
                if not israw:
                    continue
            fd.add(j)
        for k in w:
            self.last_w[k] = i
            self.readers[k] = []
        for k in r:
            self.readers.setdefault(k, []).append(i)
        self.ops.append(dict(eng=eng, fn=fn, deps=fd, dma=dma, sig=False))
        return i

    def dma(self, q, out, in_, r=(), w=(), **kw):
        return self.op(q, lambda e: e.dma_start(out=out, in_=in_, **kw), r, w, dma=True)

    def barrier(self):
        deps = set()
        lastc = {}
        for idx, o in enumerate(self.ops):
            if o['fn'] is None:
                continue
            if o['dma']:
                if idx >= self._bar_from:
                    deps.add(idx)
            else:
                lastc[o['eng']] = idx
        deps |= set(lastc.values())
        self._bar_from = len(self.ops)
        for e in self.engs:
            self.ops.append(dict(eng=e, fn=None, deps=set(deps), dma=False, sig=False))
        self.last_w = {}
        self.readers = {}

    def emit(self):
        nc = self.nc
        ops = self.ops
        for o in ops:
            for j in o['deps']:
                ops[j]['sig'] = True
        dma_sems = [nc.alloc_semaphore(name=f"dq{i}") for i in range(N_DMA_SEMS)]
        dma_cnt = [0] * N_DMA_SEMS
        dma_last = [None] * N_DMA_SEMS
        eng_sem = {}
        eng_cnt = {}
        nsem = [0]

        def new_eng_sem(e):
            nsem[0] += 1
            eng_sem[e] = nc.alloc_semaphore(name=f"s_{e}_{nsem[0]}")
            eng_cnt[e] = 0

        for e in self.engs:
            new_eng_sem(e)
        rr = 0
        for idx, o in enumerate(ops):
            if o['dma']:
                s = rr % N_DMA_SEMS
                rr += 1
                if dma_cnt[s] + 16 > SEM_LIMIT:
                    if dma_last[s] is not None:
                        o['deps'].add(dma_last[s])
                    dma_sems[s] = nc.alloc_semaphore(name=f"dq{s}_{idx}")
                    dma_cnt[s] = 0
                    dma_last[s] = None
                if dma_last[s] is not None:
                    o['deps'].add(dma_last[s])
                dma_cnt[s] += 16
                o['done'] = (dma_sems[s], dma_cnt[s])
                dma_last[s] = idx
            elif o['sig'] and o['fn'] is not None:
                e = o['eng']
                if eng_cnt[e] + 1 > SEM_LIMIT:
                    new_eng_sem(e)
                eng_cnt[e] += 1
                o['done'] = (eng_sem[e], eng_cnt[e])
            else:
                o['done'] = None

        def resolve(j, acc, seen):
            if j in seen:
                return
            seen.add(j)
            oj = ops[j]
            if oj['done'] is not None:
                acc.add(j)
            elif oj['fn'] is None:
                for jj in oj['deps']:
                    resolve(jj, acc, seen)
            else:
                raise RuntimeError("dep on unsignaled op")

        per_eng = {e: [] for e in self.engs}
        for idx, o in enumerate(ops):
            per_eng[o['eng']].append(idx)
        self.n_inst = {e: len(v) for e, v in per_eng.items()}
        with nc.Block() as block:
            def make(e):
                def body(eng):
                    waited = {}
                    for idx in per_eng[e]:
                        o = ops[idx]
                        acc = set()
                        seen = set()
                        for j in o['deps']:
                            resolve(j, acc, seen)
                        need = {}
                        for j in acc:
                            sem, val = ops[j]['done']
                            key = id(sem)
                            if waited.get(key, 0) >= val:
                                continue
                            if key not in need or need[key][1] < val:
                                need[key] = (sem, val)
                        for key, (sem, val) in need.items():
                            eng.wait_ge(sem, val)
                            waited[key] = val
                        if o['fn'] is not None:
                            inst = o['fn'](eng)
                            if o['done'] is not None:
                                sem, val = o['done']
                                inst.then_inc(sem, 16 if o['dma'] else 1)
                return body
            block.tensor(make('pe'))
            block.vector(make('dve'))
            block.scalar(make('act'))
            block.gpsimd(make('pool'))
            block.sync(make('sp'))


def _na_tile_plan():
    types = {}
    plan = []
    for j in range(32):
        r0a = min(max(2 * j - 4, 0), 56)
        r0b = min(max(2 * j + 1 - 4, 0), 56)
        tlo = r0a // 2
        thi = (r0b + 7) // 2
        lst = []
        for t in range(tlo, thi + 1):
            key = (t - j, r0a - 2 * j, r0b - 2 * j)
            if key not in types:
                types[key] = len(types)
            lst.append((t, types[key]))
        plan.append(lst)
    return plan, types


def _na_bias_index():
    plan, types = _na_tile_plan()
    nt = len(types)
    idx_r = np.zeros((nt, 128, 128), np.int64)
    idx_c = np.zeros((nt, 128, 128), np.int64)
    mask = np.zeros((nt, 128, 128), np.float32)
    col = np.arange(64)
    cs = np.clip(col - 8, 0, 48)
    for (delta, ra, rb), ty in types.items():
        for qr2 in range(2):
            r0rel = (ra, rb)[qr2]
            for kr2 in range(2):
                krel = 2 * delta + kr2
                dr = krel - qr2
                row_ok = (krel >= r0rel) and (krel < r0rel + 8)
                for qc in range(64):
                    kc = col
                    ok = row_ok & (kc >= cs[qc]) & (kc < cs[qc] + 16)
                    q = qr2 * 64 + qc
                    k = kr2 * 64 + kc
                    idx_r[ty, k, q] = np.clip(dr + 7, 0, 14)
                    idx_c[ty, k, q] = np.clip(kc - qc + 15, 0, 30)
                    mask[ty, k, q] = np.where(ok, 0.0, -1e30)
    return plan, nt, idx_r, idx_c, mask


_NA = None


def _na():
    global _NA
    if _NA is None:
        _NA = _na_bias_index()
    return _NA


def _prep_shared(inp):
    L = DEPTH
    sh = {}
    f = lambda a: np.ascontiguousarray(a, dtype=np.float32)
    for k in ('w_mod', 'w_in', 'conv_out', 's5_glu_a', 's5_glu_b', 'na_out', 'w_out',
              'mlp_w1', 'mlp_w2'):
        sh[k] = f(inp[k])
    sh['b_mod'] = f(inp['b_mod'])
    sh['norm1_g'] = f(inp['norm1_g'])
    sh['norm2_g'] = f(inp['norm2_g'])
    sh['final_g'] = f(inp['final_norm_g']).reshape(1, D)
    sh['conv_w'] = f(inp['conv_w'].reshape(L, 3, 4, 128).transpose(0, 3, 2, 1))
    def gp(a):
        a = a.reshape(L, 2, 16, 2, 64)
        return a.transpose(0, 3, 4, 1, 2).reshape(L, 128, 32)
    ls = np.broadcast_to(inp['s5_log_step'][:, :, :, None], (L, 2, 32, 64))
    sh['s5p'] = f(np.stack([gp(inp['s5_lam_re']), gp(inp['s5_lam_im']), gp(ls)], axis=2))
    def bt(B):
        out = np.zeros((L, 2, 16, 128, 128), np.float32)
        for i in range(16):
            for g2 in range(2):
                g = 2 * i + g2
                gl = g % 8
                out[:, :, i, gl * 16:(gl + 1) * 16, g2 * 64:(g2 + 1) * 64] = \
                    B[:, :, g].transpose(0, 1, 3, 2)
        return out
    sh['s5BT'] = f(np.stack([bt(inp['s5_b_re']), bt(inp['s5_b_im'])], axis=3))
    def cm(C):
        out = np.zeros((L, 2, 16, 128, 128), np.float32)
        for i in range(16):
            for g2 in range(2):
                g = 2 * i + g2
                gl = g % 8
                out[:, :, i, g2 * 64:(g2 + 1) * 64, gl * 16:(gl + 1) * 16] = \
                    C[:, :, g].transpose(0, 1, 3, 2)
        return out
    sh['s5CM'] = f(np.stack([cm(inp['s5_c_re']), cm(inp['s5_c_im'])], axis=3))
    sh['s5d'] = f(inp['s5_d'].reshape(L, 4, 128).transpose(0, 2, 1))
    plan, nt, idx_r, idx_c, mask = _na()
    rpb = inp['na_rpb']
    sh['rpbg'] = f(rpb[:, :, idx_r, idx_c])
    sh['na_mask'] = f(mask)
    sh['ident'] = np.eye(128, dtype=np.float32)
    return sh


class Builder:
    def __init__(self, layers=DEPTH, dbg=(), stop=None, only=None):
        self.layers = layers
        self.stop = stop
        self.only = only
        self.dbg = set(dbg)
        nc = bass.Bass("TRN2", target_bir_lowering=False)
        self.nc = nc
        self.P = Prog(nc)
        self.inputs = {}
        self.cnt = {}

    def din(self, name, shape, dt=F32):
        t = self.nc.dram_tensor(name, list(shape), dt, kind="ExternalInput")
        self.inputs[name] = t
        return t

    def dscr(self, name, shape, dt):
        kind = "ExternalOutput" if name in self.dbg else "Internal"
        return self.nc.dram_tensor(name, list(shape), dt, kind=kind)

    def uid(self):
        self._uid = getattr(self, '_uid', 0) + 1
        return self._uid

    def rot(self, name, n):
        c = self.cnt.get(name, 0)
        self.cnt[name] = c + 1
        return c % n

    def mm(self, out, lhsT, rhs, start, stop, r, w):
        self.P.op('pe', lambda e: e.matmul(out, lhsT, rhs, start=start, stop=stop), r, w)

    def act(self, out, in_, func, r, w, **kw):
        self.P.op('act', lambda e: e.activation(out, in_, func, **kw), r, w)

    def tt(self, eng, out, a, b, op, r, w):
        self.P.op(eng, lambda e: e.tensor_tensor(out, a, b, op), r, w)

    def ts(self, eng, out, a, s1, s2, op0, op1, r, w):
        if op1 is None:
            self.P.op(eng, lambda e: e.tensor_scalar(out, a, s1, None, op0), r, w)
        else:
            self.P.op(eng, lambda e: e.tensor_scalar(out, a, s1, s2, op0, op1), r, w)

    def stt(self, out, a, s, b, op0, op1, r, w):
        self.P.op('dve', lambda e: e.scalar_tensor_tensor(out, a, s, b, op0, op1), r, w)

    def cp(self, eng, out, in_, r, w):
        if eng == 'act':
            self.P.op('act', lambda e: e.activation(out, in_, AF.Copy), r, w)
        else:
            self.P.op(eng, lambda e: e.tensor_copy(out, in_), r, w)

    def build(self):
        nc, P = self.nc, self.P
        L = DEPTH
        nt = _na()[1]
        self.nt = nt
        I = {}
        I['xin'] = self.din('xin', [T, D])
        I['cT'] = self.din('cT', [128, 16])
        I['w_mod'] = self.din('w_mod', [L, D, 6 * D])
        I['b_mod'] = self.din('b_mod', [L, 6 * D])
        I['norm1_g'] = self.din('norm1_g', [L, D])
        I['norm2_g'] = self.din('norm2_g', [L, D])
        I['final_g'] = self.din('final_g', [1, D])
        I['w_in'] = self.din('w_in', [L, D, NIN])
        I['conv_w'] = self.din('conv_w', [L, 128, 4, 3])
        I['conv_out'] = self.din('conv_out', [L, 512, D])
        I['s5p'] = self.din('s5p', [L, 128, 3, 32])
        I['s5BT'] = self.din('s5BT', [L, 2, 16, 2, 128, 128])
        I['s5CM'] = self.din('s5CM', [L, 2, 16, 2, 128, 128])
        I['s5d'] = self.din('s5d', [L, 128, 4])
        I['s5_glu_a'] = self.din('s5_glu_a', [L, 512, D])
        I['s5_glu_b'] = self.din('s5_glu_b', [L, 512, D])
        I['rpbg'] = self.din('rpbg', [L, 8, nt, 128, 128])
        I['na_mask'] = self.din('na_mask', [nt, 128, 128])
        I['na_out'] = self.din('na_out', [L, 512, D])
        I['w_out'] = self.din('w_out', [L, D, D])
        I['mlp_w1'] = self.din('mlp_w1', [L, D, 4 * D])
        I['mlp_w2'] = self.din('mlp_w2', [L, 4 * D, D])
        I['ident'] = self.din('ident', [128, 128])
        self.I = I
        self.out = nc.dram_tensor('out', [SEQ, D], F32, kind="ExternalOutput")
        S = {}
        S['xres'] = self.dscr('xres', [T, D], F32)
        S['hxT'] = self.dscr('hxT', [D, T], BF16)
        S['h2T'] = self.dscr('h2T', [D, T], BF16)
        S['convbT'] = self.dscr('convbT', [512, T], BF16)
        S['gT'] = self.dscr('gT', [512, T], BF16)
        S['attnT'] = self.dscr('attnT', [512, T], BF16)
        S['modv'] = self.dscr('modv', [L, 2, 6 * D], F32)
        S['mgT'] = self.dscr('mgT', [D, T], BF16)
        S['hidT'] = self.dscr('hidT', [4 * D, T], BF16)
        self.S = S

        with ExitStack() as gs:
            self.ps = [gs.enter_context(nc.psum_tensor(f"ps{i}", [128, 512], F32))
                       for i in range(7)]
            self.psT = gs.enter_context(nc.psum_tensor("psT", [128, 1024], BF16))
            self.ident = gs.enter_context(nc.sbuf_tensor("ident_sb", [128, 128], BF16))
            self.ones = gs.enter_context(nc.sbuf_tensor("ones_sb", [128, 128], BF16))
            P.dma('pool', self.ident[:], I['ident'].ap(), w=['ident'])
            P.op('dve', lambda e: e.memset(self.ones[:], 1.0), w=['ones'])
            P.barrier()
            seq = [('adaln', lambda: self.phase_adaln()),
                   ('norm', lambda: self.phase_norm(0, src=I['xin'], kind='n1'))]
            for l in range(self.layers):
                seq += [(f'conv{l}', lambda l=l: self.phase_conv(l)),
                        (f'attn{l}', lambda l=l: self.phase_attn(l)),
                        (f's5{l}', lambda l=l: self.phase_s5(l)),
                        (f'merge{l}', lambda l=l: self.phase_merge(l, src=(I['xin'] if l == 0 else S['xres']))),
                        (f'mlp{l}', lambda l=l: self.phase_mlp(l))]
            for name, fn in seq:
                if self.only is not None and name not in self.only:
                    continue
                fn()
                P.barrier()
                if name == self.stop:
                    break
            P.emit()
        return nc

    def phase_adaln(self):
        nc, P, I, S = self.nc, self.P, self.I, self.S
        with ExitStack() as st:
            sb = lambda n, s, d: st.enter_context(nc.sbuf_tensor(f"{n}_u{self.uid()}", s, d))
            cT = sb("ad_cT", [128, 16], F32)
            sil = sb("ad_sil", [128, 16], BF16)
            wt = [sb(f"ad_w{i}", [128, 8, 512], BF16) for i in range(3)]
            bm = sb("ad_bm", [2, 6 * D], F32)
            row = [sb(f"ad_row{i}", [2, 512], F32) for i in range(2)]
            P.dma('sp', cT[:], I['cT'].ap(), w=['cT'])
            self.act(sil[:], cT[:], AF.Silu, r=['cT'], w=['sil'])
            silv = sil[:].rearrange("p (k j) -> p k j", j=2)
            for l in range(DEPTH):
                bsrc = bass.AP(I['b_mod'], l * 6 * D, [[0, 2], [1, 6 * D]])
                P.dma('sp', bm[:], bsrc, w=['bm'])
                wv = I['w_mod'].ap()[l].rearrange("(k p) c -> p k c", p=128)
                for ct in range(12):
                    s = self.rot('adw', 3)
                    P.dma('pool', wt[s][:], wv[:, :, ct * 512:(ct + 1) * 512], w=[('adw', s)])
                    pst = self.ps[ct % 2]
                    for k in range(8):
                        self.mm(pst[0:2, :], silv[:, k, :], wt[s][:, k, :], k == 0, k == 7,
                                r=['sil', ('adw', s)], w=[('ps', ct % 2)])
                    rs = self.rot('adrow', 2)
                    self.tt('dve', row[rs][:], pst[0:2, :], bm[:, ct * 512:(ct + 1) * 512], ALU.add,
                            r=[('ps', ct % 2), 'bm'], w=[('adrow', rs)])
                    P.dma('sp', S['modv'].ap()[l][:, ct * 512:(ct + 1) * 512], row[rs][:],
                          r=[('adrow', rs)])

    def load_rep(self, dst, tensor, offset, key):
        self.P.dma('sp', dst, bass.AP(tensor, offset, [[0, 128], [1, D]]), w=[key])

    def mod_tiles(self, st, l, cond, names, gsrc=None):
        nc = self.nc
        res = {}
        base = (l * 2 + cond) * 6 * D
        idx = {'S1': 0, 'G1': 1, 'A1': 2, 'S2': 3, 'G2': 4, 'A2': 5}
        for n in names:
            t = st.enter_context(nc.sbuf_tensor(f"mod_{n}_{cond}_{self.rot('modt', 1 << 30)}", [128, D], F32))
            key = ('mod', n, cond)
            self.load_rep(t[:], self.S['modv'], base + idx[n] * D, key)
            if n in ('G1', 'G2'):
                g = st.enter_context(nc.sbuf_tensor(f"modg_{n}_{cond}_{self.rot('modt', 1 << 30)}", [128, D], F32))
                gt = self.I['norm1_g'] if n == 'G1' else self.I['norm2_g']
                self.load_rep(g[:], gt, l * D, ('modg', n, cond))
                self.stt(t[:], t[:], 1.0, g[:], ALU.add, ALU.mult, r=[key, ('modg', n, cond)], w=[key])
            res[n] = (t, key)
        return res

    def norm_sub(self, xt, xkey, G, S_, hb, hkey, scr):
        junk, ss, rstd, tmp = scr['junk'], scr['ss'], scr['rstd'], scr['tmp']
        k = scr['k']
        self.act(junk[:], xt, AF.Square, r=[xkey], w=[('nj', k), ('ss', k)], accum_out=ss[:])
        self.ts('dve', rstd[:], ss[:], 1.0 / D, EPS, ALU.mult, ALU.add, r=[('ss', k)], w=[('rstd', k)])
        self.act(rstd[:], rstd[:], AF.Sqrt, r=[('rstd', k)], w=[('rstd', k)])
        self.P.op('dve', lambda e: e.reciprocal(rstd[:], rstd[:]), r=[('rstd', k)], w=[('rstd', k)])
        if S_ is None:
            self.stt(hb, xt, rstd[:], G[0][:], ALU.mult, ALU.mult, r=[xkey, ('rstd', k), G[1]], w=[hkey])
        else:
            self.stt(tmp[:], xt, rstd[:], G[0][:], ALU.mult, ALU.mult, r=[xkey, ('rstd', k), G[1]],
                     w=[('ntmp', k)])
            self.tt('pool', hb, tmp[:], S_[0][:], ALU.add, r=[('ntmp', k), S_[1]], w=[hkey])

    def norm_scratch(self, st, tag):
        nc = self.nc
        out = []
        for k in range(2):
            out.append(dict(
                junk=st.enter_context(nc.sbuf_tensor(f"{tag}_junk{k}_u{self.uid()}", [128, D], BF16)),
                ss=st.enter_context(nc.sbuf_tensor(f"{tag}_ss{k}_u{self.uid()}", [128, 1], F32)),
                rstd=st.enter_context(nc.sbuf_tensor(f"{tag}_rstd{k}_u{self.uid()}", [128, 1], F32)),
                tmp=st.enter_context(nc.sbuf_tensor(f"{tag}_tmp{k}_u{self.uid()}", [128, D], F32)),
                k=(tag, k)))
        return out

    def transpose_out(self, hb, hkey, hT, hTkey, sub, scol=None):
        for kc in range(8):
            self.P.op('pe', lambda e, kc=kc: e.transpose(self.psT[:, kc * 128:(kc + 1) * 128],
                                                          hb[:, kc * 128:(kc + 1) * 128], self.ident[:]),
                      r=[hkey, 'ident'], w=['psT'])
        if scol is None:
            self.cp('act', hT[:, :, sub * 128:(sub + 1) * 128],
                    self.psT[:].rearrange("p (k t) -> p k t", t=128), r=['psT'], w=[hTkey])
        else:
            for kc in range(8):
                self.act(hT[:, kc, sub * 128:(sub + 1) * 128], self.psT[:, kc * 128:(kc + 1) * 128], AF.Identity,
                         r=['psT', scol[1]], w=[hTkey], bias=scol[0][:, kc:kc + 1])

    def load_col(self, tile_, l, cond, name, key):
        idx = {'S1': 0, 'G1': 1, 'A1': 2, 'S2': 3, 'G2': 4, 'A2': 5}[name]
        base = (l * 2 + cond) * 6 * D + idx * D
        self.P.dma('sp', tile_[:], bass.AP(self.S['modv'], base, [[1, 128], [128, 8]]), w=[key],
                   allow_slow_non_contiguous=True)

    TILES = [(0, 256)] + [(256 + 512 * i, 512) for i in range(8)]

    def phase_norm(self, l, src, kind):
        nc, P, I, S = self.nc, self.P, self.I, self.S
        with ExitStack() as st:
            sb = lambda n, s, d: st.enter_context(nc.sbuf_tensor(f"{n}_u{self.uid()}", s, d))
            mods = [self.mod_tiles(st, l, c, ('G1',)) for c in (0, 1)]
            scols = []
            for c in (0, 1):
                t_ = sb(f"pn_sc{c}", [128, 8], F32)
                self.load_col(t_, l, c, 'S1', ('pn_sc', c))
                scols.append((t_, ('pn_sc', c)))
            xt = [sb(f"pn_x{i}", [128, D], F32) for i in range(3)]
            hb = [sb(f"pn_hb{i}", [128, D], BF16) for i in range(2)]
            hT = [sb(f"pn_hT{i}", [128, 8, 512], BF16) for i in range(2)]
            scr = self.norm_scratch(st, "pn")
            dst = S['hxT'].ap().rearrange("(k p) t -> p k t", p=128)
            for (t0, n) in self.TILES:
                cond = 1 if t0 < CTX else 0
                hs = self.rot('pn_hT', 2)
                for sub in range(n // 128):
                    xs = self.rot('pn_x', 3)
                    P.dma('sp', xt[xs][:], src.ap()[t0 + sub * 128:t0 + (sub + 1) * 128, :], w=[('pn_x', xs)])
                    bs = self.rot('pn_hb', 2)
                    self.norm_sub(xt[xs][:], ('pn_x', xs), mods[cond]['G1'], None,
                                  hb[bs][:], ('pn_hb', bs), scr[bs])
                    self.transpose_out(hb[bs], ('pn_hb', bs), hT[hs], ('pn_hT', hs), sub, scol=scols[cond])
                P.dma('act', dst[:, :, t0:t0 + n], hT[hs][:, :, 0:n], r=[('pn_hT', hs)])

    TT512 = [(512 * i, 512) for i in range(8)] + [(4096, 256)]

    def load_w(self, dst, src_ap, key, r=()):
        self.P.dma('pool', dst, src_ap, r=r, w=[key])

    def phase_conv(self, l):
        nc, P, I, S = self.nc, self.P, self.I, self.S
        with ExitStack() as st:
            sb = lambda n, s, d: st.enter_context(nc.sbuf_tensor(f"{n}_u{self.uid()}", s, d))
            wc = [sb(f"cv_w{i}", [128, 3, 8, 128], BF16) for i in range(2)]
            hx = [sb(f"cv_hx{i}", [128, 8, 512], BF16) for i in range(2)]
            vbs = [sb(f"cv_v{i}", [128, T + 4], F32) for i in range(2)]
            xbbs = [sb(f"cv_xb{i}", [128, T], F32) for i in range(2)]
            tmp = [sb(f"cv_tmp{i}", [128, 512], F32) for i in range(2)]
            acc = [sb(f"cv_acc{i}", [128, 1024], F32) for i in range(2)]
            ob = [sb(f"cv_o{i}", [128, T], BF16) for i in range(2)]
            cw = sb("cv_cw", [128, 12], F32)
            P.dma('sp', cw[:], I['conv_w'].ap()[l].rearrange("p q j -> p (q j)"), w=['cw'])
            for i in range(2):
                P.op('pool', lambda e, i=i: e.memset(vbs[i][:], 0.0), w=[('vb', i)])
            hsrc = S['hxT'].ap().rearrange("(k p) t -> p k t", p=128)
            win = I['w_in'].ap()[l].rearrange("(k p) c -> p k c", p=128)
            for q in range(4):
                ws = self.rot('cv_w', 2)
                vb, xbb = vbs[q % 2], xbbs[q % 2]
                vbk, xbk = ('vb', q % 2), ('xbb', q % 2)
                for j, off in enumerate((OFF_XA, OFF_XB, OFF_XC)):
                    self.load_w(wc[ws][:, j, :, :], win[:, :, off + q * 128: off + (q + 1) * 128], ('cv_w', ws, j))
                for (t0, n) in self.TT512:
                    hs = self.rot('cv_hx', 2)
                    P.dma('sp', hx[hs][:, :, 0:n], hsrc[:, :, t0:t0 + n], w=[('cv_hx', hs)])
                    for j in range(3):
                        for k in range(8):
                            self.mm(self.ps[j][:, 0:n], wc[ws][:, j, k, :], hx[hs][:, k, 0:n], k == 0, k == 7,
                                    r=[('cv_w', ws, j), ('cv_hx', hs)], w=[('ps', j)])
                    ts_ = self.rot('cv_tmp', 2)
                    self.cp('act', tmp[ts_][:, 0:n], self.ps[2][:, 0:n], r=[('ps', 2)], w=[('cv_tmp', ts_)])
                    segs = []
                    if t0 < CTX:
                        segs.append((t0, CTX - t0, 1 + t0))
                        segs.append((CTX, t0 + n - CTX, 3 + CTX))
                    else:
                        segs.append((t0, n, 3 + t0))
                    for (a0, an, c0) in segs:
                        self.tt('dve', vb[:, c0:c0 + an], self.ps[0][:, a0 - t0:a0 - t0 + an],
                                tmp[ts_][:, a0 - t0:a0 - t0 + an], ALU.mult,
                                r=[('ps', 0), ('cv_tmp', ts_), vbk], w=[vbk])
                    self.cp('act', xbb[:, t0:t0 + n], self.ps[1][:, 0:n], r=[('ps', 1)], w=[xbk])
                os_ = self.rot('cv_o', 2)
                pieces = [(0, 256, 1)] + [(256 + 1024 * i, 1024, 3 + 256 + 1024 * i) for i in range(4)]
                for (a0, an, c0) in pieces:
                    as_ = self.rot('cv_acc', 2)
                    A = acc[as_][:, 0:an]
                    ak = ('cv_acc', as_)
                    self.ts('dve', A, vb[:, c0 - 1:c0 - 1 + an], cw[:, q * 3:q * 3 + 1], None, ALU.mult, None,
                            r=[vbk, 'cw'], w=[ak])
                    self.stt(A, vb[:, c0:c0 + an], cw[:, q * 3 + 1:q * 3 + 2], A, ALU.mult, ALU.add,
                             r=[vbk, 'cw', ak], w=[ak])
                    self.stt(A, vb[:, c0 + 1:c0 + 1 + an], cw[:, q * 3 + 2:q * 3 + 3], A, ALU.mult, ALU.add,
                             r=[vbk, 'cw', ak], w=[ak])
                    self.tt('pool', ob[os_][:, a0:a0 + an], A, xbb[:, a0:a0 + an], ALU.mult,
                            r=[ak, xbk], w=[('cv_o', os_)])
                P.dma('act', S['convbT'].ap()[q * 128:(q + 1) * 128, :], ob[os_][:], r=[('cv_o', os_)])


    def phase_attn(self, l):
        nc, P, I, S = self.nc, self.P, self.I, self.S
        plan, nt = _na()[0], self.nt
        last = (l == DEPTH - 1)
        with ExitStack() as st:
            sb = lambda n, s, d: st.enter_context(nc.sbuf_tensor(f"{n}_u{self.uid()}", s, d))
            wq = [sb(f"at_w{i}", [128, 3, 8, 128], BF16) for i in range(2)]
            hx = [sb(f"at_hx{i}", [128, 8, 512], BF16) for i in range(2)]
            qT = [sb(f"at_q{i}", [128, 2, T], BF16) for i in range(2)]
            kT = [sb(f"at_k{i}", [128, T], BF16) for i in range(2)]
            V = [sb(f"at_v{i}", [128, 34, 128], BF16) for i in range(2)]
            aT = [sb(f"at_a{i}", [128, T], BF16) for i in range(2)]
            msk = sb("at_mask", [128, nt, 128], F32)
            rg = [sb(f"at_rg{i}", [128, 128], F32) for i in range(3)]
            bias = [sb(f"at_bias{i}", [128, 2, nt, 128], BF16) for i in range(2)]
            sc = [sb(f"at_sc{i}", [128, 256], F32) for i in range(3)]
            PT = [sb(f"at_pt{i}", [128, 256], BF16) for i in range(4)]
            rec = [sb(f"at_rec{i}", [128, 128], F32) for i in range(2)]
            P.dma('sp', msk[:], I['na_mask'].ap().rearrange("t k q -> k t q"), w=['msk'])
            for i in range(2):
                P.op('pool', lambda e, i=i: e.memset(qT[i][:], 0.0), w=[('at_qk', i, 0)])
            hsrc = S['hxT'].ap().rearrange("(k p) t -> p k t", p=128)
            win = I['w_in'].ap()[l].rearrange("(k p) c -> p k c", p=128)
            for hp in range(4):
                ws = self.rot('at_w', 2)
                bsl = self.rot('at_b', 2)
                for j, off in enumerate((OFF_Q, OFF_K, OFF_V)):
                    self.load_w(wq[ws][:, j, :, :], win[:, :, off + hp * 128: off + (hp + 1) * 128], ('at_w', ws, j))
                for hh in range(2):
                    for ty in range(nt):
                        rs = self.rot('at_rg', 3)
                        P.dma('sp', rg[rs][:], I['rpbg'].ap()[l, hp * 2 + hh, ty], w=[('at_rg', rs)])
                        self.tt('pool', bias[bsl][:, hh, ty, :], rg[rs][:], msk[:, ty, :], ALU.add,
                                r=[('at_rg', rs), 'msk'], w=[('at_bias', bsl)])
                for (t0, n) in self.TT512:
                    hs = self.rot('at_hx', 2)
                    P.dma('sp', hx[hs][:, :, 0:n], hsrc[:, :, t0:t0 + n], w=[('at_hx', hs)])
                    for j in range(2):
                        for k in range(8):
                            self.mm(self.ps[j][:, 0:n], wq[ws][:, j, k, :], hx[hs][:, k, 0:n], k == 0, k == 7,
                                    r=[('at_w', ws, j), ('at_hx', hs)], w=[('ps', j)])
                    self.cp('act', qT[bsl][0:64, 0, t0:t0 + n], self.ps[0][0:64, 0:n], r=[('ps', 0)],
                            w=[('at_qk', bsl, 0)])
                    self.cp('act', qT[bsl][64:128, 1, t0:t0 + n], self.ps[0][64:128, 0:n], r=[('ps', 0)],
                            w=[('at_qk', bsl, 0)])
                    self.cp('dve', kT[bsl][:, t0:t0 + n], self.ps[1][:, 0:n], r=[('ps', 1)], w=[('at_qk', bsl, 1)])
                    for sub in range(n // 128):
                        for k in range(8):
                            self.mm(self.ps[0][:, sub * 128:(sub + 1) * 128], hx[hs][:, k, sub * 128:(sub + 1) * 128],
                                    wq[ws][:, 2, k, :], k == 0, k == 7,
                                    r=[('at_w', ws, 2), ('at_hx', hs)], w=[('ps', 0)])
                    ti0 = t0 // 128
                    self.cp('act', V[bsl][:, ti0:ti0 + n // 128, :],
                            self.ps[0][:, 0:n].rearrange("p (s c) -> p s c", c=128), r=[('ps', 0)], w=[('at_v', bsl)])
                qlist = []
                if not last:
                    for qi in range(2):
                        qlist.append((qi * 128, [(0, None), (128, None)]))
                for j in range(32):
                    kt = [(CTX + t * 128, ty) for (t, ty) in plan[j]] + [(0, None), (128, None)]
                    qlist.append((CTX + j * 128, kt))
                items = []
                for (q0, kts) in qlist:
                    osl = self.rot('at_o', 2)
                    for ki, (k0, ty) in enumerate(kts):
                        items.append(dict(q0=q0, ki=ki, nk=len(kts), k0=k0, ty=ty, osl=osl))

                def stage_s(it):
                    ssl = self.rot('at_s', 3)
                    it['psl'] = self.rot('at_ptslot', 4)
                    psS = self.ps[ssl][:, 0:256]
                    skey = ('ps', ssl)
                    q0, k0, ty = it['q0'], it['k0'], it['ty']
                    self.mm(psS, kT[bsl][:, k0:k0 + 128], qT[bsl][:, :, q0:q0 + 128], True, True,
                            r=[('at_qk', bsl, 0), ('at_qk', bsl, 1)], w=[skey])
                    pk = ('at_pt', it['psl'])
                    if ty is None:
                        self.act(PT[it['psl']][:], psS, AF.Exp, r=[skey], w=[pk], scale=0.125)
                    else:
                        cs_ = self.rot('at_sc', 3)
                        self.stt(sc[cs_][:].rearrange("p (h q) -> p h q", h=2),
                                 psS.rearrange("p (h q) -> p h q", h=2), 0.125, bias[bsl][:, :, ty, :],
                                 ALU.mult, ALU.add, r=[skey, ('at_bias', bsl)], w=[('at_sc', cs_)])
                        self.act(PT[it['psl']][:], sc[cs_][:], AF.Exp, r=[('at_sc', cs_)], w=[pk])

                def stage_pv(it):
                    psl, osl, ki, nk, k0, q0 = it['psl'], it['osl'], it['ki'], it['nk'], it['k0'], it['q0']
                    psO = self.ps[3 + osl]
                    psU = self.ps[5 + osl]
                    pkey = ('at_pt', psl)
                    self.mm(psO[:, 0:256], V[bsl][:, k0 // 128, :], PT[psl][:], ki == 0, ki == nk - 1,
                            r=[('at_v', bsl), pkey], w=[('psO', osl)])
                    self.mm(psU[:, 0:256], self.ones[:], PT[psl][:], ki == 0, ki == nk - 1,
                            r=['ones', pkey], w=[('psU', osl)])
                    if ki == nk - 1:
                        fin_q.append((osl, q0))

                def finalize(osl, q0):
                    psO = self.ps[3 + osl]
                    psU = self.ps[5 + osl]
                    for hh in range(2):
                        pb = hh * 64
                        cs0 = hh * 128
                        if hh == 0:
                            self.P.op('dve', lambda e, pb=pb, cs0=cs0: e.reciprocal(
                                rec[osl][pb:pb + 64, :], psU[pb:pb + 64, cs0:cs0 + 128]),
                                r=[('psU', osl)], w=[('at_rec', osl, hh)])
                        else:
                            self.act(rec[osl][pb:pb + 64, :], psU[pb:pb + 64, cs0:cs0 + 128], AF.Ln,
                                     r=[('psU', osl)], w=[('at_rec', osl, hh)])
                            self.act(rec[osl][pb:pb + 64, :], rec[osl][pb:pb + 64, :], AF.Exp,
                                     r=[('at_rec', osl, hh)], w=[('at_rec', osl, hh)], scale=-1.0)
                        self.tt('dve', aT[bsl][pb:pb + 64, q0:q0 + 128], psO[pb:pb + 64, cs0:cs0 + 128],
                                rec[osl][pb:pb + 64, :], ALU.mult, r=[('psO', osl), ('at_rec', osl, hh)],
                                w=[('at_a', bsl)])
                fin_q = []
                fin_due = []
                LA = 2
                FD = 3
                for i in range(len(items) + LA + FD + 1):
                    if i < len(items):
                        stage_s(items[i])
                    if 0 <= i - LA < len(items):
                        itp = items[i - LA]
                        if itp['ki'] == 0:
                            for ent in [e_ for e_ in fin_due if e_[1][0] == itp['osl']]:
                                fin_due.remove(ent)
                                finalize(*ent[1])
                        stage_pv(itp)
                        while fin_q:
                            fin_due.append((i + FD, fin_q.pop(0)))
                    while fin_due and fin_due[0][0] <= i:
                        finalize(*fin_due.pop(0)[1])
                assert not fin_due and not fin_q
                a0 = CTX if last else 0
                P.dma('act', S['attnT'].ap()[hp * 128:(hp + 1) * 128, a0:T], aT[bsl][:, a0:T], r=[('at_a', bsl)])

    def phase_s5(self, l):
        nc, P, I, S = self.nc, self.P, self.I, self.S
        Lc = LCH
        TWO_PI = 2.0 * math.pi
        with ExitStack() as st:
            sb = lambda n, s, d: st.enter_context(nc.sbuf_tensor(f"{n}_u{self.uid()}", s, d))
            prm = sb("s5_prm", [128, 3, 32], F32)
            names = ['dt', 'a', 'adt', 'bdt', 'r1', 'kf', 'red', 'sn', 'shf', 'cs', 'nr', 'ni', 'den',
                     'cre', 'cim', 't1', 't2']
            A = {n: sb(f"s5_{n}", [128, 32], F32) for n in names}
            ki = sb("s5_ki", [128, 32], I32)
            phr = sb("s5_phr", [128, 9, 32], F32)
            phi = sb("s5_phi", [128, 9, 32], F32)
            dsk = sb("s5_dsk", [128, 4], F32)
            zero = sb("s5_zero", [128, Lc], F32)
            P.dma('sp', prm[:], I['s5p'].ap()[l], w=['prm'])
            P.dma('sp', dsk[:], I['s5d'].ap()[l], w=['dsk'])
            P.op('dve', lambda e: e.memset(zero[:], 0.0), w=['zero'])
            K_ = ['prm']
            lre, lim, lst = prm[:, 0, :], prm[:, 1, :], prm[:, 2, :]
            a = lambda n: A[n][:]
            self.act(a('dt'), lst, AF.Exp, r=K_, w=K_)
            self.ts('dve', a('a'), lre, -1e-4, None, ALU.min, None, r=K_, w=K_)
            self.tt('dve', a('adt'), a('a'), a('dt'), ALU.mult, r=K_, w=K_)
            self.tt('dve', a('bdt'), lim, a('dt'), ALU.mult, r=K_, w=K_)
            self.act(a('r1'), a('adt'), AF.Exp, r=K_, w=K_)
            self.ts('dve', a('kf'), a('bdt'), 1.0 / TWO_PI, None, ALU.mult, None, r=K_, w=K_)
            self.cp('dve', ki[:], a('kf'), r=K_, w=K_)
            self.cp('dve', a('kf'), ki[:], r=K_, w=K_)
            self.stt(a('red'), a('kf'), -TWO_PI, a('bdt'), ALU.mult, ALU.add, r=K_, w=K_)
            self.ts('dve', a('red'), a('red'), 3.141592, -3.141592, ALU.min, ALU.max, r=K_, w=K_)
            self.act(a('sn'), a('red'), AF.Sin, r=K_, w=K_)
            self.act(a('shf'), a('red'), AF.Sin, r=K_, w=K_, scale=0.5)
            self.tt('dve', a('cs'), a('shf'), a('shf'), ALU.mult, r=K_, w=K_)
            self.ts('dve', a('cs'), a('cs'), -2.0, 1.0, ALU.mult, ALU.add, r=K_, w=K_)
            self.tt('dve', a('nr'), a('r1'), a('cs'), ALU.mult, r=K_, w=K_)
            self.ts('dve', a('nr'), a('nr'), -1.0, None, ALU.add, None, r=K_, w=K_)
            self.tt('dve', a('ni'), a('r1'), a('sn'), ALU.mult, r=K_, w=K_)
            self.tt('dve', a('den'), a('a'), a('a'), ALU.mult, r=K_, w=K_)
            self.tt('dve', a('t1'), lim, lim, ALU.mult, r=K_, w=K_)
            self.tt('dve', a('den'), a('den'), a('t1'), ALU.add, r=K_, w=K_)
            P.op('dve', lambda e: e.reciprocal(a('den'), a('den')), r=K_, w=K_)
            self.tt('dve', a('t1'), a('nr'), a('a'), ALU.mult, r=K_, w=K_)
            self.tt('dve', a('t2'), a('ni'), lim, ALU.mult, r=K_, w=K_)
            self.tt('dve', a('t1'), a('t1'), a('t2'), ALU.add, r=K_, w=K_)
            self.tt('dve', a('cre'), a('t1'), a('den'), ALU.mult, r=K_, w=K_)
            self.tt('dve', a('t1'), a('ni'), a('a'), ALU.mult, r=K_, w=K_)
            self.tt('dve', a('t2'), a('nr'), lim, ALU.mult, r=K_, w=K_)
            self.tt('dve', a('t1'), a('t1'), a('t2'), ALU.subtract, r=K_, w=K_)
            self.tt('dve', a('cim'), a('t1'), a('den'), ALU.mult, r=K_, w=K_)
            self.cp('dve', phr[:, 0, :], a('cs'), r=K_, w=K_)
            self.cp('dve', phi[:, 0, :], a('sn'), r=K_, w=K_)
            for j in range(1, 9):
                self.tt('dve', a('t1'), phr[:, j - 1, :], phr[:, j - 1, :], ALU.mult, r=K_, w=K_)
                self.tt('dve', a('t2'), phi[:, j - 1, :], phi[:, j - 1, :], ALU.mult, r=K_, w=K_)
                self.tt('dve', phr[:, j, :], a('t1'), a('t2'), ALU.subtract, r=K_, w=K_)
                self.tt('dve', a('t1'), phr[:, j - 1, :], phi[:, j - 1, :], ALU.mult, r=K_, w=K_)
                self.ts('dve', phi[:, j, :], a('t1'), 2.0, None, ALU.mult, None, r=K_, w=K_)

            wu = [sb(f"s5_wu{i}", [128, 8, 128], BF16) for i in range(2)]
            hx = [sb(f"s5_hx{i}", [128, 8, 512], BF16) for i in range(2)]
            uT = [sb(f"s5_uT{i}", [128, T], BF16) for i in range(2)]
            BT = [sb(f"s5_BT{i}", [128, 16, 128], BF16) for i in range(2)]
            CM = [sb(f"s5_CM{i}", [128, 16, 128], BF16) for i in range(2)]
            ybuf = sb("s5_y", [128, T], F32)
            gq = [sb(f"s5_g{i}", [128, T], BF16) for i in range(2)]
            tab = [{n: sb(f"s5_tab{il}_{n}", [128, Lc], F32) for n in ('DTr', 'DTi', 'nDTi', 'nDTr', 'MTr', 'MTi', 'nMTi', 'Rc')}
                   for il in range(4)]
            ttmp = sb("s5_ttmp", [128, Lc], F32)
            NZ = 8
            zb = [{n: sb(f"s5_z{i}_{n}", [128, Lc], F32) for n in ('gr', 'gi')} for i in range(NZ)]
            zp = [{n: sb(f"s5_zp{i}_{n}", [128, Lc], BF16) for n in ('pa', 'pb', 'pc', 'pd')} for i in range(NZ)]
            db = [{n: sb(f"s5_d{i}_{n}", [128, Lc], F32) for n in ('t1', 't2', 't3', 't4')} for i in range(2)]
            hb = [[sb(f"s5_h{il}_{i}", [128, 4, Lc], BF16) for i in range(2)] for il in range(4)]
            identf = sb("s5_identf", [128, 128], F32)
            P.dma('sp', identf[:], I['ident'].ap(), w=['identf'])
            ini = [[sb(f"s5_ini{il}_{i}", [128, 2], F32) for i in range(2)] for il in range(4)]
            itmp = [sb(f"s5_itmp{i}", [128, 2], F32) for i in range(4)]
            nphi = sb("s5_nphi", [128, 32], F32)
            self.ts('dve', nphi[:], phi[:, 8, :], -1.0, None, ALU.mult, None, r=K_, w=K_)

            hsrc = S['hxT'].ap().rearrange("(k p) t -> p k t", p=128)
            win = I['w_in'].ap()[l].rearrange("(k p) c -> p k c", p=128)
            for q in range(4):
                us = self.rot('s5_u', 2)
                self.load_w(wu[us][:], win[:, :, OFF_U + q * 128: OFF_U + (q + 1) * 128], ('s5_wu', us))
                for d in range(2):
                    for c in range(2):
                        self.load_w(BT[us][:, d * 8 + c * 4: d * 8 + c * 4 + 4, :] if False else
                                    BT[us][:].rearrange("p (d i c) k -> p d i c k", d=2, i=4)[:, d, :, c, :],
                                    I['s5BT'].ap()[l, d, 4 * q:4 * q + 4, c].rearrange("i r k -> r i k"),
                                    ('s5_BT', us, d, c))
                        self.load_w(CM[us][:].rearrange("p (d i c) k -> p d i c k", d=2, i=4)[:, d, :, c, :],
                                    I['s5CM'].ap()[l, d, 4 * q:4 * q + 4, c].rearrange("i r k -> r i k"),
                                    ('s5_CM', us, d, c))
                BTv = BT[us][:].rearrange("p (d i c) k -> p d i c k", d=2, i=4)
                CMv = CM[us][:].rearrange("p (d i c) k -> p d i c k", d=2, i=4)
                ukey = ('s5_uT', us)
                for (t0, n) in self.TT512:
                    hs = self.rot('s5_hx', 2)
                    P.dma('sp', hx[hs][:, :, 0:n], hsrc[:, :, t0:t0 + n], w=[('s5_hx', hs)])
                    for k in range(8):
                        self.mm(self.ps[6][:, 0:n], wu[us][:, k, :], hx[hs][:, k, 0:n], k == 0, k == 7,
                                r=[('s5_wu', us), ('s5_hx', hs)], w=[('ps', 6)])
                    self.cp('act', uT[us][:, t0:t0 + n], self.ps[6][:, 0:n], r=[('ps', 6)], w=[ukey])
                for d in range(2):
                    for il in range(4):
                        c = d * 16 + 4 * q + il
                        tb = tab[il]
                        tk = ('s5_tab', il)
                        sc = lambda arr, j=None: (arr[:, c:c + 1] if j is None else arr[:, j, c:c + 1])
                        P.op('dve', lambda e, tb=tb: e.memset(tb['DTr'][:, 0:1], 1.0), w=[tk])
                        P.op('dve', lambda e, tb=tb: e.memset(tb['DTi'][:, 0:1], 0.0), w=[tk])
                        for j in range(8):
                            n = 1 << j
                            pr, pi = sc(phr, j), sc(phi, j)
                            self.ts('dve', ttmp[:, 0:n], tb['DTi'][:, 0:n], pi, None, ALU.mult, None,
                                    r=[tk, 'prm'], w=['ttmp'])
                            self.stt(tb['DTr'][:, n:2 * n], tb['DTr'][:, 0:n], pr, ttmp[:, 0:n], ALU.mult, ALU.subtract,
                                     r=[tk, 'prm', 'ttmp'], w=[tk])
                            self.ts('dve', ttmp[:, 0:n], tb['DTi'][:, 0:n], pr, None, ALU.mult, None,
                                    r=[tk, 'prm'], w=['ttmp'])
                            self.stt(tb['DTi'][:, n:2 * n], tb['DTr'][:, 0:n], pi, ttmp[:, 0:n], ALU.mult, ALU.add,
                                     r=[tk, 'prm', 'ttmp'], w=[tk])
                        cre, cim = sc(A['cre'][:]), sc(A['cim'][:])
                        self.ts('dve', ttmp[:], tb['DTi'][:], cim, None, ALU.mult, None, r=[tk, 'prm'], w=['ttmp'])
                        self.stt(tb['MTr'][:], tb['DTr'][:], cre, ttmp[:], ALU.mult, ALU.add, r=[tk, 'prm', 'ttmp'], w=[tk])
                        self.ts('dve', ttmp[:], tb['DTi'][:], cre, None, ALU.mult, None, r=[tk, 'prm'], w=['ttmp'])
                        self.stt(tb['MTi'][:], tb['DTr'][:], cim, ttmp[:], ALU.mult, ALU.subtract, r=[tk, 'prm', 'ttmp'], w=[tk])
                        self.ts('pool', tb['nDTi'][:], tb['DTi'][:], -1.0, None, ALU.mult, None, r=[tk], w=[tk])
                        self.ts('pool', tb['nDTr'][:], tb['DTr'][:], -1.0, None, ALU.mult, None, r=[tk], w=[tk])
                        self.ts('pool', tb['nMTi'][:], tb['MTi'][:], -1.0, None, ALU.mult, None, r=[tk], w=[tk])
                        self.ts('dve', tb['Rc'][:], zero[:], sc(A['r1'][:]), None, ALU.add, None, r=['zero', 'prm'], w=[tk])
                        P.op('dve', lambda e, il=il: e.memset(ini[il][0][:], 0.0), w=[('s5_ini', il, 0)])
                    order = list(range(NCH)) if d == 0 else [0] + list(range(NCH - 1, 0, -1))

                    def emit_y(kk, cs):
                        tok0 = cs * Lc
                        psY = self.ps[4]
                        ykey = ('ps', 4)
                        hsl = kk % 2
                        for il in range(4):
                            for j in range(4):
                                self.mm(psY[:, 0:Lc], CMv[:, d, il, j // 2, :], hb[il][hsl][:, j, :],
                                        il == 0 and j == 0, il == 3 and j == 3,
                                        r=[('s5_CM', us, d, j // 2), (('s5_h', il, hsl), j)], w=[ykey])
                        if d == 0:
                            self.stt(ybuf[:, tok0:tok0 + Lc], uT[us][:, tok0:tok0 + Lc], dsk[:, q:q + 1], psY[:, 0:Lc],
                                     ALU.mult, ALU.add, r=[ukey, 'dsk', ykey], w=[('s5_y', cs)])
                        else:
                            self.tt('dve', ybuf[:, tok0:tok0 + Lc], psY[:, 0:Lc], ybuf[:, tok0:tok0 + Lc], ALU.add,
                                    r=[ykey, ('s5_y', cs)], w=[('s5_y', cs)])

                    pend = None
                    for kk, cs in enumerate(order):
                        tok0 = cs * Lc
                        par = kk % 2
                        hsl = kk % 2
                        ctxs = []
                        def emit_bu(il_):
                            bs_ = self.rot('s5_psB', 2)
                            psB_ = self.ps[bs_]
                            for cc in range(2):
                                self.mm(psB_[:, cc * Lc:(cc + 1) * Lc], BTv[:, d, il_, cc, :], uT[us][:, tok0:tok0 + Lc],
                                        True, True, r=[('s5_BT', us, d, cc), ukey], w=[('ps', bs_)])
                            return bs_
                        nxt_bs = emit_bu(0)
                        for il in range(4):
                            bs = nxt_bs
                            if il < 3:
                                nxt_bs = emit_bu(il + 1)
                            psB = self.ps[bs]
                            bkey = ('ps', bs)
                            if d == 0:
                                bre, bim = psB[:, 0:Lc], psB[:, Lc:2 * Lc]
                            else:
                                bre, bim = psB[:, Lc - 1::-1][:, 0:Lc], psB[:, 2 * Lc - 1:Lc - 1:-1]
                            zs = self.rot('s5_z', NZ)
                            z, zk, tb, tk = zb[zs], ('s5_z', zs), tab[il], ('s5_tab', il)
                            self.tt('dve', zp[zs]['pa'][:], bre, tb['MTr'][:], ALU.mult, r=[bkey, tk], w=[(zk, 'pa')])
                            self.tt('dve', zp[zs]['pb'][:], bim, tb['nMTi'][:], ALU.mult, r=[bkey, tk], w=[(zk, 'pb')])
                            self.tt('dve', zp[zs]['pc'][:], bre, tb['MTi'][:], ALU.mult, r=[bkey, tk], w=[(zk, 'pc')])
                            self.tt('dve', zp[zs]['pd'][:], bim, tb['MTr'][:], ALU.mult, r=[bkey, tk], w=[(zk, 'pd')])
                            zbank = (2, 3, 5, 6)[il]
                            psZ = self.ps[zbank]
                            zkey = ('ps', zbank)
                            self.mm(psZ[:, 0:Lc], self.ident[:], zp[zs]['pa'][:], True, False, r=['ident', (zk, 'pa')], w=[zkey])
                            self.mm(psZ[:, 0:Lc], self.ident[:], zp[zs]['pb'][:], False, True, r=['ident', (zk, 'pb')], w=[zkey])
                            self.mm(psZ[:, Lc:2 * Lc], self.ident[:], zp[zs]['pc'][:], True, False, r=['ident', (zk, 'pc')], w=[zkey])
                            self.mm(psZ[:, Lc:2 * Lc], self.ident[:], zp[zs]['pd'][:], False, True, r=['ident', (zk, 'pd')], w=[zkey])
                            ctxs.append(dict(il=il, z=z, zk=zk, gk=('s5_g', zs), tb=tb, tk=tk, psZ=psZ, zkey=zkey,
                                             c=d * 16 + 4 * q + il))
                        for cx in ctxs:
                            z, tb, tk, gk, il, psZ, zkey = cx['z'], cx['tb'], cx['tk'], cx['gk'], cx['il'], cx['psZ'], cx['zkey']
                            ik = ('s5_ini', il, par)
                            iv = ini[il][par]
                            P.op('dve', lambda e, z=z, tb=tb, iv=iv, psZ=psZ: e.tensor_tensor_scan(
                                z['gr'][:], tb['Rc'][:], psZ[:, 0:Lc], iv[:, 0:1], ALU.mult, ALU.add),
                                r=[zkey, tk, ik], w=[(gk, 'r')])
                            P.op('dve', lambda e, z=z, tb=tb, iv=iv, psZ=psZ: e.tensor_tensor_scan(
                                z['gi'][:], tb['Rc'][:], psZ[:, Lc:2 * Lc], iv[:, 1:2], ALU.mult, ALU.add),
                                r=[zkey, tk, ik], w=[(gk, 'i')])
                        for cx in ctxs:
                            z, gk, il, c = cx['z'], cx['gk'], cx['il'], cx['c']
                            ink = ('s5_ini', il, 1 - par)
                            inx = ini[il][1 - par]
                            pr, pi, npi = phr[:, 8, c:c + 1], phi[:, 8, c:c + 1], nphi[:, c:c + 1]
                            gre, gie = z['gr'][:, Lc - 1:Lc], z['gi'][:, Lc - 1:Lc]
                            itk = ('itmp', il)
                            self.act(itmp[il][:, 0:1], gie, AF.Identity, r=[(gk, 'i'), 'prm'], w=[itk], scale=npi)
                            self.act(itmp[il][:, 1:2], gie, AF.Identity, r=[(gk, 'i'), 'prm'], w=[itk], scale=pr)
                            self.act(inx[:, 0:1], gre, AF.Identity, r=[(gk, 'r'), 'prm', itk], w=[ink], scale=pr,
                                     bias=itmp[il][:, 0:1])
                            self.act(inx[:, 1:2], gre, AF.Identity, r=[(gk, 'r'), 'prm', itk], w=[ink], scale=pi,
                                     bias=itmp[il][:, 1:2])
                        for cx in ctxs:
                            z, tb, tk, gk, il = cx['z'], cx['tb'], cx['tk'], cx['gk'], cx['il']
                            hk = ('s5_h', il, hsl)
                            hv = [hb[il][hsl][:, j, :] if d == 0 else hb[il][hsl][:, j, ::-1] for j in range(4)]
                            self.tt('pool', hv[0], z['gr'][:], tb['DTr'][:], ALU.mult, r=[(gk, 'r'), tk], w=[(hk, 0)])
                            self.tt('pool', hv[1], z['gi'][:], tb['nDTi'][:], ALU.mult, r=[(gk, 'i'), tk], w=[(hk, 1)])
                            self.tt('pool', hv[2], z['gr'][:], tb['nDTi'][:], ALU.mult, r=[(gk, 'r'), tk], w=[(hk, 2)])
                            self.tt('dve' if il < 2 else 'pool', hv[3], z['gi'][:], tb['nDTr'][:], ALU.mult, r=[(gk, 'i'), tk], w=[(hk, 3)])
                        if pend is not None:
                            emit_y(*pend)
                        pend = (kk, cs)
                    emit_y(*pend)
                gs_ = self.rot('s5_gq', 2)
                for cs in range(NCH):
                    self.act(gq[gs_][:, cs * Lc:(cs + 1) * Lc], ybuf[:, cs * Lc:(cs + 1) * Lc], AF.Gelu_apprx_tanh,
                             r=[('s5_y', cs)], w=[('s5_gq', gs_)])
                P.dma('act', S['gT'].ap()[q * 128:(q + 1) * 128, :], gq[gs_][:], r=[('s5_gq', gs_)])

    def phase_merge(self, l, src):
        nc, P, I, S = self.nc, self.P, self.I, self.S
        last = (l == DEPTH - 1)
        tiles = self.TILES[1:] if last else self.TILES
        with ExitStack() as st:
            sb = lambda n, s, d: st.enter_context(nc.sbuf_tensor(f"{n}_u{self.uid()}", s, d))
            wco = sb("mg_wco", [128, 4, D], BF16)
            wga = sb("mg_wga", [128, 4, D], BF16)
            wgb = sb("mg_wgb", [128, 4, D], BF16)
            wno = sb("mg_wno", [128, 4, D], BF16)
            wg = sb("mg_wg", [128, 8, 3 * D], BF16)
            hx = [sb(f"mg_hx{i}", [128, 8, 512], BF16) for i in range(2)]
            br = [[sb(f"mg_br{j}_{i}", [128, 4, 512], BF16) for i in range(2)] for j in range(3)]
            mg = [sb(f"mg_mg{i}", [128, 8, 512], BF16) for i in range(2)]
            sg = [sb(f"mg_sg{i}", [128, 512], F32) for i in range(4)]
            tm = [sb(f"mg_tm{i}", [128, 512], F32) for i in range(4)]
            acc = [sb(f"mg_acc{i}", [128, 512], F32) for i in range(2)]
            for wt_, nm in ((wco, 'conv_out'), (wga, 's5_glu_a'), (wgb, 's5_glu_b'), (wno, 'na_out')):
                for kc in range(4):
                    self.load_w(wt_[:, kc, :], I[nm].ap()[l][kc * 128:(kc + 1) * 128, :], ('mg_w', nm))
            win = I['w_in'].ap()[l].rearrange("(k p) c -> p k c", p=128)
            for k in range(8):
                self.load_w(wg[:, k, :], win[:, k, OFF_GA:OFF_GA + 3 * D], 'mg_wg')
            hsrc = S['hxT'].ap().rearrange("(k p) t -> p k t", p=128)
            bsrc = [S[nm].ap().rearrange("(k p) t -> p k t", p=128) for nm in ('convbT', 'gT', 'attnT')]
            mdst = S['mgT'].ap().rearrange("(k p) t -> p k t", p=128)
            for (t0, n) in tiles:
                hs = self.rot('mg_hx', 2)
                P.dma('sp', hx[hs][:, :, 0:n], hsrc[:, :, t0:t0 + n], w=[('mg_hx', hs)])
                for j in range(3):
                    P.dma('sp', br[j][hs][:, :, 0:n], bsrc[j][:, :, t0:t0 + n], w=[('mg_br', j, hs)])
                ms = self.rot('mg_mg', 2)
                for fo in range(8):
                    fs = slice(fo * 128, (fo + 1) * 128)

                    def proj(bank, w_, x_, nk, wkeys, xkey, coff=0):
                        for k in range(nk):
                            self.mm(self.ps[bank][:, 0:n], w_[:, k, coff + fo * 128: coff + (fo + 1) * 128],
                                    x_[:, k, 0:n], k == 0, k == nk - 1, r=wkeys + [xkey], w=[('ps', bank)])
                    hk = ('mg_hx', hs)
                    proj(0, wco, br[0][hs], 4, [('mg_w', 'conv_out')], ('mg_br', 0, hs))
                    proj(1, wg, hx[hs], 8, ['mg_wg'], hk, 0)
                    proj(2, wga, br[1][hs], 4, [('mg_w', 's5_glu_a')], ('mg_br', 1, hs))
                    proj(3, wgb, br[1][hs], 4, [('mg_w', 's5_glu_b')], ('mg_br', 1, hs))
                    proj(4, wg, hx[hs], 8, ['mg_wg'], hk, D)
                    proj(5, wno, br[2][hs], 4, [('mg_w', 'na_out')], ('mg_br', 2, hs))
                    proj(6, wg, hx[hs], 8, ['mg_wg'], hk, 2 * D)
                    a_ = self.rot('mg_acc', 2)
                    A = acc[a_][:, 0:n]
                    ak = ('mg_acc', a_)
                    sgs = [self.rot('mg_sg', 4) for _ in range(4)]
                    tms = [self.rot('mg_tm', 4) for _ in range(2)]
                    self.act(sg[sgs[0]][:, 0:n], self.ps[1][:, 0:n], AF.Sigmoid, r=[('ps', 1)], w=[('mg_sg', sgs[0])])
                    self.tt('dve', A, self.ps[0][:, 0:n], sg[sgs[0]][:, 0:n], ALU.mult,
                            r=[('ps', 0), ('mg_sg', sgs[0])], w=[ak])
                    self.act(sg[sgs[1]][:, 0:n], self.ps[3][:, 0:n], AF.Sigmoid, r=[('ps', 3)], w=[('mg_sg', sgs[1])])
                    self.tt('dve', tm[tms[0]][:, 0:n], self.ps[2][:, 0:n], sg[sgs[1]][:, 0:n], ALU.mult,
                            r=[('ps', 2), ('mg_sg', sgs[1])], w=[('mg_tm', tms[0])])
                    self.act(sg[sgs[2]][:, 0:n], self.ps[4][:, 0:n], AF.Sigmoid, r=[('ps', 4)], w=[('mg_sg', sgs[2])])
                    self.tt('pool', tm[tms[0]][:, 0:n], tm[tms[0]][:, 0:n], sg[sgs[2]][:, 0:n], ALU.mult,
                            r=[('mg_tm', tms[0]), ('mg_sg', sgs[2])], w=[('mg_tm', tms[0])])
                    self.tt('pool', A, A, tm[tms[0]][:, 0:n], ALU.add, r=[ak, ('mg_tm', tms[0])], w=[ak])
                    self.act(sg[sgs[3]][:, 0:n], self.ps[6][:, 0:n], AF.Sigmoid, r=[('ps', 6)], w=[('mg_sg', sgs[3])])
                    self.tt('dve', tm[tms[1]][:, 0:n], self.ps[5][:, 0:n], sg[sgs[3]][:, 0:n], ALU.mult,
                            r=[('ps', 5), ('mg_sg', sgs[3])], w=[('mg_tm', tms[1])])
                    self.tt('pool', mg[ms][:, fo, 0:n], A, tm[tms[1]][:, 0:n], ALU.add,
                            r=[ak, ('mg_tm', tms[1])], w=[('mg_mg', ms)])
                P.dma('act', mdst[:, :, t0:t0 + n], mg[ms][:, :, 0:n], r=[('mg_mg', ms)])
        P.barrier()
        with ExitStack() as st:
            sb = lambda n, s, d: st.enter_context(nc.sbuf_tensor(f"{n}_u{self.uid()}", s, d))
            wo = sb("mo_wo", [128, 8, D], BF16)
            wov = I['w_out'].ap()[l].rearrange("(k p) c -> p k c", p=128)
            scol2 = (sb("mo_sc", [128, 8], F32), 'mo_sc')
            mgt = [sb(f"mo_mg{i}", [128, 8, 512], BF16) for i in range(2)]
            xt = [sb(f"mo_x{i}", [128, D], F32) for i in range(2)]
            xn = [sb(f"mo_xn{i}", [128, D], F32) for i in range(2)]
            hb = [sb(f"mo_hb{i}", [128, D], BF16) for i in range(2)]
            hT = [sb(f"mo_hT{i}", [128, 8, 512], BF16) for i in range(2)]
            scr = self.norm_scratch(st, "mo")
            msrc = S['mgT'].ap().rearrange("(k p) t -> p k t", p=128)
            hdst = S['h2T'].ap().rearrange("(k p) t -> p k t", p=128)
            mods = None
            cur = None
            for (t0, n) in tiles:
                cond = 1 if t0 < CTX else 0
                if cur != cond:
                    if mods is None:
                        mods = self.mod_tiles(st, l, cond, ('A1', 'G2'))
                        gt2 = st.enter_context(nc.sbuf_tensor(f"mo_g2_u{self.uid()}", [128, D], F32))
                    else:
                        self.reload_mod(mods, l, cond, gt2)
                    self.load_col(scol2[0], l, cond, 'S2', 'mo_sc')
                    for k in range(8):
                        self.load_w(wo[:, k, :], wov[:, k, :], ('mo_wo', k))
                        self.tt('pool' if k % 2 else 'dve', wo[:, k, :], wo[:, k, :], mods['A1'][0][:], ALU.mult,
                                r=[('mo_wo', k), mods['A1'][1]], w=[('mo_wo', k)])
                    cur = cond
                hs = self.rot('mo_mg', 2)
                P.dma('sp', mgt[hs][:, :, 0:n], msrc[:, :, t0:t0 + n], w=[('mo_mg', hs)])
                ts_ = self.rot('mo_hT', 2)
                for sub in range(n // 128):
                    xs = self.rot('mo_x', 2)
                    r0 = t0 + sub * 128
                    P.dma('sp', xt[xs][:], src.ap()[r0:r0 + 128, :], w=[('mo_x', xs)])
                    for half in range(2):
                        for k in range(8):
                            self.mm(self.ps[half][:, :], mgt[hs][:, k, sub * 128:(sub + 1) * 128],
                                    wo[:, k, half * 512:(half + 1) * 512], k == 0, k == 7,
                                    r=[('mo_mg', hs), ('mo_wo', k)], w=[('ps', half)])
                        self.tt('dve', xn[xs][:, half * 512:(half + 1) * 512], self.ps[half][:, :],
                                xt[xs][:, half * 512:(half + 1) * 512], ALU.add,
                                r=[('ps', half), ('mo_x', xs)], w=[('mo_xn', xs)])
                    P.dma('act', S['xres'].ap()[r0:r0 + 128, :], xn[xs][:], r=[('mo_xn', xs)])
                    bs = self.rot('mo_hb', 2)
                    self.norm_sub(xn[xs][:], ('mo_xn', xs), mods['G2'], None, hb[bs][:], ('mo_hb', bs), scr[bs])
                    self.transpose_out(hb[bs], ('mo_hb', bs), hT[ts_], ('mo_hT', ts_), sub, scol=scol2)
                P.dma('act', hdst[:, :, t0:t0 + n], hT[ts_][:, :, 0:n], r=[('mo_hT', ts_)])

    def reload_mod(self, mods, l, cond, gtmp):
        base = (l * 2 + cond) * 6 * D
        idx = {'S1': 0, 'G1': 1, 'A1': 2, 'S2': 3, 'G2': 4, 'A2': 5}
        for n, (t, key) in mods.items():
            self.load_rep(t[:], self.S['modv'], base + idx[n] * D, key)
            if n in ('G1', 'G2'):
                gt = self.I['norm1_g'] if n == 'G1' else self.I['norm2_g']
                self.load_rep(gtmp[:], gt, l * D, 'modgtmp')
                self.stt(t[:], t[:], 1.0, gtmp[:], ALU.add, ALU.mult, r=[key, 'modgtmp'], w=[key])

    def phase_mlp(self, l):
        nc, P, I, S = self.nc, self.P, self.I, self.S
        last = (l == DEPTH - 1)
        tiles = self.TILES[1:] if last else self.TILES
        with ExitStack() as st:
            sb = lambda n, s, d: st.enter_context(nc.sbuf_tensor(f"{n}_u{self.uid()}", s, d))
            w1 = sb("ml_w1", [128, 8, 4 * D], BF16)
            w1v = I['mlp_w1'].ap()[l].rearrange("(k p) c -> p k c", p=128)
            for k in range(8):
                for hf in range(2):
                    self.load_w(w1[:, k, hf * 2048:(hf + 1) * 2048], w1v[:, k, hf * 2048:(hf + 1) * 2048], 'ml_w1')
            h2 = [sb(f"ml_h2{i}", [128, 8, 512], BF16) for i in range(2)]
            rl = [sb(f"ml_rl{i}", [128, 512], F32) for i in range(3)]
            hd = [sb(f"ml_hd{i}", [128, 8, 512], BF16) for i in range(2)]
            hsrc = S['h2T'].ap().rearrange("(k p) t -> p k t", p=128)
            ddst = S['hidT'].ap().rearrange("(k p) t -> p k t", p=128)
            for (t0, n) in tiles:
                hs = self.rot('ml_h2', 2)
                P.dma('sp', h2[hs][:, :, 0:n], hsrc[:, :, t0:t0 + n], w=[('ml_h2', hs)])
                for fg in range(4):
                    ds = self.rot('ml_hd', 2)
                    for fi in range(8):
                        fc = fg * 8 + fi
                        bank = self.rot('ml_bank', 6)
                        for k in range(8):
                            self.mm(self.ps[bank][:, 0:n], w1[:, k, fc * 128:(fc + 1) * 128], h2[hs][:, k, 0:n],
                                    k == 0, k == 7, r=['ml_w1', ('ml_h2', hs)], w=[('ps', bank)])
                        rs = self.rot('ml_rl', 3)
                        self.act(rl[rs][:, 0:n], self.ps[bank][:, 0:n], AF.Relu, r=[('ps', bank)], w=[('ml_rl', rs)])
                        self.tt('dve' if fi % 2 == 0 else 'pool', hd[ds][:, fi, 0:n], rl[rs][:, 0:n], rl[rs][:, 0:n],
                                ALU.mult, r=[('ml_rl', rs)], w=[('ml_hd', ds)])
                    P.dma('act', ddst[:, fg * 8:(fg + 1) * 8, t0:t0 + n], hd[ds][:, :, 0:n], r=[('ml_hd', ds)])
        P.barrier()
        with ExitStack() as st:
            sb = lambda n, s, d: st.enter_context(nc.sbuf_tensor(f"{n}_u{self.uid()}", s, d))
            w2 = sb("ml_w2", [128, 32, D], BF16)
            w2v = I['mlp_w2'].ap()[l].rearrange("(k p) c -> p k c", p=128)
            scol1 = (sb("ml_sc", [128, 8], F32), 'ml_sc')
            hdt = [sb(f"ml_hdt{i}", [128, 32, 256], BF16) for i in range(2)]
            xt = [sb(f"ml_x{i}", [128, D], F32) for i in range(2)]
            xn = [sb(f"ml_xn{i}", [128, D], F32) for i in range(2)]
            hb = [sb(f"ml_hb{i}", [128, D], BF16) for i in range(2)]
            hT = [sb(f"ml_hT{i}", [128, 8, 256], BF16) for i in range(2)]
            ob = [sb(f"ml_ob{i}", [128, D], F32) for i in range(2)]
            scr = self.norm_scratch(st, "ml")
            dsrc = S['hidT'].ap().rearrange("(k p) t -> p k t", p=128)
            hdst = S['hxT'].ap().rearrange("(k p) t -> p k t", p=128)
            fin = None
            if last:
                fin = sb("ml_fin", [128, D], F32)
                self.load_rep(fin[:], I['final_g'], 0, 'ml_fin')
            amods = None
            nmods = None
            cur = None
            tiles256 = []
            for (t0, n) in tiles:
                for h in range(n // 256):
                    tiles256.append((t0 + h * 256, 256))
            for (t0, n) in tiles256:
                cond = 1 if t0 < CTX else 0
                if cur != cond:
                    if amods is None:
                        amods = self.mod_tiles(st, l, cond, ('A2',))
                        gt1 = st.enter_context(nc.sbuf_tensor(f"ml_g1_u{self.uid()}", [128, D], F32))
                        if not last:
                            nmods = self.mod_tiles(st, l + 1, cond, ('G1',))
                    else:
                        self.reload_mod(amods, l, cond, gt1)
                        if not last:
                            self.reload_mod(nmods, l + 1, cond, gt1)
                    if not last:
                        self.load_col(scol1[0], l + 1, cond, 'S1', 'ml_sc')
                    for k in range(32):
                        self.load_w(w2[:, k, :], w2v[:, k, :], ('ml_w2', k))
                        self.tt('pool' if k % 2 else 'dve', w2[:, k, :], w2[:, k, :], amods['A2'][0][:], ALU.mult,
                                r=[('ml_w2', k), amods['A2'][1]], w=[('ml_w2', k)])
                    cur = cond
                hs = self.rot('ml_hdt', 2)
                for kq in range(4):
                    P.dma('sp', hdt[hs][:, kq * 8:(kq + 1) * 8, :], dsrc[:, kq * 8:(kq + 1) * 8, t0:t0 + n],
                          w=[('ml_hdt', hs)])
                ts_ = self.rot('ml_hT', 2)
                for sub in range(2):
                    xs = self.rot('ml_x', 2)
                    r0 = t0 + sub * 128
                    P.dma('sp', xt[xs][:], S['xres'].ap()[r0:r0 + 128, :], w=[('ml_x', xs)])
                    for half in range(2):
                        bank = self.rot('ml_bank2', 4)
                        for k in range(32):
                            self.mm(self.ps[bank][:, :], hdt[hs][:, k, sub * 128:(sub + 1) * 128],
                                    w2[:, k, half * 512:(half + 1) * 512], k == 0, k == 31,
                                    r=[('ml_hdt', hs), ('ml_w2', k)], w=[('ps', bank)])
                        self.tt('dve', xn[xs][:, half * 512:(half + 1) * 512], self.ps[bank][:, :],
                                xt[xs][:, half * 512:(half + 1) * 512], ALU.add,
                                r=[('ps', bank), ('ml_x', xs)], w=[('ml_xn', xs)])
                    bs = self.rot('ml_hb', 2)
                    if not last:
                        P.dma('act', S['xres'].ap()[r0:r0 + 128, :], xn[xs][:], r=[('ml_xn', xs)])
                        self.norm_sub(xn[xs][:], ('ml_xn', xs), nmods['G1'], None, hb[bs][:], ('ml_hb', bs), scr[bs])
                        self.transpose_out(hb[bs], ('ml_hb', bs), hT[ts_], ('ml_hT', ts_), sub, scol=scol1)
                    else:
                        self.norm_sub(xn[xs][:], ('ml_xn', xs), (fin, 'ml_fin'), None, ob[bs][:], ('ml_ob', bs), scr[bs])
                        P.dma('act', self.out.ap()[r0 - CTX:r0 - CTX + 128, :], ob[bs][:], r=[('ml_ob', bs)])
                if not last:
                    P.dma('act', hdst[:, :, t0:t0 + n], hT[ts_][:, :, 0:n], r=[('ml_hT', ts_)])

def _core_inputs(inp, sh, b):
    d = dict(sh)
    d['xin'] = np.ascontiguousarray(np.concatenate([inp['ctx'][b], inp['x'][b]], axis=0), dtype=np.float32)
    cT = np.stack([inp['c'][b].reshape(8, 128).T, inp['c_ctx'].reshape(8, 128).T], axis=2)
    d['cT'] = np.ascontiguousarray(cT.reshape(128, 16), dtype=np.float32)
    return d


def kernel(**inputs):
    inp = {k: np.asarray(v) for k, v in inputs.items()}
    sh = _prep_shared(inp)
    bld = Builder()
    nc = bld.build()
    in_maps = [_core_inputs(inp, sh, b) for b in range(8)]
    in_maps = [{k: m[k] for k in bld.inputs} for m in in_maps]
    res = run_bass_kernel_spmd(nc, in_maps, core_ids=list(range(8)))
    return np.stack([np.asarray(r['out']) for r in res.results], axis=0).astype(np.float32)
```

```python
import math
from contextlib import ExitStack
import numpy as np
import concourse.bass as bass
import concourse.mybir as mybir
from concourse.bass_utils import run_bass_kernel_spmd

F32 = mybir.dt.float32
BF16 = mybir.dt.bfloat16
I32 = mybir.dt.int32
ALU = mybir.AluOpType
AF = mybir.ActivationFunctionType

SEM_LIMIT = 30000
SAME_ENG_WAITS = True
N_DMA_SEMS = 40

DEPTH = 4
D = 1024
T = 4352
CTX = 256
SEQ = 4096
NIN = 6656
OFF_XA, OFF_XB, OFF_XC, OFF_U, OFF_Q, OFF_K, OFF_V, OFF_GA, OFF_GB, OFF_GC = (
    0, 512, 1024, 1536, 2048, 2560, 3072, 3584, 4608, 5632)
EPS = 1e-6
NTYPE = 21
LCH = 256
NCH = T // LCH


class Prog:
    def __init__(self, nc):
        self.nc = nc
        self.ops = []
        self.last_w = {}
        self.readers = {}
        self.engs = {'pe': nc.tensor, 'dve': nc.vector, 'act': nc.scalar,
                     'pool': nc.gpsimd, 'sp': nc.sync}
        self._bar_from = 0

    def op(self, eng, fn, r=(), w=(), dma=False):
        i = len(self.ops)
        deps = set()
        for k in r:
            if k in self.last_w:
                deps.add(self.last_w[k])
        for k in w:
            if k in self.last_w:
                deps.add(self.last_w[k])
            for j in self.readers.get(k, ()):
                deps.add(j)
        fd = set()
        for j in deps:
            oj = self.ops[j]
            if oj['eng'] == eng and not oj['dma'] and not dma:
                if eng == 'pe' or not SAME_ENG_WAITS:
                    continue
                israw = any(self.last_w.get(k) == j for k in list(r) + list(w))
                if not israw:
                    continue
            fd.add(j)
        for k in w:
            self.last_w[k] = i
            self.readers[k] = []
        for k in r:
            self.readers.setdefault(k, []).append(i)
        self.ops.append(dict(eng=eng, fn=fn, deps=fd, dma=dma, sig=False))
        return i

    def dma(self, q, out, in_, r=(), w=(), **kw):
        return self.op(q, lambda e: e.dma_start(out=out, in_=in_, **kw), r, w, dma=True)

    def barrier(self):
        deps = set()
        lastc = {}
        for idx, o in enumerate(self.ops):
            if o['fn'] is None:
                continue
            if o['dma']:
                if idx >= self._bar_from:
                    deps.add(idx)
            else:
                lastc[o['eng']] = idx
        deps |= set(lastc.values())
        self._bar_from = len(self.ops)
        for e in self.engs:
            self.ops.append(dict(eng=e, fn=None, deps=set(deps), dma=False, sig=False))
        self.last_w = {}
        self.readers = {}

    def emit(self):
        nc = self.nc
        ops = self.ops
        for o in ops:
            for j in o['deps']:
                ops[j]['sig'] = True
        dma_sems = [nc.alloc_semaphore(name=f"dq{i}") for i in range(N_DMA_SEMS)]
        dma_cnt = [0] * N_DMA_SEMS
        dma_last = [None] * N_DMA_SEMS
        eng_sem = {}
        eng_cnt = {}
        nsem = [0]

        def new_eng_sem(e):
            nsem[0] += 1
            eng_sem[e] = nc.alloc_semaphore(name=f"s_{e}_{nsem[0]}")
            eng_cnt[e] = 0

        for e in self.engs:
            new_eng_sem(e)
        rr = 0
        for idx, o in enumerate(ops):
            if o['dma']:
                s = rr % N_DMA_SEMS
                rr += 1
                if dma_cnt[s] + 16 > SEM_LIMIT:
                    if dma_last[s] is not None:
                        o['deps'].add(dma_last[s])
                    dma_sems[s] = nc.alloc_semaphore(name=f"dq{s}_{idx}")
                    dma_cnt[s] = 0
                    dma_last[s] = None
                if dma_last[s] is not None:
                    o['deps'].add(dma_last[s])
                dma_cnt[s] += 16
                o['done'] = (dma_sems[s], dma_cnt[s])
                dma_last[s] = idx
            elif o['sig'] and o['fn'] is not None:
                e = o['eng']
                if eng_cnt[e] + 1 > SEM_LIMIT:
                    new_eng_sem(e)
                eng_cnt[e] += 1
                o['done'] = (eng_sem[e], eng_cnt[e])
            else:
                o['done'] = None

        def resolve(j, acc, seen):
            if j in seen:
                return
            seen.add(j)
            oj = ops[j]
            if oj['done'] is not None:
                acc.add(j)
            elif oj['fn'] is None:
                for jj in oj['deps']:
                    resolve(jj, acc, seen)
            else:
                raise RuntimeError("dep on unsignaled op")

        per_eng = {e: [] for e in self.engs}
        for idx, o in enumerate(ops):
            per_eng[o['eng']].append(idx)
        self.n_inst = {e: len(v) for e, v in per_eng.items()}
        with nc.Block() as block:
            def make(e):
                def body(eng):
                    waited = {}
                    for idx in per_eng[e]:
                        o = ops[idx]
                        acc = set()
                        seen = set()
                        for j in o['deps']:
                            resolve(j, acc, seen)
                        need = {}
                        for j in acc:
                            sem, val = ops[j]['done']
                            key = id(sem)
                            if waited.get(key, 0) >= val:
                                continue
                            if key not in need or need[key][1] < val:
                                need[key] = (sem, val)
                        for key, (sem, val) in need.items():
                            eng.wait_ge(sem, val)
                            waited[key] = val
                        if o['fn'] is not None:
                            inst = o['fn'](eng)
                            if o['done'] is not None:
                                sem, val = o['done']
                                inst.then_inc(sem, 16 if o['dma'] else 1)
                return body
            block.tensor(make('pe'))
            block.vector(make('dve'))
            block.scalar(make('act'))
            block.gpsimd(make('pool'))
            block.sync(make('sp'))


def _na_tile_plan():
    types = {}
    plan = []
    for j in range(32):
        r0a = min(max(2 * j - 4, 0), 56)
        r0b = min(max(2 * j + 1 - 4, 0), 56)
        tlo = r0a // 2
        thi = (r0b + 7) // 2
        lst = []
        for t in range(tlo, thi + 1):
            key = (t - j, r0a - 2 * j, r0b - 2 * j)
            if key not in types:
                types[key] = len(types)
            lst.append((t, types[key]))
        plan.append(lst)
    return plan, types


def _na_bias_index():
    plan, types = _na_tile_plan()
    nt = len(types)
    idx_r = np.zeros((nt, 128, 128), np.int64)
    idx_c = np.zeros((nt, 128, 128), np.int64)
    mask = np.zeros((nt, 128, 128), np.float32)
    col = np.arange(64)
    cs = np.clip(col - 8, 0, 48)
    for (delta, ra, rb), ty in types.items():
        for qr2 in range(2):
            r0rel = (ra, rb)[qr2]
            for kr2 in range(2):
                krel = 2 * delta + kr2
                dr = krel - qr2
                row_ok = (krel >= r0rel) and (krel < r0rel + 8)
                for qc in range(64):
                    kc = col
                    ok = row_ok & (kc >= cs[qc]) & (kc < cs[qc] + 16)
                    q = qr2 * 64 + qc
                    k = kr2 * 64 + kc
                    idx_r[ty, k, q] = np.clip(dr + 7, 0, 14)
                    idx_c[ty, k, q] = np.clip(kc - qc + 15, 0, 30)
                    mask[ty, k, q] = np.where(ok, 0.0, -1e30)
    return plan, nt, idx_r, idx_c, mask


_NA = None


def _na():
    global _NA
    if _NA is None:
        _NA = _na_bias_index()
    return _NA


def _prep_shared(inp):
    L = DEPTH
    sh = {}
    f = lambda a: np.ascontiguousarray(a, dtype=np.float32)
    for k in ('w_mod', 'w_in', 'conv_out', 's5_glu_a', 's5_glu_b', 'na_out', 'w_out',
              'mlp_w1', 'mlp_w2'):
        sh[k] = f(inp[k])
    sh['b_mod'] = f(inp['b_mod'])
    sh['norm1_g'] = f(inp['norm1_g'])
    sh['norm2_g'] = f(inp['norm2_g'])
    sh['final_g'] = f(inp['final_norm_g']).reshape(1, D)
    sh['conv_w'] = f(inp['conv_w'].reshape(L, 3, 4, 128).transpose(0, 3, 2, 1))
    def gp(a):
        a = a.reshape(L, 2, 16, 2, 64)
        return a.transpose(0, 3, 4, 1, 2).reshape(L, 128, 32)
    ls = np.broadcast_to(inp['s5_log_step'][:, :, :, None], (L, 2, 32, 64))
    sh['s5p'] = f(np.stack([gp(inp['s5_lam_re']), gp(inp['s5_lam_im']), gp(ls)], axis=2))
    def bt(B):
        out = np.zeros((L, 2, 16, 128, 128), np.float32)
        for i in range(16):
            for g2 in range(2):
                g = 2 * i + g2
                gl = g % 8
                out[:, :, i, gl * 16:(gl + 1) * 16, g2 * 64:(g2 + 1) * 64] = \
                    B[:, :, g].transpose(0, 1, 3, 2)
        return out
    sh['s5BT'] = f(np.stack([bt(inp['s5_b_re']), bt(inp['s5_b_im'])], axis=3))
    def cm(C):
        out = np.zeros((L, 2, 16, 128, 128), np.float32)
        for i in range(16):
            for g2 in range(2):
                g = 2 * i + g2
                gl = g % 8
                out[:, :, i, g2 * 64:(g2 + 1) * 64, gl * 16:(gl + 1) * 16] = \
                    C[:, :, g].transpose(0, 1, 3, 2)
        return out
    sh['s5CM'] = f(np.stack([cm(inp['s5_c_re']), cm(inp['s5_c_im'])], axis=3))
    sh['s5d'] = f(inp['s5_d'].reshape(L, 4, 128).transpose(0, 2, 1))
    plan, nt, idx_r, idx_c, mask = _na()
    rpb = inp['na_rpb']
    sh['rpbg'] = f(rpb[:, :, idx_r, idx_c])
    sh['na_mask'] = f(mask)
    sh['ident'] = np.eye(128, dtype=np.float32)
    return sh


class Builder:
    def __init__(self, layers=DEPTH, dbg=(), stop=None, only=None):
        self.layers = layers
        self.stop = stop
        self.only = only
        self.dbg = set(dbg)
        nc = bass.Bass("TRN2", target_bir_lowering=False)
        self.nc = nc
        self.P = Prog(nc)
        self.inputs = {}
        self.cnt = {}

    def din(self, name, shape, dt=F32):
        t = self.nc.dram_tensor(name, list(shape), dt, kind="ExternalInput")
        self.inputs[name] = t
        return t

    def dscr(self, name, shape, dt):
        kind = "ExternalOutput" if name in self.dbg else "Internal"
        return self.nc.dram_tensor(name, list(shape), dt, kind=kind)

    def uid(self):
        self._uid = getattr(self, '_uid', 0) + 1
        return self._uid

    def rot(self, name, n):
        c = self.cnt.get(name, 0)
        self.cnt[name] = c + 1
        return c % n

    def mm(self, out, lhsT, rhs, start, stop, r, w):
        self.P.op('pe', lambda e: e.matmul(out, lhsT, rhs, start=start, stop=stop), r, w)

    def act(self, out, in_, func, r, w, **kw):
        self.P.op('act', lambda e: e.activation(out, in_, func, **kw), r, w)

    def tt(self, eng, out, a, b, op, r, w):
        self.P.op(eng, lambda e: e.tensor_tensor(out, a, b, op), r, w)

    def ts(self, eng, out, a, s1, s2, op0, op1, r, w):
        if op1 is None:
            self.P.op(eng, lambda e: e.tensor_scalar(out, a, s1, None, op0), r, w)
        else:
            self.P.op(eng, lambda e: e.tensor_scalar(out, a, s1, s2, op0, op1), r, w)

    def stt(self, out, a, s, b, op0, op1, r, w):
        self.P.op('dve', lambda e: e.scalar_tensor_tensor(out, a, s, b, op0, op1), r, w)

    def cp(self, eng, out, in_, r, w):
        if eng == 'act':
            self.P.op('act', lambda e: e.activation(out, in_, AF.Copy), r, w)
        else:
            self.P.op(eng, lambda e: e.tensor_copy(out, in_), r, w)

    def build(self):
        nc, P = self.nc, self.P
        L = DEPTH
        nt = _na()[1]
        self.nt = nt
        I = {}
        I['xin'] = self.din('xin', [T, D])
        I['cT'] = self.din('cT', [128, 16])
        I['w_mod'] = self.din('w_mod', [L, D, 6 * D])
        I['b_mod'] = self.din('b_mod', [L, 6 * D])
        I['norm1_g'] = self.din('norm1_g', [L, D])
        I['norm2_g'] = self.din('norm2_g', [L, D])
        I['final_g'] = self.din('final_g', [1, D])
        I['w_in'] = self.din('w_in', [L, D, NIN])
        I['conv_w'] = self.din('conv_w', [L, 128, 4, 3])
        I['conv_out'] = self.din('conv_out', [L, 512, D])
        I['s5p'] = self.din('s5p', [L, 128, 3, 32])
        I['s5BT'] = self.din('s5BT', [L, 2, 16, 2, 128, 128])
        I['s5CM'] = self.din('s5CM', [L, 2, 16, 2, 128, 128])
        I['s5d'] = self.din('s5d', [L, 128, 4])
        I['s5_glu_a'] = self.din('s5_glu_a', [L, 512, D])
        I['s5_glu_b'] = self.din('s5_glu_b', [L, 512, D])
        I['rpbg'] = self.din('rpbg', [L, 8, nt, 128, 128])
        I['na_mask'] = self.din('na_mask', [nt, 128, 128])
        I['na_out'] = self.din('na_out', [L, 512, D])
        I['w_out'] = self.din('w_out', [L, D, D])
        I['mlp_w1'] = self.din('mlp_w1', [L, D, 4 * D])
        I['mlp_w2'] = self.din('mlp_w2', [L, 4 * D, D])
        I['ident'] = self.din('ident', [128, 128])
        self.I = I
        self.out = nc.dram_tensor('out', [SEQ, D], F32, kind="ExternalOutput")
        S = {}
        S['xres'] = self.dscr('xres', [T, D], F32)
        S['hxT'] = self.dscr('hxT', [D, T], BF16)
        S['h2T'] = self.dscr('h2T', [D, T], BF16)
        S['convbT'] = self.dscr('convbT', [512, T], BF16)
        S['gT'] = self.dscr('gT', [512, T], BF16)
        S['attnT'] = self.dscr('attnT', [512, T], BF16)
        S['modv'] = self.dscr('modv', [L, 2, 6 * D], F32)
        S['mgT'] = self.dscr('mgT', [D, T], BF16)
        S['hidT'] = self.dscr('hidT', [4 * D, T], BF16)
        self.S = S

        with ExitStack() as gs:
            self.ps = [gs.enter_context(nc.psum_tensor(f"ps{i}", [128, 512], F32))
                       for i in range(7)]
            self.psT = gs.enter_context(nc.psum_tensor("psT", [128, 1024], BF16))
            self.ident = gs.enter_context(nc.sbuf_tensor("ident_sb", [128, 128], BF16))
            self.ones = gs.enter_context(nc.sbuf_tensor("ones_sb", [128, 128], BF16))
            P.dma('pool', self.ident[:], I['ident'].ap(), w=['ident'])
            P.op('dve', lambda e: e.memset(self.ones[:], 1.0), w=['ones'])
            P.barrier()
            seq = [('adaln', lambda: self.phase_adaln()),
                   ('norm', lambda: self.phase_norm(0, src=I['xin'], kind='n1'))]
            for l in range(self.layers):
                seq += [(f'conv{l}', lambda l=l: self.phase_conv(l)),
                        (f'attn{l}', lambda l=l: self.phase_attn(l)),
                        (f's5{l}', lambda l=l: self.phase_s5(l)),
                        (f'merge{l}', lambda l=l: self.phase_merge(l, src=(I['xin'] if l == 0 else S['xres']))),
                        (f'mlp{l}', lambda l=l: self.phase_mlp(l))]
            for name, fn in seq:
                if self.only is not None and name not in self.only:
                    continue
                fn()
                P.barrier()
                if name == self.stop:
                    break
            P.emit()
        return nc

    def phase_adaln(self):
        nc, P, I, S = self.nc, self.P, self.I, self.S
        with ExitStack() as st:
            sb = lambda n, s, d: st.enter_context(nc.sbuf_tensor(f"{n}_u{self.uid()}", s, d))
            cT = sb("ad_cT", [128, 16], F32)
            sil = sb("ad_sil", [128, 16], BF16)
            wt = [sb(f"ad_w{i}", [128, 8, 512], BF16) for i in range(3)]
            bm = sb("ad_bm", [2, 6 * D], F32)
            row = [sb(f"ad_row{i}", [2, 512], F32) for i in range(2)]
            P.dma('sp', cT[:], I['cT'].ap(), w=['cT'])
            self.act(sil[:], cT[:], AF.Silu, r=['cT'], w=['sil'])
            silv = sil[:].rearrange("p (k j) -> p k j", j=2)
            for l in range(DEPTH):
                bsrc = bass.AP(I['b_mod'], l * 6 * D, [[0, 2], [1, 6 * D]])
                P.dma('sp', bm[:], bsrc, w=['bm'])
                wv = I['w_mod'].ap()[l].rearrange("(k p) c -> p k c", p=128)
                for ct in range(12):
                    s = self.rot('adw', 3)
                    P.dma('pool', wt[s][:], wv[:, :, ct * 512:(ct + 1) * 512], w=[('adw', s)])
                    pst = self.ps[ct % 2]
                    for k in range(8):
                        self.mm(pst[0:2, :], silv[:, k, :], wt[s][:, k, :], k == 0, k == 7,
                                r=['sil', ('adw', s)], w=[('ps', ct % 2)])
                    rs = self.rot('adrow', 2)
                    self.tt('dve', row[rs][:], pst[0:2, :], bm[:, ct * 512:(ct + 1) * 512], ALU.add,
                            r=[('ps', ct % 2), 'bm'], w=[('adrow', rs)])
                    P.dma('sp', S['modv'].ap()[l][:, ct * 512:(ct + 1) * 512], row[rs][:],
                          r=[('adrow', rs)])

    def load_rep(self, dst, tensor, offset, key):
        self.P.dma('sp', dst, bass.AP(tensor, offset, [[0, 128], [1, D]]), w=[key])

    def mod_tiles(self, st, l, cond, names, gsrc=None):
        nc = self.nc
        res = {}
        base = (l * 2 + cond) * 6 * D
        idx = {'S1': 0, 'G1': 1, 'A1': 2, 'S2': 3, 'G2': 4, 'A2': 5}
        for n in names:
            t = st.enter_context(nc.sbuf_tensor(f"mod_{n}_{cond}_{self.rot('modt', 1 << 30)}", [128, D], F32))
            key = ('mod', n, cond)
            self.load_rep(t[:], self.S['modv'], base + idx[n] * D, key)
            if n in ('G1', 'G2'):
                g = st.enter_context(nc.sbuf_tensor(f"modg_{n}_{cond}_{self.rot('modt', 1 << 30)}", [128, D], F32))
                gt = self.I['norm1_g'] if n == 'G1' else self.I['norm2_g']
                self.load_rep(g[:], gt, l * D, ('modg', n, cond))
                self.stt(t[:], t[:], 1.0, g[:], ALU.add, ALU.mult, r=[key, ('modg', n, cond)], w=[key])
            res[n] = (t, key)
        return res

    def norm_sub(self, xt, xkey, G, S_, hb, hkey, scr):
        junk, ss, rstd, tmp = scr['junk'], scr['ss'], scr['rstd'], scr['tmp']
        k = scr['k']
        self.act(junk[:], xt, AF.Square, r=[xkey], w=[('nj', k), ('ss', k)], accum_out=ss[:])
        self.ts('dve', rstd[:], ss[:], 1.0 / D, EPS, ALU.mult, ALU.add, r=[('ss', k)], w=[('rstd', k)])
        self.act(rstd[:], rstd[:], AF.Sqrt, r=[('rstd', k)], w=[('rstd', k)])
        self.P.op('dve', lambda e: e.reciprocal(rstd[:], rstd[:]), r=[('rstd', k)], w=[('rstd', k)])
        if S_ is None:
            self.stt(hb, xt, rstd[:], G[0][:], ALU.mult, ALU.mult, r=[xkey, ('rstd', k), G[1]], w=[hkey])
        else:
            self.stt(tmp[:], xt, rstd[:], G[0][:], ALU.mult, ALU.mult, r=[xkey, ('rstd', k), G[1]],
                     w=[('ntmp', k)])
            self.tt('pool', hb, tmp[:], S_[0][:], ALU.add, r=[('ntmp', k), S_[1]], w=[hkey])

    def norm_scratch(self, st, tag):
        nc = self.nc
        out = []
        for k in range(2):
            out.append(dict(
                junk=st.enter_context(nc.sbuf_tensor(f"{tag}_junk{k}_u{self.uid()}", [128, D], BF16)),
                ss=st.enter_context(nc.sbuf_tensor(f"{tag}_ss{k}_u{self.uid()}", [128, 1], F32)),
                rstd=st.enter_context(nc.sbuf_tensor(f"{tag}_rstd{k}_u{self.uid()}", [128, 1], F32)),
                tmp=st.enter_context(nc.sbuf_tensor(f"{tag}_tmp{k}_u{self.uid()}", [128, D], F32)),
                k=(tag, k)))
        return out

    def transpose_out(self, hb, hkey, hT, hTkey, sub, scol=None):
        for kc in range(8):
            self.P.op('pe', lambda e, kc=kc: e.transpose(self.psT[:, kc * 128:(kc + 1) * 128],
                                                          hb[:, kc * 128:(kc + 1) * 128], self.ident[:]),
                      r=[hkey, 'ident'], w=['psT'])
        if scol is None:
            self.cp('act', hT[:, :, sub * 128:(sub + 1) * 128],
                    self.psT[:].rearrange("p (k t) -> p k t", t=128), r=['psT'], w=[hTkey])
        else:
            for kc in range(8):
                self.act(hT[:, kc, sub * 128:(sub + 1) * 128], self.psT[:, kc * 128:(kc + 1) * 128], AF.Identity,
                         r=['psT', scol[1]], w=[hTkey], bias=scol[0][:, kc:kc + 1])

    def load_col(self, tile_, l, cond, name, key):
        idx = {'S1': 0, 'G1': 1, 'A1': 2, 'S2': 3, 'G2': 4, 'A2': 5}[name]
        base = (l * 2 + cond) * 6 * D + idx * D
        self.P.dma('sp', tile_[:], bass.AP(self.S['modv'], base, [[1, 128], [128, 8]]), w=[key],
                   allow_slow_non_contiguous=True)

    TILES = [(0, 256)] + [(256 + 512 * i, 512) for i in range(8)]

    def phase_norm(self, l, src, kind):
        nc, P, I, S = self.nc, self.P, self.I, self.S
        with ExitStack() as st:
            sb = lambda n, s, d: st.enter_context(nc.sbuf_tensor(f"{n}_u{self.uid()}", s, d))
            mods = [self.mod_tiles(st, l, c, ('G1',)) for c in (0, 1)]
            scols = []
            for c in (0, 1):
                t_ = sb(f"pn_sc{c}", [128, 8], F32)
                self.load_col(t_, l, c, 'S1', ('pn_sc', c))
                scols.append((t_, ('pn_sc', c)))
            xt = [sb(f"pn_x{i}", [128, D], F32) for i in range(3)]
            hb = [sb(f"pn_hb{i}", [128, D], BF16) for i in range(2)]
            hT = [sb(f"pn_hT{i}", [128, 8, 512], BF16) for i in range(2)]
            scr = self.norm_scratch(st, "pn")
            dst = S['hxT'].ap().rearrange("(k p) t -> p k t", p=128)
            for (t0, n) in self.TILES:
                cond = 1 if t0 < CTX else 0
                hs = self.rot('pn_hT', 2)
                for sub in range(n // 128):
                    xs = self.rot('pn_x', 3)
                    P.dma('sp', xt[xs][:], src.ap()[t0 + sub * 128:t0 + (sub + 1) * 128, :], w=[('pn_x', xs)])
                    bs = self.rot('pn_hb', 2)
                    self.norm_sub(xt[xs][:], ('pn_x', xs), mods[cond]['G1'], None,
                                  hb[bs][:], ('pn_hb', bs), scr[bs])
                    self.transpose_out(hb[bs], ('pn_hb', bs), hT[hs], ('pn_hT', hs), sub, scol=scols[cond])
                P.dma('act', dst[:, :, t0:t0 + n], hT[hs][:, :, 0:n], r=[('pn_hT', hs)])

    TT512 = [(512 * i, 512) for i in range(8)] + [(4096, 256)]

    def load_w(self, dst, src_ap, key, r=()):
        self.P.dma('pool', dst, src_ap, r=r, w=[key])

    def phase_conv(self, l):
        nc, P, I, S = self.nc, self.P, self.I, self.S
        with ExitStack() as st:
            sb = lambda n, s, d: st.enter_context(nc.sbuf_tensor(f"{n}_u{self.uid()}", s, d))
            wc = [sb(f"cv_w{i}", [128, 3, 8, 128], BF16) for i in range(2)]
            hx = [sb(f"cv_hx{i}", [128, 8, 512], BF16) for i in range(2)]
            vbs = [sb(f"cv_v{i}", [128, T + 4], F32) for i in range(2)]
            xbbs = [sb(f"cv_xb{i}", [128, T], F32) for i in range(2)]
            tmp = [sb(f"cv_tmp{i}", [128, 512], F32) for i in range(2)]
            acc = [sb(f"cv_acc{i}", [128, 1024], F32) for i in range(2)]
            ob = [sb(f"cv_o{i}", [128, T], BF16) for i in range(2)]
            cw = sb("cv_cw", [128, 12], F32)
            P.dma('sp', cw[:], I['conv_w'].ap()[l].rearrange("p q j -> p (q j)"), w=['cw'])
            for i in range(2):
                P.op('pool', lambda e, i=i: e.memset(vbs[i][:], 0.0), w=[('vb', i)])
            hsrc = S['hxT'].ap().rearrange("(k p) t -> p k t", p=128)
            win = I['w_in'].ap()[l].rearrange("(k p) c -> p k c", p=128)
            for q in range(4):
                ws = self.rot('cv_w', 2)
                vb, xbb = vbs[q % 2], xbbs[q % 2]
                vbk, xbk = ('vb', q % 2), ('xbb', q % 2)
                for j, off in enumerate((OFF_XA, OFF_XB, OFF_XC)):
                    self.load_w(wc[ws][:, j, :, :], win[:, :, off + q * 128: off + (q + 1) * 128], ('cv_w', ws, j))
                for (t0, n) in self.TT512:
                    hs = self.rot('cv_hx', 2)
                    P.dma('sp', hx[hs][:, :, 0:n], hsrc[:, :, t0:t0 + n], w=[('cv_hx', hs)])
                    for j in range(3):
                        for k in range(8):
                            self.mm(self.ps[j][:, 0:n], wc[ws][:, j, k, :], hx[hs][:, k, 0:n], k == 0, k == 7,
                                    r=[('cv_w', ws, j), ('cv_hx', hs)], w=[('ps', j)])
                    ts_ = self.rot('cv_tmp', 2)
                    self.cp('act', tmp[ts_][:, 0:n], self.ps[2][:, 0:n], r=[('ps', 2)], w=[('cv_tmp', ts_)])
                    segs = []
                    if t0 < CTX:
                        segs.append((t0, CTX - t0, 1 + t0))
                        segs.append((CTX, t0 + n - CTX, 3 + CTX))
                    else:
                        segs.append((t0, n, 3 + t0))
                    for (a0, an, c0) in segs:
                        self.tt('dve', vb[:, c0:c0 + an], self.ps[0][:, a0 - t0:a0 - t0 + an],
                                tmp[ts_][:, a0 - t0:a0 - t0 + an], ALU.mult,
                                r=[('ps', 0), ('cv_tmp', ts_), vbk], w=[vbk])
                    self.cp('act', xbb[:, t0:t0 + n], self.ps[1][:, 0:n], r=[('ps', 1)], w=[xbk])
                os_ = self.rot('cv_o', 2)
                pieces = [(0, 256, 1)] + [(256 + 1024 * i, 1024, 3 + 256 + 1024 * i) for i in range(4)]
                for (a0, an, c0) in pieces:
                    as_ = self.rot('cv_acc', 2)
                    A = acc[as_][:, 0:an]
                    ak = ('cv_acc', as_)
                    self.ts('dve', A, vb[:, c0 - 1:c0 - 1 + an], cw[:, q * 3:q * 3 + 1], None, ALU.mult, None,
                            r=[vbk, 'cw'], w=[ak])
                    self.stt(A, vb[:, c0:c0 + an], cw[:, q * 3 + 1:q * 3 + 2], A, ALU.mult, ALU.add,
                             r=[vbk, 'cw', ak], w=[ak])
                    self.stt(A, vb[:, c0 + 1:c0 + 1 + an], cw[:, q * 3 + 2:q * 3 + 3], A, ALU.mult, ALU.add,
                             r=[vbk, 'cw', ak], w=[ak])
                    self.tt('pool', ob[os_][:, a0:a0 + an], A, xbb[:, a0:a0 + an], ALU.mult,
                            r=[ak, xbk], w=[('cv_o', os_)])
                P.dma('act', S['convbT'].ap()[q * 128:(q + 1) * 128, :], ob[os_][:], r=[('cv_o', os_)])


    def phase_attn(self, l):
        nc, P, I, S = self.nc, self.P, self.I, self.S
        plan, nt = _na()[0], self.nt
        last = (l == DEPTH - 1)
        with ExitStack() as st:
            sb = lambda n, s, d: st.enter_context(nc.sbuf_tensor(f"{n}_u{self.uid()}", s, d))
            wq = [sb(f"at_w{i}", [128, 3, 8, 128], BF16) for i in range(2)]
            hx = [sb(f"at_hx{i}", [128, 8, 512], BF16) for i in range(2)]
            qT = [sb(f"at_q{i}", [128, 2, T], BF16) for i in range(2)]
            kT = [sb(f"at_k{i}", [128, T], BF16) for i in range(2)]
            V = [sb(f"at_v{i}", [128, 34, 128], BF16) for i in range(2)]
            aT = [sb(f"at_a{i}", [128, T], BF16) for i in range(2)]
            msk = sb("at_mask", [128, nt, 128], F32)
            rg = [sb(f"at_rg{i}", [128, 128], F32) for i in range(3)]
            bias = [sb(f"at_bias{i}", [128, 2, nt, 128], BF16) for i in range(2)]
            sc = [sb(f"at_sc{i}", [128, 256], F32) for i in range(3)]
            PT = [sb(f"at_pt{i}", [128, 256], BF16) for i in range(4)]
            rec = [sb(f"at_rec{i}", [128, 128], F32) for i in range(2)]
            P.dma('sp', msk[:], I['na_mask'].ap().rearrange("t k q -> k t q"), w=['msk'])
            for i in range(2):
                P.op('pool', lambda e, i=i: e.memset(qT[i][:], 0.0), w=[('at_qk', i, 0)])
            hsrc = S['hxT'].ap().rearrange("(k p) t -> p k t", p=128)
            win = I['w_in'].ap()[l].rearrange("(k p) c -> p k c", p=128)
            for hp in range(4):
                ws = self.rot('at_w', 2)
                bsl = self.rot('at_b', 2)
                for j, off in enumerate((OFF_Q, OFF_K, OFF_V)):
                    self.load_w(wq[ws][:, j, :, :], win[:, :, off + hp * 128: off + (hp + 1) * 128], ('at_w', ws, j))
                for hh in range(2):
                    for ty in range(nt):
                        rs = self.rot('at_rg', 3)
                        P.dma('sp', rg[rs][:], I['rpbg'].ap()[l, hp * 2 + hh, ty], w=[('at_rg', rs)])
                        self.tt('pool', bias[bsl][:, hh, ty, :], rg[rs][:], msk[:, ty, :], ALU.add,
                                r=[('at_rg', rs), 'msk'], w=[('at_bias', bsl)])
                for (t0, n) in self.TT512:
                    hs = self.rot('at_hx', 2)
                    P.dma('sp', hx[hs][:, :, 0:n], hsrc[:, :, t0:t0 + n], w=[('at_hx', hs)])
                    for j in range(2):
                        for k in range(8):
                            self.mm(self.ps[j][:, 0:n], wq[ws][:, j, k, :], hx[hs][:, k, 0:n], k == 0, k == 7,
                                    r=[('at_w', ws, j), ('at_hx', hs)], w=[('ps', j)])
                    self.cp('act', qT[bsl][0:64, 0, t0:t0 + n], self.ps[0][0:64, 0:n], r=[('ps', 0)],
                            w=[('at_qk', bsl, 0)])
                    self.cp('act', qT[bsl][64:128, 1, t0:t0 + n], self.ps[0][64:128, 0:n], r=[('ps', 0)],
                            w=[('at_qk', bsl, 0)])
                    self.cp('dve', kT[bsl][:, t0:t0 + n], self.ps[1][:, 0:n], r=[('ps', 1)], w=[('at_qk', bsl, 1)])
                    for sub in range(n // 128):
                        for k in range(8):
                            self.mm(self.ps[0][:, sub * 128:(sub + 1) * 128], hx[hs][:, k, sub * 128:(sub + 1) * 128],
                                    wq[ws][:, 2, k, :], k == 0, k == 7,
                                    r=[('at_w', ws, 2), ('at_hx', hs)], w=[('ps', 0)])
                    ti0 = t0 // 128
                    self.cp('act', V[bsl][:, ti0:ti0 + n // 128, :],
                            self.ps[0][:, 0:n].rearrange("p (s c) -> p s c", c=128), r=[('ps', 0)], w=[('at_v', bsl)])
                qlist = []
                if not last:
                    for qi in range(2):
                        qlist.append((qi * 128, [(0, None), (128, None)]))
                for j in range(32):
                    kt = [(CTX + t * 128, ty) for (t, ty) in plan[j]] + [(0, None), (128, None)]
                    qlist.append((CTX + j * 128, kt))
                items = []
                for (q0, kts) in qlist:
                    osl = self.rot('at_o', 2)
                    for ki, (k0, ty) in enumerate(kts):
                        items.append(dict(q0=q0, ki=ki, nk=len(kts), k0=k0, ty=ty, osl=osl))

                def stage_s(it):
                    ssl = self.rot('at_s', 3)
                    it['psl'] = self.rot('at_ptslot', 4)
                    psS = self.ps[ssl][:, 0:256]
                    skey = ('ps', ssl)
                    q0, k0, ty = it['q0'], it['k0'], it['ty']
                    self.mm(psS, kT[bsl][:, k0:k0 + 128], qT[bsl][:, :, q0:q0 + 128], True, True,
                            r=[('at_qk', bsl, 0), ('at_qk', bsl, 1)], w=[skey])
                    pk = ('at_pt', it['psl'])
                    if ty is None:
                        self.act(PT[it['psl']][:], psS, AF.Exp, r=[skey], w=[pk], scale=0.125)
                    else:
                        cs_ = self.rot('at_sc', 3)
                        self.stt(sc[cs_][:].rearrange("p (h q) -> p h q", h=2),
                                 psS.rearrange("p (h q) -> p h q", h=2), 0.125, bias[bsl][:, :, ty, :],
                                 ALU.mult, ALU.add, r=[skey, ('at_bias', bsl)], w=[('at_sc', cs_)])
                        self.act(PT[it['psl']][:], sc[cs_][:], AF.Exp, r=[('at_sc', cs_)], w=[pk])

                def stage_pv(it):
                    psl, osl, ki, nk, k0, q0 = it['psl'], it['osl'], it['ki'], it['nk'], it['k0'], it['q0']
                    psO = self.ps[3 + osl]
                    psU = self.ps[5 + osl]
                    pkey = ('at_pt', psl)
                    self.mm(psO[:, 0:256], V[bsl][:, k0 // 128, :], PT[psl][:], ki == 0, ki == nk - 1,
                            r=[('at_v', bsl), pkey], w=[('psO', osl)])
                    self.mm(psU[:, 0:256], self.ones[:], PT[psl][:], ki == 0, ki == nk - 1,
                            r=['ones', pkey], w=[('psU', osl)])
                    if ki == nk - 1:
                        fin_q.append((osl, q0))

                def finalize(osl, q0):
                    psO = self.ps[3 + osl]
                    psU = self.ps[5 + osl]
                    for hh in range(2):
                        pb = hh * 64
                        cs0 = hh * 128
                        if hh == 0:
                            self.P.op('dve', lambda e, pb=pb, cs0=cs0: e.reciprocal(
                                rec[osl][pb:pb + 64, :], psU[pb:pb + 64, cs0:cs0 + 128]),
                                r=[('psU', osl)], w=[('at_rec', osl, hh)])
                        else:
                            self.act(rec[osl][pb:pb + 64, :], psU[pb:pb + 64, cs0:cs0 + 128], AF.Ln,
                                     r=[('psU', osl)], w=[('at_rec', osl, hh)])
                            self.act(rec[osl][pb:pb + 64, :], rec[osl][pb:pb + 64, :], AF.Exp,
                                     r=[('at_rec', osl, hh)], w=[('at_rec', osl, hh)], scale=-1.0)
                        self.tt('dve', aT[bsl][pb:pb + 64, q0:q0 + 128], psO[pb:pb + 64, cs0:cs0 + 128],
                                rec[osl][pb:pb + 64, :], ALU.mult, r=[('psO', osl), ('at_rec', osl, hh)],
                                w=[('at_a', bsl)])
                fin_q = []
                fin_due = []
                LA = 2
                FD = 3
                for i in range(len(items) + LA + FD + 1):
                    if i < len(items):
                        stage_s(items[i])
                    if 0 <= i - LA < len(items):
                        itp = items[i - LA]
                        if itp['ki'] == 0:
                            for ent in [e_ for e_ in fin_due if e_[1][0] == itp['osl']]:
                                fin_due.remove(ent)
                                finalize(*ent[1])
                        stage_pv(itp)
                        while fin_q:
                            fin_due.append((i + FD, fin_q.pop(0)))
                    while fin_due and fin_due[0][0] <= i:
                        finalize(*fin_due.pop(0)[1])
                assert not fin_due and not fin_q
                a0 = CTX if last else 0
                P.dma('act', S['attnT'].ap()[hp * 128:(hp + 1) * 128, a0:T], aT[bsl][:, a0:T], r=[('at_a', bsl)])

    def phase_s5(self, l):
        nc, P, I, S = self.nc, self.P, self.I, self.S
        Lc = LCH
        TWO_PI = 2.0 * math.pi
        with ExitStack() as st:
            sb = lambda n, s, d: st.enter_context(nc.sbuf_tensor(f"{n}_u{self.uid()}", s, d))
            prm = sb("s5_prm", [128, 3, 32], F32)
            names = ['dt', 'a', 'adt', 'bdt', 'r1', 'kf', 'red', 'sn', 'shf', 'cs', 'nr', 'ni', 'den',
                     'cre', 'cim', 't1', 't2']
            A = {n: sb(f"s5_{n}", [128, 32], F32) for n in names}
            ki = sb("s5_ki", [128, 32], I32)
            phr = sb("s5_phr", [128, 9, 32], F32)
            phi = sb("s5_phi", [128, 9, 32], F32)
            dsk = sb("s5_dsk", [128, 4], F32)
            zero = sb("s5_zero", [128, Lc], F32)
            P.dma('sp', prm[:], I['s5p'].ap()[l], w=['prm'])
            P.dma('sp', dsk[:], I['s5d'].ap()[l], w=['dsk'])
            P.op('dve', lambda e: e.memset(zero[:], 0.0), w=['zero'])
            K_ = ['prm']
            lre, lim, lst = prm[:, 0, :], prm[:, 1, :], prm[:, 2, :]
            a = lambda n: A[n][:]
            self.act(a('dt'), lst, AF.Exp, r=K_, w=K_)
            self.ts('dve', a('a'), lre, -1e-4, None, ALU.min, None, r=K_, w=K_)
            self.tt('dve', a('adt'), a('a'), a('dt'), ALU.mult, r=K_, w=K_)
            self.tt('dve', a('bdt'), lim, a('dt'), ALU.mult, r=K_, w=K_)
            self.act(a('r1'), a('adt'), AF.Exp, r=K_, w=K_)
            self.ts('dve', a('kf'), a('bdt'), 1.0 / TWO_PI, None, ALU.mult, None, r=K_, w=K_)
            self.cp('dve', ki[:], a('kf'), r=K_, w=K_)
            self.cp('dve', a('kf'), ki[:], r=K_, w=K_)
            self.stt(a('red'), a('kf'), -TWO_PI, a('bdt'), ALU.mult, ALU.add, r=K_, w=K_)
            self.ts('dve', a('red'), a('red'), 3.141592, -3.141592, ALU.min, ALU.max, r=K_, w=K_)
            self.act(a('sn'), a('red'), AF.Sin, r=K_, w=K_)
            self.act(a('shf'), a('red'), AF.Sin, r=K_, w=K_, scale=0.5)
            self.tt('dve', a('cs'), a('shf'), a('shf'), ALU.mult, r=K_, w=K_)
            self.ts('dve', a('cs'), a('cs'), -2.0, 1.0, ALU.mult, ALU.add, r=K_, w=K_)
            self.tt('dve', a('nr'), a('r1'), a('cs'), ALU.mult, r=K_, w=K_)
            self.ts('dve', a('nr'), a('nr'), -1.0, None, ALU.add, None, r=K_, w=K_)
            self.tt('dve', a('ni'), a('r1'), a('sn'), ALU.mult, r=K_, w=K_)
            self.tt('dve', a('den'), a('a'), a('a'), ALU.mult, r=K_, w=K_)
            self.tt('dve', a('t1'), lim, lim, ALU.mult, r=K_, w=K_)
            self.tt('dve', a('den'), a('den'), a('t1'), ALU.add, r=K_, w=K_)
            P.op('dve', lambda e: e.reciprocal(a('den'), a('den')), r=K_, w=K_)
            self.tt('dve', a('t1'), a('nr'), a('a'), ALU.mult, r=K_, w=K_)
            self.tt('dve', a('t2'), a('ni'), lim, ALU.mult, r=K_, w=K_)
            self.tt('dve', a('t1'), a('t1'), a('t2'), ALU.add, r=K_, w=K_)
            self.tt('dve', a('cre'), a('t1'), a('den'), ALU.mult, r=K_, w=K_)
            self.tt('dve', a('t1'), a('ni'), a('a'), ALU.mult, r=K_, w=K_)
            self.tt('dve', a('t2'), a('nr'), lim, ALU.mult, r=K_, w=K_)
            self.tt('dve', a('t1'), a('t1'), a('t2'), ALU.subtract, r=K_, w=K_)
            self.tt('dve', a('cim'), a('t1'), a('den'), ALU.mult, r=K_, w=K_)
            self.cp('dve', phr[:, 0, :], a('cs'), r=K_, w=K_)
            self.cp('dve', phi[:, 0, :], a('sn'), r=K_, w=K_)
            for j in range(1, 9):
                self.tt('dve', a('t1'), phr[:, j - 1, :], phr[:, j - 1, :], ALU.mult, r=K_, w=K_)
                self.tt('dve', a('t2'), phi[:, j - 1, :], phi[:, j - 1, :], ALU.mult, r=K_, w=K_)
                self.tt('dve', phr[:, j, :], a('t1'), a('t2'), ALU.subtract, r=K_, w=K_)
                self.tt('dve', a('t1'), phr[:, j - 1, :], phi[:, j - 1, :], ALU.mult, r=K_, w=K_)
                self.ts('dve', phi[:, j, :], a('t1'), 2.0, None, ALU.mult, None, r=K_, w=K_)

            wu = [sb(f"s5_wu{i}", [128, 8, 128], BF16) for i in range(2)]
            hx = [sb(f"s5_hx{i}", [128, 8, 512], BF16) for i in range(2)]
            uT = [sb(f"s5_uT{i}", [128, T], BF16) for i in range(2)]
            BT = [sb(f"s5_BT{i}", [128, 16, 128], BF16) for i in range(2)]
            CM = [sb(f"s5_CM{i}", [128, 16, 128], BF16) for i in range(2)]
            ybuf = sb("s5_y", [128, T], F32)
            gq = [sb(f"s5_g{i}", [128, T], BF16) for i in range(2)]
            tab = [{n: sb(f"s5_tab{il}_{n}", [128, Lc], F32) for n in ('DTr', 'DTi', 'nDTi', 'nDTr', 'MTr', 'MTi', 'nMTi', 'Rc')}
                   for il in range(4)]
            ttmp = sb("s5_ttmp", [128, Lc], F32)
            NZ = 8
            zb = [{n: sb(f"s5_z{i}_{n}", [128, Lc], F32) for n in ('gr', 'gi')} for i in range(NZ)]
            zp = [{n: sb(f"s5_zp{i}_{n}", [128, Lc], BF16) for n in ('pa', 'pb', 'pc', 'pd')} for i in range(NZ)]
            db = [{n: sb(f"s5_d{i}_{n}", [128, Lc], F32) for n in ('t1', 't2', 't3', 't4')} for i in range(2)]
            hb = [[sb(f"s5_h{il}_{i}", [128, 4, Lc], BF16) for i in range(2)] for il in range(4)]
            identf = sb("s5_identf", [128, 128], F32)
            P.dma('sp', identf[:], I['ident'].ap(), w=['identf'])
            ini = [[sb(f"s5_ini{il}_{i}", [128, 2], F32) for i in range(2)] for il in range(4)]
            itmp = [sb(f"s5_itmp{i}", [128, 2], F32) for i in range(4)]
            nphi = sb("s5_nphi", [128, 32], F32)
            self.ts('dve', nphi[:], phi[:, 8, :], -1.0, None, ALU.mult, None, r=K_, w=K_)

            hsrc = S['hxT'].ap().rearrange("(k p) t -> p k t", p=128)
            win = I['w_in'].ap()[l].rearrange("(k p) c -> p k c", p=128)
            for q in range(4):
                us = self.rot('s5_u', 2)
                self.load_w(wu[us][:], win[:, :, OFF_U + q * 128: OFF_U + (q + 1) * 128], ('s5_wu', us))
                for d in range(2):
                    for c in range(2):
                        self.load_w(BT[us][:, d * 8 + c * 4: d * 8 + c * 4 + 4, :] if False else
                                    BT[us][:].rearrange("p (d i c) k -> p d i c k", d=2, i=4)[:, d, :, c, :],
                                    I['s5BT'].ap()[l, d, 4 * q:4 * q + 4, c].rearrange("i r k -> r i k"),
                                    ('s5_BT', us, d, c))
                        self.load_w(CM[us][:].rearrange("p (d i c) k -> p d i c k", d=2, i=4)[:, d, :, c, :],
                                    I['s5CM'].ap()[l, d, 4 * q:4 * q + 4, c].rearrange("i r k -> r i k"),
                                    ('s5_CM', us, d, c))
                BTv = BT[us][:].rearrange("p (d i c) k -> p d i c k", d=2, i=4)
                CMv = CM[us][:].rearrange("p (d i c) k -> p d i c k", d=2, i=4)
                ukey = ('s5_uT', us)
                for (t0, n) in self.TT512:
                    hs = self.rot('s5_hx', 2)
                    P.dma('sp', hx[hs][:, :, 0:n], hsrc[:, :, t0:t0 + n], w=[('s5_hx', hs)])
                    for k in range(8):
                        self.mm(self.ps[6][:, 0:n], wu[us][:, k, :], hx[hs][:, k, 0:n], k == 0, k == 7,
                                r=[('s5_wu', us), ('s5_hx', hs)], w=[('ps', 6)])
                    self.cp('act', uT[us][:, t0:t0 + n], self.ps[6][:, 0:n], r=[('ps', 6)], w=[ukey])
                for d in range(2):
                    for il in range(4):
                        c = d * 16 + 4 * q + il
                        tb = tab[il]
                        tk = ('s5_tab', il)
                        sc = lambda arr, j=None: (arr[:, c:c + 1] if j is None else arr[:, j, c:c + 1])
                        P.op('dve', lambda e, tb=tb: e.memset(tb['DTr'][:, 0:1], 1.0), w=[tk])
                        P.op('dve', lambda e, tb=tb: e.memset(tb['DTi'][:, 0:1], 0.0), w=[tk])
                        for j in range(8):
                            n = 1 << j
                            pr, pi = sc(phr, j), sc(phi, j)
                            self.ts('dve', ttmp[:, 0:n], tb['DTi'][:, 0:n], pi, None, ALU.mult, None,
                                    r=[tk, 'prm'], w=['ttmp'])
                            self.stt(tb['DTr'][:, n:2 * n], tb['DTr'][:, 0:n], pr, ttmp[:, 0:n], ALU.mult, ALU.subtract,
                                     r=[tk, 'prm', 'ttmp'], w=[tk])
                            self.ts('dve', ttmp[:, 0:n], tb['DTi'][:, 0:n], pr, None, ALU.mult, None,
                                    r=[tk, 'prm'], w=['ttmp'])
                            self.stt(tb['DTi'][:, n:2 * n], tb['DTr'][:, 0:n], pi, ttmp[:, 0:n], ALU.mult, ALU.add,
                                     r=[tk, 'prm', 'ttmp'], w=[tk])
                        cre, cim = sc(A['cre'][:]), sc(A['cim'][:])
                        self.ts('dve', ttmp[:], tb['DTi'][:], cim, None, ALU.mult, None, r=[tk, 'prm'], w=['ttmp'])
                        self.stt(tb['MTr'][:], tb['DTr'][:], cre, ttmp[:], ALU.mult, ALU.add, r=[tk, 'prm', 'ttmp'], w=[tk])
                        self.ts('dve', ttmp[:], tb['DTi'][:], cre, None, ALU.mult, None, r=[tk, 'prm'], w=['ttmp'])
                        self.stt(tb['MTi'][:], tb['DTr'][:], cim, ttmp[:], ALU.mult, ALU.subtract, r=[tk, 'prm', 'ttmp'], w=[tk])
                        self.ts('pool', tb['nDTi'][:], tb['DTi'][:], -1.0, None, ALU.mult, None, r=[tk], w=[tk])
                        self.ts('pool', tb['nDTr'][:], tb['DTr'][:], -1.0, None, ALU.mult, None, r=[tk], w=[tk])
                        self.ts('pool', tb['nMTi'][:], tb['MTi'][:], -1.0, None, ALU.mult, None, r=[tk], w=[tk])
                        self.ts('dve', tb['Rc'][:], zero[:], sc(A['r1'][:]), None, ALU.add, None, r=['zero', 'prm'], w=[tk])
                        P.op('dve', lambda e, il=il: e.memset(ini[il][0][:], 0.0), w=[('s5_ini', il, 0)])
                    order = list(range(NCH)) if d == 0 else [0] + list(range(NCH - 1, 0, -1))

                    def emit_y(kk, cs):
                        tok0 = cs * Lc
                        psY = self.ps[4]
                        ykey = ('ps', 4)
                        hsl = kk % 2
                        for il in range(4):
                            for j in range(4):
                                self.mm(psY[:, 0:Lc], CMv[:, d, il, j // 2, :], hb[il][hsl][:, j, :],
                                        il == 0 and j == 0, il == 3 and j == 3,
                                        r=[('s5_CM', us, d, j // 2), (('s5_h', il, hsl), j)], w=[ykey])
                        if d == 0:
                            self.stt(ybuf[:, tok0:tok0 + Lc], uT[us][:, tok0:tok0 + Lc], dsk[:, q:q + 1], psY[:, 0:Lc],
                                     ALU.mult, ALU.add, r=[ukey, 'dsk', ykey], w=[('s5_y', cs)])
                        else:
                            self.tt('dve', ybuf[:, tok0:tok0 + Lc], psY[:, 0:Lc], ybuf[:, tok0:tok0 + Lc], ALU.add,
                                    r=[ykey, ('s5_y', cs)], w=[('s5_y', cs)])

                    pend = None
                    for kk, cs in enumerate(order):
                        tok0 = cs * Lc
                        par = kk % 2
                        hsl = kk % 2
                        ctxs = []
                        def emit_bu(il_):
                            bs_ = self.rot('s5_psB', 2)
                            psB_ = self.ps[bs_]
                            for cc in range(2):
                                self.mm(psB_[:, cc * Lc:(cc + 1) * Lc], BTv[:, d, il_, cc, :], uT[us][:, tok0:tok0 + Lc],
                                        True, True, r=[('s5_BT', us, d, cc), ukey], w=[('ps', bs_)])
                            return bs_
                        nxt_bs = emit_bu(0)
                        for il in range(4):
                            bs = nxt_bs
                            if il < 3:
                                nxt_bs = emit_bu(il + 1)
                            psB = self.ps[bs]
                            bkey = ('ps', bs)
                            if d == 0:
                                bre, bim = psB[:, 0:Lc], psB[:, Lc:2 * Lc]
                            else:
                                bre, bim = psB[:, Lc - 1::-1][:, 0:Lc], psB[:, 2 * Lc - 1:Lc - 1:-1]
                            zs = self.rot('s5_z', NZ)
                            z, zk, tb, tk = zb[zs], ('s5_z', zs), tab[il], ('s5_tab', il)
                            self.tt('dve', zp[zs]['pa'][:], bre, tb['MTr'][:], ALU.mult, r=[bkey, tk], w=[(zk, 'pa')])
                            self.tt('dve', zp[zs]['pb'][:], bim, tb['nMTi'][:], ALU.mult, r=[bkey, tk], w=[(zk, 'pb')])
                            self.tt('dve', zp[zs]['pc'][:], bre, tb['MTi'][:], ALU.mult, r=[bkey, tk], w=[(zk, 'pc')])
                            self.tt('dve', zp[zs]['pd'][:], bim, tb['MTr'][:], ALU.mult, r=[bkey, tk], w=[(zk, 'pd')])
                            zbank = (2, 3, 5, 6)[il]
                            psZ = self.ps[zbank]
                            zkey = ('ps', zbank)
                            self.mm(psZ[:, 0:Lc], self.ident[:], zp[zs]['pa'][:], True, False, r=['ident', (zk, 'pa')], w=[zkey])
                            self.mm(psZ[:, 0:Lc], self.ident[:], zp[zs]['pb'][:], False, True, r=['ident', (zk, 'pb')], w=[zkey])
                            self.mm(psZ[:, Lc:2 * Lc], self.ident[:], zp[zs]['pc'][:], True, False, r=['ident', (zk, 'pc')], w=[zkey])
                            self.mm(psZ[:, Lc:2 * Lc], self.ident[:], zp[zs]['pd'][:], False, True, r=['ident', (zk, 'pd')], w=[zkey])
                            ctxs.append(dict(il=il, z=z, zk=zk, gk=('s5_g', zs), tb=tb, tk=tk, psZ=psZ, zkey=zkey,
                                             c=d * 16 + 4 * q + il))
                        for cx in ctxs:
                            z, tb, tk, gk, il, psZ, zkey = cx['z'], cx['tb'], cx['tk'], cx['gk'], cx['il'], cx['psZ'], cx['zkey']
                            ik = ('s5_ini', il, par)
                            iv = ini[il][par]
                            P.op('dve', lambda e, z=z, tb=tb, iv=iv, psZ=psZ: e.tensor_tensor_scan(
                                z['gr'][:], tb['Rc'][:], psZ[:, 0:Lc], iv[:, 0:1], ALU.mult, ALU.add),
                                r=[zkey, tk, ik], w=[(gk, 'r')])
                            P.op('dve', lambda e, z=z, tb=tb, iv=iv, psZ=psZ: e.tensor_tensor_scan(
                                z['gi'][:], tb['Rc'][:], psZ[:, Lc:2 * Lc], iv[:, 1:2], ALU.mult, ALU.add),
                                r=[zkey, tk, ik], w=[(gk, 'i')])
                        for cx in ctxs:
                            z, gk, il, c = cx['z'], cx['gk'], cx['il'], cx['c']
                            ink = ('s5_ini', il, 1 - par)
                            inx = ini[il][1 - par]
                            pr, pi, npi = phr[:, 8, c:c + 1], phi[:, 8, c:c + 1], nphi[:, c:c + 1]
                            gre, gie = z['gr'][:, Lc - 1:Lc], z['gi'][:, Lc - 1:Lc]
                            itk = ('itmp', il)
                            self.act(itmp[il][:, 0:1], gie, AF.Identity, r=[(gk, 'i'), 'prm'], w=[itk], scale=npi)
                            self.act(itmp[il][:, 1:2], gie, AF.Identity, r=[(gk, 'i'), 'prm'], w=[itk], scale=pr)
                            self.act(inx[:, 0:1], gre, AF.Identity, r=[(gk, 'r'), 'prm', itk], w=[ink], scale=pr,
                                     bias=itmp[il][:, 0:1])
                            self.act(inx[:, 1:2], gre, AF.Identity, r=[(gk, 'r'), 'prm', itk], w=[ink], scale=pi,
                                     bias=itmp[il][:, 1:2])
                        for cx in ctxs:
                            z, tb, tk, gk, il = cx['z'], cx['tb'], cx['tk'], cx['gk'], cx['il']
                            hk = ('s5_h', il, hsl)
                            hv = [hb[il][hsl][:, j, :] if d == 0 else hb[il][hsl][:, j, ::-1] for j in range(4)]
                            self.tt('pool', hv[0], z['gr'][:], tb['DTr'][:], ALU.mult, r=[(gk, 'r'), tk], w=[(hk, 0)])
                            self.tt('pool', hv[1], z['gi'][:], tb['nDTi'][:], ALU.mult, r=[(gk, 'i'), tk], w=[(hk, 1)])
                            self.tt('pool', hv[2], z['gr'][:], tb['nDTi'][:], ALU.mult, r=[(gk, 'r'), tk], w=[(hk, 2)])
                            self.tt('dve' if il < 2 else 'pool', hv[3], z['gi'][:], tb['nDTr'][:], ALU.mult, r=[(gk, 'i'), tk], w=[(hk, 3)])
                        if pend is not None:
                            emit_y(*pend)
                        pend = (kk, cs)
                    emit_y(*pend)
                gs_ = self.rot('s5_gq', 2)
                for cs in range(NCH):
                    self.act(gq[gs_][:, cs * Lc:(cs + 1) * Lc], ybuf[:, cs * Lc:(cs + 1) * Lc], AF.Gelu_apprx_tanh,
                             r=[('s5_y', cs)], w=[('s5_gq', gs_)])
                P.dma('act', S['gT'].ap()[q * 128:(q + 1) * 128, :], gq[gs_][:], r=[('s5_gq', gs_)])

    def phase_merge(self, l, src):
        nc, P, I, S = self.nc, self.P, self.I, self.S
        last = (l == DEPTH - 1)
        tiles = self.TILES[1:] if last else self.TILES
        with ExitStack() as st:
            sb = lambda n, s, d: st.enter_context(nc.sbuf_tensor(f"{n}_u{self.uid()}", s, d))
            wco = sb("mg_wco", [128, 4, D], BF16)
            wga = sb("mg_wga", [128, 4, D], BF16)
            wgb = sb("mg_wgb", [128, 4, D], BF16)
            wno = sb("mg_wno", [128, 4, D], BF16)
            wg = sb("mg_wg", [128, 8, 3 * D], BF16)
            hx = [sb(f"mg_hx{i}", [128, 8, 512], BF16) for i in range(2)]
            br = [[sb(f"mg_br{j}_{i}", [128, 4, 512], BF16) for i in range(2)] for j in range(3)]
            mg = [sb(f"mg_mg{i}", [128, 8, 512], BF16) for i in range(2)]
            sg = [sb(f"mg_sg{i}", [128, 512], F32) for i in range(4)]
            tm = [sb(f"mg_tm{i}", [128, 512], F32) for i in range(4)]
            acc = [sb(f"mg_acc{i}", [128, 512], F32) for i in range(2)]
            for wt_, nm in ((wco, 'conv_out'), (wga, 's5_glu_a'), (wgb, 's5_glu_b'), (wno, 'na_out')):
                for kc in range(4):
                    self.load_w(wt_[:, kc, :], I[nm].ap()[l][kc * 128:(kc + 1) * 128, :], ('mg_w', nm))
            win = I['w_in'].ap()[l].rearrange("(k p) c -> p k c", p=128)
            for k in range(8):
                self.load_w(wg[:, k, :], win[:, k, OFF_GA:OFF_GA + 3 * D], 'mg_wg')
            hsrc = S['hxT'].ap().rearrange("(k p) t -> p k t", p=128)
            bsrc = [S[nm].ap().rearrange("(k p) t -> p k t", p=128) for nm in ('convbT', 'gT', 'attnT')]
            mdst = S['mgT'].ap().rearrange("(k p) t -> p k t", p=128)
            for (t0, n) in tiles:
                hs = self.rot('mg_hx', 2)
                P.dma('sp', hx[hs][:, :, 0:n], hsrc[:, :, t0:t0 + n], w=[('mg_hx', hs)])
                for j in range(3):
                    P.dma('sp', br[j][hs][:, :, 0:n], bsrc[j][:, :, t0:t0 + n], w=[('mg_br', j, hs)])
                ms = self.rot('mg_mg', 2)
                for fo in range(8):
                    fs = slice(fo * 128, (fo + 1) * 128)

                    def proj(bank, w_, x_, nk, wkeys, xkey, coff=0):
                        for k in range(nk):
                            self.mm(self.ps[bank][:, 0:n], w_[:, k, coff + fo * 128: coff + (fo + 1) * 128],
                                    x_[:, k, 0:n], k == 0, k == nk - 1, r=wkeys + [xkey], w=[('ps', bank)])
                    hk = ('mg_hx', hs)
                    proj(0, wco, br[0][hs], 4, [('mg_w', 'conv_out')], ('mg_br', 0, hs))
                    proj(1, wg, hx[hs], 8, ['mg_wg'], hk, 0)
                    proj(2, wga, br[1][hs], 4, [('mg_w', 's5_glu_a')], ('mg_br', 1, hs))
                    proj(3, wgb, br[1][hs], 4, [('mg_w', 's5_glu_b')], ('mg_br', 1, hs))
                    proj(4, wg, hx[hs], 8, ['mg_wg'], hk, D)
                    proj(5, wno, br[2][hs], 4, [('mg_w', 'na_out')], ('mg_br', 2, hs))
                    proj(6, wg, hx[hs], 8, ['mg_wg'], hk, 2 * D)
                    a_ = self.rot('mg_acc', 2)
                    A = acc[a_][:, 0:n]
                    ak = ('mg_acc', a_)
                    sgs = [self.rot('mg_sg', 4) for _ in range(4)]
                    tms = [self.rot('mg_tm', 4) for _ in range(2)]
                    self.act(sg[sgs[0]][:, 0:n], self.ps[1][:, 0:n], AF.Sigmoid, r=[('ps', 1)], w=[('mg_sg', sgs[0])])
                    self.tt('dve', A, self.ps[0][:, 0:n], sg[sgs[0]][:, 0:n], ALU.mult,
                            r=[('ps', 0), ('mg_sg', sgs[0])], w=[ak])
                    self.act(sg[sgs[1]][:, 0:n], self.ps[3][:, 0:n], AF.Sigmoid, r=[('ps', 3)], w=[('mg_sg', sgs[1])])
                    self.tt('dve', tm[tms[0]][:, 0:n], self.ps[2][:, 0:n], sg[sgs[1]][:, 0:n], ALU.mult,
                            r=[('ps', 2), ('mg_sg', sgs[1])], w=[('mg_tm', tms[0])])
                    self.act(sg[sgs[2]][:, 0:n], self.ps[4][:, 0:n], AF.Sigmoid, r=[('ps', 4)], w=[('mg_sg', sgs[2])])
                    self.tt('pool', tm[tms[0]][:, 0:n], tm[tms[0]][:, 0:n], sg[sgs[2]][:, 0:n], ALU.mult,
                            r=[('mg_tm', tms[0]), ('mg_sg', sgs[2])], w=[('mg_tm', tms[0])])
                    self.tt('pool', A, A, tm[tms[0]][:, 0:n], ALU.add, r=[ak, ('mg_tm', tms[0])], w=[ak])
                    self.act(sg[sgs[3]][:, 0:n], self.ps[6][:, 0:n], AF.Sigmoid, r=[('ps', 6)], w=[('mg_sg', sgs[3])])
                    self.tt('dve', tm[tms[1]][:, 0:n], self.ps[5][:, 0:n], sg[sgs[3]][:, 0:n], ALU.mult,
                            r=[('ps', 5), ('mg_sg', sgs[3])], w=[('mg_tm', tms[1])])
                    self.tt('pool', mg[ms][:, fo, 0:n], A, tm[tms[1]][:, 0:n], ALU.add,
                            r=[ak, ('mg_tm', tms[1])], w=[('mg_mg', ms)])
                P.dma('act', mdst[:, :, t0:t0 + n], mg[ms][:, :, 0:n], r=[('mg_mg', ms)])
        P.barrier()
        with ExitStack() as st:
            sb = lambda n, s, d: st.enter_context(nc.sbuf_tensor(f"{n}_u{self.uid()}", s, d))
            wo = sb("mo_wo", [128, 8, D], BF16)
            wov = I['w_out'].ap()[l].rearrange("(k p) c -> p k c", p=128)
            scol2 = (sb("mo_sc", [128, 8], F32), 'mo_sc')
            mgt = [sb(f"mo_mg{i}", [128, 8, 512], BF16) for i in range(2)]
            xt = [sb(f"mo_x{i}", [128, D], F32) for i in range(2)]
            xn = [sb(f"mo_xn{i}", [128, D], F32) for i in range(2)]
            hb = [sb(f"mo_hb{i}", [128, D], BF16) for i in range(2)]
            hT = [sb(f"mo_hT{i}", [128, 8, 512], BF16) for i in range(2)]
            scr = self.norm_scratch(st, "mo")
            msrc = S['mgT'].ap().rearrange("(k p) t -> p k t", p=128)
            hdst = S['h2T'].ap().rearrange("(k p) t -> p k t", p=128)
            mods = None
            cur = None
            for (t0, n) in tiles:
                cond = 1 if t0 < CTX else 0
                if cur != cond:
                    if mods is None:
                        mods = self.mod_tiles(st, l, cond, ('A1', 'G2'))
                        gt2 = st.enter_context(nc.sbuf_tensor(f"mo_g2_u{self.uid()}", [128, D], F32))
                    else:
                        self.reload_mod(mods, l, cond, gt2)
                    self.load_col(scol2[0], l, cond, 'S2', 'mo_sc')
                    for k in range(8):
                        self.load_w(wo[:, k, :], wov[:, k, :], ('mo_wo', k))
                        self.tt('pool' if k % 2 else 'dve', wo[:, k, :], wo[:, k, :], mods['A1'][0][:], ALU.mult,
                                r=[('mo_wo', k), mods['A1'][1]], w=[('mo_wo', k)])
                    cur = cond
                hs = self.rot('mo_mg', 2)
                P.dma('sp', mgt[hs][:, :, 0:n], msrc[:, :, t0:t0 + n], w=[('mo_mg', hs)])
                ts_ = self.rot('mo_hT', 2)
                for sub in range(n // 128):
                    xs = self.rot('mo_x', 2)
                    r0 = t0 + sub * 128
                    P.dma('sp', xt[xs][:], src.ap()[r0:r0 + 128, :], w=[('mo_x', xs)])
                    for half in range(2):
                        for k in range(8):
                            self.mm(self.ps[half][:, :], mgt[hs][:, k, sub * 128:(sub + 1) * 128],
                                    wo[:, k, half * 512:(half + 1) * 512], k == 0, k == 7,
                                    r=[('mo_mg', hs), ('mo_wo', k)], w=[('ps', half)])
                        self.tt('dve', xn[xs][:, half * 512:(half + 1) * 512], self.ps[half][:, :],
                                xt[xs][:, half * 512:(half + 1) * 512], ALU.add,
                                r=[('ps', half), ('mo_x', xs)], w=[('mo_xn', xs)])
                    P.dma('act', S['xres'].ap()[r0:r0 + 128, :], xn[xs][:], r=[('mo_xn', xs)])
                    bs = self.rot('mo_hb', 2)
                    self.norm_sub(xn[xs][:], ('mo_xn', xs), mods['G2'], None, hb[bs][:], ('mo_hb', bs), scr[bs])
                    self.transpose_out(hb[bs], ('mo_hb', bs), hT[ts_], ('mo_hT', ts_), sub, scol=scol2)
                P.dma('act', hdst[:, :, t0:t0 + n], hT[ts_][:, :, 0:n], r=[('mo_hT', ts_)])

    def reload_mod(self, mods, l, cond, gtmp):
        base = (l * 2 + cond) * 6 * D
        idx = {'S1': 0, 'G1': 1, 'A1': 2, 'S2': 3, 'G2': 4, 'A2': 5}
        for n, (t, key) in mods.items():
            self.load_rep(t[:], self.S['modv'], base + idx[n] * D, key)
            if n in ('G1', 'G2'):
                gt = self.I['norm1_g'] if n == 'G1' else self.I['norm2_g']
                self.load_rep(gtmp[:], gt, l * D, 'modgtmp')
                self.stt(t[:], t[:], 1.0, gtmp[:], ALU.add, ALU.mult, r=[key, 'modgtmp'], w=[key])

    def phase_mlp(self, l):
        nc, P, I, S = self.nc, self.P, self.I, self.S
        last = (l == DEPTH - 1)
        tiles = self.TILES[1:] if last else self.TILES
        st0 = ExitStack()
        w2 = st0.enter_context(nc.sbuf_tensor(f"ml_w2_u{self.uid()}", [128, 32, D], BF16))
        w2v = I['mlp_w2'].ap()[l].rearrange("(k p) c -> p k c", p=128)
        with ExitStack() as st:
            sb = lambda n, s, d: st.enter_context(nc.sbuf_tensor(f"{n}_u{self.uid()}", s, d))
            w1 = sb("ml_w1", [128, 8, 4 * D], BF16)
            w1v = I['mlp_w1'].ap()[l].rearrange("(k p) c -> p k c", p=128)
            for k in range(8):
                for hf in range(2):
                    self.load_w(w1[:, k, hf * 2048:(hf + 1) * 2048], w1v[:, k, hf * 2048:(hf + 1) * 2048], 'ml_w1')
            for k in range(32):
                self.load_w(w2[:, k, :], w2v[:, k, :], ('ml_w2', k))
            h2 = [sb(f"ml_h2{i}", [128, 8, 512], BF16) for i in range(2)]
            rl = [sb(f"ml_rl{i}", [128, 512], F32) for i in range(3)]
            hd = [sb(f"ml_hd{i}", [128, 8, 512], BF16) for i in range(2)]
            hsrc = S['h2T'].ap().rearrange("(k p) t -> p k t", p=128)
            ddst = S['hidT'].ap().rearrange("(k p) t -> p k t", p=128)
            for (t0, n) in tiles:
                hs = self.rot('ml_h2', 2)
                P.dma('sp', h2[hs][:, :, 0:n], hsrc[:, :, t0:t0 + n], w=[('ml_h2', hs)])
                for fg in range(4):
                    ds = self.rot('ml_hd', 2)
                    for fi in range(8):
                        fc = fg * 8 + fi
                        bank = self.rot('ml_bank', 6)
                        for k in range(8):
                            self.mm(self.ps[bank][:, 0:n], w1[:, k, fc * 128:(fc + 1) * 128], h2[hs][:, k, 0:n],
                                    k == 0, k == 7, r=['ml_w1', ('ml_h2', hs)], w=[('ps', bank)])
                        rs = self.rot('ml_rl', 3)
                        self.act(rl[rs][:, 0:n], self.ps[bank][:, 0:n], AF.Relu, r=[('ps', bank)], w=[('ml_rl', rs)])
                        self.tt('dve' if fi % 2 == 0 else 'pool', hd[ds][:, fi, 0:n], rl[rs][:, 0:n], rl[rs][:, 0:n],
                                ALU.mult, r=[('ml_rl', rs)], w=[('ml_hd', ds)])
                    P.dma('act', ddst[:, fg * 8:(fg + 1) * 8, t0:t0 + n], hd[ds][:, :, 0:n], r=[('ml_hd', ds)])
        P.barrier()
        with ExitStack() as st:
            sb = lambda n, s, d: st.enter_context(nc.sbuf_tensor(f"{n}_u{self.uid()}", s, d))
            scol1 = (sb("ml_sc", [128, 8], F32), 'ml_sc')
            hdt = [sb(f"ml_hdt{i}", [128, 32, 256], BF16) for i in range(2)]
            xt = [sb(f"ml_x{i}", [128, D], F32) for i in range(2)]
            xn = [sb(f"ml_xn{i}", [128, D], F32) for i in range(2)]
            hb = [sb(f"ml_hb{i}", [128, D], BF16) for i in range(2)]
            hT = [sb(f"ml_hT{i}", [128, 8, 256], BF16) for i in range(2)]
            ob = [sb(f"ml_ob{i}", [128, D], F32) for i in range(2)]
            scr = self.norm_scratch(st, "ml")
            dsrc = S['hidT'].ap().rearrange("(k p) t -> p k t", p=128)
            hdst = S['hxT'].ap().rearrange("(k p) t -> p k t", p=128)
            fin = None
            if last:
                fin = sb("ml_fin", [128, D], F32)
                self.load_rep(fin[:], I['final_g'], 0, 'ml_fin')
            amods = None
            nmods = None
            cur = None
            tiles256 = []
            for (t0, n) in tiles:
                for h in range(n // 256):
                    tiles256.append((t0 + h * 256, 256))
            for (t0, n) in tiles256:
                cond = 1 if t0 < CTX else 0
                if cur != cond:
                    if amods is None:
                        amods = self.mod_tiles(st, l, cond, ('A2',))
                        gt1 = st.enter_context(nc.sbuf_tensor(f"ml_g1_u{self.uid()}", [128, D], F32))
                        if not last:
                            nmods = self.mod_tiles(st, l + 1, cond, ('G1',))
                    else:
                        self.reload_mod(amods, l, cond, gt1)
                        if not last:
                            self.reload_mod(nmods, l + 1, cond, gt1)
                    if not last:
                        self.load_col(scol1[0], l + 1, cond, 'S1', 'ml_sc')
                    if cond == 0:
                        for k in range(32):
                            self.tt('pool' if k % 2 else 'dve', w2[:, k, :], w2[:, k, :], amods['A2'][0][:], ALU.mult,
                                    r=[('ml_w2', k), amods['A2'][1]], w=[('ml_w2', k)])
                    cur = cond
                hs = self.rot('ml_hdt', 2)
                for kq in range(4):
                    P.dma('sp', hdt[hs][:, kq * 8:(kq + 1) * 8, :], dsrc[:, kq * 8:(kq + 1) * 8, t0:t0 + n],
                          w=[('ml_hdt', hs)])
                ts_ = self.rot('ml_hT', 2)
                for sub in range(2):
                    xs = self.rot('ml_x', 2)
                    r0 = t0 + sub * 128
                    P.dma('sp', xt[xs][:], S['xres'].ap()[r0:r0 + 128, :], w=[('ml_x', xs)])
                    for half in range(2):
                        bank = self.rot('ml_bank2', 4)
                        for k in range(32):
                            self.mm(self.ps[bank][:, :], hdt[hs][:, k, sub * 128:(sub + 1) * 128],
                                    w2[:, k, half * 512:(half + 1) * 512], k == 0, k == 31,
                                    r=[('ml_hdt', hs), ('ml_w2', k)], w=[('ps', bank)])
                        hsl_ = slice(half * 512, (half + 1) * 512)
                        if cond == 0:
                            self.tt('dve', xn[xs][:, hsl_], self.ps[bank][:, :], xt[xs][:, hsl_], ALU.add,
                                    r=[('ps', bank), ('ml_x', xs)], w=[('ml_xn', xs)])
                        else:
                            self.tt('dve', xn[xs][:, hsl_], self.ps[bank][:, :], amods['A2'][0][:, hsl_], ALU.mult,
                                    r=[('ps', bank), amods['A2'][1]], w=[('ml_xn', xs)])
                            self.tt('dve', xn[xs][:, hsl_], xn[xs][:, hsl_], xt[xs][:, hsl_], ALU.add,
                                    r=[('ml_xn', xs), ('ml_x', xs)], w=[('ml_xn', xs)])
                    bs = self.rot('ml_hb', 2)
                    if not last:
                        P.dma('act', S['xres'].ap()[r0:r0 + 128, :], xn[xs][:], r=[('ml_xn', xs)])
                        self.norm_sub(xn[xs][:], ('ml_xn', xs), nmods['G1'], None, hb[bs][:], ('ml_hb', bs), scr[bs])
                        self.transpose_out(hb[bs], ('ml_hb', bs), hT[ts_], ('ml_hT', ts_), sub, scol=scol1)
                    else:
                        self.norm_sub(xn[xs][:], ('ml_xn', xs), (fin, 'ml_fin'), None, ob[bs][:], ('ml_ob', bs), scr[bs])
                        P.dma('act', self.out.ap()[r0 - CTX:r0 - CTX + 128, :], ob[bs][:], r=[('ml_ob', bs)])
                if not last:
                    P.dma('act', hdst[:, :, t0:t0 + n], hT[ts_][:, :, 0:n], r=[('ml_hT', ts_)])
        st0.close()

def _core_inputs(inp, sh, b):
    d = dict(sh)
    d['xin'] = np.ascontiguousarray(np.concatenate([inp['ctx'][b], inp['x'][b]], axis=0), dtype=np.float32)
    cT = np.stack([inp['c'][b].reshape(8, 128).T, inp['c_ctx'].reshape(8, 128).T], axis=2)
    d['cT'] = np.ascontiguousarray(cT.reshape(128, 16), dtype=np.float32)
    return d


def kernel(**inputs):
    inp = {k: np.asarray(v) for k, v in inputs.items()}
    sh = _prep_shared(inp)
    bld = Builder()
    nc = bld.build()
    in_maps = [_core_inputs(inp, sh, b) for b in range(8)]
    in_maps = [{k: m[k] for k in bld.inputs} for m in in_maps]
    res = run_bass_kernel_spmd(nc, in_maps, core_ids=list(range(8)))
    return np.stack([np.asarray(r['out']) for r in res.results], axis=0).astype(np.float32)
```
